# Optimizing a Trainium2 kernel written in Bass

```python
import math
import jax, jax.numpy as jnp
from jax import lax
import numpy as np

D_MODEL = 1024
BATCH = 8
SEQ = 4096
DEPTH = 4

GRID_W = 64
CTX_LEN = 256
EPS = 1e-6

MLA_HEADS = 4
MLA_Q_LORA = 256
MLA_KV_LORA = 128
MLA_NOPE = 128
MLA_ROPE = 64
MLA_V = 128
MLA_WIDTH = MLA_HEADS * MLA_V
MLA_SCALE = (MLA_NOPE + MLA_ROPE) ** -0.5
ROPE_BASE = 10000.0
Q_BLOCK = 128

CM_GROUPS = 4
CM_WIDTH = 256
CM_GROUP_DIM = CM_WIDTH // CM_GROUPS
CM_CHUNK = 128

SSD_WIDTH = 256
SSD_HEAD_DIM = 64
SSD_HEADS = SSD_WIDTH // SSD_HEAD_DIM
SSD_GROUPS = 2
SSD_STATE = 128
SSD_CONV = 3
SSD_CHUNK = 128
SSD_CONV_DIM = SSD_WIDTH + 2 * SSD_GROUPS * SSD_STATE

MIX_WIDTH = MLA_WIDTH + CM_WIDTH + SSD_WIDTH
D_FF = 4 * D_MODEL

IN_PARTS = [MLA_Q_LORA, MLA_KV_LORA, MLA_ROPE, 2 * CM_WIDTH, SSD_WIDTH, SSD_CONV_DIM, 2 * SSD_HEADS]
IN_WIDTH = sum(IN_PARTS)
IN_SPLIT_IDX = [int(i) for i in np.cumsum(IN_PARTS)[:-1]]

kernel_name = 'hybrid_mla_chunkmlp_ssd_dit_trunk'


def rmsnorm(x, g):
    xf = x.astype(jnp.float32)
    y = xf * lax.rsqrt(jnp.mean(xf * xf, axis=-1, keepdims=True) + EPS)
    return (y * g.astype(jnp.float32)).astype(x.dtype)


def layernorm(x, g):
    xf = x.astype(jnp.float32)
    mu = jnp.mean(xf, axis=-1, keepdims=True)
    var = jnp.mean(jnp.square(xf - mu), axis=-1, keepdims=True)
    return ((xf - mu) * lax.rsqrt(var + EPS) * g.astype(jnp.float32)).astype(x.dtype)


def modulate(h, shift, scale):
    return h * (1 + scale) + shift


def split_projection(p):
    return jnp.split(p, IN_SPLIT_IDX, axis=-1)


def axial_rope_angles(n_tokens):
    rows_n = n_tokens // GRID_W
    row = jnp.repeat(jnp.arange(rows_n, dtype=jnp.float32), GRID_W)
    col = jnp.tile(jnp.arange(GRID_W, dtype=jnp.float32), rows_n)
    axis_dim = MLA_ROPE // 2
    inv_freq = ROPE_BASE ** (-jnp.arange(0, axis_dim, 2, dtype=jnp.float32) / axis_dim)
    return row[:, None] * inv_freq, col[:, None] * inv_freq


def rotate_half(x, ang):
    x1, x2 = jnp.split(x, 2, axis=-1)
    cos = jnp.cos(ang)[None, :, None, :].astype(x.dtype)
    sin = jnp.sin(ang)[None, :, None, :].astype(x.dtype)
    return jnp.concatenate([x1 * cos - x2 * sin, x1 * sin + x2 * cos], axis=-1)


def apply_axial_rope(x, ang_row, ang_col):
    xr, xc = jnp.split(x, 2, axis=-1)
    return jnp.concatenate([rotate_half(xr, ang_row), rotate_half(xc, ang_col)], axis=-1)


def mla_q(x_q, g_q, w_uq):
    b, l, _ = x_q.shape
    q = (rmsnorm(x_q, g_q) @ w_uq).reshape(b, l, MLA_HEADS, MLA_NOPE + MLA_ROPE)
    return q[..., :MLA_NOPE], q[..., MLA_NOPE:]


def mla_kv(x_kv, x_kr, g_kv, w_ukv):
    b, l, _ = x_kv.shape
    kv = (rmsnorm(x_kv, g_kv) @ w_ukv).reshape(b, l, MLA_HEADS, MLA_NOPE + MLA_V)
    return kv[..., :MLA_NOPE], x_kr[:, :, None, :], kv[..., MLA_NOPE:]


def join_heads(nope, rope):
    rope = jnp.broadcast_to(rope, nope.shape[:-1] + (MLA_ROPE,))
    return jnp.concatenate([nope, rope], axis=-1)


def softmax_attend(q, k, v):
    s = jnp.einsum('bqhd,bkhd->bhqk', q, k).astype(jnp.float32) * MLA_SCALE
    p = jax.nn.softmax(s, axis=-1).astype(v.dtype)
    return jnp.einsum('bhqk,bkhd->bqhd', p, v)


def blocked_attention(q, k, v):
    b, l, h, d = q.shape
    nb = l // Q_BLOCK
    qb = q.reshape(b, nb, Q_BLOCK, h, d).transpose(1, 0, 2, 3, 4)
    o = lax.map(lambda blk: softmax_attend(blk, k, v), qb)
    return o.transpose(1, 0, 2, 3, 4).reshape(b, l, h * v.shape[-1])


def chunk_token_mlp(x_cm, g_norm, w_s, b_s):
    b, l, _ = x_cm.shape
    u, v = jnp.split(jax.nn.gelu(x_cm), 2, axis=-1)
    v = layernorm(v, g_norm).reshape(b, l // CM_CHUNK, CM_CHUNK, CM_GROUPS, CM_GROUP_DIM)
    v = jnp.einsum('gts,bcsgd->bctgd', w_s, v) + b_s.T[:, :, None]
    return u * v.reshape(b, l, CM_WIDTH)


def depthwise_conv(x, w, bias):
    ch = x.shape[-1]
    y = lax.conv_general_dilated(x, w[:, None, :], window_strides=(1,),
                                 padding=[(SSD_CONV // 2, SSD_CONV // 2)],
                                 dimension_numbers=('NWC', 'WIO', 'NWC'),
                                 feature_group_count=ch)
    return y + bias


def segsum(a):
    t = a.shape[-1]
    x = jnp.broadcast_to(a[..., None], a.shape + (t,))
    x = jnp.where(jnp.tril(jnp.ones((t, t), dtype=bool), -1), x, 0.0)
    xs = jnp.cumsum(x, axis=-2)
    return jnp.where(jnp.tril(jnp.ones((t, t), dtype=bool), 0), xs, -jnp.inf)


def ssd_chunked_scan(x, a, bm, cm, init_state, want_y):
    b, l, h, p = x.shape
    nc = l // SSD_CHUNK
    x = x.reshape(b, nc, SSD_CHUNK, h, p)
    bm = bm.reshape(b, nc, SSD_CHUNK, h, -1)
    cm = cm.reshape(b, nc, SSD_CHUNK, h, -1)
    a = a.reshape(b, nc, SSD_CHUNK, h).transpose(0, 3, 1, 2)
    a_cum = jnp.cumsum(a, axis=-1)
    decay_to_end = jnp.exp(a_cum[..., -1:] - a_cum).transpose(0, 2, 3, 1)[..., None]
    chunk_states = jnp.einsum('bclhn,bclhp->bchpn', bm * decay_to_end, x)
    states = jnp.concatenate([init_state[:, None], chunk_states], axis=1)
    chunk_decay = jnp.exp(segsum(jnp.pad(a_cum[..., -1], ((0, 0), (0, 0), (1, 0)))))
    states = jnp.einsum('bhzc,bchpn->bzhpn', chunk_decay, states)
    final_state = states[:, -1]
    if not want_y:
        return None, final_state
    scores = jnp.einsum('bclhn,bcshn->bhcls', cm, bm) * jnp.exp(segsum(a))
    y_diag = jnp.einsum('bhcls,bcshp->bclhp', scores, x)
    decay_from_start = jnp.exp(a_cum).transpose(0, 2, 3, 1)[..., None]
    y_off = jnp.einsum('bclhn,bchpn->bclhp', cm * decay_from_start, states[:, :-1])
    return (y_diag + y_off).reshape(b, l, h, p), final_state


def flip_seq(t, rev):
    return jnp.flip(t, axis=1) if rev else t


def ssd_mixer(z, xbc, dt_raw, conv_w, conv_b, dt_bias, a_log, d_skip, g_norm, init_states, want_y):
    b, l, _ = xbc.shape
    f32 = jnp.float32
    xbc = jax.nn.silu(depthwise_conv(xbc, conv_w, conv_b)).astype(f32)
    xs, bm, cm = jnp.split(xbc, [SSD_WIDTH, SSD_WIDTH + SSD_GROUPS * SSD_STATE], axis=-1)
    xs = xs.reshape(b, l, SSD_HEADS, SSD_HEAD_DIM)
    rep = SSD_HEADS // SSD_GROUPS
    bm = jnp.repeat(bm.reshape(b, l, SSD_GROUPS, SSD_STATE), rep, axis=2)
    cm = jnp.repeat(cm.reshape(b, l, SSD_GROUPS, SSD_STATE), rep, axis=2)
    dt = jax.nn.softplus(dt_raw.astype(f32).reshape(b, l, 2, SSD_HEADS) + dt_bias.astype(f32))
    a = -jnp.exp(a_log.astype(f32))
    d = d_skip.astype(f32)
    ys, finals = [], []
    for direction in range(2):
        rev = direction == 1
        dt_d = flip_seq(dt[:, :, direction], rev)
        x_d = flip_seq(xs, rev)
        y_d, fin = ssd_chunked_scan(x_d * dt_d[..., None], dt_d * a[direction],
                                    flip_seq(bm, rev), flip_seq(cm, rev),
                                    init_states[direction].astype(f32), want_y)
        finals.append(fin)
        if want_y:
            ys.append(flip_seq(y_d, rev) + d[direction][:, None] * xs)
    final_states = jnp.stack(finals)
    if not want_y:
        return None, final_states
    y = (ys[0] + ys[1]).reshape(b, l, SSD_WIDTH).astype(z.dtype)
    return rmsnorm(y * jax.nn.silu(z), g_norm), final_states


def sq_relu_mlp(h, w1, w2):
    return jnp.square(jax.nn.relu(h @ w1)) @ w2


def setup_inputs(seed: int = 0) -> dict:
    key = jax.random.key(seed)
    ks = jax.random.split(key, 32)
    f32 = jnp.float32

    def nrm(k, shape, scale):
        return jax.random.normal(k, shape, f32) * scale

    def gain(k, shape):
        return 1.0 + 0.05 * jax.random.normal(k, shape, f32)

    dt0 = jnp.exp(jax.random.uniform(ks[20], (DEPTH, 2, SSD_HEADS), f32, math.log(1e-3), math.log(1e-1)))
    return {
        'x': nrm(ks[0], (BATCH, SEQ, D_MODEL), 1.0),
        'c': nrm(ks[1], (BATCH, D_MODEL), 1.0),
        'ctx': nrm(ks[2], (BATCH, CTX_LEN, D_MODEL), 1.0),
        'c_ctx': nrm(ks[3], (D_MODEL,), 1.0),
        'w_ada': nrm(ks[4], (DEPTH, D_MODEL, 6 * D_MODEL), 0.5 * D_MODEL ** -0.5),
        'b_ada': nrm(ks[5], (DEPTH, 6 * D_MODEL), 0.02),
        'g_pre_mix': gain(ks[6], (DEPTH, D_MODEL)),
        'g_post_mix': gain(ks[7], (DEPTH, D_MODEL)),
        'g_pre_ff': gain(ks[8], (DEPTH, D_MODEL)),
        'g_post_ff': gain(ks[9], (DEPTH, D_MODEL)),
        'w_in': nrm(ks[10], (DEPTH, D_MODEL, IN_WIDTH), D_MODEL ** -0.5),
        'g_q': gain(ks[11], (DEPTH, MLA_Q_LORA)),
        'w_uq': nrm(ks[12], (DEPTH, MLA_Q_LORA, MLA_HEADS * (MLA_NOPE + MLA_ROPE)), MLA_Q_LORA ** -0.5),
        'g_kv': gain(ks[13], (DEPTH, MLA_KV_LORA)),
        'w_ukv': nrm(ks[14], (DEPTH, MLA_KV_LORA, MLA_HEADS * (MLA_NOPE + MLA_V)), MLA_KV_LORA ** -0.5),
        'cm_norm_g': gain(ks[15], (DEPTH, CM_WIDTH)),
        'cm_w_s': nrm(ks[16], (DEPTH, CM_GROUPS, CM_CHUNK, CM_CHUNK), CM_CHUNK ** -0.5),
        'cm_b_s': 1.0 + nrm(ks[17], (DEPTH, CM_GROUPS, CM_CHUNK), 0.02),
        'ssd_conv_w': nrm(ks[18], (DEPTH, SSD_CONV, SSD_CONV_DIM), SSD_CONV ** -0.5),
        'ssd_conv_b': nrm(ks[19], (DEPTH, SSD_CONV_DIM), 0.02),
        'ssd_dt_bias': dt0 + jnp.log(-jnp.expm1(-dt0)),
        'ssd_a_log': jnp.log(jax.random.uniform(ks[21], (DEPTH, 2, SSD_HEADS), f32, 1.0, 16.0)),
        'ssd_d': gain(ks[22], (DEPTH, 2, SSD_HEADS)),
        'ssd_norm_g': gain(ks[23], (DEPTH, SSD_WIDTH)),
        'w_out': nrm(ks[24], (DEPTH, MIX_WIDTH, D_MODEL), MIX_WIDTH ** -0.5),
        'w_ff1': nrm(ks[25], (DEPTH, D_MODEL, D_FF), D_MODEL ** -0.5),
        'w_ff2': nrm(ks[26], (DEPTH, D_FF, D_MODEL), D_FF ** -0.5),
    }


def reference(x, c, ctx, c_ctx, w_ada, b_ada, g_pre_mix, g_post_mix, g_pre_ff, g_post_ff,
              w_in, g_q, w_uq, g_kv, w_ukv, cm_norm_g, cm_w_s, cm_b_s,
              ssd_conv_w, ssd_conv_b, ssd_dt_bias, ssd_a_log, ssd_d, ssd_norm_g,
              w_out, w_ff1, w_ff2):
    b, n_lat, _ = x.shape
    n_ctx = ctx.shape[1]
    ang_row, ang_col = axial_rope_angles(n_lat)
    xc = ctx
    for layer in range(DEPTH):
        last = layer == DEPTH - 1
        mod = (jax.nn.silu(c) @ w_ada[layer] + b_ada[layer])[:, None, :]
        mod_c = (jax.nn.silu(c_ctx) @ w_ada[layer] + b_ada[layer])[None, None, :]
        shift1, scale1, gate1, shift2, scale2, gate2 = jnp.split(mod, 6, axis=-1)
        cshift1, cscale1, cgate1, cshift2, cscale2, cgate2 = jnp.split(mod_c, 6, axis=-1)
        ssd_params = (ssd_conv_w[layer], ssd_conv_b[layer], ssd_dt_bias[layer],
                      ssd_a_log[layer], ssd_d[layer], ssd_norm_g[layer])

        hc = modulate(rmsnorm(xc, g_pre_mix[layer]), cshift1, cscale1)
        cq_lo, ckv_lo, ckr, ccm, cz, cxbc, cdt = split_projection(hc @ w_in[layer])
        ck_nope, ck_rope, cv = mla_kv(ckv_lo, ckr, g_kv[layer], w_ukv[layer])
        ck = join_heads(ck_nope, ck_rope)
        zero_states = jnp.zeros((2, b, SSD_HEADS, SSD_HEAD_DIM, SSD_STATE), jnp.float32)
        cy_ssd, ctx_states = ssd_mixer(cz, cxbc, cdt, *ssd_params, zero_states, not last)
        if not last:
            cq_nope, cq_rope = mla_q(cq_lo, g_q[layer], w_uq[layer])
            cy_attn = softmax_attend(join_heads(cq_nope, cq_rope), ck, cv).reshape(b, n_ctx, MLA_WIDTH)
            cy_cm = chunk_token_mlp(ccm, cm_norm_g[layer], cm_w_s[layer], cm_b_s[layer])
            cy = jnp.concatenate([cy_attn, cy_cm, cy_ssd], axis=-1) @ w_out[layer]
            xc = xc + cgate1 * rmsnorm(cy, g_post_mix[layer])
            hc2 = modulate(rmsnorm(xc, g_pre_ff[layer]), cshift2, cscale2)
            xc = xc + cgate2 * rmsnorm(sq_relu_mlp(hc2, w_ff1[layer], w_ff2[layer]), g_post_ff[layer])

        h = modulate(rmsnorm(x, g_pre_mix[layer]), shift1, scale1)
        q_lo, kv_lo, kr, xcm, z, xbc, dt = split_projection(h @ w_in[layer])
        q_nope, q_rope = mla_q(q_lo, g_q[layer], w_uq[layer])
        k_nope, k_rope, v = mla_kv(kv_lo, kr, g_kv[layer], w_ukv[layer])
        q = join_heads(q_nope, apply_axial_rope(q_rope, ang_row, ang_col))
        k = join_heads(k_nope, apply_axial_rope(k_rope, ang_row, ang_col))
        y_attn = blocked_attention(q, jnp.concatenate([k, ck], axis=1), jnp.concatenate([v, cv], axis=1))
        y_cm = chunk_token_mlp(xcm, cm_norm_g[layer], cm_w_s[layer], cm_b_s[layer])
        y_ssd, _ = ssd_mixer(z, xbc, dt, *ssd_params, ctx_states, True)
        y = jnp.concatenate([y_attn, y_cm, y_ssd], axis=-1) @ w_out[layer]
        x = x + gate1 * rmsnorm(y, g_post_mix[layer])
        h2 = modulate(rmsnorm(x, g_pre_ff[layer]), shift2, scale2)
        x = x + gate2 * rmsnorm(sq_relu_mlp(h2, w_ff1[layer], w_ff2[layer]), g_post_ff[layer])
    return x
```

```python
import contextlib
import numpy as np
import concourse.bass as bass
import concourse.mybir as mybir
from concourse.bass_utils import run_bass_kernel_spmd

F32 = mybir.dt.float32
BF16 = mybir.dt.bfloat16
AF = mybir.ActivationFunctionType
ALU = mybir.AluOpType

EPS = 1e-6
NL = 4
D = 1024
SEQ = 4096
NCTX = 256
NS = SEQ + NCTX
NCH = NS // 128
SCALE = 192.0 ** -0.5
XB_COLS = 4356


def _esize(dt):
    return 2 if dt == BF16 else 4
class _Op:
    __slots__ = ("eng", "fn", "dma", "semkey", "waits", "sig", "cnt", "idx", "dmaval", "gid")


def _rect(ap):
    t = ap.ap
    es = _esize(ap.dtype)
    off = int(ap.offset)
    if str(ap.space) == "DRAM":
        ext = 0
        for s, c in t:
            ext += (c - 1) * abs(s)
        return (ap.name, 0, 1, off * es, (off + ext + 1) * es)
    pstep, pcnt = t[0]
    if pstep == 0:
        pstep = 1 << 40
    p0 = off // pstep
    f0 = off % pstep
    ext = 0
    for s, c in t[1:]:
        ext += (c - 1) * abs(s)
    if str(ap.space) == "PSUM":
        return (ap.name, 0, 128, (f0 * es) // 2048 * 2048, ((f0 + ext + 1) * es + 2047) // 2048 * 2048)
    return (ap.name, p0, p0 + pcnt, f0 * es, (f0 + ext + 1) * es)


class Sched:
    ENG = ("pe", "act", "dve", "pool", "sp")

    def __init__(self, nc):
        self.nc = nc
        self.ops = []
        self.acc = {}
        self.eng_ops = {e: [] for e in self.ENG}
        self.waited = {e: {x: -1 for x in self.ENG} for e in self.ENG}
        self.dma_last = {}
        self.dma_keys = {}
        self.dma_waited = {e: set() for e in self.ENG}
        self.unwaited = set()

    def add(self, eng, fn, reads=(), writes=(), dma=None):
        op = _Op()
        op.eng = eng
        op.fn = fn
        op.dma = dma is not None
        op.semkey = dma
        op.sig = False
        op.gid = len(self.ops)
        deps = set()
        rrects = [_rect(a) for a in reads]
        wrects = [_rect(a) for a in writes]
        for r in rrects:
            for rec in self.acc.get(r[0], ()):
                q = rec[0]
                if rec[2] and q[1] < r[2] and r[1] < q[2] and q[3] < r[4] and r[3] < q[4]:
                    deps.add(rec[1])
        for r in wrects:
            for rec in self.acc.get(r[0], ()):
                q = rec[0]
                if q[1] < r[2] and r[1] < q[2] and q[3] < r[4] and r[3] < q[4]:
                    deps.add(rec[1])
        if op.dma:
            prev = self.dma_last.get(dma)
            if prev is not None:
                deps.add(prev)
            self.dma_last[dma] = op.gid
            cnt = self.dma_keys.get(dma, 0) + 1
            self.dma_keys[dma] = cnt
            op.dmaval = 16 * cnt
            self.unwaited.add(op.gid)
        deps.discard(op.gid)
        waits = []
        best = {}
        for d in deps:
            o = self.ops[d]
            if o.dma:
                if d not in self.dma_waited[eng]:
                    self.dma_waited[eng].add(d)
                    self.unwaited.discard(d)
                    waits.append(("dma", d))
            else:
                if o.eng == "pe" and eng == "pe" and not op.dma:
                    continue
                if o.idx > best.get(o.eng, -1):
                    best[o.eng] = o.idx
        for x, i in best.items():
            if self.waited[eng][x] < i:
                self.waited[eng][x] = i
                waits.append(("eng", x, i))
                self.eng_ops[x][i].sig = True
        op.waits = waits
        if not op.dma:
            op.idx = len(self.eng_ops[eng])
            self.eng_ops[eng].append(op)
        else:
            op.idx = -1
        self.ops.append(op)
        for r in wrects:
            lst = self.acc.setdefault(r[0], [])
            lst[:] = [rec for rec in lst if not (r[1] <= rec[0][1] and rec[0][2] <= r[2]
                                                  and r[3] <= rec[0][3] and rec[0][4] <= r[4])]
            lst.append((r, op.gid, True, eng if not op.dma else None))
        for r in rrects:
            lst = self.acc.setdefault(r[0], [])
            if not op.dma:
                lst[:] = [rec for rec in lst if not (not rec[2] and rec[3] == eng and rec[0] == r)]
            lst.append((r, op.gid, False, eng if not op.dma else None))
        return op

    def fence(self, eng, reads=(), writes=()):
        return self.add(eng, None, reads, writes)

    def barrier(self):
        pend = sorted(self.unwaited)
        self.unwaited = set()
        last = {e: (self.eng_ops[e][-1].gid if self.eng_ops[e] else None) for e in self.ENG}
        for e in self.ENG:
            o = _Op()
            o.eng = e
            o.fn = None
            o.dma = False
            o.semkey = None
            o.sig = False
            o.gid = len(self.ops)
            waits = []
            for x in self.ENG:
                if x == e or last[x] is None:
                    continue
                i = self.ops[last[x]].idx
                if self.waited[e][x] < i:
                    self.waited[e][x] = i
                    waits.append(("eng", x, i))
                    self.eng_ops[x][i].sig = True
            for d in pend:
                if d not in self.dma_waited[e]:
                    self.dma_waited[e].add(d)
                    waits.append(("dma", d))
            o.waits = waits
            o.idx = len(self.eng_ops[e])
            self.eng_ops[e].append(o)
            self.ops.append(o)
        self.acc = {}

    def emit(self, sems_ctx):
        nc = self.nc
        engobj = {"pe": nc.tensor, "act": nc.scalar, "dve": nc.vector, "pool": nc.gpsimd, "sp": nc.sync}
        esem = {e: sems_ctx.enter_context(nc.semaphore("s_" + e)) for e in self.ENG}
        dsem = {k: sems_ctx.enter_context(nc.semaphore("d_%d" % i)) for i, k in enumerate(self.dma_keys)}
        for e in self.ENG:
            c = 0
            for o in self.eng_ops[e]:
                if o.sig:
                    c += 1
                    o.cnt = c
        n_inst = 0
        for o in self.ops:
            eo = engobj[o.eng]
            for w in o.waits:
                if w[0] == "dma":
                    d = self.ops[w[1]]
                    eo.wait_ge(dsem[d.semkey], d.dmaval)
                else:
                    eo.wait_ge(esem[w[1]], self.eng_ops[w[1]][w[2]].cnt)
            if o.fn is None:
                if o.sig:
                    eo.nop().then_inc(esem[o.eng], 1)
                continue
            inst = o.fn(eo)
            n_inst += 1
            if o.dma:
                inst.then_inc(dsem[o.semkey], 16)
            elif o.sig:
                inst.then_inc(esem[o.eng], 1)
        return n_inst


class KB:
    def __init__(self, nc):
        self.nc = nc
        self.S = Sched(nc)
        self.uid = 0

    def name(self, n):
        self.uid += 1
        return "%s_%d" % (n, self.uid)

    def dma(self, out, in_, key, eng="sp", slow=False):
        if slow:
            self.S.add(eng, lambda e: e.dma_start(out=out, in_=in_, allow_slow_non_contiguous=True), reads=[in_], writes=[out], dma=key)
        else:
            self.S.add(eng, lambda e: e.dma_start(out=out, in_=in_), reads=[in_], writes=[out], dma=key)

    def mm(self, out, lhsT, rhs, start=True, stop=True):
        self.S.add("pe", lambda e: e.matmul(out, lhsT=lhsT, rhs=rhs, start=start, stop=stop),
                   reads=[lhsT, rhs], writes=[out])

    def tr(self, out, in_, ident):
        self.S.add("pe", lambda e: e.transpose(out=out, in_=in_, identity=ident), reads=[in_, ident], writes=[out])

    def act(self, out, in_, func, bias=None, scale=None, accum=None):
        kw = {}
        rd = [in_]
        wr = [out]
        if bias is not None:
            kw["bias"] = bias
            if not isinstance(bias, float):
                rd.append(bias)
        if scale is not None:
            kw["scale"] = scale
            if not isinstance(scale, float):
                rd.append(scale)
        if accum is not None:
            kw["accum_out"] = accum
            wr.append(accum)
        self.S.add("act", lambda e: e.activation(out=out, in_=in_, func=func, **kw), reads=rd, writes=wr)

    def copy(self, eng, out, in_):
        if eng == "act":
            self.S.add("act", lambda e: e.copy(out=out, in_=in_), reads=[in_], writes=[out])
        else:
            self.S.add(eng, lambda e: e.tensor_copy(out=out, in_=in_), reads=[in_], writes=[out])

    def tt(self, eng, out, in0, in1, op):
        self.S.add(eng, lambda e: e.tensor_tensor(out=out, in0=in0, in1=in1, op=op), reads=[in0, in1], writes=[out])

    def ts(self, eng, out, in0, s1, s2, op0, op1=None):
        rd = [in0]
        if not isinstance(s1, float):
            rd.append(s1)
        if s2 is not None and not isinstance(s2, float):
            rd.append(s2)
        if op1 is None:
            self.S.add(eng, lambda e: e.tensor_scalar(out=out, in0=in0, scalar1=s1, scalar2=None, op0=op0), reads=rd, writes=[out])
        else:
            self.S.add(eng, lambda e: e.tensor_scalar(out=out, in0=in0, scalar1=s1, scalar2=s2, op0=op0, op1=op1), reads=rd, writes=[out])

    def stt(self, out, in0, scalar, in1, op0, op1):
        rd = [in0, in1]
        if not isinstance(scalar, float):
            rd.append(scalar)
        self.S.add("dve", lambda e: e.scalar_tensor_tensor(out=out, in0=in0, scalar=scalar, in1=in1, op0=op0, op1=op1),
                   reads=rd, writes=[out])

    def memset(self, eng, ap, val):
        self.S.add(eng, lambda e: e.memset(ap, val), writes=[ap])

    def recip(self, out, in_):
        self.S.add("dve", lambda e: e.reciprocal(out=out, in_=in_), reads=[in_], writes=[out])

    def bn_stats(self, out, in_):
        self.S.add("dve", lambda e: e.bn_stats(out=out, in_=in_), reads=[in_], writes=[out])

    def bn_aggr(self, out, in_):
        self.S.add("dve", lambda e: e.bn_aggr(out=out, in_=in_), reads=[in_], writes=[out])

    def rstd(self, out, in_, n, tmp):
        self.act(tmp, in_, AF.Ln, bias=EPS, scale=1.0 / n)
        self.act(out, tmp, AF.Exp, scale=-0.5)


def bc3(ap2, n):
    return ap2.unsqueeze(2).to_broadcast([ap2.shape[0], ap2.shape[1], n])


def bcmid(ap2, n):
    return ap2.unsqueeze(1).to_broadcast([ap2.shape[0], n, ap2.shape[1]])


def build(n_layers=NL, dbg=False):
    nc = bass.Bass("TRN2", target_bir_lowering=False)
    K = KB(nc)
    S = K.S

    def din(name, shape, dt=F32):
        return nc.dram_tensor(name, shape, dt, kind="ExternalInput").ap()

    def dscr(name, shape, dt=F32):
        return nc.dram_tensor(name, shape, dt, kind=("ExternalOutput" if dbg else "Internal")).ap()

    x_in = din("x", [SEQ, D]); ctx_in = din("ctx", [NCTX, D])
    c_pk = din("c_pk", [128, 8]); cc_pk = din("cc_pk", [128, 8])
    w_ada = din("w_ada", [NL, D, 6 * D]); b_ada = din("b_ada", [NL, 6 * D])
    g4 = din("g4", [NL, 4 * D])
    w_in = din("w_in", [NL, D, 1992]); w_kr2 = din("w_kr2", [NL, D, 256])
    gq_pc = din("gq_pc", [NL, 128, 2]); gkv_p = din("gkv_p", [NL, 128, 1])
    w_uqx = din("w_uqx", [NL, 256, 1024]); w_ukvr = din("w_ukvr", [NL, 128, 1024])
    cm_g = din("cm_g", [NL, 256]); cm_ws = din("cm_ws", [NL, 4, 128, 128]); cm_bt = din("cm_bt", [NL, 128, 4])
    convw = din("convw", [NL, 128, 18]); convb = din("convb", [NL, 128, 6])
    ssd_sm = din("ssd_sm", [NL, 24]); ssd_g = din("ssd_g", [NL, 256])
    w_out = din("w_out", [NL, D, D]); w_ff1 = din("w_ff1", [NL, D, 4 * D]); w_ff2 = din("w_ff2", [NL, 4 * D, D])
    cosd = din("cosd", [128, SEQ]); sind = din("sind", [128, SEQ]); csq = din("csq", [128, SEQ]); csc = din("csc", [128, 512])
    tri = din("tri", [128, 4 * 128])
    out = nc.dram_tensor("out", [SEQ, D], F32, kind="ExternalOutput").ap()

    xscr = dscr("xscr", [NS, D])
    mods = dscr("mods", [NL, 2, 6, D])
    xbcs = dscr("xbcs", [768, XB_COLS])
    zdts = dscr("zdts", [NS, 264])
    yTs = dscr("yTs", [D, NS], BF16)
    qnTs = dscr("qnTs", [256, NS], BF16)
    wf1s = dscr("wf1s", [NL, 32, 128, 1024], BF16)

    GROUPS = [(0, 0, 256)] + [(g, 256 + (g - 1) * 512, 512) for g in range(1, 9)]

    def xsrc(l, s, n):
        if l == 0:
            return ctx_in[s:s + n, :] if s < NCTX else x_in[s - NCTX:s - NCTX + n, :]
        return xscr[s:s + n, :]

    def xdst(l, s, n):
        if l == n_layers - 1 and s >= NCTX:
            return out[s - NCTX:s - NCTX + n, :]
        return xscr[s:s + n, :]

    def xbcol(s):
        return 1 + s if s < NCTX else 259 + (s - NCTX)

    def brow(ap_row, n):
        return ap_row.broadcast_to([128, n])

    es0 = contextlib.ExitStack()
    sb0 = lambda n, sh, dt: es0.enter_context(nc.sbuf_tensor(n, sh, dt))
    identf = sb0("identf", [128, 128], F32)
    identb = sb0("identb", [128, 128], BF16)
    onesb = sb0("onesb", [128, 128], BF16)
    onesf = sb0("onesf", [128, 128], F32)
    trit = sb0("trit", [128, 4, 128], F32)
    zrow = sb0("zrow", [128, 8], F32)
    K.dma(trit[:].rearrange("p a b -> p (a b)"), tri[:, :], "trit")
    K.memset("pool", onesf[:], 1.0)
    K.memset("pool", onesb[:], 1.0)
    K.memset("pool", zrow[:], 0.0)
    K.tt("dve", identf[:], trit[:, 0, :], trit[:, 2, :], ALU.mult)
    K.copy("dve", identb[:], identf[:])
    for c in range(6):
        for col in (0, 257, 258, 4355):
            K.dma(xbcs[c * 128:(c + 1) * 128, col:col + 1], zrow[:, 0:1], "zpad", slow=True)
    def phase_m(l):
        with contextlib.ExitStack() as es:
            sb = lambda n, sh, dt: es.enter_context(nc.sbuf_tensor(K.name(n), sh, dt))
            ps = lambda n, sh, dt: es.enter_context(nc.psum_tensor(K.name(n), sh, dt))
            sc = sb("sc", [128, 2, 8], F32)
            scb = sb("scb", [128, 2, 8, 128], F32)
            bada = sb("bada", [128, 6 * D], F32)
            gt = sb("gt", [128, 4 * D], F32)
            wa = [sb("wa%d" % i, [128, 8, 512], F32) for i in range(2)]
            mt = [sb("mt%d" % i, [128, 512], F32) for i in range(2)]
            res = [sb("res%d" % i, [128, 512], F32) for i in range(4)]
            pm = [ps("pm%d" % i, [128, 512], F32) for i in range(4)]
            K.dma(sc[:, 0, :], c_pk[:, :], "sc0")
            K.dma(sc[:, 1, :], cc_pk[:, :], "sc1")
            K.dma(bada[:], brow(b_ada[l:l + 1, :], 6 * D), "bada")
            K.dma(gt[:], brow(g4[l:l + 1, :], 4 * D), "gt")
            K.act(sc[:], sc[:], AF.Silu)
            for v in range(2):
                K.copy("dve", scb[:, v, :, :], bc3(sc[:, v, :], 128))
            wv = w_ada[l].rearrange("(k p) n -> p k n", p=128)
            cnt = 0
            for j in range(12):
                w_ = wa[j % 2]
                K.dma(w_[:], wv[:, :, j * 512:(j + 1) * 512], "wa%d" % (j % 2))
                m, half = j // 2, j % 2
                for v in range(2):
                    p_ = pm[cnt % 4]
                    for k in range(8):
                        K.mm(p_[:], scb[:, v, k, :], w_[:, k, :], start=(k == 0), stop=(k == 7))
                    t_ = mt[cnt % 2]
                    r_ = res[cnt % 4]
                    K.tt("dve", t_[:], p_[:], bada[:, j * 512:(j + 1) * 512], ALU.add)
                    if m in (0, 3):
                        K.copy("dve", r_[:], t_[:])
                    elif m in (1, 4):
                        gsl = gt[:, (0 if m == 1 else 2) * D + half * 512:(0 if m == 1 else 2) * D + half * 512 + 512]
                        K.stt(r_[:], t_[:], 1.0, gsl, ALU.add, ALU.mult)
                    else:
                        gsl = gt[:, (1 if m == 2 else 3) * D + half * 512:(1 if m == 2 else 3) * D + half * 512 + 512]
                        K.tt("dve", r_[:], t_[:], gsl, ALU.mult)
                    K.dma(mods[l, v, m:m + 1, half * 512:(half + 1) * 512], r_[0:1, :], "res%d" % (cnt % 4))
                    cnt += 1
            S.barrier()

    def phase_1(l, kvnT, krT2):
        with contextlib.ExitStack() as es:
            sb = lambda n, sh, dt: es.enter_context(nc.sbuf_tensor(K.name(n), sh, dt))
            ps = lambda n, sh, dt: es.enter_context(nc.psum_tensor(K.name(n), sh, dt))
            win = sb("win", [128, 8, 1992], BF16)
            wkr2 = sb("wkr2", [128, 8, 256], BF16)
            gq = sb("gq", [128, 2], F32); gkv = sb("gkv", [128, 1], F32)
            cmg = sb("cmg", [128, 256], F32); cmb = sb("cmb", [128, 4], F32)
            wsf = sb("wsf", [128, 4, 128], F32); wsT = sb("wsT", [128, 4, 128], BF16)
            ab = sb("ab", [128, 2, D], F32)
            xg = [sb("xg%d" % i, [128, D], F32) for i in range(3)]
            hT = [sb("hT%d" % i, [128, 8, 512], BF16) for i in range(2)]
            hb = [sb("hb%d" % i, [128, D], BF16) for i in range(2)]
            tmpf = sb("tmpf", [128, D], F32)
            junk = sb("junk", [128, D], BF16)
            st4 = sb("st4", [128, 3, 4], F32)
            sq = sb("sq", [128, 3, 512], BF16)
            lnq = sb("lnq", [128, 512], F32)
            rst = [sb("rst%d" % i, [128, 512], F32) for i in range(2)]
            qn = [sb("qn%d" % i, [128, 2, 512], BF16) for i in range(2)]
            cosk = sb("cosk", [128, 512], F32); sink = sb("sink", [128, 512], F32)
            t1 = sb("t1", [128, 512], F32); t2 = sb("t2", [128, 512], F32)
            xbst = sb("xbst", [128, 6, 512], F32)
            xcms = [sb("xcm%d" % i, [128, 4, 512], F32) for i in range(2)]
            zdt = [sb("zdt%d" % i, [128, 4, 264], F32) for i in range(2)]
            st6 = sb("st6", [128, 4, 6], F32); mv = sb("mv", [128, 4, 2], F32)
            vpe = sb("vpe", [128, 4], F32); rscm = sb("rscm", [128, 4], F32); cneg = sb("cneg", [128, 4], F32)
            vnf = [sb("vnf%d" % i, [128, 256], F32) for i in range(2)]
            vnb = [sb("vnb%d" % i, [128, 256], BF16) for i in range(2)]
            ycm = [sb("ycm%d" % i, [128, 256], BF16) for i in range(2)]
            ycmT = [sb("ycmT%d" % i, [128, 2, 512], BF16) for i in range(2)]
            pT = ps("pT", [128, D], BF16)
            pf = [ps("pf%d" % i, [128, 512], F32) for i in range(7)]
            bank = [0]

            def nb():
                bank[0] += 1
                return pf[3 + bank[0] % 4]

            K.dma(win[:], w_in[l].rearrange("(k p) n -> p k n", p=128), "win", eng="pool")
            K.dma(wkr2[:], w_kr2[l].rearrange("(k p) n -> p k n", p=128), "wkr2", eng="pool")
            K.dma(gq[:], gq_pc[l], "gq"); K.dma(gkv[:], gkv_p[l], "gkv")
            K.dma(cmg[:], brow(cm_g[l:l + 1, :], 256), "cmg"); K.dma(cmb[:], cm_bt[l], "cmb")
            K.dma(wsf[:], cm_ws[l].rearrange("g t s -> t g s"), "wsf")
            K.memset("pool", cneg[:], -0.5)
            for gi in range(4):
                p_ = nb()
                K.tr(p_[:, 0:128], wsf[:, gi, :], identf[:])
                K.copy("dve", wsT[:, gi, :], p_[:, 0:128])

            xi = [0]

            def prep(g, s0, G):
                nt = G // 128
                v = 1 if g == 0 else 0
                if g in (0, 1):
                    K.dma(ab[:, 0, :], brow(mods[l, v, 1:2, :], D), "ab0")
                    K.dma(ab[:, 1, :], brow(mods[l, v, 0:1, :], D), "ab1")
                hT_ = hT[g % 2]
                for t in range(nt):
                    x_ = xg[xi[0] % 3]; xi[0] += 1
                    K.dma(x_[:], xsrc(l, s0 + t * 128, 128), "xg%d" % ((xi[0] - 1) % 3))
                    K.act(junk[:], x_[:], AF.Square, accum=st4[:, 0, t:t + 1])
                    K.rstd(st4[:, 2, t:t + 1], st4[:, 0, t:t + 1], D, st4[:, 1, t:t + 1])
                    K.stt(tmpf[:], x_[:], st4[:, 2, t:t + 1], ab[:, 0, :], ALU.mult, ALU.mult)
                    hb_ = hb[t % 2]
                    K.tt("dve", hb_[:], tmpf[:], ab[:, 1, :], ALU.add)
                    for k in range(8):
                        K.tr(pT[:, k * 128:(k + 1) * 128], hb_[:, k * 128:(k + 1) * 128], identb[:])
                    K.copy("act", hT_[:, :, t * 128:(t + 1) * 128], pT[:].rearrange("p (k t) -> p k t", k=8))

            def body(g, s0, G):
                nt = G // 128
                hT_ = hT[g % 2]

                def fm(col0, m, wt=win, p_=None):
                    if p_ is None:
                        p_ = nb()
                    for k in range(8):
                        K.mm(p_[0:m, 0:G], wt[:, k, col0:col0 + m], hT_[:, k, 0:G], start=(k == 0), stop=(k == 7))
                    return p_

                pq = [fm(0, 128, p_=pf[0]), fm(128, 128, p_=pf[1])]
                pkv = fm(256, 128, p_=pf[2])
                for c in range(2):
                    K.act(sq[:, c, 0:G], pq[c][:, 0:G], AF.Square)
                K.act(sq[:, 2, 0:G], pkv[:, 0:G], AF.Square)
                pkr = fm(0, 128, wkr2)
                if g == 0:
                    K.copy("dve", krT2[:, s0:s0 + G], pkr[:, 0:G])
                else:
                    pkrr = fm(128, 128, wkr2)
                    K.dma(cosk[:], cosd[:, s0 - NCTX:s0 - NCTX + G], "cosk")
                    K.dma(sink[:], sind[:, s0 - NCTX:s0 - NCTX + G], "sink")
                    K.tt("dve", t1[:, 0:G], pkr[:, 0:G], cosk[:, 0:G], ALU.mult)
                    K.tt("dve", t2[:, 0:G], pkrr[:, 0:G], sink[:, 0:G], ALU.mult)
                    K.tt("pool", krT2[:, s0:s0 + G], t1[:, 0:G], t2[:, 0:G], ALU.add)
                for c in range(6):
                    p_ = fm(1216 + c * 128, 128)
                    K.copy("act", xbst[:, c, 0:G], p_[:, 0:G])
                K.dma(xbcs.rearrange("(c p) n -> p c n", p=128)[:, :, xbcol(s0):xbcol(s0) + G], xbst[:, :, 0:G], "xbst")
                psq = nb()
                for c in range(2):
                    K.mm(psq[:, 0:G], onesb[:], sq[:, c, 0:G], start=(c == 0), stop=(c == 1))
                pskv = nb()
                K.mm(pskv[:, 0:G], onesb[:], sq[:, 2, 0:G])
                K.rstd(rst[0][:, 0:G], psq[:, 0:G], 256, lnq[:, 0:G])
                K.rstd(rst[1][:, 0:G], pskv[:, 0:G], 128, lnq[:, 0:G])
                qn_ = qn[g % 2]
                for c in range(2):
                    K.stt(qn_[:, c, 0:G], pq[c][:, 0:G], gq[:, c:c + 1], rst[0][:, 0:G], ALU.mult, ALU.mult)
                    K.dma(qnTs[c * 128:(c + 1) * 128, s0:s0 + G], qn_[:, c, 0:G], "qn%d_%d" % (g % 2, c))
                K.stt(kvnT[:, s0:s0 + G], pkv[:, 0:G], gkv[:, 0:1], rst[1][:, 0:G], ALU.mult, ALU.mult)
                zdt_ = zdt[g % 2]
                xcm = xcms[g % 2]
                for t in range(nt):
                    tsl = slice(t * 128, (t + 1) * 128)
                    p_ = nb()
                    for k in range(8):
                        K.mm(p_[:, :], hT_[:, k, tsl], win[:, k, 448:960], start=(k == 0), stop=(k == 7))
                    K.copy("act", xcm[:, t, :], p_[:, :])
                    p2 = nb()
                    for k in range(8):
                        K.mm(p2[:, 0:256], hT_[:, k, tsl], win[:, k, 960:1216], start=(k == 0), stop=(k == 7))
                    for k in range(8):
                        K.mm(p2[:, 256:264], hT_[:, k, tsl], win[:, k, 1984:1992], start=(k == 0), stop=(k == 7))
                    K.copy("dve", zdt_[:, t, :], p2[:, 0:264])
                K.act(xcm[:, 0:nt, :], xcm[:, 0:nt, :], AF.Gelu_apprx_tanh)
                K.act(zdt_[:, 0:nt, 0:256], zdt_[:, 0:nt, 0:256], AF.Silu)
                K.dma(zdts[s0:s0 + G, :].rearrange("(t p) n -> p t n", p=128), zdt_[:, 0:nt, :], "zdt%d" % (g % 2))

            def body_b(g, s0, G):
                nt = G // 128
                xcm = xcms[g % 2]
                for t in range(nt):
                    K.bn_stats(st6[:, t, :], xcm[:, t, 256:512])
                    K.bn_aggr(mv[:, t, :], st6[:, t, :])
                K.ts("dve", vpe[:, 0:nt], mv[:, 0:nt, 1], EPS, None, ALU.add)
                K.tt("pool", rscm[:, 0:nt], vpe[:, 0:nt], cneg[:, 0:nt], ALU.pow)
                ycmT_ = ycmT[g % 2]
                for t in range(nt):
                    vf = vnf[t % 2]; vb = vnb[t % 2]; yc = ycm[t % 2]
                    K.ts("dve", vf[:], xcm[:, t, 256:512], mv[:, t, 0:1], rscm[:, t:t + 1], ALU.subtract, ALU.mult)
                    K.tt("pool", vb[:], vf[:], cmg[:], ALU.mult)
                    p_ = nb()
                    for gi in range(4):
                        K.mm(p_[:, gi * 64:(gi + 1) * 64], wsT[:, gi, :], vb[:, gi * 64:(gi + 1) * 64])
                    for gi in range(4):
                        gs = slice(gi * 64, (gi + 1) * 64)
                        K.stt(yc[:, gs], p_[:, gs], cmb[:, gi:gi + 1], xcm[:, t, gs], ALU.add, ALU.mult)
                    for c in range(2):
                        K.tr(pT[:, c * 128:(c + 1) * 128], yc[:, c * 128:(c + 1) * 128], identb[:])
                    K.copy("act", ycmT_[:, :, t * 128:(t + 1) * 128], pT[:, 0:256].rearrange("p (c t) -> p c t", c=2))
                K.dma(yTs[512:768, s0:s0 + G].rearrange("(c p) n -> p c n", p=128), ycmT_[:, :, 0:G], "ycmT%d" % (g % 2))

            prep(*GROUPS[0])
            for i_, grp_ in enumerate(GROUPS):
                if i_ + 1 < len(GROUPS):
                    prep(*GROUPS[i_ + 1])
                body(*grp_)
                if i_ >= 1:
                    body_b(*GROUPS[i_ - 1])
            body_b(*GROUPS[-1])
            S.barrier()

    def phase_a(l, kvnT, krT2):
        with contextlib.ExitStack() as es:
            sb = lambda n, sh, dt: es.enter_context(nc.sbuf_tensor(K.name(n), sh, dt))
            ps = lambda n, sh, dt: es.enter_context(nc.psum_tensor(K.name(n), sh, dt))
            KT = sb("KT", [128, 4, NS], BF16)
            Vaug = sb("Vaug", [128, NCH, 4, 130], BF16)
            wuq = sb("wuq", [128, 2, 1024], BF16)
            wukv = sb("wukv", [128, 1024], BF16)
            qn = [sb("qna%d" % i, [128, 2, 512], BF16) for i in range(2)]
            cs = [sb("csa%d" % i, [128, 512], F32) for i in range(2)]
            qh = [sb("qh%d" % i, [128, 512], BF16) for i in range(2)]
            qr = [sb("qr%d" % i, [128, 512], BF16) for i in range(2)]
            PT = [sb("PT%d" % i, [128, 512], BF16) for i in range(4)]
            yat = [sb("yat%d" % i, [128, 512], F32) for i in range(4)]
            yaT = [sb("yaT%d" % i, [128, 4, 512], BF16) for i in range(2)]
            rden = sb("rden", [128, 8], F32)
            acc = [ps("acc%d" % i, [128, 512], F32) for i in range(4)]
            psc = [ps("psc%d" % i, [128, 512], F32) for i in range(3)]
            pqu = ps("pqu", [128, 512], F32)
            K.dma(wuq[:], w_uqx[l].rearrange("(c p) n -> p c n", p=128), "wuq", eng="pool")
            K.dma(wukv[:], w_ukvr[l], "wukv", eng="pool")
            K.memset("pool", Vaug[:], 1.0)
            for j in range(32):
                K.dma(wf1s[l, j].rearrange("p (k c) -> p k c", k=8),
                      w_ff1[l].rearrange("(k p) n -> p k n", p=128)[:, :, j * 128:(j + 1) * 128], "wf1cast%d" % (j % 4), eng="pool")
            for (g, s0, G) in GROUPS:
                for h in range(4):
                    p_ = psc[h % 2]
                    K.mm(p_[:, 0:G], wukv[:, h * 128:(h + 1) * 128], kvnT[:, s0:s0 + G])
                    K.copy("act" if h % 2 else "dve", KT[:, h, s0:s0 + G], p_[:, 0:G])
                for t in range(G // 128):
                    p_ = acc[t]
                    K.mm(p_[:, :], kvnT[:, s0 + t * 128:s0 + (t + 1) * 128], wukv[:, 512:1024])
                    K.copy("act" if t % 2 else "dve", Vaug[:, s0 // 128 + t, :, 0:128], p_[:, :].rearrange("p (h d) -> p h d", h=4))
            pti = [0]
            for (g, s0, G) in GROUPS:
                nt = G // 128
                kts = [0, 1] if g == 0 else list(range(NCH))
                qn_ = qn[g % 2]; cs_ = cs[g % 2]; yaT_ = yaT[g % 2]
                K.dma(qn_[:, :, 0:G], qnTs[:, s0:s0 + G].rearrange("(c p) n -> p c n", p=128), "qna%d" % (g % 2))
                if g == 0:
                    K.dma(cs_[:, 0:G], csc[:, 0:G], "csa%d" % (g % 2))
                else:
                    K.dma(cs_[:, 0:G], csq[:, s0 - NCTX:s0 - NCTX + G], "csa%d" % (g % 2))
                for h in range(4):
                    qh_ = qh[h % 2]; qr_ = qr[h % 2]
                    for c in range(2):
                        K.mm(pqu[:, 0:G], wuq[:, c, h * 256:h * 256 + 128], qn_[:, c, 0:G], start=(c == 0), stop=(c == 1))
                    K.copy("dve", qh_[:, 0:G], pqu[:, 0:G])
                    for c in range(2):
                        K.mm(pqu[:, 0:G], wuq[:, c, h * 256 + 128:h * 256 + 256], qn_[:, c, 0:G], start=(c == 0), stop=(c == 1))
                    K.tt("dve", qr_[:, 0:G], pqu[:, 0:G], cs_[:, 0:G], ALU.mult)
                    stash = {}

                    def score(i):
                        kt = kts[i]
                        ksl = slice(kt * 128, (kt + 1) * 128)
                        p_ = psc[pti[0] % 3]
                        P_ = PT[pti[0] % 4]
                        pti[0] += 1
                        K.mm(p_[:, 0:G], KT[:, h, ksl], qh_[:, 0:G], start=True, stop=False)
                        K.mm(p_[:, 0:G], krT2[:, ksl], qr_[:, 0:G], start=False, stop=True)
                        K.act(P_[:, 0:G], p_[:, 0:G], AF.Exp, scale=SCALE)
                        stash[i] = P_

                    score(0)
                    if len(kts) > 1:
                        score(1)
                    for i, kt in enumerate(kts):
                        if i + 2 < len(kts):
                            score(i + 2)
                        P_ = stash.pop(i)
                        for qt in range(nt):
                            K.mm(acc[qt][:, 0:129], P_[:, qt * 128:(qt + 1) * 128], Vaug[:, kt, h, 0:129],
                                 start=(i == 0), stop=(i == len(kts) - 1))
                    for qt in range(nt):
                        K.recip(rden[:, qt:qt + 1], acc[qt][:, 128:129])
                        K.ts("dve", yat[qt][:, h * 128:(h + 1) * 128], acc[qt][:, 0:128], rden[:, qt:qt + 1], None, ALU.mult)
                for qt in range(nt):
                    for h in range(4):
                        K.tr(pqu[:, h * 128:(h + 1) * 128], yat[qt][:, h * 128:(h + 1) * 128], identf[:])
                    K.copy("act", yaT_[:, :, qt * 128:(qt + 1) * 128], pqu[:, 0:512].rearrange("p (h t) -> p h t", h=4))
                K.dma(yTs[0:512, s0:s0 + G].rearrange("(h p) n -> p h n", p=128), yaT_[:, :, 0:G], "yaT%d" % (g % 2))
            S.barrier()

    def phase_s(l):
        with contextlib.ExitStack() as es:
            sb = lambda n, sh, dt: es.enter_context(nc.sbuf_tensor(K.name(n), sh, dt))
            ps = lambda n, sh, dt: es.enter_context(nc.psum_tensor(K.name(n), sh, dt))
            cw = sb("cw", [128, 18], F32); cbias = sb("cbias", [128, 6], F32)
            sm = sb("sm", [128, 24], F32); Abc = sb("Abc", [128, 8], F32); dsum = sb("dsum", [128, 4], F32)
            sgb = sb("sgb", [128, 256], F32)
            Sb_all = sb("Sb_all", [128, NCH, 256], F32)
            yp_all = sb("yp_all", [128, NCH, 256], F32)
            cmT_all = sb("cmT_all", [128, 2, NS], BF16)
            dfsb_all = sb("dfsb_all", [128, NCH, 4], F32); decb_all = sb("decb_all", [128, NCH, 4], F32)
            stf = sb("stf", [128, 256], F32); stfb = sb("stfb", [128, 256], BF16)
            stb = sb("stb", [128, 256], F32); stbb = sb("stbb", [128, 256], BF16)
            xr = [sb("xr%d" % i, [128, 6, 514], F32) for i in range(2)]
            cv = sb("cv", [128, 6, 512], F32)
            bcT = sb("bcT", [128, 4, 512], BF16)
            dtr = [sb("dtr%d" % i, [128, 4, 8], F32) for i in range(2)]
            sp_ = [sb("sp%d" % i, [128, 4, 8], F32) for i in range(6)]
            E = sb("E", [128, 40], F32)
            xs_tok = sb("xs_tok", [128, 256], F32)
            bm_tok = sb("bm_tok", [128, 2, 128], BF16)
            wde = sb("wde", [128, 8], F32)
            xdt = sb("xdt", [128, 2, 256], BF16); xdte = sb("xdte", [128, 2, 256], BF16)
            Lm = sb("Lm", [128, 8, 128], F32); eL = sb("eL", [128, 8, 128], F32)
            GTm = sb("GTm", [128, 2, 2, 128], F32); W = sb("W", [128, 8, 128], BF16)
            t1 = sb("t1s", [128, 256], F32); t2 = sb("t2s", [128, 256], F32)
            sz = [sb("sz%d" % i, [128, 256], F32) for i in range(3)]
            yb = sb("yb", [128, 256], BF16); junk = sb("junks", [128, 256], BF16)
            st3 = sb("st3", [128, 3], F32)
            ysT = [sb("ysT%d" % i, [128, 2, 512], BF16) for i in range(2)]
            pcs = ps("pcs", [128, 512], F32)
            pseg = ps("pseg", [128, 1024], F32)
            pxs = ps("pxs", [128, 512], F32)
            pbm = ps("pbm", [128, D], BF16)
            pG = ps("pG", [128, 512], F32)
            pst = ps("pst", [128, 512], F32)
            pyo = ps("pyo", [128, 512], F32)

            K.dma(cw[:], convw[l], "cw"); K.dma(cbias[:], convb[l], "cbias")
            K.dma(sm[:], brow(ssd_sm[l:l + 1, :], 24), "sm")
            K.dma(sgb[:], brow(ssd_g[l:l + 1, :], 256), "sgb")
            K.act(Abc[:], sm[:, 8:16], AF.Exp)
            K.ts("dve", Abc[:], Abc[:], -1.0, None, ALU.mult)
            K.tt("dve", dsum[:], sm[:, 16:20], sm[:, 20:24], ALU.add)
            K.memset("pool", stf[:], 0.0); K.memset("pool", stfb[:], 0.0)
            K.memset("pool", stb[:], 0.0); K.memset("pool", stbb[:], 0.0)
            v4 = lambda ap: ap.rearrange("p (h d) -> p h d", h=4)

            for (g, s0, G) in GROUPS:
                nt = G // 128
                xr_ = xr[g % 2]; dtr_ = dtr[g % 2]
                c0 = xbcol(s0)
                K.dma(xr_[:, :, 0:G + 2], xbcs.rearrange("(c p) n -> p c n", p=128)[:, :, c0 - 1:c0 + G + 1], "xr%d" % (g % 2))
                K.dma(dtr_[:, 0:nt, :], zdts[s0:s0 + G, 256:264].rearrange("(t p) n -> p t n", p=128), "dtr%d" % (g % 2))
                xsp, nx, mn, lg, dt, a = [t_[:, 0:nt, :] for t_ in sp_]
                K.tt("dve", xsp, dtr_[:, 0:nt, :], bcmid(sm[:, 0:8], nt), ALU.add)
                K.ts("dve", nx, xsp, -1.0, None, ALU.mult)
                K.tt("dve", mn, xsp, nx, ALU.min)
                K.act(nx, mn, AF.Exp)
                K.act(lg, nx, AF.Ln, bias=1.0)
                K.stt(dt, xsp, 0.0, lg, ALU.max, ALU.add)
                K.tt("dve", a, dt, bcmid(Abc[:], nt), ALU.mult)
                for c in range(6):
                    K.ts("dve", cv[:, c, 0:G], xr_[:, c, 0:G], cw[:, c * 3:c * 3 + 1], cbias[:, c:c + 1], ALU.mult, ALU.add)
                    K.stt(cv[:, c, 0:G], xr_[:, c, 1:G + 1], cw[:, c * 3 + 1:c * 3 + 2], cv[:, c, 0:G], ALU.mult, ALU.add)
                    K.stt(cv[:, c, 0:G], xr_[:, c, 2:G + 2], cw[:, c * 3 + 2:c * 3 + 3], cv[:, c, 0:G], ALU.mult, ALU.add)
                K.act(cv[:, :, 0:G], cv[:, :, 0:G], AF.Silu)
                K.copy("pool", bcT[:, :, 0:G], cv[:, 2:6, 0:G])
                K.copy("pool", cmT_all[:, :, s0:s0 + G], cv[:, 4:6, 0:G])
                for t in range(nt):
                    ci = s0 // 128 + t
                    tsl = slice(t * 128, (t + 1) * 128)
                    a_t = sp_[5][:, t, :]; dt_t = sp_[4][:, t, :]
                    for i, lt in enumerate([trit[:, 0, :], trit[:, 3, :], trit[:, 1, :], trit[:, 2, :], onesf[:]]):
                        K.mm(pcs[:, i * 8:(i + 1) * 8], lt, a_t)
                    K.act(E[:], pcs[:, 0:40], AF.Exp)
                    K.copy("pool", dfsb_all[:, ci, :], E[:, 28:32])
                    K.copy("pool", decb_all[:, ci, :], E[:, 36:40])
                    for c in range(2):
                        K.tr(pxs[:, c * 128:(c + 1) * 128], cv[:, c, tsl], identf[:])
                    K.copy("act", xs_tok[:], pxs[:, 0:256])
                    for grp in range(2):
                        K.tr(pbm[:, grp * 128:(grp + 1) * 128], bcT[:, grp, tsl], identb[:])
                    K.copy("act", bm_tok[:].rearrange("p a b -> p (a b)"), pbm[:, 0:256])
                    K.tt("dve", wde[:, 0:4], dt_t[:, 0:4], E[:, 8:12], ALU.mult)
                    K.tt("dve", wde[:, 4:8], dt_t[:, 4:8], E[:, 20:24], ALU.mult)
                    for d in range(2):
                        K.tt("dve", v4(xdt[:, d, :]), v4(xs_tok[:]), bc3(dt_t[:, d * 4:(d + 1) * 4], 64), ALU.mult)
                        K.tt("pool", v4(xdte[:, d, :]), v4(xs_tok[:]), bc3(wde[:, d * 4:(d + 1) * 4], 64), ALU.mult)
                    for j in range(8):
                        d = j // 4
                        K.ts("dve" if j % 2 else "pool", Lm[:, j, :], trit[:, 3 if d == 0 else 1, :], a_t[:, j:j + 1], 0.0, ALU.mult, ALU.add)
                        K.mm(pseg[:, j * 128:(j + 1) * 128], Lm[:, j, :], trit[:, 0 if d == 0 else 2, :])
                    K.act(eL[:].rearrange("p a b -> p (a b)"), pseg[:], AF.Exp)
                    for grp in range(2):
                        K.mm(pG[:, grp * 128:(grp + 1) * 128], bcT[:, grp, tsl], bcT[:, 2 + grp, tsl])
                    for d in range(2):
                        K.tt("dve", GTm[:, d, :, :], pG[:, 0:256].rearrange("p (a b) -> p a b", a=2),
                             bcmid(trit[:, 0 if d == 0 else 2, :], 2), ALU.mult)
                    for d in range(2):
                        for grp in range(2):
                            j0 = d * 4 + grp * 2
                            K.tt("dve", W[:, j0:j0 + 2, :], eL[:, j0:j0 + 2, :], bcmid(GTm[:, d, grp, :], 2), ALU.mult)
                    for h in range(4):
                        hs = slice(h * 64, (h + 1) * 64)
                        for d in range(2):
                            K.mm(pG[:, 256 + h * 64:256 + (h + 1) * 64], W[:, d * 4 + h, :], xdt[:, d, hs], start=(d == 0), stop=(d == 1))
                    for j in range(8):
                        d, h = j // 4, j % 4
                        K.mm(pst[:, j * 64:(j + 1) * 64], bm_tok[:, h // 2, :], xdte[:, d, h * 64:(h + 1) * 64])
                    for h in range(4):
                        hs = slice(h * 64, (h + 1) * 64)
                        K.mm(pyo[:, hs], bcT[:, 2 + h // 2, tsl], stfb[:, hs])
                    K.tt("dve", v4(t1[:]), v4(pyo[:, 0:256]), bc3(E[:, 0:4], 64), ALU.mult)
                    K.tt("dve", t1[:], pG[:, 256:512], t1[:], ALU.add)
                    K.tt("pool", v4(t2[:]), v4(xs_tok[:]), bc3(dsum[:], 64), ALU.mult)
                    K.tt("pool", yp_all[:, ci, :], t1[:], t2[:], ALU.add)
                    K.tt("dve", v4(stf[:]), v4(stf[:]), bc3(E[:, 32:36], 64), ALU.mult)
                    K.tt("dve", stf[:], pst[:, 0:256], stf[:], ALU.add)
                    K.copy("pool", stfb[:], stf[:])
                    K.copy("act", Sb_all[:, ci, :], pst[:, 256:512])

            order = [1, 0] + list(range(NCH - 1, 1, -1))
            for n_, ci in enumerate(order):
                csl = slice(ci * 128, (ci + 1) * 128)
                sz_ = sz[n_ % 3]
                K.dma(sz_[:], zdts[csl, 0:256], "sz%d" % (n_ % 3))
                for h in range(4):
                    hs = slice(h * 64, (h + 1) * 64)
                    K.mm(pyo[:, hs], cmT_all[:, h // 2, csl], stbb[:, hs])
                K.tt("dve", v4(t1[:]), v4(pyo[:, 0:256]), bc3(dfsb_all[:, ci, :], 64), ALU.mult)
                K.tt("dve", t1[:], t1[:], yp_all[:, ci, :], ALU.add)
                K.tt("dve", t1[:], t1[:], sz_[:], ALU.mult)
                K.act(junk[:], t1[:], AF.Square, accum=st3[:, 0:1])
                K.rstd(st3[:, 2:3], st3[:, 0:1], 256, st3[:, 1:2])
                K.stt(yb[:], t1[:], st3[:, 2:3], sgb[:], ALU.mult, ALU.mult)
                if ci < 2:
                    buf, slot, last = ysT[0], ci, (ci == 0)
                else:
                    gidx = (ci - 2) // 4
                    buf, slot, last = ysT[(gidx + 1) % 2], (ci - 2) % 4, ((ci - 2) % 4 == 0)
                for c in range(2):
                    K.tr(pbm[:, c * 128:(c + 1) * 128], yb[:, c * 128:(c + 1) * 128], identb[:])
                K.copy("act", buf[:, :, slot * 128:(slot + 1) * 128], pbm[:, 0:256].rearrange("p (c t) -> p c t", c=2))
                if last:
                    if ci < 2:
                        K.dma(yTs[768:1024, 0:256].rearrange("(c p) n -> p c n", p=128), buf[:, :, 0:256], "ysT0")
                    else:
                        K.dma(yTs[768:1024, 256 + gidx * 512:256 + (gidx + 1) * 512].rearrange("(c p) n -> p c n", p=128),
                              buf[:, :, :], "ysT%d" % ((gidx + 1) % 2))
                K.tt("dve", v4(stb[:]), v4(stb[:]), bc3(decb_all[:, ci, :], 64), ALU.mult)
                K.tt("dve", stb[:], stb[:], Sb_all[:, ci, :], ALU.add)
                K.copy("pool", stbb[:], stb[:])
            S.barrier()

    def phase_o(l):
        with contextlib.ExitStack() as es:
            sb = lambda n, sh, dt: es.enter_context(nc.sbuf_tensor(K.name(n), sh, dt))
            ps = lambda n, sh, dt: es.enter_context(nc.psum_tensor(K.name(n), sh, dt))
            wout = sb("wout", [128, 8, D], BF16)
            wff2 = sb("wff2", [128, 32, D], BF16)
            w1r = [sb("w1r%d" % i, [128, 8, 128], BF16) for i in range(6)]
            md = sb("md", [128, 4, D], F32)
            yT = sb("yTo", [128, 8, 512], BF16)
            xt = [sb("xt%d" % i, [128, D], F32) for i in range(5)]
            g2c = sb("g2c", [128, D], F32)
            tmpf = sb("tmpo", [128, D], F32)
            junk = sb("junko", [128, D], BF16)
            hb = [sb("hbo%d" % i, [128, D], BF16) for i in range(2)]
            h2T = [sb("h2T%d" % i, [128, 8, 512], BF16) for i in range(2)]
            rl = [sb("rl%d" % i, [128, 512], BF16) for i in range(2)]
            aT = sb("aT", [128, 32, 512], BF16)
            st = sb("sto", [128, 3, 16], F32)
            pT = ps("pTo", [128, D], BF16)
            pa = [ps("pa%d" % i, [128, 512], F32) for i in range(7)]
            bank = [0]

            def nb():
                bank[0] += 1
                return pa[bank[0] % 7]

            K.dma(wout[:], w_out[l].rearrange("(k p) n -> p k n", p=128), "wout", eng="pool")
            for jj in range(4):
                K.dma(wff2[:, jj * 8:(jj + 1) * 8, :], w_ff2[l, jj * 1024:(jj + 1) * 1024, :].rearrange("(k p) n -> p k n", p=128),
                      "wff2_%d" % jj, eng="pool")
            xi = [0]
            w1i = [0]
            sti = [0]

            def newx():
                x_ = xt[xi[0] % 5]; xkey = "xt%d" % (xi[0] % 5); xi[0] += 1
                return x_, xkey

            def post(pp, gcol, x_, gt_=None):
                si = sti[0] % 16; sti[0] += 1
                K.act(junk[:, 0:512], pp[0][:, :], AF.Square, accum=st[:, 0, si:si + 1])
                K.act(junk[:, 512:1024], pp[1][:, :], AF.Square, accum=st[:, 1, si:si + 1])
                K.tt("dve", st[:, 0, si:si + 1], st[:, 0, si:si + 1], st[:, 1, si:si + 1], ALU.add)
                K.rstd(st[:, 2, si:si + 1], st[:, 0, si:si + 1], D, st[:, 1, si:si + 1])
                for half in range(2):
                    hs = slice(half * 512, (half + 1) * 512)
                    K.stt(tmpf[:, hs], pp[half][:, :], st[:, 2, si:si + 1], (md[:, gcol, hs] if gt_ is None else gt_[:, hs]), ALU.mult, ALU.mult)
                K.tt("pool", x_[:], x_[:], tmpf[:], ALU.add)

            pend = {}

            def prep_load(g, s0, G):
                nt = G // 128
                v = 1 if g == 0 else 0
                if g in (0, 1):
                    for i, m in enumerate((2, 4, 3, 5)):
                        K.dma(md[:, i, :], brow(mods[l, v, m:m + 1, :], D), "md%d" % i)
                    if g == 0:
                        K.dma(g2c[:], brow(mods[l, 1, 5:6, :], D), "g2c")
                K.dma(yT[:, :, 0:G], yTs[:, s0:s0 + G].rearrange("(k p) n -> p k n", p=128), "yTo")
                xs_ = []
                for t in range(nt):
                    x_, xkey = newx()
                    xs_.append((x_, xkey))
                    K.dma(x_[:], xsrc(l, s0 + t * 128, 128), xkey)
                pend[g] = xs_

            def prep_op(g, s0, G, t):
                tsl = slice(t * 128, (t + 1) * 128)
                x_, xkey = pend[g][t]
                pp = [nb(), nb()]
                for half in range(2):
                    for k in range(8):
                        K.mm(pp[half][:, :], yT[:, k, tsl], wout[:, k, half * 512:(half + 1) * 512], start=(k == 0), stop=(k == 7))
                post(pp, 0, x_)
                K.dma(xscr[s0 + t * 128:s0 + (t + 1) * 128, :], x_[:], xkey)
                si2 = sti[0] % 16; sti[0] += 1
                K.act(junk[:], x_[:], AF.Square, accum=st[:, 0, si2:si2 + 1])
                K.rstd(st[:, 2, si2:si2 + 1], st[:, 0, si2:si2 + 1], D, st[:, 1, si2:si2 + 1])
                K.stt(tmpf[:], x_[:], st[:, 2, si2:si2 + 1], md[:, 1, :], ALU.mult, ALU.mult)
                K.tt("dve", hb[t % 2][:], tmpf[:], md[:, 2, :], ALU.add)

            def prep_tr(g, s0, G, t):
                tsl = slice(t * 128, (t + 1) * 128)
                hb_ = hb[t % 2]
                for k in range(8):
                    K.tr(pT[:, k * 128:(k + 1) * 128], hb_[:, k * 128:(k + 1) * 128], identb[:])
                K.copy("act", h2T[g % 2][:, :, tsl], pT[:].rearrange("p (k t) -> p k t", k=8))

            def ff1(g, s0, G, j0, j1):
                for j in range(j0, j1):
                    w_ = w1r[w1i[0] % 6]
                    K.dma(w_[:], wf1s[l, j].rearrange("p (k c) -> p k c", k=8), "w1r%d" % (w1i[0] % 6))
                    w1i[0] += 1
                    p_ = nb()
                    for k in range(8):
                        K.mm(p_[:, 0:G], w_[:, k, :], h2T[g % 2][:, k, 0:G], start=(k == 0), stop=(k == 7))
                    r_ = rl[j % 2]
                    K.act(r_[:, 0:G], p_[:, 0:G], AF.Relu)
                    K.tt("dve", aT[:, j, 0:G], r_[:, 0:G], p_[:, 0:G], ALU.mult)

            def ff2(g, s0, G, t):
                if True:
                    tsl = slice(t * 128, (t + 1) * 128)
                    x_, xkey = newx()
                    K.dma(x_[:], xscr[s0 + t * 128:s0 + (t + 1) * 128, :], xkey)
                    pp = [nb(), nb()]
                    for half in range(2):
                        for j in range(32):
                            K.mm(pp[half][:, :], aT[:, j, tsl], wff2[:, j, half * 512:(half + 1) * 512], start=(j == 0), stop=(j == 31))
                    post(pp, 3, x_, g2c if g == 0 else None)
                    K.dma(xdst(l, s0 + t * 128, 128), x_[:], xkey)

            g0 = GROUPS[0]
            prep_load(*g0)
            for t in range(g0[2] // 128):
                prep_op(*g0, t)
                prep_tr(*g0, t)
            for i_, cur in enumerate(GROUPS):
                nxt = GROUPS[i_ + 1] if i_ + 1 < len(GROUPS) else None
                nt_c = cur[2] // 128
                if nxt is not None:
                    prep_load(*nxt)
                for q_ in range(4):
                    ff1(*cur, q_ * 8, (q_ + 1) * 8)
                    if nxt is not None:
                        prep_op(*nxt, q_)
                        if q_ >= 1:
                            prep_tr(*nxt, q_ - 1)
                ff2(*cur, 0)
                if nxt is not None:
                    prep_tr(*nxt, 3)
                for t in range(1, nt_c):
                    ff2(*cur, t)
            S.barrier()

    for l in range(n_layers):
        phase_m(l)
        with contextlib.ExitStack() as esr:
            kvnT = esr.enter_context(nc.sbuf_tensor(K.name("kvnT"), [128, NS], BF16))
            krT2 = esr.enter_context(nc.sbuf_tensor(K.name("krT2"), [128, NS], BF16))
            phase_1(l, kvnT, krT2)
            phase_a(l, kvnT, krT2)
        phase_s(l)
        phase_o(l)
    S.barrier()
    with contextlib.ExitStack() as es2:
        n = S.emit(es2)
    es0.close()
    return nc, n


def _rope_tables():
    rows_n = SEQ // 64
    row = np.repeat(np.arange(rows_n, dtype=np.float32), 64)
    col = np.tile(np.arange(64, dtype=np.float32), rows_n)
    inv = (np.float32(10000.0) ** (-np.arange(0, 32, 2, dtype=np.float32) / np.float32(32))).astype(np.float32)
    ar = (row[:, None] * inv).astype(np.float32)
    ac = (col[:, None] * inv).astype(np.float32)
    cos = np.zeros((64, SEQ), np.float32); sin = np.zeros((64, SEQ), np.float32)
    for d in range(64):
        ang = ar if d < 32 else ac
        f = d % 16
        cos[d] = np.cos(ang[:, f])
        sgn = -1.0 if (d % 32) < 16 else 1.0
        sin[d] = sgn * np.sin(ang[:, f])
    return cos, sin


_PERM = np.array([d + 16 if (d % 32) < 16 else d - 16 for d in range(64)])


def _host_layout(inp):
    f = lambda a: np.ascontiguousarray(a, dtype=np.float32)
    sh = {}
    sh["w_ada"] = f(inp["w_ada"]); sh["b_ada"] = f(inp["b_ada"])
    sh["g4"] = f(np.concatenate([inp["g_pre_mix"], inp["g_post_mix"], inp["g_pre_ff"], inp["g_post_ff"]], axis=1))
    w_in = inp["w_in"]
    sh["w_in"] = f(w_in)
    kr = w_in[:, :, 384:448]
    rot = kr[:, :, _PERM]
    sh["w_kr2"] = f(np.concatenate([kr, kr, rot, rot], axis=2))
    sh["gq_pc"] = f(inp["g_q"].reshape(NL, 2, 128).transpose(0, 2, 1))
    sh["gkv_p"] = f(inp["g_kv"].reshape(NL, 128, 1))
    wuq = inp["w_uq"].reshape(NL, 256, 4, 192)
    sh["w_uqx"] = f(np.concatenate([wuq, wuq[:, :, :, 128 + _PERM]], axis=3).reshape(NL, 256, 1024))
    wukv = inp["w_ukv"].reshape(NL, 128, 4, 256)
    sh["w_ukvr"] = f(np.concatenate([wukv[:, :, :, :128].reshape(NL, 128, 512), wukv[:, :, :, 128:].reshape(NL, 128, 512)], axis=2))
    sh["cm_g"] = f(inp["cm_norm_g"]); sh["cm_ws"] = f(inp["cm_w_s"])
    sh["cm_bt"] = f(inp["cm_b_s"].transpose(0, 2, 1))
    cw = inp["ssd_conv_w"].reshape(NL, 3, 6, 128).transpose(0, 3, 2, 1)
    sh["convw"] = f(cw.reshape(NL, 128, 18))
    sh["convb"] = f(inp["ssd_conv_b"].reshape(NL, 6, 128).transpose(0, 2, 1))
    sh["ssd_sm"] = f(np.concatenate([inp["ssd_dt_bias"].reshape(NL, 8), inp["ssd_a_log"].reshape(NL, 8), inp["ssd_d"].reshape(NL, 8)], axis=1))
    sh["ssd_g"] = f(inp["ssd_norm_g"])
    sh["w_out"] = f(inp["w_out"]); sh["w_ff1"] = f(inp["w_ff1"]); sh["w_ff2"] = f(inp["w_ff2"])
    cos, sin = _rope_tables()
    sh["cosd"] = f(np.concatenate([cos, cos], axis=0)); sh["sind"] = f(np.concatenate([sin, sin], axis=0))
    sh["csq"] = f(np.concatenate([cos, sin], axis=0))
    sh["csc"] = f(np.concatenate([np.ones((64, 512), np.float32), np.zeros((64, 512), np.float32)], axis=0))
    k = np.arange(128)[:, None]; l_ = np.arange(128)[None, :]
    tri = np.stack([(k <= l_), (k < l_), (k >= l_), (k > l_)], axis=1).astype(np.float32)
    sh["tri"] = f(tri.reshape(128, 512))
    sh["cc_pk"] = f(inp["c_ctx"].reshape(8, 128).T)
    return sh


_CACHE = {}


def kernel(**inputs):
    inp = {k: np.asarray(v) for k, v in inputs.items()}
    if "nc" not in _CACHE:
        _CACHE["nc"] = build()[0]
    nc = _CACHE["nc"]
    shared = _host_layout(inp)
    in_maps = []
    for b in range(8):
        m = dict(shared)
        m["x"] = np.ascontiguousarray(inp["x"][b], dtype=np.float32)
        m["ctx"] = np.ascontiguousarray(inp["ctx"][b], dtype=np.float32)
        m["c_pk"] = np.ascontiguousarray(inp["c"][b].reshape(8, 128).T, dtype=np.float32)
        in_maps.append(m)
    res = run_bass_kernel_spmd(nc, in_maps, core_ids=list(range(8)))
    return np.stack([np.asarray(r["out"], dtype=np.float32) for r in res.results], axis=0)
```

```python
import contextlib
import numpy as np
import concourse.bass as bass
import concourse.mybir as mybir
from concourse.bass_utils import run_bass_kernel_spmd

F32 = mybir.dt.float32
BF16 = mybir.dt.bfloat16
AF = mybir.ActivationFunctionType
ALU = mybir.AluOpType

EPS = 1e-6
NL = 4
D = 1024
SEQ = 4096
NCTX = 256
NS = SEQ + NCTX
NCH = NS // 128
SCALE = 192.0 ** -0.5
XB_COLS = 4356


def _esize(dt):
    return 2 if dt == BF16 else 4
class _Op:
    __slots__ = ("eng", "fn", "dma", "semkey", "waits", "sig", "cnt", "idx", "dmaval", "gid")


def _rect(ap):
    t = ap.ap
    es = _esize(ap.dtype)
    off = int(ap.offset)
    if str(ap.space) == "DRAM":
        ext = 0
        for s, c in t:
            ext += (c - 1) * abs(s)
        return (ap.name, 0, 1, off * es, (off + ext + 1) * es)
    pstep, pcnt = t[0]
    if pstep == 0:
        pstep = 1 << 40
    p0 = off // pstep
    f0 = off % pstep
    ext = 0
    for s, c in t[1:]:
        ext += (c - 1) * abs(s)
    if str(ap.space) == "PSUM":
        return (ap.name, 0, 128, (f0 * es) // 2048 * 2048, ((f0 + ext + 1) * es + 2047) // 2048 * 2048)
    return (ap.name, p0, p0 + pcnt, f0 * es, (f0 + ext + 1) * es)


class Sched:
    ENG = ("pe", "act", "dve", "pool", "sp")

    def __init__(self, nc):
        self.nc = nc
        self.ops = []
        self.acc = {}
        self.eng_ops = {e: [] for e in self.ENG}
        self.waited = {e: {x: -1 for x in self.ENG} for e in self.ENG}
        self.dma_last = {}
        self.dma_keys = {}
        self.dma_waited = {e: set() for e in self.ENG}
        self.unwaited = set()

    def add(self, eng, fn, reads=(), writes=(), dma=None):
        op = _Op()
        op.eng = eng
        op.fn = fn
        op.dma = dma is not None
        op.semkey = dma
        op.sig = False
        op.gid = len(self.ops)
        deps = set()
        rrects = [_rect(a) for a in reads]
        wrects = [_rect(a) for a in writes]
        for r in rrects:
            for rec in self.acc.get(r[0], ()):
                q = rec[0]
                if rec[2] and q[1] < r[2] and r[1] < q[2] and q[3] < r[4] and r[3] < q[4]:
                    deps.add(rec[1])
        for r in wrects:
            for rec in self.acc.get(r[0], ()):
                q = rec[0]
                if q[1] < r[2] and r[1] < q[2] and q[3] < r[4] and r[3] < q[4]:
                    deps.add(rec[1])
        if op.dma:
            prev = self.dma_last.get(dma)
            if prev is not None:
                deps.add(prev)
            self.dma_last[dma] = op.gid
            cnt = self.dma_keys.get(dma, 0) + 1
            self.dma_keys[dma] = cnt
            op.dmaval = 16 * cnt
            self.unwaited.add(op.gid)
        deps.discard(op.gid)
        waits = []
        best = {}
        for d in deps:
            o = self.ops[d]
            if o.dma:
                if d not in self.dma_waited[eng]:
                    self.dma_waited[eng].add(d)
                    self.unwaited.discard(d)
                    waits.append(("dma", d))
            else:
                if o.eng == "pe" and eng == "pe" and not op.dma:
                    continue
                if o.idx > best.get(o.eng, -1):
                    best[o.eng] = o.idx
        for x, i in best.items():
            if self.waited[eng][x] < i:
                self.waited[eng][x] = i
                waits.append(("eng", x, i))
                self.eng_ops[x][i].sig = True
        op.waits = waits
        if not op.dma:
            op.idx = len(self.eng_ops[eng])
            self.eng_ops[eng].append(op)
        else:
            op.idx = -1
        self.ops.append(op)
        for r in wrects:
            lst = self.acc.setdefault(r[0], [])
            lst[:] = [rec for rec in lst if not (r[1] <= rec[0][1] and rec[0][2] <= r[2]
                                                  and r[3] <= rec[0][3] and rec[0][4] <= r[4])]
            lst.append((r, op.gid, True, eng if not op.dma else None))
        for r in rrects:
            lst = self.acc.setdefault(r[0], [])
            if not op.dma:
                lst[:] = [rec for rec in lst if not (not rec[2] and rec[3] == eng and rec[0] == r)]
            lst.append((r, op.gid, False, eng if not op.dma else None))
        return op

    def fence(self, eng, reads=(), writes=()):
        return self.add(eng, None, reads, writes)

    def barrier(self):
        pend = sorted(self.unwaited)
        self.unwaited = set()
        last = {e: (self.eng_ops[e][-1].gid if self.eng_ops[e] else None) for e in self.ENG}
        for e in self.ENG:
            o = _Op()
            o.eng = e
            o.fn = None
            o.dma = False
            o.semkey = None
            o.sig = False
            o.gid = len(self.ops)
            waits = []
            for x in self.ENG:
                if x == e or last[x] is None:
                    continue
                i = self.ops[last[x]].idx
                if self.waited[e][x] < i:
                    self.waited[e][x] = i
                    waits.append(("eng", x, i))
                    self.eng_ops[x][i].sig = True
            for d in pend:
                if d not in self.dma_waited[e]:
                    self.dma_waited[e].add(d)
                    waits.append(("dma", d))
            o.waits = waits
            o.idx = len(self.eng_ops[e])
            self.eng_ops[e].append(o)
            self.ops.append(o)
        self.acc = {}

    def emit(self, sems_ctx):
        nc = self.nc
        engobj = {"pe": nc.tensor, "act": nc.scalar, "dve": nc.vector, "pool": nc.gpsimd, "sp": nc.sync}
        esem = {e: sems_ctx.enter_context(nc.semaphore("s_" + e)) for e in self.ENG}
        dsem = {k: sems_ctx.enter_context(nc.semaphore("d_%d" % i)) for i, k in enumerate(self.dma_keys)}
        for e in self.ENG:
            c = 0
            for o in self.eng_ops[e]:
                if o.sig:
                    c += 1
                    o.cnt = c
        n_inst = 0
        for o in self.ops:
            eo = engobj[o.eng]
            for w in o.waits:
                if w[0] == "dma":
                    d = self.ops[w[1]]
                    eo.wait_ge(dsem[d.semkey], d.dmaval)
                else:
                    eo.wait_ge(esem[w[1]], self.eng_ops[w[1]][w[2]].cnt)
            if o.fn is None:
                if o.sig:
                    eo.nop().then_inc(esem[o.eng], 1)
                continue
            inst = o.fn(eo)
            n_inst += 1
            if o.dma:
                inst.then_inc(dsem[o.semkey], 16)
            elif o.sig:
                inst.then_inc(esem[o.eng], 1)
        return n_inst


class KB:
    def __init__(self, nc):
        self.nc = nc
        self.S = Sched(nc)
        self.uid = 0

    def name(self, n):
        self.uid += 1
        return "%s_%d" % (n, self.uid)

    def dma(self, out, in_, key, eng="sp", slow=False):
        if slow:
            self.S.add(eng, lambda e: e.dma_start(out=out, in_=in_, allow_slow_non_contiguous=True), reads=[in_], writes=[out], dma=key)
        else:
            self.S.add(eng, lambda e: e.dma_start(out=out, in_=in_), reads=[in_], writes=[out], dma=key)

    def mm(self, out, lhsT, rhs, start=True, stop=True):
        self.S.add("pe", lambda e: e.matmul(out, lhsT=lhsT, rhs=rhs, start=start, stop=stop),
                   reads=[lhsT, rhs], writes=[out])

    def tr(self, out, in_, ident):
        self.S.add("pe", lambda e: e.transpose(out=out, in_=in_, identity=ident), reads=[in_, ident], writes=[out])

    def act(self, out, in_, func, bias=None, scale=None, accum=None):
        kw = {}
        rd = [in_]
        wr = [out]
        if bias is not None:
            kw["bias"] = bias
            if not isinstance(bias, float):
                rd.append(bias)
        if scale is not None:
            kw["scale"] = scale
            if not isinstance(scale, float):
                rd.append(scale)
        if accum is not None:
            kw["accum_out"] = accum
            wr.append(accum)
        self.S.add("act", lambda e: e.activation(out=out, in_=in_, func=func, **kw), reads=rd, writes=wr)

    def copy(self, eng, out, in_):
        if eng == "act":
            self.S.add("act", lambda e: e.copy(out=out, in_=in_), reads=[in_], writes=[out])
        else:
            self.S.add(eng, lambda e: e.tensor_copy(out=out, in_=in_), reads=[in_], writes=[out])

    def tt(self, eng, out, in0, in1, op):
        self.S.add(eng, lambda e: e.tensor_tensor(out=out, in0=in0, in1=in1, op=op), reads=[in0, in1], writes=[out])

    def ts(self, eng, out, in0, s1, s2, op0, op1=None):
        rd = [in0]
        if not isinstance(s1, float):
            rd.append(s1)
        if s2 is not None and not isinstance(s2, float):
            rd.append(s2)
        if op1 is None:
            self.S.add(eng, lambda e: e.tensor_scalar(out=out, in0=in0, scalar1=s1, scalar2=None, op0=op0), reads=rd, writes=[out])
        else:
            self.S.add(eng, lambda e: e.tensor_scalar(out=out, in0=in0, scalar1=s1, scalar2=s2, op0=op0, op1=op1), reads=rd, writes=[out])

    def stt(self, out, in0, scalar, in1, op0, op1):
        rd = [in0, in1]
        if not isinstance(scalar, float):
            rd.append(scalar)
        self.S.add("dve", lambda e: e.scalar_tensor_tensor(out=out, in0=in0, scalar=scalar, in1=in1, op0=op0, op1=op1),
                   reads=rd, writes=[out])

    def memset(self, eng, ap, val):
        self.S.add(eng, lambda e: e.memset(ap, val), writes=[ap])

    def recip(self, out, in_):
        self.S.add("dve", lambda e: e.reciprocal(out=out, in_=in_), reads=[in_], writes=[out])

    def bn_stats(self, out, in_):
        self.S.add("dve", lambda e: e.bn_stats(out=out, in_=in_), reads=[in_], writes=[out])

    def bn_aggr(self, out, in_):
        self.S.add("dve", lambda e: e.bn_aggr(out=out, in_=in_), reads=[in_], writes=[out])

    def rstd(self, out, in_, n, tmp):
        self.act(tmp, in_, AF.Ln, bias=EPS, scale=1.0 / n)
        self.act(out, tmp, AF.Exp, scale=-0.5)


def bc3(ap2, n):
    return ap2.unsqueeze(2).to_broadcast([ap2.shape[0], ap2.shape[1], n])


def bcmid(ap2, n):
    return ap2.unsqueeze(1).to_broadcast([ap2.shape[0], n, ap2.shape[1]])


def build(n_layers=NL, dbg=False):
    nc = bass.Bass("TRN2", target_bir_lowering=False)
    K = KB(nc)
    S = K.S

    def din(name, shape, dt=F32):
        return nc.dram_tensor(name, shape, dt, kind="ExternalInput").ap()

    def dscr(name, shape, dt=F32):
        return nc.dram_tensor(name, shape, dt, kind=("ExternalOutput" if dbg else "Internal")).ap()

    x_in = din("x", [SEQ, D]); ctx_in = din("ctx", [NCTX, D])
    c_pk = din("c_pk", [128, 8]); cc_pk = din("cc_pk", [128, 8])
    w_ada = din("w_ada", [NL, D, 6 * D]); b_ada = din("b_ada", [NL, 6 * D])
    g4 = din("g4", [NL, 4 * D])
    w_in = din("w_in", [NL, D, 1992]); w_kr2 = din("w_kr2", [NL, D, 256])
    gq_pc = din("gq_pc", [NL, 128, 2]); gkv_p = din("gkv_p", [NL, 128, 1])
    w_uqx = din("w_uqx", [NL, 256, 1024]); w_ukvr = din("w_ukvr", [NL, 128, 1024])
    cm_g = din("cm_g", [NL, 256]); cm_ws = din("cm_ws", [NL, 4, 128, 128]); cm_bt = din("cm_bt", [NL, 128, 4])
    convw = din("convw", [NL, 128, 18]); convb = din("convb", [NL, 128, 6])
    ssd_sm = din("ssd_sm", [NL, 24]); ssd_g = din("ssd_g", [NL, 256])
    w_out = din("w_out", [NL, D, D]); w_ff1 = din("w_ff1", [NL, D, 4 * D]); w_ff2 = din("w_ff2", [NL, 4 * D, D])
    cosd = din("cosd", [128, SEQ]); sind = din("sind", [128, SEQ]); csq = din("csq", [128, SEQ]); csc = din("csc", [128, 512])
    tri = din("tri", [128, 4 * 128])
    out = nc.dram_tensor("out", [SEQ, D], F32, kind="ExternalOutput").ap()

    xscr = dscr("xscr", [NS, D])
    mods = dscr("mods", [NL, 2, 6, D])
    xbcs = dscr("xbcs", [768, XB_COLS])
    zdts = dscr("zdts", [NS, 264])
    yTs = dscr("yTs", [D, NS], BF16)
    qnTs = dscr("qnTs", [256, NS], BF16)
    wf1s = dscr("wf1s", [NL, 32, 128, 1024], BF16)

    GROUPS = [(0, 0, 256)] + [(g, 256 + (g - 1) * 512, 512) for g in range(1, 9)]

    def xsrc(l, s, n):
        if l == 0:
            return ctx_in[s:s + n, :] if s < NCTX else x_in[s - NCTX:s - NCTX + n, :]
        return xscr[s:s + n, :]

    def xdst(l, s, n):
        if l == n_layers - 1 and s >= NCTX:
            return out[s - NCTX:s - NCTX + n, :]
        return xscr[s:s + n, :]

    def xbcol(s):
        return 1 + s if s < NCTX else 259 + (s - NCTX)

    def brow(ap_row, n):
        return ap_row.broadcast_to([128, n])

    es0 = contextlib.ExitStack()
    sb0 = lambda n, sh, dt: es0.enter_context(nc.sbuf_tensor(n, sh, dt))
    identf = sb0("identf", [128, 128], F32)
    identb = sb0("identb", [128, 128], BF16)
    onesb = sb0("onesb", [128, 128], BF16)
    onesf = sb0("onesf", [128, 128], F32)
    trit = sb0("trit", [128, 4, 128], F32)
    zrow = sb0("zrow", [128, 8], F32)
    K.dma(trit[:].rearrange("p a b -> p (a b)"), tri[:, :], "trit")
    K.memset("pool", onesf[:], 1.0)
    K.memset("pool", onesb[:], 1.0)
    K.memset("pool", zrow[:], 0.0)
    K.tt("dve", identf[:], trit[:, 0, :], trit[:, 2, :], ALU.mult)
    K.copy("dve", identb[:], identf[:])
    for c in range(6):
        for col in (0, 257, 258, 4355):
            K.dma(xbcs[c * 128:(c + 1) * 128, col:col + 1], zrow[:, 0:1], "zpad", slow=True)
    def phase_m(l):
        with contextlib.ExitStack() as es:
            sb = lambda n, sh, dt: es.enter_context(nc.sbuf_tensor(K.name(n), sh, dt))
            ps = lambda n, sh, dt: es.enter_context(nc.psum_tensor(K.name(n), sh, dt))
            sc = sb("sc", [128, 2, 8], F32)
            scb = sb("scb", [128, 2, 8, 128], F32)
            bada = sb("bada", [128, 6 * D], F32)
            gt = sb("gt", [128, 4 * D], F32)
            wa = [sb("wa%d" % i, [128, 8, 512], F32) for i in range(2)]
            mt = [sb("mt%d" % i, [128, 512], F32) for i in range(2)]
            res = [sb("res%d" % i, [128, 512], F32) for i in range(4)]
            pm = [ps("pm%d" % i, [128, 512], F32) for i in range(4)]
            K.dma(sc[:, 0, :], c_pk[:, :], "sc0")
            K.dma(sc[:, 1, :], cc_pk[:, :], "sc1")
            K.dma(bada[:], brow(b_ada[l:l + 1, :], 6 * D), "bada")
            K.dma(gt[:], brow(g4[l:l + 1, :], 4 * D), "gt")
            K.act(sc[:], sc[:], AF.Silu)
            for v in range(2):
                K.copy("dve", scb[:, v, :, :], bc3(sc[:, v, :], 128))
            wv = w_ada[l].rearrange("(k p) n -> p k n", p=128)
            cnt = 0
            for j in range(12):
                w_ = wa[j % 2]
                K.dma(w_[:], wv[:, :, j * 512:(j + 1) * 512], "wa%d" % (j % 2))
                m, half = j // 2, j % 2
                for v in range(2):
                    p_ = pm[cnt % 4]
                    for k in range(8):
                        K.mm(p_[:], scb[:, v, k, :], w_[:, k, :], start=(k == 0), stop=(k == 7))
                    t_ = mt[cnt % 2]
                    r_ = res[cnt % 4]
                    K.tt("dve", t_[:], p_[:], bada[:, j * 512:(j + 1) * 512], ALU.add)
                    if m in (0, 3):
                        K.copy("dve", r_[:], t_[:])
                    elif m in (1, 4):
                        gsl = gt[:, (0 if m == 1 else 2) * D + half * 512:(0 if m == 1 else 2) * D + half * 512 + 512]
                        K.stt(r_[:], t_[:], 1.0, gsl, ALU.add, ALU.mult)
                    else:
                        gsl = gt[:, (1 if m == 2 else 3) * D + half * 512:(1 if m == 2 else 3) * D + half * 512 + 512]
                        K.tt("dve", r_[:], t_[:], gsl, ALU.mult)
                    K.dma(mods[l, v, m:m + 1, half * 512:(half + 1) * 512], r_[0:1, :], "res%d" % (cnt % 4))
                    cnt += 1
            S.barrier()

    def phase_1(l, kvnT, krT2):
        with contextlib.ExitStack() as es:
            sb = lambda n, sh, dt: es.enter_context(nc.sbuf_tensor(K.name(n), sh, dt))
            ps = lambda n, sh, dt: es.enter_context(nc.psum_tensor(K.name(n), sh, dt))
            win = sb("win", [128, 8, 1992], BF16)
            wkr2 = sb("wkr2", [128, 8, 256], BF16)
            gq = sb("gq", [128, 2], F32); gkv = sb("gkv", [128, 1], F32)
            cmg = sb("cmg", [128, 256], F32); cmb = sb("cmb", [128, 4], F32)
            wsf = sb("wsf", [128, 4, 128], F32); wsT = sb("wsT", [128, 4, 128], BF16)
            ab = sb("ab", [128, 2, D], F32)
            xg = [sb("xg%d" % i, [128, D], F32) for i in range(3)]
            hT = [sb("hT%d" % i, [128, 8, 512], BF16) for i in range(2)]
            hb = [sb("hb%d" % i, [128, D], BF16) for i in range(2)]
            tmpf = sb("tmpf", [128, D], F32)
            junk = sb("junk", [128, D], BF16)
            st4 = sb("st4", [128, 3, 4], F32)
            sq = sb("sq", [128, 3, 512], BF16)
            lnq = sb("lnq", [128, 512], F32)
            rst = [sb("rst%d" % i, [128, 512], F32) for i in range(2)]
            qn = [sb("qn%d" % i, [128, 2, 512], BF16) for i in range(2)]
            cosk = sb("cosk", [128, 512], F32); sink = sb("sink", [128, 512], F32)
            t1 = sb("t1", [128, 512], F32); t2 = sb("t2", [128, 512], F32)
            xbst = [sb("xbst%d" % i, [128, 6, 512], F32) for i in range(2)]
            xcms = [sb("xcm%d" % i, [128, 4, 512], F32) for i in range(2)]
            zdt = [sb("zdt%d" % i, [128, 4, 264], F32) for i in range(2)]
            st6 = sb("st6", [128, 4, 6], F32); mv = sb("mv", [128, 4, 2], F32)
            vpe = sb("vpe", [128, 4], F32); rscm = sb("rscm", [128, 4], F32); cneg = sb("cneg", [128, 4], F32)
            vnf = [sb("vnf%d" % i, [128, 256], F32) for i in range(2)]
            vnb = [sb("vnb%d" % i, [128, 256], BF16) for i in range(2)]
            ycm = [sb("ycm%d" % i, [128, 256], BF16) for i in range(2)]
            ycmT = [sb("ycmT%d" % i, [128, 2, 512], BF16) for i in range(2)]
            pT = ps("pT", [128, D], BF16)
            pf = [ps("pf%d" % i, [128, 512], F32) for i in range(7)]
            bank = [0]

            def nb():
                bank[0] += 1
                return pf[3 + bank[0] % 4]

            K.dma(win[:], w_in[l].rearrange("(k p) n -> p k n", p=128), "win", eng="pool")
            K.dma(wkr2[:], w_kr2[l].rearrange("(k p) n -> p k n", p=128), "wkr2", eng="pool")
            K.dma(gq[:], gq_pc[l], "gq"); K.dma(gkv[:], gkv_p[l], "gkv")
            K.dma(cmg[:], brow(cm_g[l:l + 1, :], 256), "cmg"); K.dma(cmb[:], cm_bt[l], "cmb")
            K.dma(wsf[:], cm_ws[l].rearrange("g t s -> t g s"), "wsf")
            K.memset("pool", cneg[:], -0.5)
            for gi in range(4):
                p_ = nb()
                K.tr(p_[:, 0:128], wsf[:, gi, :], identf[:])
                K.copy("dve", wsT[:, gi, :], p_[:, 0:128])

            xi = [0]
            deferred = []

            def dstore(out_, in__, key):
                deferred.append((out_, in__, key))

            def flush():
                for (o_, i_, k_) in deferred:
                    K.dma(o_, i_, k_)
                del deferred[:]

            def prep(g, s0, G):
                nt = G // 128
                v = 1 if g == 0 else 0
                if g in (0, 1):
                    K.dma(ab[:, 0, :], brow(mods[l, v, 1:2, :], D), "ab0")
                    K.dma(ab[:, 1, :], brow(mods[l, v, 0:1, :], D), "ab1")
                hT_ = hT[g % 2]
                for t in range(nt):
                    x_ = xg[xi[0] % 3]; xi[0] += 1
                    K.dma(x_[:], xsrc(l, s0 + t * 128, 128), "xg%d" % ((xi[0] - 1) % 3))
                    K.act(junk[:], x_[:], AF.Square, accum=st4[:, 0, t:t + 1])
                    K.rstd(st4[:, 2, t:t + 1], st4[:, 0, t:t + 1], D, st4[:, 1, t:t + 1])
                    K.stt(tmpf[:], x_[:], st4[:, 2, t:t + 1], ab[:, 0, :], ALU.mult, ALU.mult)
                    hb_ = hb[t % 2]
                    K.tt("dve", hb_[:], tmpf[:], ab[:, 1, :], ALU.add)
                    for k in range(8):
                        K.tr(pT[:, k * 128:(k + 1) * 128], hb_[:, k * 128:(k + 1) * 128], identb[:])
                    K.copy("act", hT_[:, :, t * 128:(t + 1) * 128], pT[:].rearrange("p (k t) -> p k t", k=8))

            def body(g, s0, G):
                nt = G // 128
                hT_ = hT[g % 2]

                def fm(col0, m, wt=win, p_=None):
                    if p_ is None:
                        p_ = nb()
                    for k in range(8):
                        K.mm(p_[0:m, 0:G], wt[:, k, col0:col0 + m], hT_[:, k, 0:G], start=(k == 0), stop=(k == 7))
                    return p_

                pq = [fm(0, 128, p_=pf[0]), fm(128, 128, p_=pf[1])]
                pkv = fm(256, 128, p_=pf[2])
                for c in range(2):
                    K.act(sq[:, c, 0:G], pq[c][:, 0:G], AF.Square)
                K.act(sq[:, 2, 0:G], pkv[:, 0:G], AF.Square)
                pkr = fm(0, 128, wkr2)
                if g == 0:
                    K.copy("dve", krT2[:, s0:s0 + G], pkr[:, 0:G])
                else:
                    pkrr = fm(128, 128, wkr2)
                    K.dma(cosk[:], cosd[:, s0 - NCTX:s0 - NCTX + G], "cosk")
                    K.dma(sink[:], sind[:, s0 - NCTX:s0 - NCTX + G], "sink")
                    K.tt("dve", t1[:, 0:G], pkr[:, 0:G], cosk[:, 0:G], ALU.mult)
                    K.tt("dve", t2[:, 0:G], pkrr[:, 0:G], sink[:, 0:G], ALU.mult)
                    K.tt("pool", krT2[:, s0:s0 + G], t1[:, 0:G], t2[:, 0:G], ALU.add)
                for c in range(6):
                    p_ = fm(1216 + c * 128, 128)
                    K.copy("act", xbst[g % 2][:, c, 0:G], p_[:, 0:G])
                dstore(xbcs.rearrange("(c p) n -> p c n", p=128)[:, :, xbcol(s0):xbcol(s0) + G], xbst[g % 2][:, :, 0:G], "xbst%d" % (g % 2))
                psq = nb()
                for c in range(2):
                    K.mm(psq[:, 0:G], onesb[:], sq[:, c, 0:G], start=(c == 0), stop=(c == 1))
                pskv = nb()
                K.mm(pskv[:, 0:G], onesb[:], sq[:, 2, 0:G])
                K.rstd(rst[0][:, 0:G], psq[:, 0:G], 256, lnq[:, 0:G])
                K.rstd(rst[1][:, 0:G], pskv[:, 0:G], 128, lnq[:, 0:G])
                qn_ = qn[g % 2]
                for c in range(2):
                    K.stt(qn_[:, c, 0:G], pq[c][:, 0:G], gq[:, c:c + 1], rst[0][:, 0:G], ALU.mult, ALU.mult)
                    dstore(qnTs[c * 128:(c + 1) * 128, s0:s0 + G], qn_[:, c, 0:G], "qn%d_%d" % (g % 2, c))
                K.stt(kvnT[:, s0:s0 + G], pkv[:, 0:G], gkv[:, 0:1], rst[1][:, 0:G], ALU.mult, ALU.mult)
                zdt_ = zdt[g % 2]
                xcm = xcms[g % 2]
                for t in range(nt):
                    tsl = slice(t * 128, (t + 1) * 128)
                    p_ = nb()
                    for k in range(8):
                        K.mm(p_[:, :], hT_[:, k, tsl], win[:, k, 448:960], start=(k == 0), stop=(k == 7))
                    K.copy("act", xcm[:, t, :], p_[:, :])
                    p2 = nb()
                    for k in range(8):
                        K.mm(p2[:, 0:256], hT_[:, k, tsl], win[:, k, 960:1216], start=(k == 0), stop=(k == 7))
                    for k in range(8):
                        K.mm(p2[:, 256:264], hT_[:, k, tsl], win[:, k, 1984:1992], start=(k == 0), stop=(k == 7))
                    K.copy("dve", zdt_[:, t, :], p2[:, 0:264])
                K.act(xcm[:, 0:nt, :], xcm[:, 0:nt, :], AF.Gelu_apprx_tanh)
                K.act(zdt_[:, 0:nt, 0:256], zdt_[:, 0:nt, 0:256], AF.Silu)
                dstore(zdts[s0:s0 + G, :].rearrange("(t p) n -> p t n", p=128), zdt_[:, 0:nt, :], "zdt%d" % (g % 2))

            def body_b(g, s0, G):
                nt = G // 128
                xcm = xcms[g % 2]
                for t in range(nt):
                    K.bn_stats(st6[:, t, :], xcm[:, t, 256:512])
                    K.bn_aggr(mv[:, t, :], st6[:, t, :])
                K.ts("dve", vpe[:, 0:nt], mv[:, 0:nt, 1], EPS, None, ALU.add)
                K.tt("pool", rscm[:, 0:nt], vpe[:, 0:nt], cneg[:, 0:nt], ALU.pow)
                ycmT_ = ycmT[g % 2]
                for t in range(nt):
                    vf = vnf[t % 2]; vb = vnb[t % 2]; yc = ycm[t % 2]
                    K.ts("dve", vf[:], xcm[:, t, 256:512], mv[:, t, 0:1], rscm[:, t:t + 1], ALU.subtract, ALU.mult)
                    K.tt("pool", vb[:], vf[:], cmg[:], ALU.mult)
                    p_ = nb()
                    for gi in range(4):
                        K.mm(p_[:, gi * 64:(gi + 1) * 64], wsT[:, gi, :], vb[:, gi * 64:(gi + 1) * 64])
                    for gi in range(4):
                        gs = slice(gi * 64, (gi + 1) * 64)
                        K.stt(yc[:, gs], p_[:, gs], cmb[:, gi:gi + 1], xcm[:, t, gs], ALU.add, ALU.mult)
                    for c in range(2):
                        K.tr(pT[:, c * 128:(c + 1) * 128], yc[:, c * 128:(c + 1) * 128], identb[:])
                    K.copy("act", ycmT_[:, :, t * 128:(t + 1) * 128], pT[:, 0:256].rearrange("p (c t) -> p c t", c=2))
                dstore(yTs[512:768, s0:s0 + G].rearrange("(c p) n -> p c n", p=128), ycmT_[:, :, 0:G], "ycmT%d" % (g % 2))

            prep(*GROUPS[0])
            for i_, grp_ in enumerate(GROUPS):
                if i_ + 1 < len(GROUPS):
                    prep(*GROUPS[i_ + 1])
                flush()
                body(*grp_)
                if i_ >= 1:
                    body_b(*GROUPS[i_ - 1])
            body_b(*GROUPS[-1])
            flush()
            S.barrier()

    def phase_a(l, kvnT, krT2):
        with contextlib.ExitStack() as es:
            sb = lambda n, sh, dt: es.enter_context(nc.sbuf_tensor(K.name(n), sh, dt))
            ps = lambda n, sh, dt: es.enter_context(nc.psum_tensor(K.name(n), sh, dt))
            KT = sb("KT", [128, 4, NS], BF16)
            Vaug = sb("Vaug", [128, NCH, 4, 130], BF16)
            wuq = sb("wuq", [128, 2, 1024], BF16)
            wukv = sb("wukv", [128, 1024], BF16)
            qn = [sb("qna%d" % i, [128, 2, 512], BF16) for i in range(2)]
            cs = [sb("csa%d" % i, [128, 512], F32) for i in range(2)]
            qh = [sb("qh%d" % i, [128, 512], BF16) for i in range(2)]
            qr = [sb("qr%d" % i, [128, 512], BF16) for i in range(2)]
            PT = [sb("PT%d" % i, [128, 512], BF16) for i in range(4)]
            yat = [sb("yat%d" % i, [128, 512], F32) for i in range(4)]
            yaT = [sb("yaT%d" % i, [128, 4, 512], BF16) for i in range(2)]
            rden = sb("rden", [128, 8], F32)
            acc = [ps("acc%d" % i, [128, 512], F32) for i in range(4)]
            psc = [ps("psc%d" % i, [128, 512], F32) for i in range(3)]
            pqu = ps("pqu", [128, 512], F32)
            K.dma(wuq[:], w_uqx[l].rearrange("(c p) n -> p c n", p=128), "wuq", eng="pool")
            K.dma(wukv[:], w_ukvr[l], "wukv", eng="pool")
            K.memset("pool", Vaug[:], 1.0)
            for j in range(32):
                K.dma(wf1s[l, j].rearrange("p (k c) -> p k c", k=8),
                      w_ff1[l].rearrange("(k p) n -> p k n", p=128)[:, :, j * 128:(j + 1) * 128], "wf1cast%d" % (j % 4), eng="pool")
            for (g, s0, G) in GROUPS:
                for h in range(4):
                    p_ = psc[h % 2]
                    K.mm(p_[:, 0:G], wukv[:, h * 128:(h + 1) * 128], kvnT[:, s0:s0 + G])
                    K.copy("act" if h % 2 else "dve", KT[:, h, s0:s0 + G], p_[:, 0:G])
                for t in range(G // 128):
                    p_ = acc[t]
                    K.mm(p_[:, :], kvnT[:, s0 + t * 128:s0 + (t + 1) * 128], wukv[:, 512:1024])
                    K.copy("act" if t % 2 else "dve", Vaug[:, s0 // 128 + t, :, 0:128], p_[:, :].rearrange("p (h d) -> p h d", h=4))
            pti = [0]

            def aload(g, s0, G):
                K.dma(qn[g % 2][:, :, 0:G], qnTs[:, s0:s0 + G].rearrange("(c p) n -> p c n", p=128), "qna%d" % (g % 2))
                if g == 0:
                    K.dma(cs[g % 2][:, 0:G], csc[:, 0:G], "csa%d" % (g % 2))
                else:
                    K.dma(cs[g % 2][:, 0:G], csq[:, s0 - NCTX:s0 - NCTX + G], "csa%d" % (g % 2))

            for (g, s0, G) in GROUPS:
                nt = G // 128
                kts = [0, 1] if g == 0 else list(range(NCH))
                qn_ = qn[g % 2]; cs_ = cs[g % 2]; yaT_ = yaT[g % 2]
                if g == 0:
                    aload(g, s0, G)
                if g + 1 < len(GROUPS):
                    aload(*GROUPS[g + 1])
                for h in range(4):
                    qh_ = qh[h % 2]; qr_ = qr[h % 2]
                    for c in range(2):
                        K.mm(pqu[:, 0:G], wuq[:, c, h * 256:h * 256 + 128], qn_[:, c, 0:G], start=(c == 0), stop=(c == 1))
                    K.copy("dve", qh_[:, 0:G], pqu[:, 0:G])
                    for c in range(2):
                        K.mm(pqu[:, 0:G], wuq[:, c, h * 256 + 128:h * 256 + 256], qn_[:, c, 0:G], start=(c == 0), stop=(c == 1))
                    K.tt("dve", qr_[:, 0:G], pqu[:, 0:G], cs_[:, 0:G], ALU.mult)
                    stash = {}

                    def score(i):
                        kt = kts[i]
                        ksl = slice(kt * 128, (kt + 1) * 128)
                        p_ = psc[pti[0] % 3]
                        P_ = PT[pti[0] % 4]
                        pti[0] += 1
                        K.mm(p_[:, 0:G], KT[:, h, ksl], qh_[:, 0:G], start=True, stop=False)
                        K.mm(p_[:, 0:G], krT2[:, ksl], qr_[:, 0:G], start=False, stop=True)
                        K.act(P_[:, 0:G], p_[:, 0:G], AF.Exp, scale=SCALE)
                        stash[i] = P_

                    score(0)
                    if len(kts) > 1:
                        score(1)
                    for i, kt in enumerate(kts):
                        if i + 2 < len(kts):
                            score(i + 2)
                        P_ = stash.pop(i)
                        for qt in range(nt):
                            K.mm(acc[qt][:, 0:129], P_[:, qt * 128:(qt + 1) * 128], Vaug[:, kt, h, 0:129],
                                 start=(i == 0), stop=(i == len(kts) - 1))
                    for qt in range(nt):
                        K.recip(rden[:, qt:qt + 1], acc[qt][:, 128:129])
                        K.ts("dve", yat[qt][:, h * 128:(h + 1) * 128], acc[qt][:, 0:128], rden[:, qt:qt + 1], None, ALU.mult)
                for qt in range(nt):
                    for h in range(4):
                        K.tr(pqu[:, h * 128:(h + 1) * 128], yat[qt][:, h * 128:(h + 1) * 128], identf[:])
                    K.copy("act", yaT_[:, :, qt * 128:(qt + 1) * 128], pqu[:, 0:512].rearrange("p (h t) -> p h t", h=4))
                K.dma(yTs[0:512, s0:s0 + G].rearrange("(h p) n -> p h n", p=128), yaT_[:, :, 0:G], "yaT%d" % (g % 2))
            S.barrier()

    def phase_s(l):
        with contextlib.ExitStack() as es:
            sb = lambda n, sh, dt: es.enter_context(nc.sbuf_tensor(K.name(n), sh, dt))
            ps = lambda n, sh, dt: es.enter_context(nc.psum_tensor(K.name(n), sh, dt))
            cw = sb("cw", [128, 18], F32); cbias = sb("cbias", [128, 6], F32)
            sm = sb("sm", [128, 24], F32); Abc = sb("Abc", [128, 8], F32); dsum = sb("dsum", [128, 4], F32)
            sgb = sb("sgb", [128, 256], F32)
            Sb_all = sb("Sb_all", [128, NCH, 256], F32)
            yp_all = sb("yp_all", [128, NCH, 256], F32)
            cmT_all = sb("cmT_all", [128, 2, NS], BF16)
            dfsb_all = sb("dfsb_all", [128, NCH, 4], F32); decb_all = sb("decb_all", [128, NCH, 4], F32)
            stf = sb("stf", [128, 256], F32); stfb = sb("stfb", [128, 256], BF16)
            stb = sb("stb", [128, 256], F32); stbb = sb("stbb", [128, 256], BF16)
            xr = [sb("xr%d" % i, [128, 6, 514], F32) for i in range(2)]
            cv = sb("cv", [128, 6, 512], F32)
            bcT = sb("bcT", [128, 4, 512], BF16)
            dtr = [sb("dtr%d" % i, [128, 4, 8], F32) for i in range(2)]
            sp_ = [sb("sp%d" % i, [128, 4, 8], F32) for i in range(6)]
            E = sb("E", [128, 40], F32)
            xs_tok = sb("xs_tok", [128, 256], F32)
            bm_tok = sb("bm_tok", [128, 2, 128], BF16)
            wde = sb("wde", [128, 8], F32)
            xdt = sb("xdt", [128, 2, 256], BF16); xdte = sb("xdte", [128, 2, 256], BF16)
            Lm = sb("Lm", [128, 8, 128], F32); eL = sb("eL", [128, 8, 128], F32)
            GTm = sb("GTm", [128, 2, 2, 128], F32); W = sb("W", [128, 8, 128], BF16)
            t1 = sb("t1s", [128, 256], F32); t2 = sb("t2s", [128, 256], F32)
            sz = [sb("sz%d" % i, [128, 256], F32) for i in range(3)]
            yb = sb("yb", [128, 256], BF16); junk = sb("junks", [128, 256], BF16)
            st3 = sb("st3", [128, 3], F32)
            ysT = [sb("ysT%d" % i, [128, 2, 512], BF16) for i in range(2)]
            pcs = ps("pcs", [128, 512], F32)
            pseg = ps("pseg", [128, 1024], F32)
            pxs = ps("pxs", [128, 512], F32)
            pbm = ps("pbm", [128, D], BF16)
            pG = ps("pG", [128, 512], F32)
            pst = ps("pst", [128, 512], F32)
            pyo = ps("pyo", [128, 512], F32)

            K.dma(cw[:], convw[l], "cw"); K.dma(cbias[:], convb[l], "cbias")
            K.dma(sm[:], brow(ssd_sm[l:l + 1, :], 24), "sm")
            K.dma(sgb[:], brow(ssd_g[l:l + 1, :], 256), "sgb")
            K.act(Abc[:], sm[:, 8:16], AF.Exp)
            K.ts("dve", Abc[:], Abc[:], -1.0, None, ALU.mult)
            K.tt("dve", dsum[:], sm[:, 16:20], sm[:, 20:24], ALU.add)
            K.memset("pool", stf[:], 0.0); K.memset("pool", stfb[:], 0.0)
            K.memset("pool", stb[:], 0.0); K.memset("pool", stbb[:], 0.0)
            v4 = lambda ap: ap.rearrange("p (h d) -> p h d", h=4)

            for (g, s0, G) in GROUPS:
                nt = G // 128
                xr_ = xr[g % 2]; dtr_ = dtr[g % 2]
                c0 = xbcol(s0)
                K.dma(xr_[:, :, 0:G + 2], xbcs.rearrange("(c p) n -> p c n", p=128)[:, :, c0 - 1:c0 + G + 1], "xr%d" % (g % 2))
                K.dma(dtr_[:, 0:nt, :], zdts[s0:s0 + G, 256:264].rearrange("(t p) n -> p t n", p=128), "dtr%d" % (g % 2))
                xsp, nx, mn, lg, dt, a = [t_[:, 0:nt, :] for t_ in sp_]
                K.tt("dve", xsp, dtr_[:, 0:nt, :], bcmid(sm[:, 0:8], nt), ALU.add)
                K.ts("dve", nx, xsp, -1.0, None, ALU.mult)
                K.tt("dve", mn, xsp, nx, ALU.min)
                K.act(nx, mn, AF.Exp)
                K.act(lg, nx, AF.Ln, bias=1.0)
                K.stt(dt, xsp, 0.0, lg, ALU.max, ALU.add)
                K.tt("dve", a, dt, bcmid(Abc[:], nt), ALU.mult)
                for c in range(6):
                    K.ts("dve", cv[:, c, 0:G], xr_[:, c, 0:G], cw[:, c * 3:c * 3 + 1], cbias[:, c:c + 1], ALU.mult, ALU.add)
                    K.stt(cv[:, c, 0:G], xr_[:, c, 1:G + 1], cw[:, c * 3 + 1:c * 3 + 2], cv[:, c, 0:G], ALU.mult, ALU.add)
                    K.stt(cv[:, c, 0:G], xr_[:, c, 2:G + 2], cw[:, c * 3 + 2:c * 3 + 3], cv[:, c, 0:G], ALU.mult, ALU.add)
                K.act(cv[:, :, 0:G], cv[:, :, 0:G], AF.Silu)
                K.copy("pool", bcT[:, :, 0:G], cv[:, 2:6, 0:G])
                K.copy("pool", cmT_all[:, :, s0:s0 + G], cv[:, 4:6, 0:G])
                for t in range(nt):
                    ci = s0 // 128 + t
                    tsl = slice(t * 128, (t + 1) * 128)
                    a_t = sp_[5][:, t, :]; dt_t = sp_[4][:, t, :]
                    for i, lt in enumerate([trit[:, 0, :], trit[:, 3, :], trit[:, 1, :], trit[:, 2, :], onesf[:]]):
                        K.mm(pcs[:, i * 8:(i + 1) * 8], lt, a_t)
                    K.act(E[:], pcs[:, 0:40], AF.Exp)
                    K.copy("pool", dfsb_all[:, ci, :], E[:, 28:32])
                    K.copy("pool", decb_all[:, ci, :], E[:, 36:40])
                    for c in range(2):
                        K.tr(pxs[:, c * 128:(c + 1) * 128], cv[:, c, tsl], identf[:])
                    K.copy("act", xs_tok[:], pxs[:, 0:256])
                    for grp in range(2):
                        K.tr(pbm[:, grp * 128:(grp + 1) * 128], bcT[:, grp, tsl], identb[:])
                    K.copy("act", bm_tok[:].rearrange("p a b -> p (a b)"), pbm[:, 0:256])
                    K.tt("dve", wde[:, 0:4], dt_t[:, 0:4], E[:, 8:12], ALU.mult)
                    K.tt("dve", wde[:, 4:8], dt_t[:, 4:8], E[:, 20:24], ALU.mult)
                    for d in range(2):
                        K.tt("dve", v4(xdt[:, d, :]), v4(xs_tok[:]), bc3(dt_t[:, d * 4:(d + 1) * 4], 64), ALU.mult)
                        K.tt("pool", v4(xdte[:, d, :]), v4(xs_tok[:]), bc3(wde[:, d * 4:(d + 1) * 4], 64), ALU.mult)
                    for j in range(8):
                        d = j // 4
                        K.ts("dve" if j % 2 else "pool", Lm[:, j, :], trit[:, 3 if d == 0 else 1, :], a_t[:, j:j + 1], 0.0, ALU.mult, ALU.add)
                        K.mm(pseg[:, j * 128:(j + 1) * 128], Lm[:, j, :], trit[:, 0 if d == 0 else 2, :])
                    K.act(eL[:].rearrange("p a b -> p (a b)"), pseg[:], AF.Exp)
                    for grp in range(2):
                        K.mm(pG[:, grp * 128:(grp + 1) * 128], bcT[:, grp, tsl], bcT[:, 2 + grp, tsl])
                    for d in range(2):
                        K.tt("dve", GTm[:, d, :, :], pG[:, 0:256].rearrange("p (a b) -> p a b", a=2),
                             bcmid(trit[:, 0 if d == 0 else 2, :], 2), ALU.mult)
                    for d in range(2):
                        for grp in range(2):
                            j0 = d * 4 + grp * 2
                            K.tt("dve", W[:, j0:j0 + 2, :], eL[:, j0:j0 + 2, :], bcmid(GTm[:, d, grp, :], 2), ALU.mult)
                    for h in range(4):
                        hs = slice(h * 64, (h + 1) * 64)
                        for d in range(2):
                            K.mm(pG[:, 256 + h * 64:256 + (h + 1) * 64], W[:, d * 4 + h, :], xdt[:, d, hs], start=(d == 0), stop=(d == 1))
                    for j in range(8):
                        d, h = j // 4, j % 4
                        K.mm(pst[:, j * 64:(j + 1) * 64], bm_tok[:, h // 2, :], xdte[:, d, h * 64:(h + 1) * 64])
                    for h in range(4):
                        hs = slice(h * 64, (h + 1) * 64)
                        K.mm(pyo[:, hs], bcT[:, 2 + h // 2, tsl], stfb[:, hs])
                    K.tt("dve", v4(t1[:]), v4(pyo[:, 0:256]), bc3(E[:, 0:4], 64), ALU.mult)
                    K.tt("dve", t1[:], pG[:, 256:512], t1[:], ALU.add)
                    K.tt("pool", v4(t2[:]), v4(xs_tok[:]), bc3(dsum[:], 64), ALU.mult)
                    K.tt("pool", yp_all[:, ci, :], t1[:], t2[:], ALU.add)
                    K.tt("dve", v4(stf[:]), v4(stf[:]), bc3(E[:, 32:36], 64), ALU.mult)
                    K.tt("dve", stf[:], pst[:, 0:256], stf[:], ALU.add)
                    K.copy("pool", stfb[:], stf[:])
                    K.copy("act", Sb_all[:, ci, :], pst[:, 256:512])

            order = [1, 0] + list(range(NCH - 1, 1, -1))
            for n_, ci in enumerate(order):
                csl = slice(ci * 128, (ci + 1) * 128)
                sz_ = sz[n_ % 3]
                if n_ == 0:
                    for m_ in range(2):
                        K.dma(sz[m_ % 3][:], zdts[order[m_] * 128:(order[m_] + 1) * 128, 0:256], "sz%d" % (m_ % 3))
                if n_ + 2 < len(order):
                    K.dma(sz[(n_ + 2) % 3][:], zdts[order[n_ + 2] * 128:(order[n_ + 2] + 1) * 128, 0:256], "sz%d" % ((n_ + 2) % 3))
                for h in range(4):
                    hs = slice(h * 64, (h + 1) * 64)
                    K.mm(pyo[:, hs], cmT_all[:, h // 2, csl], stbb[:, hs])
                K.tt("dve", v4(t1[:]), v4(pyo[:, 0:256]), bc3(dfsb_all[:, ci, :], 64), ALU.mult)
                K.tt("dve", t1[:], t1[:], yp_all[:, ci, :], ALU.add)
                K.tt("dve", t1[:], t1[:], sz_[:], ALU.mult)
                K.act(junk[:], t1[:], AF.Square, accum=st3[:, 0:1])
                K.rstd(st3[:, 2:3], st3[:, 0:1], 256, st3[:, 1:2])
                K.stt(yb[:], t1[:], st3[:, 2:3], sgb[:], ALU.mult, ALU.mult)
                if ci < 2:
                    buf, slot, last = ysT[0], ci, (ci == 0)
                else:
                    gidx = (ci - 2) // 4
                    buf, slot, last = ysT[(gidx + 1) % 2], (ci - 2) % 4, ((ci - 2) % 4 == 0)
                for c in range(2):
                    K.tr(pbm[:, c * 128:(c + 1) * 128], yb[:, c * 128:(c + 1) * 128], identb[:])
                K.copy("act", buf[:, :, slot * 128:(slot + 1) * 128], pbm[:, 0:256].rearrange("p (c t) -> p c t", c=2))
                if last:
                    if ci < 2:
                        K.dma(yTs[768:1024, 0:256].rearrange("(c p) n -> p c n", p=128), buf[:, :, 0:256], "ysT0")
                    else:
                        K.dma(yTs[768:1024, 256 + gidx * 512:256 + (gidx + 1) * 512].rearrange("(c p) n -> p c n", p=128),
                              buf[:, :, :], "ysT%d" % ((gidx + 1) % 2))
                K.tt("dve", v4(stb[:]), v4(stb[:]), bc3(decb_all[:, ci, :], 64), ALU.mult)
                K.tt("dve", stb[:], stb[:], Sb_all[:, ci, :], ALU.add)
                K.copy("pool", stbb[:], stb[:])
            S.barrier()

    def phase_o(l):
        with contextlib.ExitStack() as es:
            sb = lambda n, sh, dt: es.enter_context(nc.sbuf_tensor(K.name(n), sh, dt))
            ps = lambda n, sh, dt: es.enter_context(nc.psum_tensor(K.name(n), sh, dt))
            wout = sb("wout", [128, 8, D], BF16)
            wff2 = sb("wff2", [128, 32, D], BF16)
            w1r = [sb("w1r%d" % i, [128, 8, 128], BF16) for i in range(6)]
            md = sb("md", [128, 4, D], F32)
            yT = sb("yTo", [128, 8, 512], BF16)
            xt = [sb("xt%d" % i, [128, D], F32) for i in range(5)]
            g2c = sb("g2c", [128, D], F32)
            tmpf = sb("tmpo", [128, D], F32)
            junk = sb("junko", [128, D], BF16)
            hb = [sb("hbo%d" % i, [128, D], BF16) for i in range(2)]
            h2T = [sb("h2T%d" % i, [128, 8, 512], BF16) for i in range(2)]
            rl = [sb("rl%d" % i, [128, 512], BF16) for i in range(2)]
            aT = sb("aT", [128, 32, 512], BF16)
            st = sb("sto", [128, 3, 16], F32)
            pT = ps("pTo", [128, D], BF16)
            pa = [ps("pa%d" % i, [128, 512], F32) for i in range(7)]
            bank = [0]

            def nb():
                bank[0] += 1
                return pa[bank[0] % 7]

            K.dma(wout[:], w_out[l].rearrange("(k p) n -> p k n", p=128), "wout", eng="pool")
            for jj in range(4):
                K.dma(wff2[:, jj * 8:(jj + 1) * 8, :], w_ff2[l, jj * 1024:(jj + 1) * 1024, :].rearrange("(k p) n -> p k n", p=128),
                      "wff2_%d" % jj, eng="pool")
            xi = [0]
            w1i = [0]
            sti = [0]

            def newx():
                x_ = xt[xi[0] % 5]; xkey = "xt%d" % (xi[0] % 5); xi[0] += 1
                return x_, xkey

            def post(pp, gcol, x_, gt_=None):
                si = sti[0] % 16; sti[0] += 1
                K.act(junk[:, 0:512], pp[0][:, :], AF.Square, accum=st[:, 0, si:si + 1])
                K.act(junk[:, 512:1024], pp[1][:, :], AF.Square, accum=st[:, 1, si:si + 1])
                K.tt("dve", st[:, 0, si:si + 1], st[:, 0, si:si + 1], st[:, 1, si:si + 1], ALU.add)
                K.rstd(st[:, 2, si:si + 1], st[:, 0, si:si + 1], D, st[:, 1, si:si + 1])
                for half in range(2):
                    hs = slice(half * 512, (half + 1) * 512)
                    K.stt(tmpf[:, hs], pp[half][:, :], st[:, 2, si:si + 1], (md[:, gcol, hs] if gt_ is None else gt_[:, hs]), ALU.mult, ALU.mult)
                K.tt("pool", x_[:], x_[:], tmpf[:], ALU.add)

            pend = {}

            def prep_load(g, s0, G):
                nt = G // 128
                v = 1 if g == 0 else 0
                if g in (0, 1):
                    for i, m in enumerate((2, 4, 3, 5)):
                        K.dma(md[:, i, :], brow(mods[l, v, m:m + 1, :], D), "md%d" % i)
                    if g == 0:
                        K.dma(g2c[:], brow(mods[l, 1, 5:6, :], D), "g2c")
                K.dma(yT[:, :, 0:G], yTs[:, s0:s0 + G].rearrange("(k p) n -> p k n", p=128), "yTo")
                xs_ = []
                for t in range(nt):
                    x_, xkey = newx()
                    xs_.append((x_, xkey))
                    K.dma(x_[:], xsrc(l, s0 + t * 128, 128), xkey)
                pend[g] = xs_

            def prep_op(g, s0, G, t):
                tsl = slice(t * 128, (t + 1) * 128)
                x_, xkey = pend[g][t]
                pp = [nb(), nb()]
                for half in range(2):
                    for k in range(8):
                        K.mm(pp[half][:, :], yT[:, k, tsl], wout[:, k, half * 512:(half + 1) * 512], start=(k == 0), stop=(k == 7))
                post(pp, 0, x_)
                K.dma(xscr[s0 + t * 128:s0 + (t + 1) * 128, :], x_[:], xkey, eng="pool")
                si2 = sti[0] % 16; sti[0] += 1
                K.act(junk[:], x_[:], AF.Square, accum=st[:, 0, si2:si2 + 1])
                K.rstd(st[:, 2, si2:si2 + 1], st[:, 0, si2:si2 + 1], D, st[:, 1, si2:si2 + 1])
                K.stt(tmpf[:], x_[:], st[:, 2, si2:si2 + 1], md[:, 1, :], ALU.mult, ALU.mult)
                K.tt("dve", hb[t % 2][:], tmpf[:], md[:, 2, :], ALU.add)

            def prep_tr(g, s0, G, t):
                tsl = slice(t * 128, (t + 1) * 128)
                hb_ = hb[t % 2]
                for k in range(8):
                    K.tr(pT[:, k * 128:(k + 1) * 128], hb_[:, k * 128:(k + 1) * 128], identb[:])
                K.copy("act", h2T[g % 2][:, :, tsl], pT[:].rearrange("p (k t) -> p k t", k=8))

            def ff1(g, s0, G, j0, j1):
                for j in range(j0, j1):
                    w_ = w1r[w1i[0] % 6]
                    K.dma(w_[:], wf1s[l, j].rearrange("p (k c) -> p k c", k=8), "w1r%d" % (w1i[0] % 6))
                    w1i[0] += 1
                    p_ = nb()
                    for k in range(8):
                        K.mm(p_[:, 0:G], w_[:, k, :], h2T[g % 2][:, k, 0:G], start=(k == 0), stop=(k == 7))
                    r_ = rl[j % 2]
                    K.act(r_[:, 0:G], p_[:, 0:G], AF.Relu)
                    K.tt("dve", aT[:, j, 0:G], r_[:, 0:G], p_[:, 0:G], ALU.mult)

            def ff2(g, s0, G, t):
                if True:
                    tsl = slice(t * 128, (t + 1) * 128)
                    x_, xkey = newx()
                    K.dma(x_[:], xscr[s0 + t * 128:s0 + (t + 1) * 128, :], xkey)
                    pp = [nb(), nb()]
                    for half in range(2):
                        for j in range(32):
                            K.mm(pp[half][:, :], aT[:, j, tsl], wff2[:, j, half * 512:(half + 1) * 512], start=(j == 0), stop=(j == 31))
                    post(pp, 3, x_, g2c if g == 0 else None)
                    K.dma(xdst(l, s0 + t * 128, 128), x_[:], xkey, eng="pool")

            g0 = GROUPS[0]
            prep_load(*g0)
            for t in range(g0[2] // 128):
                prep_op(*g0, t)
                prep_tr(*g0, t)
            for i_, cur in enumerate(GROUPS):
                nxt = GROUPS[i_ + 1] if i_ + 1 < len(GROUPS) else None
                nt_c = cur[2] // 128
                if nxt is not None:
                    prep_load(*nxt)
                for q_ in range(4):
                    ff1(*cur, q_ * 8, (q_ + 1) * 8)
                    if nxt is not None:
                        prep_op(*nxt, q_)
                        if q_ >= 1:
                            prep_tr(*nxt, q_ - 1)
                ff2(*cur, 0)
                if nxt is not None:
                    prep_tr(*nxt, 3)
                for t in range(1, nt_c):
                    ff2(*cur, t)
            S.barrier()

    for l in range(n_layers):
        phase_m(l)
        with contextlib.ExitStack() as esr:
            kvnT = esr.enter_context(nc.sbuf_tensor(K.name("kvnT"), [128, NS], BF16))
            krT2 = esr.enter_context(nc.sbuf_tensor(K.name("krT2"), [128, NS], BF16))
            phase_1(l, kvnT, krT2)
            phase_a(l, kvnT, krT2)
        phase_s(l)
        phase_o(l)
    S.barrier()
    with contextlib.ExitStack() as es2:
        n = S.emit(es2)
    es0.close()
    return nc, n


def _rope_tables():
    rows_n = SEQ // 64
    row = np.repeat(np.arange(rows_n, dtype=np.float32), 64)
    col = np.tile(np.arange(64, dtype=np.float32), rows_n)
    inv = (np.float32(10000.0) ** (-np.arange(0, 32, 2, dtype=np.float32) / np.float32(32))).astype(np.float32)
    ar = (row[:, None] * inv).astype(np.float32)
    ac = (col[:, None] * inv).astype(np.float32)
    cos = np.zeros((64, SEQ), np.float32); sin = np.zeros((64, SEQ), np.float32)
    for d in range(64):
        ang = ar if d < 32 else ac
        f = d % 16
        cos[d] = np.cos(ang[:, f])
        sgn = -1.0 if (d % 32) < 16 else 1.0
        sin[d] = sgn * np.sin(ang[:, f])
    return cos, sin


_PERM = np.array([d + 16 if (d % 32) < 16 else d - 16 for d in range(64)])


def _host_layout(inp):
    f = lambda a: np.ascontiguousarray(a, dtype=np.float32)
    sh = {}
    sh["w_ada"] = f(inp["w_ada"]); sh["b_ada"] = f(inp["b_ada"])
    sh["g4"] = f(np.concatenate([inp["g_pre_mix"], inp["g_post_mix"], inp["g_pre_ff"], inp["g_post_ff"]], axis=1))
    w_in = inp["w_in"]
    sh["w_in"] = f(w_in)
    kr = w_in[:, :, 384:448]
    rot = kr[:, :, _PERM]
    sh["w_kr2"] = f(np.concatenate([kr, kr, rot, rot], axis=2))
    sh["gq_pc"] = f(inp["g_q"].reshape(NL, 2, 128).transpose(0, 2, 1))
    sh["gkv_p"] = f(inp["g_kv"].reshape(NL, 128, 1))
    wuq = inp["w_uq"].reshape(NL, 256, 4, 192)
    sh["w_uqx"] = f(np.concatenate([wuq, wuq[:, :, :, 128 + _PERM]], axis=3).reshape(NL, 256, 1024))
    wukv = inp["w_ukv"].reshape(NL, 128, 4, 256)
    sh["w_ukvr"] = f(np.concatenate([wukv[:, :, :, :128].reshape(NL, 128, 512), wukv[:, :, :, 128:].reshape(NL, 128, 512)], axis=2))
    sh["cm_g"] = f(inp["cm_norm_g"]); sh["cm_ws"] = f(inp["cm_w_s"])
    sh["cm_bt"] = f(inp["cm_b_s"].transpose(0, 2, 1))
    cw = inp["ssd_conv_w"].reshape(NL, 3, 6, 128).transpose(0, 3, 2, 1)
    sh["convw"] = f(cw.reshape(NL, 128, 18))
    sh["convb"] = f(inp["ssd_conv_b"].reshape(NL, 6, 128).transpose(0, 2, 1))
    sh["ssd_sm"] = f(np.concatenate([inp["ssd_dt_bias"].reshape(NL, 8), inp["ssd_a_log"].reshape(NL, 8), inp["ssd_d"].reshape(NL, 8)], axis=1))
    sh["ssd_g"] = f(inp["ssd_norm_g"])
    sh["w_out"] = f(inp["w_out"]); sh["w_ff1"] = f(inp["w_ff1"]); sh["w_ff2"] = f(inp["w_ff2"])
    cos, sin = _rope_tables()
    sh["cosd"] = f(np.concatenate([cos, cos], axis=0)); sh["sind"] = f(np.concatenate([sin, sin], axis=0))
    sh["csq"] = f(np.concatenate([cos, sin], axis=0))
    sh["csc"] = f(np.concatenate([np.ones((64, 512), np.float32), np.zeros((64, 512), np.float32)], axis=0))
    k = np.arange(128)[:, None]; l_ = np.arange(128)[None, :]
    tri = np.stack([(k <= l_), (k < l_), (k >= l_), (k > l_)], axis=1).astype(np.float32)
    sh["tri"] = f(tri.reshape(128, 512))
    sh["cc_pk"] = f(inp["c_ctx"].reshape(8, 128).T)
    return sh


_CACHE = {}


def kernel(**inputs):
    inp = {k: np.asarray(v) for k, v in inputs.items()}
    if "nc" not in _CACHE:
        _CACHE["nc"] = build()[0]
    nc = _CACHE["nc"]
    shared = _host_layout(inp)
    in_maps = []
    for b in range(8):
        m = dict(shared)
        m["x"] = np.ascontiguousarray(inp["x"][b], dtype=np.float32)
        m["ctx"] = np.ascontiguousarray(inp["ctx"][b], dtype=np.float32)
        m["c_pk"] = np.ascontiguousarray(inp["c"][b].reshape(8, 128).T, dtype=np.float32)
        in_maps.append(m)
    res = run_bass_kernel_spmd(nc, in_maps, core_ids=list(range(8)))
    return np.stack([np.asarray(r["out"], dtype=np.float32) for r in res.results], axis=0)
```

```python
import contextlib
import numpy as np
import concourse.bass as bass
import concourse.mybir as mybir
from concourse.bass_utils import run_bass_kernel_spmd

F32 = mybir.dt.float32
BF16 = mybir.dt.bfloat16
AF = mybir.ActivationFunctionType
ALU = mybir.AluOpType

EPS = 1e-6
NL = 4
D = 1024
SEQ = 4096
NCTX = 256
NS = SEQ + NCTX
NCH = NS // 128
SCALE = 192.0 ** -0.5
XB_COLS = 4356


def _esize(dt):
    return 2 if dt == BF16 else 4
class _Op:
    __slots__ = ("eng", "fn", "dma", "semkey", "waits", "sig", "cnt", "idx", "dmaval", "gid")


def _rect(ap):
    t = ap.ap
    es = _esize(ap.dtype)
    off = int(ap.offset)
    if str(ap.space) == "DRAM":
        ext = 0
        for s, c in t:
            ext += (c - 1) * abs(s)
        return (ap.name, 0, 1, off * es, (off + ext + 1) * es)
    pstep, pcnt = t[0]
    if pstep == 0:
        pstep = 1 << 40
    p0 = off // pstep
    f0 = off % pstep
    ext = 0
    for s, c in t[1:]:
        ext += (c - 1) * abs(s)
    if str(ap.space) == "PSUM":
        return (ap.name, 0, 128, (f0 * es) // 2048 * 2048, ((f0 + ext + 1) * es + 2047) // 2048 * 2048)
    return (ap.name, p0, p0 + pcnt, f0 * es, (f0 + ext + 1) * es)


class Sched:
    ENG = ("pe", "act", "dve", "pool", "sp")

    def __init__(self, nc):
        self.nc = nc
        self.ops = []
        self.acc = {}
        self.eng_ops = {e: [] for e in self.ENG}
        self.waited = {e: {x: -1 for x in self.ENG} for e in self.ENG}
        self.dma_last = {}
        self.dma_keys = {}
        self.dma_waited = {e: set() for e in self.ENG}
        self.unwaited = set()

    def add(self, eng, fn, reads=(), writes=(), dma=None):
        op = _Op()
        op.eng = eng
        op.fn = fn
        op.dma = dma is not None
        op.semkey = dma
        op.sig = False
        op.gid = len(self.ops)
        deps = set()
        rrects = [_rect(a) for a in reads]
        wrects = [_rect(a) for a in writes]
        for r in rrects:
            for rec in self.acc.get(r[0], ()):
                q = rec[0]
                if rec[2] and q[1] < r[2] and r[1] < q[2] and q[3] < r[4] and r[3] < q[4]:
                    deps.add(rec[1])
        for r in wrects:
            for rec in self.acc.get(r[0], ()):
                q = rec[0]
                if q[1] < r[2] and r[1] < q[2] and q[3] < r[4] and r[3] < q[4]:
                    deps.add(rec[1])
        if op.dma:
            prev = self.dma_last.get(dma)
            if prev is not None:
                deps.add(prev)
            self.dma_last[dma] = op.gid
            cnt = self.dma_keys.get(dma, 0) + 1
            self.dma_keys[dma] = cnt
            op.dmaval = 16 * cnt
            self.unwaited.add(op.gid)
        deps.discard(op.gid)
        waits = []
        best = {}
        for d in deps:
            o = self.ops[d]
            if o.dma:
                if d not in self.dma_waited[eng]:
                    self.dma_waited[eng].add(d)
                    self.unwaited.discard(d)
                    waits.append(("dma", d))
            else:
                if o.eng == "pe" and eng == "pe" and not op.dma:
                    continue
                if o.idx > best.get(o.eng, -1):
                    best[o.eng] = o.idx
        for x, i in best.items():
            if self.waited[eng][x] < i:
                self.waited[eng][x] = i
                waits.append(("eng", x, i))
                self.eng_ops[x][i].sig = True
        op.waits = waits
        if not op.dma:
            op.idx = len(self.eng_ops[eng])
            self.eng_ops[eng].append(op)
        else:
            op.idx = -1
        self.ops.append(op)
        for r in wrects:
            lst = self.acc.setdefault(r[0], [])
            lst[:] = [rec for rec in lst if not (r[1] <= rec[0][1] and rec[0][2] <= r[2]
                                                  and r[3] <= rec[0][3] and rec[0][4] <= r[4])]
            lst.append((r, op.gid, True, eng if not op.dma else None))
        for r in rrects:
            lst = self.acc.setdefault(r[0], [])
            if not op.dma:
                lst[:] = [rec for rec in lst if not (not rec[2] and rec[3] == eng and rec[0] == r)]
            lst.append((r, op.gid, False, eng if not op.dma else None))
        return op

    def fence(self, eng, reads=(), writes=()):
        return self.add(eng, None, reads, writes)

    def barrier(self):
        pend = sorted(self.unwaited)
        self.unwaited = set()
        last = {e: (self.eng_ops[e][-1].gid if self.eng_ops[e] else None) for e in self.ENG}
        for e in self.ENG:
            o = _Op()
            o.eng = e
            o.fn = None
            o.dma = False
            o.semkey = None
            o.sig = False
            o.gid = len(self.ops)
            waits = []
            for x in self.ENG:
                if x == e or last[x] is None:
                    continue
                i = self.ops[last[x]].idx
                if self.waited[e][x] < i:
                    self.waited[e][x] = i
                    waits.append(("eng", x, i))
                    self.eng_ops[x][i].sig = True
            for d in pend:
                if d not in self.dma_waited[e]:
                    self.dma_waited[e].add(d)
                    waits.append(("dma", d))
            o.waits = waits
            o.idx = len(self.eng_ops[e])
            self.eng_ops[e].append(o)
            self.ops.append(o)
        self.acc = {}

    def emit(self, sems_ctx):
        nc = self.nc
        engobj = {"pe": nc.tensor, "act": nc.scalar, "dve": nc.vector, "pool": nc.gpsimd, "sp": nc.sync}
        esem = {e: sems_ctx.enter_context(nc.semaphore("s_" + e)) for e in self.ENG}
        dsem = {k: sems_ctx.enter_context(nc.semaphore("d_%d" % i)) for i, k in enumerate(self.dma_keys)}
        for e in self.ENG:
            c = 0
            for o in self.eng_ops[e]:
                if o.sig:
                    c += 1
                    o.cnt = c
        n_inst = 0
        for o in self.ops:
            eo = engobj[o.eng]
            for w in o.waits:
                if w[0] == "dma":
                    d = self.ops[w[1]]
                    eo.wait_ge(dsem[d.semkey], d.dmaval)
                else:
                    eo.wait_ge(esem[w[1]], self.eng_ops[w[1]][w[2]].cnt)
            if o.fn is None:
                if o.sig:
                    eo.nop().then_inc(esem[o.eng], 1)
                continue
            inst = o.fn(eo)
            n_inst += 1
            if o.dma:
                inst.then_inc(dsem[o.semkey], 16)
            elif o.sig:
                inst.then_inc(esem[o.eng], 1)
        return n_inst


class KB:
    def __init__(self, nc):
        self.nc = nc
        self.S = Sched(nc)
        self.uid = 0

    def name(self, n):
        self.uid += 1
        return "%s_%d" % (n, self.uid)

    def dma(self, out, in_, key, eng="sp", slow=False):
        if slow:
            self.S.add(eng, lambda e: e.dma_start(out=out, in_=in_, allow_slow_non_contiguous=True), reads=[in_], writes=[out], dma=key)
        else:
            self.S.add(eng, lambda e: e.dma_start(out=out, in_=in_), reads=[in_], writes=[out], dma=key)

    def mm(self, out, lhsT, rhs, start=True, stop=True):
        self.S.add("pe", lambda e: e.matmul(out, lhsT=lhsT, rhs=rhs, start=start, stop=stop),
                   reads=[lhsT, rhs], writes=[out])

    def tr(self, out, in_, ident):
        self.S.add("pe", lambda e: e.transpose(out=out, in_=in_, identity=ident), reads=[in_, ident], writes=[out])

    def act(self, out, in_, func, bias=None, scale=None, accum=None):
        kw = {}
        rd = [in_]
        wr = [out]
        if bias is not None:
            kw["bias"] = bias
            if not isinstance(bias, float):
                rd.append(bias)
        if scale is not None:
            kw["scale"] = scale
            if not isinstance(scale, float):
                rd.append(scale)
        if accum is not None:
            kw["accum_out"] = accum
            wr.append(accum)
        self.S.add("act", lambda e: e.activation(out=out, in_=in_, func=func, **kw), reads=rd, writes=wr)

    def copy(self, eng, out, in_):
        if eng == "act":
            self.S.add("act", lambda e: e.copy(out=out, in_=in_), reads=[in_], writes=[out])
        else:
            self.S.add(eng, lambda e: e.tensor_copy(out=out, in_=in_), reads=[in_], writes=[out])

    def tt(self, eng, out, in0, in1, op):
        self.S.add(eng, lambda e: e.tensor_tensor(out=out, in0=in0, in1=in1, op=op), reads=[in0, in1], writes=[out])

    def ts(self, eng, out, in0, s1, s2, op0, op1=None):
        rd = [in0]
        if not isinstance(s1, float):
            rd.append(s1)
        if s2 is not None and not isinstance(s2, float):
            rd.append(s2)
        if op1 is None:
            self.S.add(eng, lambda e: e.tensor_scalar(out=out, in0=in0, scalar1=s1, scalar2=None, op0=op0), reads=rd, writes=[out])
        else:
            self.S.add(eng, lambda e: e.tensor_scalar(out=out, in0=in0, scalar1=s1, scalar2=s2, op0=op0, op1=op1), reads=rd, writes=[out])

    def stt(self, out, in0, scalar, in1, op0, op1):
        rd = [in0, in1]
        if not isinstance(scalar, float):
            rd.append(scalar)
        self.S.add("dve", lambda e: e.scalar_tensor_tensor(out=out, in0=in0, scalar=scalar, in1=in1, op0=op0, op1=op1),
                   reads=rd, writes=[out])

    def memset(self, eng, ap, val):
        self.S.add(eng, lambda e: e.memset(ap, val), writes=[ap])

    def recip(self, out, in_):
        self.S.add("dve", lambda e: e.reciprocal(out=out, in_=in_), reads=[in_], writes=[out])

    def bn_stats(self, out, in_):
        self.S.add("dve", lambda e: e.bn_stats(out=out, in_=in_), reads=[in_], writes=[out])

    def bn_aggr(self, out, in_):
        self.S.add("dve", lambda e: e.bn_aggr(out=out, in_=in_), reads=[in_], writes=[out])

    def rstd(self, out, in_, n, tmp):
        self.act(tmp, in_, AF.Ln, bias=EPS, scale=1.0 / n)
        self.act(out, tmp, AF.Exp, scale=-0.5)


def bc3(ap2, n):
    return ap2.unsqueeze(2).to_broadcast([ap2.shape[0], ap2.shape[1], n])


def bcmid(ap2, n):
    return ap2.unsqueeze(1).to_broadcast([ap2.shape[0], n, ap2.shape[1]])


def build(n_layers=NL, dbg=False):
    nc = bass.Bass("TRN2", target_bir_lowering=False)
    K = KB(nc)
    S = K.S

    def din(name, shape, dt=F32):
        return nc.dram_tensor(name, shape, dt, kind="ExternalInput").ap()

    def dscr(name, shape, dt=F32):
        return nc.dram_tensor(name, shape, dt, kind=("ExternalOutput" if dbg else "Internal")).ap()

    x_in = din("x", [SEQ, D]); ctx_in = din("ctx", [NCTX, D])
    c_pk = din("c_pk", [128, 8]); cc_pk = din("cc_pk", [128, 8])
    w_ada = din("w_ada", [NL, D, 6 * D]); b_ada = din("b_ada", [NL, 6 * D])
    g4 = din("g4", [NL, 4 * D])
    w_in = din("w_in", [NL, D, 1992]); w_kr2 = din("w_kr2", [NL, D, 256])
    gq_pc = din("gq_pc", [NL, 128, 2]); gkv_p = din("gkv_p", [NL, 128, 1])
    w_uqx = din("w_uqx", [NL, 256, 1024]); w_ukvr = din("w_ukvr", [NL, 128, 1024])
    cm_g = din("cm_g", [NL, 256]); cm_ws = din("cm_ws", [NL, 4, 128, 128]); cm_bt = din("cm_bt", [NL, 128, 4])
    convw = din("convw", [NL, 128, 18]); convb = din("convb", [NL, 128, 6])
    ssd_sm = din("ssd_sm", [NL, 24]); ssd_g = din("ssd_g", [NL, 256])
    w_out = din("w_out", [NL, D, D]); w_ff1 = din("w_ff1", [NL, D, 4 * D]); w_ff2 = din("w_ff2", [NL, 4 * D, D])
    cosd = din("cosd", [128, SEQ]); sind = din("sind", [128, SEQ]); csq = din("csq", [128, SEQ]); csc = din("csc", [128, 512])
    tri = din("tri", [128, 4 * 128])
    out = nc.dram_tensor("out", [SEQ, D], F32, kind="ExternalOutput").ap()

    xscr = dscr("xscr", [NS, D])
    mods = dscr("mods", [NL, 2, 6, D])
    xbcs = dscr("xbcs", [768, XB_COLS])
    zdts = dscr("zdts", [NS, 264])
    yTs = dscr("yTs", [D, NS], BF16)
    qnTs = dscr("qnTs", [256, NS], BF16)
    wf1s = dscr("wf1s", [NL, 32, 128, 1024], BF16)

    GROUPS = [(0, 0, 256)] + [(g, 256 + (g - 1) * 512, 512) for g in range(1, 9)]

    def xsrc(l, s, n):
        if l == 0:
            return ctx_in[s:s + n, :] if s < NCTX else x_in[s - NCTX:s - NCTX + n, :]
        return xscr[s:s + n, :]

    def xdst(l, s, n):
        if l == n_layers - 1 and s >= NCTX:
            return out[s - NCTX:s - NCTX + n, :]
        return xscr[s:s + n, :]

    def xbcol(s):
        return 1 + s if s < NCTX else 259 + (s - NCTX)

    def brow(ap_row, n):
        return ap_row.broadcast_to([128, n])

    es0 = contextlib.ExitStack()
    sb0 = lambda n, sh, dt: es0.enter_context(nc.sbuf_tensor(n, sh, dt))
    identf = sb0("identf", [128, 128], F32)
    identb = sb0("identb", [128, 128], BF16)
    onesb = sb0("onesb", [128, 128], BF16)
    onesf = sb0("onesf", [128, 128], F32)
    trit = sb0("trit", [128, 4, 128], F32)
    zrow = sb0("zrow", [128, 8], F32)
    K.dma(trit[:].rearrange("p a b -> p (a b)"), tri[:, :], "trit")
    K.memset("pool", onesf[:], 1.0)
    K.memset("pool", onesb[:], 1.0)
    K.memset("pool", zrow[:], 0.0)
    K.tt("dve", identf[:], trit[:, 0, :], trit[:, 2, :], ALU.mult)
    K.copy("dve", identb[:], identf[:])
    for c in range(6):
        for col in (0, 257, 258, 4355):
            K.dma(xbcs[c * 128:(c + 1) * 128, col:col + 1], zrow[:, 0:1], "zpad", slow=True)
    def phase_m(l):
        with contextlib.ExitStack() as es:
            sb = lambda n, sh, dt: es.enter_context(nc.sbuf_tensor(K.name(n), sh, dt))
            ps = lambda n, sh, dt: es.enter_context(nc.psum_tensor(K.name(n), sh, dt))
            sca = sb("sca", [128, 2, 8], F32)
            sc2 = sb("sc2", [128, 8, 2], F32)
            bada = sb("bada", [2, 6 * D], F32)
            gt = sb("gt", [2, 4 * D], F32)
            wa = [sb("wa%d" % i, [128, 8, 512], F32) for i in range(3)]
            mt = [sb("mt%d" % i, [2, 512], F32) for i in range(2)]
            res = [sb("res%d" % i, [2, 512], F32) for i in range(4)]
            pm = [ps("pm%d" % i, [128, 512], F32) for i in range(4)]
            K.dma(sca[:, 0, :], c_pk[:, :], "sc0")
            K.dma(sca[:, 1, :], cc_pk[:, :], "sc1")
            K.dma(bada[:], b_ada[l:l + 1, :].broadcast_to([2, 6 * D]), "bada")
            K.dma(gt[:], g4[l:l + 1, :].broadcast_to([2, 4 * D]), "gt")
            K.act(sca[:], sca[:], AF.Silu)
            for v in range(2):
                K.copy("dve", sc2[:, :, v], sca[:, v, :])
            wv = w_ada[l].rearrange("(k p) n -> p k n", p=128)
            for j in range(12):
                w_ = wa[j % 3]
                K.dma(w_[:], wv[:, :, j * 512:(j + 1) * 512], "wa%d" % (j % 3))
                m, half = j // 2, j % 2
                p_ = pm[j % 4]
                for k in range(8):
                    K.mm(p_[0:2, :], sc2[:, k, :], w_[:, k, :], start=(k == 0), stop=(k == 7))
                t_ = mt[j % 2]
                r_ = res[j % 4]
                K.tt("dve", t_[:], p_[0:2, :], bada[:, j * 512:(j + 1) * 512], ALU.add)
                if m in (0, 3):
                    K.copy("dve", r_[:], t_[:])
                elif m in (1, 4):
                    o_ = (0 if m == 1 else 2) * D + half * 512
                    K.stt(r_[:], t_[:], 1.0, gt[:, o_:o_ + 512], ALU.add, ALU.mult)
                else:
                    o_ = (1 if m == 2 else 3) * D + half * 512
                    K.tt("dve", r_[:], t_[:], gt[:, o_:o_ + 512], ALU.mult)
                K.dma(mods[l, :, m, half * 512:(half + 1) * 512], r_[:], "res%d" % (j % 4))
            S.barrier()

    def phase_1(l, kvnT, krT2):
        with contextlib.ExitStack() as es:
            sb = lambda n, sh, dt: es.enter_context(nc.sbuf_tensor(K.name(n), sh, dt))
            ps = lambda n, sh, dt: es.enter_context(nc.psum_tensor(K.name(n), sh, dt))
            win = sb("win", [128, 8, 1992], BF16)
            wkr2 = sb("wkr2", [128, 8, 256], BF16)
            gq = sb("gq", [128, 2], F32); gkv = sb("gkv", [128, 1], F32)
            cmg = sb("cmg", [128, 256], F32); cmb = sb("cmb", [128, 4], F32)
            wsf = sb("wsf", [128, 4, 128], F32); wsT = sb("wsT", [128, 4, 128], BF16)
            ab = sb("ab", [128, 2, D], F32)
            xg = [sb("xg%d" % i, [128, D], F32) for i in range(3)]
            hT = [sb("hT%d" % i, [128, 8, 512], BF16) for i in range(2)]
            hb = [sb("hb%d" % i, [128, D], BF16) for i in range(4)]
            tmpf = sb("tmpf", [128, D], F32)
            junk = sb("junk", [128, D], BF16)
            st4 = sb("st4", [128, 3, 4], F32)
            sq = sb("sq", [128, 3, 512], BF16)
            lnq = sb("lnq", [128, 512], F32)
            rst = [sb("rst%d" % i, [128, 512], F32) for i in range(2)]
            qn = [sb("qn%d" % i, [128, 2, 512], BF16) for i in range(2)]
            cosk = sb("cosk", [128, 512], F32); sink = sb("sink", [128, 512], F32)
            t1 = sb("t1", [128, 512], F32); t2 = sb("t2", [128, 512], F32)
            xbst = [sb("xbst%d" % i, [128, 6, 512], F32) for i in range(2)]
            xcms = [sb("xcm%d" % i, [128, 4, 512], F32) for i in range(2)]
            zdt = [sb("zdt%d" % i, [128, 4, 264], F32) for i in range(2)]
            st6 = sb("st6", [128, 4, 6], F32); mv = sb("mv", [128, 4, 2], F32)
            vpe = sb("vpe", [128, 4], F32); rscm = sb("rscm", [128, 4], F32); cneg = sb("cneg", [128, 4], F32)
            vnf = [sb("vnf%d" % i, [128, 256], F32) for i in range(2)]
            vnb = [sb("vnb%d" % i, [128, 256], BF16) for i in range(2)]
            ycm = [sb("ycm%d" % i, [128, 256], BF16) for i in range(2)]
            ycmT = [sb("ycmT%d" % i, [128, 2, 512], BF16) for i in range(2)]
            pT = ps("pT", [128, D], BF16)
            pf = [ps("pf%d" % i, [128, 512], F32) for i in range(7)]
            bank = [0]

            def nb():
                bank[0] += 1
                return pf[3 + bank[0] % 4]

            K.dma(win[:], w_in[l].rearrange("(k p) n -> p k n", p=128), "win", eng="pool")
            K.dma(wkr2[:], w_kr2[l].rearrange("(k p) n -> p k n", p=128), "wkr2", eng="pool")
            K.dma(gq[:], gq_pc[l], "gq"); K.dma(gkv[:], gkv_p[l], "gkv")
            K.dma(cmg[:], brow(cm_g[l:l + 1, :], 256), "cmg"); K.dma(cmb[:], cm_bt[l], "cmb")
            K.dma(wsf[:], cm_ws[l].rearrange("g t s -> t g s"), "wsf")
            K.memset("pool", cneg[:], -0.5)
            for gi in range(4):
                p_ = nb()
                K.tr(p_[:, 0:128], wsf[:, gi, :], identf[:])
                K.copy("dve", wsT[:, gi, :], p_[:, 0:128])

            xi = [0]
            deferred = []

            def dstore(out_, in__, key):
                deferred.append((out_, in__, key))

            def flush():
                for (o_, i_, k_) in deferred:
                    K.dma(o_, i_, k_)
                del deferred[:]

            def prep(g, s0, G):
                nt = G // 128
                v = 1 if g == 0 else 0
                if g in (0, 1):
                    K.dma(ab[:, 0, :], brow(mods[l, v, 1:2, :], D), "ab0")
                    K.dma(ab[:, 1, :], brow(mods[l, v, 0:1, :], D), "ab1")
                hT_ = hT[g % 2]
                for t in range(nt):
                    x_ = xg[xi[0] % 3]; xi[0] += 1
                    K.dma(x_[:], xsrc(l, s0 + t * 128, 128), "xg%d" % ((xi[0] - 1) % 3))
                    K.act(junk[:], x_[:], AF.Square, accum=st4[:, 0, t:t + 1])
                    K.rstd(st4[:, 2, t:t + 1], st4[:, 0, t:t + 1], D, st4[:, 1, t:t + 1])
                    K.stt(tmpf[:], x_[:], st4[:, 2, t:t + 1], ab[:, 0, :], ALU.mult, ALU.mult)
                    K.tt("dve", hb[t][:], tmpf[:], ab[:, 1, :], ALU.add)

            def prep_b(g, s0, G):
                nt = G // 128
                hT_ = hT[g % 2]
                for t in range(nt):
                    for k in range(8):
                        K.tr(pT[:, k * 128:(k + 1) * 128], hb[t][:, k * 128:(k + 1) * 128], identb[:])
                    K.copy("act", hT_[:, :, t * 128:(t + 1) * 128], pT[:].rearrange("p (k t) -> p k t", k=8))

            def body(g, s0, G):
                nt = G // 128
                hT_ = hT[g % 2]

                def fm(col0, m, wt=win, p_=None):
                    if p_ is None:
                        p_ = nb()
                    for k in range(8):
                        K.mm(p_[0:m, 0:G], wt[:, k, col0:col0 + m], hT_[:, k, 0:G], start=(k == 0), stop=(k == 7))
                    return p_

                pq = [fm(0, 128, p_=pf[0]), fm(128, 128, p_=pf[1])]
                pkv = fm(256, 128, p_=pf[2])
                for c in range(2):
                    K.act(sq[:, c, 0:G], pq[c][:, 0:G], AF.Square)
                K.act(sq[:, 2, 0:G], pkv[:, 0:G], AF.Square)
                pkr = fm(0, 128, wkr2)
                if g == 0:
                    K.copy("dve", krT2[:, s0:s0 + G], pkr[:, 0:G])
                else:
                    pkrr = fm(128, 128, wkr2)
                    K.dma(cosk[:], cosd[:, s0 - NCTX:s0 - NCTX + G], "cosk")
                    K.dma(sink[:], sind[:, s0 - NCTX:s0 - NCTX + G], "sink")
                    K.tt("dve", t1[:, 0:G], pkr[:, 0:G], cosk[:, 0:G], ALU.mult)
                    K.tt("dve", t2[:, 0:G], pkrr[:, 0:G], sink[:, 0:G], ALU.mult)
                    K.tt("pool", krT2[:, s0:s0 + G], t1[:, 0:G], t2[:, 0:G], ALU.add)
                for c in range(6):
                    p_ = fm(1216 + c * 128, 128)
                    K.copy("act", xbst[g % 2][:, c, 0:G], p_[:, 0:G])
                dstore(xbcs.rearrange("(c p) n -> p c n", p=128)[:, :, xbcol(s0):xbcol(s0) + G], xbst[g % 2][:, :, 0:G], "xbst%d" % (g % 2))
                psq = nb()
                for c in range(2):
                    K.mm(psq[:, 0:G], onesb[:], sq[:, c, 0:G], start=(c == 0), stop=(c == 1))
                pskv = nb()
                K.mm(pskv[:, 0:G], onesb[:], sq[:, 2, 0:G])
                K.rstd(rst[0][:, 0:G], psq[:, 0:G], 256, lnq[:, 0:G])
                K.rstd(rst[1][:, 0:G], pskv[:, 0:G], 128, lnq[:, 0:G])
                qn_ = qn[g % 2]
                for c in range(2):
                    K.stt(qn_[:, c, 0:G], pq[c][:, 0:G], gq[:, c:c + 1], rst[0][:, 0:G], ALU.mult, ALU.mult)
                    dstore(qnTs[c * 128:(c + 1) * 128, s0:s0 + G], qn_[:, c, 0:G], "qn%d_%d" % (g % 2, c))
                K.stt(kvnT[:, s0:s0 + G], pkv[:, 0:G], gkv[:, 0:1], rst[1][:, 0:G], ALU.mult, ALU.mult)
                zdt_ = zdt[g % 2]
                xcm = xcms[g % 2]
                for t in range(nt):
                    tsl = slice(t * 128, (t + 1) * 128)
                    p_ = nb()
                    for k in range(8):
                        K.mm(p_[:, :], hT_[:, k, tsl], win[:, k, 448:960], start=(k == 0), stop=(k == 7))
                    K.copy("act", xcm[:, t, :], p_[:, :])
                    p2 = nb()
                    for k in range(8):
                        K.mm(p2[:, 0:256], hT_[:, k, tsl], win[:, k, 960:1216], start=(k == 0), stop=(k == 7))
                    for k in range(8):
                        K.mm(p2[:, 256:264], hT_[:, k, tsl], win[:, k, 1984:1992], start=(k == 0), stop=(k == 7))
                    K.copy("dve", zdt_[:, t, :], p2[:, 0:264])
                K.act(xcm[:, 0:nt, :], xcm[:, 0:nt, :], AF.Gelu_apprx_tanh)
                K.act(zdt_[:, 0:nt, 0:256], zdt_[:, 0:nt, 0:256], AF.Silu)
                dstore(zdts[s0:s0 + G, :].rearrange("(t p) n -> p t n", p=128), zdt_[:, 0:nt, :], "zdt%d" % (g % 2))

            def body_b(g, s0, G):
                nt = G // 128
                xcm = xcms[g % 2]
                for t in range(nt):
                    K.bn_stats(st6[:, t, :], xcm[:, t, 256:512])
                    K.bn_aggr(mv[:, t, :], st6[:, t, :])
                K.ts("dve", vpe[:, 0:nt], mv[:, 0:nt, 1], EPS, None, ALU.add)
                K.tt("pool", rscm[:, 0:nt], vpe[:, 0:nt], cneg[:, 0:nt], ALU.pow)
                ycmT_ = ycmT[g % 2]
                for t in range(nt):
                    vf = vnf[t % 2]; vb = vnb[t % 2]; yc = ycm[t % 2]
                    K.ts("dve", vf[:], xcm[:, t, 256:512], mv[:, t, 0:1], rscm[:, t:t + 1], ALU.subtract, ALU.mult)
                    K.tt("pool", vb[:], vf[:], cmg[:], ALU.mult)
                    p_ = nb()
                    for gi in range(4):
                        K.mm(p_[:, gi * 64:(gi + 1) * 64], wsT[:, gi, :], vb[:, gi * 64:(gi + 1) * 64])
                    for gi in range(4):
                        gs = slice(gi * 64, (gi + 1) * 64)
                        K.stt(yc[:, gs], p_[:, gs], cmb[:, gi:gi + 1], xcm[:, t, gs], ALU.add, ALU.mult)
                    for c in range(2):
                        K.tr(pT[:, c * 128:(c + 1) * 128], yc[:, c * 128:(c + 1) * 128], identb[:])
                    K.copy("act", ycmT_[:, :, t * 128:(t + 1) * 128], pT[:, 0:256].rearrange("p (c t) -> p c t", c=2))
                dstore(yTs[512:768, s0:s0 + G].rearrange("(c p) n -> p c n", p=128), ycmT_[:, :, 0:G], "ycmT%d" % (g % 2))

            prep(*GROUPS[0])
            prep_b(*GROUPS[0])
            for i_, grp_ in enumerate(GROUPS):
                if i_ + 1 < len(GROUPS):
                    prep(*GROUPS[i_ + 1])
                flush()
                body(*grp_)
                if i_ + 1 < len(GROUPS):
                    prep_b(*GROUPS[i_ + 1])
                if i_ >= 1:
                    body_b(*GROUPS[i_ - 1])
            body_b(*GROUPS[-1])
            flush()
            S.barrier()

    def phase_a(l, kvnT, krT2):
        with contextlib.ExitStack() as es:
            sb = lambda n, sh, dt: es.enter_context(nc.sbuf_tensor(K.name(n), sh, dt))
            ps = lambda n, sh, dt: es.enter_context(nc.psum_tensor(K.name(n), sh, dt))
            KT = sb("KT", [128, 4, NS], BF16)
            Vaug = sb("Vaug", [128, NCH, 4, 130], BF16)
            wuq = sb("wuq", [128, 2, 1024], BF16)
            wukv = sb("wukv", [128, 1024], BF16)
            qn = [sb("qna%d" % i, [128, 2, 512], BF16) for i in range(2)]
            cs = [sb("csa%d" % i, [128, 512], F32) for i in range(2)]
            qh = [sb("qh%d" % i, [128, 512], BF16) for i in range(2)]
            qr = [sb("qr%d" % i, [128, 512], BF16) for i in range(2)]
            PT = [sb("PT%d" % i, [128, 512], BF16) for i in range(4)]
            yat = [sb("yat%d" % i, [128, 512], F32) for i in range(4)]
            yaT = [sb("yaT%d" % i, [128, 4, 512], BF16) for i in range(2)]
            rden = sb("rden", [128, 8], F32)
            acc = [ps("acc%d" % i, [128, 512], F32) for i in range(4)]
            psc = [ps("psc%d" % i, [128, 512], F32) for i in range(3)]
            pqu = ps("pqu", [128, 512], F32)
            K.dma(wuq[:], w_uqx[l].rearrange("(c p) n -> p c n", p=128), "wuq", eng="pool")
            K.dma(wukv[:], w_ukvr[l], "wukv", eng="pool")
            K.memset("pool", Vaug[:], 1.0)
            for j in range(32):
                K.dma(wf1s[l, j].rearrange("p (k c) -> p k c", k=8),
                      w_ff1[l].rearrange("(k p) n -> p k n", p=128)[:, :, j * 128:(j + 1) * 128], "wf1cast%d" % (j % 4), eng="pool")
            for (g, s0, G) in GROUPS:
                for h in range(4):
                    p_ = psc[h % 2]
                    K.mm(p_[:, 0:G], wukv[:, h * 128:(h + 1) * 128], kvnT[:, s0:s0 + G])
                    K.copy("act" if h % 2 else "dve", KT[:, h, s0:s0 + G], p_[:, 0:G])
                for t in range(G // 128):
                    p_ = acc[t]
                    K.mm(p_[:, :], kvnT[:, s0 + t * 128:s0 + (t + 1) * 128], wukv[:, 512:1024])
                    K.copy("act" if t % 2 else "dve", Vaug[:, s0 // 128 + t, :, 0:128], p_[:, :].rearrange("p (h d) -> p h d", h=4))
            pti = [0]

            def aload(g, s0, G):
                K.dma(qn[g % 2][:, :, 0:G], qnTs[:, s0:s0 + G].rearrange("(c p) n -> p c n", p=128), "qna%d" % (g % 2))
                if g == 0:
                    K.dma(cs[g % 2][:, 0:G], csc[:, 0:G], "csa%d" % (g % 2))
                else:
                    K.dma(cs[g % 2][:, 0:G], csq[:, s0 - NCTX:s0 - NCTX + G], "csa%d" % (g % 2))

            for (g, s0, G) in GROUPS:
                nt = G // 128
                kts = [0, 1] if g == 0 else list(range(NCH))
                qn_ = qn[g % 2]; cs_ = cs[g % 2]; yaT_ = yaT[g % 2]
                if g == 0:
                    aload(g, s0, G)
                if g + 1 < len(GROUPS):
                    aload(*GROUPS[g + 1])
                for h in range(4):
                    qh_ = qh[h % 2]; qr_ = qr[h % 2]
                    for c in range(2):
                        K.mm(pqu[:, 0:G], wuq[:, c, h * 256:h * 256 + 128], qn_[:, c, 0:G], start=(c == 0), stop=(c == 1))
                    K.copy("dve", qh_[:, 0:G], pqu[:, 0:G])
                    for c in range(2):
                        K.mm(pqu[:, 0:G], wuq[:, c, h * 256 + 128:h * 256 + 256], qn_[:, c, 0:G], start=(c == 0), stop=(c == 1))
                    K.tt("dve", qr_[:, 0:G], pqu[:, 0:G], cs_[:, 0:G], ALU.mult)
                    stash = {}

                    def score(i):
                        kt = kts[i]
                        ksl = slice(kt * 128, (kt + 1) * 128)
                        p_ = psc[pti[0] % 3]
                        P_ = PT[pti[0] % 4]
                        pti[0] += 1
                        K.mm(p_[:, 0:G], KT[:, h, ksl], qh_[:, 0:G], start=True, stop=False)
                        K.mm(p_[:, 0:G], krT2[:, ksl], qr_[:, 0:G], start=False, stop=True)
                        K.act(P_[:, 0:G], p_[:, 0:G], AF.Exp, scale=SCALE)
                        stash[i] = P_

                    score(0)
                    if len(kts) > 1:
                        score(1)
                    for i, kt in enumerate(kts):
                        if i + 2 < len(kts):
                            score(i + 2)
                        P_ = stash.pop(i)
                        for qt in range(nt):
                            K.mm(acc[qt][:, 0:129], P_[:, qt * 128:(qt + 1) * 128], Vaug[:, kt, h, 0:129],
                                 start=(i == 0), stop=(i == len(kts) - 1))
                    for qt in range(nt):
                        K.recip(rden[:, qt:qt + 1], acc[qt][:, 128:129])
                        K.ts("dve", yat[qt][:, h * 128:(h + 1) * 128], acc[qt][:, 0:128], rden[:, qt:qt + 1], None, ALU.mult)
                for qt in range(nt):
                    for h in range(4):
                        K.tr(pqu[:, h * 128:(h + 1) * 128], yat[qt][:, h * 128:(h + 1) * 128], identf[:])
                    K.copy("act", yaT_[:, :, qt * 128:(qt + 1) * 128], pqu[:, 0:512].rearrange("p (h t) -> p h t", h=4))
                K.dma(yTs[0:512, s0:s0 + G].rearrange("(h p) n -> p h n", p=128), yaT_[:, :, 0:G], "yaT%d" % (g % 2))
            S.barrier()

    def phase_s(l):
        with contextlib.ExitStack() as es:
            sb = lambda n, sh, dt: es.enter_context(nc.sbuf_tensor(K.name(n), sh, dt))
            ps = lambda n, sh, dt: es.enter_context(nc.psum_tensor(K.name(n), sh, dt))
            cw = sb("cw", [128, 18], F32); cbias = sb("cbias", [128, 6], F32)
            sm = sb("sm", [128, 24], F32); Abc = sb("Abc", [128, 8], F32); dsum = sb("dsum", [128, 4], F32)
            sgb = sb("sgb", [128, 256], F32)
            Sb_all = sb("Sb_all", [128, NCH, 256], F32)
            yp_all = sb("yp_all", [128, NCH, 256], F32)
            cmT_all = sb("cmT_all", [128, 2, NS], BF16)
            dfsb_all = sb("dfsb_all", [128, NCH, 4], F32); decb_all = sb("decb_all", [128, NCH, 4], F32)
            stf = sb("stf", [128, 256], F32); stfb = sb("stfb", [128, 256], BF16)
            stb = sb("stb", [128, 256], F32); stbb = sb("stbb", [128, 256], BF16)
            xr = [sb("xr%d" % i, [128, 6, 514], F32) for i in range(2)]
            cv = sb("cv", [128, 6, 512], F32)
            bcT = sb("bcT", [128, 4, 512], BF16)
            dtr = [sb("dtr%d" % i, [128, 4, 8], F32) for i in range(2)]
            sp_ = [sb("sp%d" % i, [128, 4, 8], F32) for i in range(6)]
            E = sb("E", [128, 40], F32)
            xs_tok = sb("xs_tok", [128, 256], F32)
            bm_tok = sb("bm_tok", [128, 2, 128], BF16)
            wde = sb("wde", [128, 8], F32)
            xdt = sb("xdt", [128, 2, 256], BF16); xdte = sb("xdte", [128, 2, 256], BF16)
            Lm = sb("Lm", [128, 8, 128], F32); eL = sb("eL", [128, 8, 128], F32)
            GTm = sb("GTm", [128, 2, 2, 128], F32); W = sb("W", [128, 8, 128], BF16)
            t1 = sb("t1s", [128, 256], F32); t2 = sb("t2s", [128, 256], F32)
            sz = [sb("sz%d" % i, [128, 256], F32) for i in range(3)]
            yb = sb("yb", [128, 256], BF16); junk = sb("junks", [128, 256], BF16)
            st3 = sb("st3", [128, 3], F32)
            ysT = [sb("ysT%d" % i, [128, 2, 512], BF16) for i in range(2)]
            pcs = ps("pcs", [128, 512], F32)
            pseg = ps("pseg", [128, 1024], F32)
            pxs = ps("pxs", [128, 512], F32)
            pbm = ps("pbm", [128, D], BF16)
            pG = ps("pG", [128, 512], F32)
            pst = ps("pst", [128, 512], F32)
            pyo = ps("pyo", [128, 512], F32)

            K.dma(cw[:], convw[l], "cw"); K.dma(cbias[:], convb[l], "cbias")
            K.dma(sm[:], brow(ssd_sm[l:l + 1, :], 24), "sm")
            K.dma(sgb[:], brow(ssd_g[l:l + 1, :], 256), "sgb")
            K.act(Abc[:], sm[:, 8:16], AF.Exp)
            K.ts("dve", Abc[:], Abc[:], -1.0, None, ALU.mult)
            K.tt("dve", dsum[:], sm[:, 16:20], sm[:, 20:24], ALU.add)
            K.memset("pool", stf[:], 0.0); K.memset("pool", stfb[:], 0.0)
            K.memset("pool", stb[:], 0.0); K.memset("pool", stbb[:], 0.0)
            v4 = lambda ap: ap.rearrange("p (h d) -> p h d", h=4)

            for (g, s0, G) in GROUPS:
                nt = G // 128
                xr_ = xr[g % 2]; dtr_ = dtr[g % 2]
                c0 = xbcol(s0)
                K.dma(xr_[:, :, 0:G + 2], xbcs.rearrange("(c p) n -> p c n", p=128)[:, :, c0 - 1:c0 + G + 1], "xr%d" % (g % 2))
                K.dma(dtr_[:, 0:nt, :], zdts[s0:s0 + G, 256:264].rearrange("(t p) n -> p t n", p=128), "dtr%d" % (g % 2))
                xsp, nx, mn, lg, dt, a = [t_[:, 0:nt, :] for t_ in sp_]
                K.tt("dve", xsp, dtr_[:, 0:nt, :], bcmid(sm[:, 0:8], nt), ALU.add)
                K.ts("dve", nx, xsp, -1.0, None, ALU.mult)
                K.tt("dve", mn, xsp, nx, ALU.min)
                K.act(nx, mn, AF.Exp)
                K.act(lg, nx, AF.Ln, bias=1.0)
                K.stt(dt, xsp, 0.0, lg, ALU.max, ALU.add)
                K.tt("dve", a, dt, bcmid(Abc[:], nt), ALU.mult)
                for c in range(6):
                    K.ts("dve", cv[:, c, 0:G], xr_[:, c, 0:G], cw[:, c * 3:c * 3 + 1], cbias[:, c:c + 1], ALU.mult, ALU.add)
                    K.stt(cv[:, c, 0:G], xr_[:, c, 1:G + 1], cw[:, c * 3 + 1:c * 3 + 2], cv[:, c, 0:G], ALU.mult, ALU.add)
                    K.stt(cv[:, c, 0:G], xr_[:, c, 2:G + 2], cw[:, c * 3 + 2:c * 3 + 3], cv[:, c, 0:G], ALU.mult, ALU.add)
                K.act(cv[:, :, 0:G], cv[:, :, 0:G], AF.Silu)
                K.copy("pool", bcT[:, :, 0:G], cv[:, 2:6, 0:G])
                K.copy("pool", cmT_all[:, :, s0:s0 + G], cv[:, 4:6, 0:G])
                for t in range(nt):
                    ci = s0 // 128 + t
                    tsl = slice(t * 128, (t + 1) * 128)
                    a_t = sp_[5][:, t, :]; dt_t = sp_[4][:, t, :]
                    for i, lt in enumerate([trit[:, 0, :], trit[:, 3, :], trit[:, 1, :], trit[:, 2, :], onesf[:]]):
                        K.mm(pcs[:, i * 8:(i + 1) * 8], lt, a_t)
                    K.act(E[:], pcs[:, 0:40], AF.Exp)
                    K.copy("pool", dfsb_all[:, ci, :], E[:, 28:32])
                    K.copy("pool", decb_all[:, ci, :], E[:, 36:40])
                    for c in range(2):
                        K.tr(pxs[:, c * 128:(c + 1) * 128], cv[:, c, tsl], identf[:])
                    K.copy("act", xs_tok[:], pxs[:, 0:256])
                    for grp in range(2):
                        K.tr(pbm[:, grp * 128:(grp + 1) * 128], bcT[:, grp, tsl], identb[:])
                    K.copy("act", bm_tok[:].rearrange("p a b -> p (a b)"), pbm[:, 0:256])
                    K.tt("dve", wde[:, 0:4], dt_t[:, 0:4], E[:, 8:12], ALU.mult)
                    K.tt("dve", wde[:, 4:8], dt_t[:, 4:8], E[:, 20:24], ALU.mult)
                    for d in range(2):
                        K.tt("dve", v4(xdt[:, d, :]), v4(xs_tok[:]), bc3(dt_t[:, d * 4:(d + 1) * 4], 64), ALU.mult)
                        K.tt("pool", v4(xdte[:, d, :]), v4(xs_tok[:]), bc3(wde[:, d * 4:(d + 1) * 4], 64), ALU.mult)
                    for j in range(8):
                        d = j // 4
                        K.ts("dve" if j % 2 else "pool", Lm[:, j, :], trit[:, 3 if d == 0 else 1, :], a_t[:, j:j + 1], 0.0, ALU.mult, ALU.add)
                        K.mm(pseg[:, j * 128:(j + 1) * 128], Lm[:, j, :], trit[:, 0 if d == 0 else 2, :])
                    K.act(eL[:].rearrange("p a b -> p (a b)"), pseg[:], AF.Exp)
                    for grp in range(2):
                        K.mm(pG[:, grp * 128:(grp + 1) * 128], bcT[:, grp, tsl], bcT[:, 2 + grp, tsl])
                    for d in range(2):
                        K.tt("dve", GTm[:, d, :, :], pG[:, 0:256].rearrange("p (a b) -> p a b", a=2),
                             bcmid(trit[:, 0 if d == 0 else 2, :], 2), ALU.mult)
                    for d in range(2):
                        for grp in range(2):
                            j0 = d * 4 + grp * 2
                            K.tt("dve", W[:, j0:j0 + 2, :], eL[:, j0:j0 + 2, :], bcmid(GTm[:, d, grp, :], 2), ALU.mult)
                    for h in range(4):
                        hs = slice(h * 64, (h + 1) * 64)
                        for d in range(2):
                            K.mm(pG[:, 256 + h * 64:256 + (h + 1) * 64], W[:, d * 4 + h, :], xdt[:, d, hs], start=(d == 0), stop=(d == 1))
                    for j in range(8):
                        d, h = j // 4, j % 4
                        K.mm(pst[:, j * 64:(j + 1) * 64], bm_tok[:, h // 2, :], xdte[:, d, h * 64:(h + 1) * 64])
                    for h in range(4):
                        hs = slice(h * 64, (h + 1) * 64)
                        K.mm(pyo[:, hs], bcT[:, 2 + h // 2, tsl], stfb[:, hs])
                    K.tt("dve", v4(t1[:]), v4(pyo[:, 0:256]), bc3(E[:, 0:4], 64), ALU.mult)
                    K.tt("dve", t1[:], pG[:, 256:512], t1[:], ALU.add)
                    K.tt("pool", v4(t2[:]), v4(xs_tok[:]), bc3(dsum[:], 64), ALU.mult)
                    K.tt("pool", yp_all[:, ci, :], t1[:], t2[:], ALU.add)
                    K.tt("dve", v4(stf[:]), v4(stf[:]), bc3(E[:, 32:36], 64), ALU.mult)
                    K.tt("dve", stf[:], pst[:, 0:256], stf[:], ALU.add)
                    K.copy("pool", stfb[:], stf[:])
                    K.copy("act", Sb_all[:, ci, :], pst[:, 256:512])

            order = [1, 0] + list(range(NCH - 1, 1, -1))
            for n_, ci in enumerate(order):
                csl = slice(ci * 128, (ci + 1) * 128)
                sz_ = sz[n_ % 3]
                if n_ == 0:
                    for m_ in range(2):
                        K.dma(sz[m_ % 3][:], zdts[order[m_] * 128:(order[m_] + 1) * 128, 0:256], "sz%d" % (m_ % 3))
                if n_ + 2 < len(order):
                    K.dma(sz[(n_ + 2) % 3][:], zdts[order[n_ + 2] * 128:(order[n_ + 2] + 1) * 128, 0:256], "sz%d" % ((n_ + 2) % 3))
                for h in range(4):
                    hs = slice(h * 64, (h + 1) * 64)
                    K.mm(pyo[:, hs], cmT_all[:, h // 2, csl], stbb[:, hs])
                K.tt("dve", v4(t1[:]), v4(pyo[:, 0:256]), bc3(dfsb_all[:, ci, :], 64), ALU.mult)
                K.tt("dve", t1[:], t1[:], yp_all[:, ci, :], ALU.add)
                K.tt("dve", t1[:], t1[:], sz_[:], ALU.mult)
                K.act(junk[:], t1[:], AF.Square, accum=st3[:, 0:1])
                K.rstd(st3[:, 2:3], st3[:, 0:1], 256, st3[:, 1:2])
                K.stt(yb[:], t1[:], st3[:, 2:3], sgb[:], ALU.mult, ALU.mult)
                if ci < 2:
                    buf, slot, last = ysT[0], ci, (ci == 0)
                else:
                    gidx = (ci - 2) // 4
                    buf, slot, last = ysT[(gidx + 1) % 2], (ci - 2) % 4, ((ci - 2) % 4 == 0)
                for c in range(2):
                    K.tr(pbm[:, c * 128:(c + 1) * 128], yb[:, c * 128:(c + 1) * 128], identb[:])
                K.copy("act", buf[:, :, slot * 128:(slot + 1) * 128], pbm[:, 0:256].rearrange("p (c t) -> p c t", c=2))
                if last:
                    if ci < 2:
                        K.dma(yTs[768:1024, 0:256].rearrange("(c p) n -> p c n", p=128), buf[:, :, 0:256], "ysT0")
                    else:
                        K.dma(yTs[768:1024, 256 + gidx * 512:256 + (gidx + 1) * 512].rearrange("(c p) n -> p c n", p=128),
                              buf[:, :, :], "ysT%d" % ((gidx + 1) % 2))
                K.tt("dve", v4(stb[:]), v4(stb[:]), bc3(decb_all[:, ci, :], 64), ALU.mult)
                K.tt("dve", stb[:], stb[:], Sb_all[:, ci, :], ALU.add)
                K.copy("pool", stbb[:], stb[:])
            S.barrier()

    def phase_o(l):
        with contextlib.ExitStack() as es:
            sb = lambda n, sh, dt: es.enter_context(nc.sbuf_tensor(K.name(n), sh, dt))
            ps = lambda n, sh, dt: es.enter_context(nc.psum_tensor(K.name(n), sh, dt))
            wout = sb("wout", [128, 8, D], BF16)
            wff2 = sb("wff2", [128, 32, D], BF16)
            w1r = [sb("w1r%d" % i, [128, 8, 128], BF16) for i in range(6)]
            md = sb("md", [128, 4, D], F32)
            yT = sb("yTo", [128, 8, 512], BF16)
            xt = [sb("xt%d" % i, [128, D], F32) for i in range(5)]
            g2c = sb("g2c", [128, D], F32)
            tmpf = sb("tmpo", [128, D], F32)
            junk = sb("junko", [128, D], BF16)
            hb = [sb("hbo%d" % i, [128, D], BF16) for i in range(2)]
            h2T = [sb("h2T%d" % i, [128, 8, 512], BF16) for i in range(2)]
            rl = [sb("rl%d" % i, [128, 512], BF16) for i in range(2)]
            aT = sb("aT", [128, 32, 512], BF16)
            st = sb("sto", [128, 3, 16], F32)
            pT = ps("pTo", [128, D], BF16)
            pa = [ps("pa%d" % i, [128, 512], F32) for i in range(7)]
            bank = [0]

            def nb():
                bank[0] += 1
                return pa[bank[0] % 7]

            K.dma(wout[:], w_out[l].rearrange("(k p) n -> p k n", p=128), "wout", eng="pool")
            for jj in range(4):
                K.dma(wff2[:, jj * 8:(jj + 1) * 8, :], w_ff2[l, jj * 1024:(jj + 1) * 1024, :].rearrange("(k p) n -> p k n", p=128),
                      "wff2_%d" % jj, eng="pool")
            xi = [0]
            w1i = [0]
            sti = [0]

            def newx():
                x_ = xt[xi[0] % 5]; xkey = "xt%d" % (xi[0] % 5); xi[0] += 1
                return x_, xkey

            def post(pp, gcol, x_, gt_=None):
                si = sti[0] % 16; sti[0] += 1
                K.act(junk[:, 0:512], pp[0][:, :], AF.Square, accum=st[:, 0, si:si + 1])
                K.act(junk[:, 512:1024], pp[1][:, :], AF.Square, accum=st[:, 1, si:si + 1])
                K.tt("dve", st[:, 0, si:si + 1], st[:, 0, si:si + 1], st[:, 1, si:si + 1], ALU.add)
                K.rstd(st[:, 2, si:si + 1], st[:, 0, si:si + 1], D, st[:, 1, si:si + 1])
                for half in range(2):
                    hs = slice(half * 512, (half + 1) * 512)
                    K.stt(tmpf[:, hs], pp[half][:, :], st[:, 2, si:si + 1], (md[:, gcol, hs] if gt_ is None else gt_[:, hs]), ALU.mult, ALU.mult)
                K.tt("pool", x_[:], x_[:], tmpf[:], ALU.add)

            pend = {}

            def prep_load(g, s0, G):
                nt = G // 128
                v = 1 if g == 0 else 0
                if g in (0, 1):
                    for i, m in enumerate((2, 4, 3, 5)):
                        K.dma(md[:, i, :], brow(mods[l, v, m:m + 1, :], D), "md%d" % i)
                    if g == 0:
                        K.dma(g2c[:], brow(mods[l, 1, 5:6, :], D), "g2c")
                K.dma(yT[:, :, 0:G], yTs[:, s0:s0 + G].rearrange("(k p) n -> p k n", p=128), "yTo")
                xs_ = []
                for t in range(nt):
                    x_, xkey = newx()
                    xs_.append((x_, xkey))
                    K.dma(x_[:], xsrc(l, s0 + t * 128, 128), xkey)
                pend[g] = xs_

            def prep_op(g, s0, G, t):
                tsl = slice(t * 128, (t + 1) * 128)
                x_, xkey = pend[g][t]
                pp = [nb(), nb()]
                for half in range(2):
                    for k in range(8):
                        K.mm(pp[half][:, :], yT[:, k, tsl], wout[:, k, half * 512:(half + 1) * 512], start=(k == 0), stop=(k == 7))
                post(pp, 0, x_)
                K.dma(xscr[s0 + t * 128:s0 + (t + 1) * 128, :], x_[:], xkey, eng="pool")
                si2 = sti[0] % 16; sti[0] += 1
                K.act(junk[:], x_[:], AF.Square, accum=st[:, 0, si2:si2 + 1])
                K.rstd(st[:, 2, si2:si2 + 1], st[:, 0, si2:si2 + 1], D, st[:, 1, si2:si2 + 1])
                K.stt(tmpf[:], x_[:], st[:, 2, si2:si2 + 1], md[:, 1, :], ALU.mult, ALU.mult)
                K.tt("dve", hb[t % 2][:], tmpf[:], md[:, 2, :], ALU.add)

            def prep_tr(g, s0, G, t):
                tsl = slice(t * 128, (t + 1) * 128)
                hb_ = hb[t % 2]
                for k in range(8):
                    K.tr(pT[:, k * 128:(k + 1) * 128], hb_[:, k * 128:(k + 1) * 128], identb[:])
                K.copy("act", h2T[g % 2][:, :, tsl], pT[:].rearrange("p (k t) -> p k t", k=8))

            def ff1(g, s0, G, j0, j1):
                for j in range(j0, j1):
                    w_ = w1r[w1i[0] % 6]
                    K.dma(w_[:], wf1s[l, j].rearrange("p (k c) -> p k c", k=8), "w1r%d" % (w1i[0] % 6))
                    w1i[0] += 1
                    p_ = nb()
                    for k in range(8):
                        K.mm(p_[:, 0:G], w_[:, k, :], h2T[g % 2][:, k, 0:G], start=(k == 0), stop=(k == 7))
                    r_ = rl[j % 2]
                    K.act(r_[:, 0:G], p_[:, 0:G], AF.Relu)
                    K.tt("dve", aT[:, j, 0:G], r_[:, 0:G], p_[:, 0:G], ALU.mult)

            def ff2(g, s0, G, t):
                if True:
                    tsl = slice(t * 128, (t + 1) * 128)
                    x_, xkey = newx()
                    K.dma(x_[:], xscr[s0 + t * 128:s0 + (t + 1) * 128, :], xkey)
                    pp = [nb(), nb()]
                    for half in range(2):
                        for j in range(32):
                            K.mm(pp[half][:, :], aT[:, j, tsl], wff2[:, j, half * 512:(half + 1) * 512], start=(j == 0), stop=(j == 31))
                    post(pp, 3, x_, g2c if g == 0 else None)
                    K.dma(xdst(l, s0 + t * 128, 128), x_[:], xkey, eng="pool")

            g0 = GROUPS[0]
            prep_load(*g0)
            for t in range(g0[2] // 128):
                prep_op(*g0, t)
                prep_tr(*g0, t)
            for i_, cur in enumerate(GROUPS):
                nxt = GROUPS[i_ + 1] if i_ + 1 < len(GROUPS) else None
                nt_c = cur[2] // 128
                if nxt is not None:
                    prep_load(*nxt)
                for q_ in range(4):
                    ff1(*cur, q_ * 8, (q_ + 1) * 8)
                    if nxt is not None:
                        prep_op(*nxt, q_)
                        if q_ >= 1:
                            prep_tr(*nxt, q_ - 1)
                ff2(*cur, 0)
                if nxt is not None:
                    prep_tr(*nxt, 3)
                for t in range(1, nt_c):
                    ff2(*cur, t)
            S.barrier()

    for l in range(n_layers):
        phase_m(l)
        with contextlib.ExitStack() as esr:
            kvnT = esr.enter_context(nc.sbuf_tensor(K.name("kvnT"), [128, NS], BF16))
            krT2 = esr.enter_context(nc.sbuf_tensor(K.name("krT2"), [128, NS], BF16))
            phase_1(l, kvnT, krT2)
            phase_a(l, kvnT, krT2)
        phase_s(l)
        phase_o(l)
    S.barrier()
    with contextlib.ExitStack() as es2:
        n = S.emit(es2)
    es0.close()
    return nc, n


def _rope_tables():
    rows_n = SEQ // 64
    row = np.repeat(np.arange(rows_n, dtype=np.float32), 64)
    col = np.tile(np.arange(64, dtype=np.float32), rows_n)
    inv = (np.float32(10000.0) ** (-np.arange(0, 32, 2, dtype=np.float32) / np.float32(32))).astype(np.float32)
    ar = (row[:, None] * inv).astype(np.float32)
    ac = (col[:, None] * inv).astype(np.float32)
    cos = np.zeros((64, SEQ), np.float32); sin = np.zeros((64, SEQ), np.float32)
    for d in range(64):
        ang = ar if d < 32 else ac
        f = d % 16
        cos[d] = np.cos(ang[:, f])
        sgn = -1.0 if (d % 32) < 16 else 1.0
        sin[d] = sgn * np.sin(ang[:, f])
    return cos, sin


_PERM = np.array([d + 16 if (d % 32) < 16 else d - 16 for d in range(64)])


def _host_layout(inp):
    f = lambda a: np.ascontiguousarray(a, dtype=np.float32)
    sh = {}
    sh["w_ada"] = f(inp["w_ada"]); sh["b_ada"] = f(inp["b_ada"])
    sh["g4"] = f(np.concatenate([inp["g_pre_mix"], inp["g_post_mix"], inp["g_pre_ff"], inp["g_post_ff"]], axis=1))
    w_in = inp["w_in"]
    sh["w_in"] = f(w_in)
    kr = w_in[:, :, 384:448]
    rot = kr[:, :, _PERM]
    sh["w_kr2"] = f(np.concatenate([kr, kr, rot, rot], axis=2))
    sh["gq_pc"] = f(inp["g_q"].reshape(NL, 2, 128).transpose(0, 2, 1))
    sh["gkv_p"] = f(inp["g_kv"].reshape(NL, 128, 1))
    wuq = inp["w_uq"].reshape(NL, 256, 4, 192)
    sh["w_uqx"] = f(np.concatenate([wuq, wuq[:, :, :, 128 + _PERM]], axis=3).reshape(NL, 256, 1024))
    wukv = inp["w_ukv"].reshape(NL, 128, 4, 256)
    sh["w_ukvr"] = f(np.concatenate([wukv[:, :, :, :128].reshape(NL, 128, 512), wukv[:, :, :, 128:].reshape(NL, 128, 512)], axis=2))
    sh["cm_g"] = f(inp["cm_norm_g"]); sh["cm_ws"] = f(inp["cm_w_s"])
    sh["cm_bt"] = f(inp["cm_b_s"].transpose(0, 2, 1))
    cw = inp["ssd_conv_w"].reshape(NL, 3, 6, 128).transpose(0, 3, 2, 1)
    sh["convw"] = f(cw.reshape(NL, 128, 18))
    sh["convb"] = f(inp["ssd_conv_b"].reshape(NL, 6, 128).transpose(0, 2, 1))
    sh["ssd_sm"] = f(np.concatenate([inp["ssd_dt_bias"].reshape(NL, 8), inp["ssd_a_log"].reshape(NL, 8), inp["ssd_d"].reshape(NL, 8)], axis=1))
    sh["ssd_g"] = f(inp["ssd_norm_g"])
    sh["w_out"] = f(inp["w_out"]); sh["w_ff1"] = f(inp["w_ff1"]); sh["w_ff2"] = f(inp["w_ff2"])
    cos, sin = _rope_tables()
    sh["cosd"] = f(np.concatenate([cos, cos], axis=0)); sh["sind"] = f(np.concatenate([sin, sin], axis=0))
    sh["csq"] = f(np.concatenate([cos, sin], axis=0))
    sh["csc"] = f(np.concatenate([np.ones((64, 512), np.float32), np.zeros((64, 512), np.float32)], axis=0))
    k = np.arange(128)[:, None]; l_ = np.arange(128)[None, :]
    tri = np.stack([(k <= l_), (k < l_), (k >= l_), (k > l_)], axis=1).astype(np.float32)
    sh["tri"] = f(tri.reshape(128, 512))
    sh["cc_pk"] = f(inp["c_ctx"].reshape(8, 128).T)
    return sh


_CACHE = {}


def kernel(**inputs):
    inp = {k: np.asarray(v) for k, v in inputs.items()}
    if "nc" not in _CACHE:
        _CACHE["nc"] = build()[0]
    nc = _CACHE["nc"]
    shared = _host_layout(inp)
    in_maps = []
    for b in range(8):
        m = dict(shared)
        m["x"] = np.ascontiguousarray(inp["x"][b], dtype=np.float32)
        m["ctx"] = np.ascontiguousarray(inp["ctx"][b], dtype=np.float32)
        m["c_pk"] = np.ascontiguousarray(inp["c"][b].reshape(8, 128).T, dtype=np.float32)
        in_maps.append(m)
    res = run_bass_kernel_spmd(nc, in_maps, core_ids=list(range(8)))
    return np.stack([np.asarray(r["out"], dtype=np.float32) for r in res.results], axis=0)
```

```python
import contextlib
import numpy as np
import concourse.bass as bass
import concourse.mybir as mybir
from concourse.bass_utils import run_bass_kernel_spmd

F32 = mybir.dt.float32
BF16 = mybir.dt.bfloat16
AF = mybir.ActivationFunctionType
ALU = mybir.AluOpType

EPS = 1e-6
NL = 4
D = 1024
SEQ = 4096
NCTX = 256
NS = SEQ + NCTX
NCH = NS // 128
SCALE = 192.0 ** -0.5
XB_COLS = 4356


def _esize(dt):
    return 2 if dt == BF16 else 4
class _Op:
    __slots__ = ("eng", "fn", "dma", "semkey", "waits", "sig", "cnt", "idx", "dmaval", "gid")


def _rect(ap):
    t = ap.ap
    es = _esize(ap.dtype)
    off = int(ap.offset)
    if str(ap.space) == "DRAM":
        ext = 0
        for s, c in t:
            ext += (c - 1) * abs(s)
        return (ap.name, 0, 1, off * es, (off + ext + 1) * es)
    pstep, pcnt = t[0]
    if pstep == 0:
        pstep = 1 << 40
    p0 = off // pstep
    f0 = off % pstep
    ext = 0
    for s, c in t[1:]:
        ext += (c - 1) * abs(s)
    if str(ap.space) == "PSUM":
        return (ap.name, 0, 128, (f0 * es) // 2048 * 2048, ((f0 + ext + 1) * es + 2047) // 2048 * 2048)
    return (ap.name, p0, p0 + pcnt, f0 * es, (f0 + ext + 1) * es)


class Sched:
    ENG = ("pe", "act", "dve", "pool", "sp")

    def __init__(self, nc):
        self.nc = nc
        self.ops = []
        self.acc = {}
        self.eng_ops = {e: [] for e in self.ENG}
        self.waited = {e: {x: -1 for x in self.ENG} for e in self.ENG}
        self.dma_last = {}
        self.dma_keys = {}
        self.dma_waited = {e: set() for e in self.ENG}
        self.unwaited = set()

    def add(self, eng, fn, reads=(), writes=(), dma=None):
        op = _Op()
        op.eng = eng
        op.fn = fn
        op.dma = dma is not None
        op.semkey = dma
        op.sig = False
        op.gid = len(self.ops)
        deps = set()
        rrects = [_rect(a) for a in reads]
        wrects = [_rect(a) for a in writes]
        for r in rrects:
            for rec in self.acc.get(r[0], ()):
                q = rec[0]
                if rec[2] and q[1] < r[2] and r[1] < q[2] and q[3] < r[4] and r[3] < q[4]:
                    deps.add(rec[1])
        for r in wrects:
            for rec in self.acc.get(r[0], ()):
                q = rec[0]
                if q[1] < r[2] and r[1] < q[2] and q[3] < r[4] and r[3] < q[4]:
                    deps.add(rec[1])
        if op.dma:
            prev = self.dma_last.get(dma)
            if prev is not None:
                deps.add(prev)
            self.dma_last[dma] = op.gid
            cnt = self.dma_keys.get(dma, 0) + 1
            self.dma_keys[dma] = cnt
            op.dmaval = 16 * cnt
            self.unwaited.add(op.gid)
        deps.discard(op.gid)
        waits = []
        best = {}
        for d in deps:
            o = self.ops[d]
            if o.dma:
                if d not in self.dma_waited[eng]:
                    self.dma_waited[eng].add(d)
                    self.unwaited.discard(d)
                    waits.append(("dma", d))
            else:
                if o.eng == "pe" and eng == "pe" and not op.dma:
                    continue
                if o.idx > best.get(o.eng, -1):
                    best[o.eng] = o.idx
        for x, i in best.items():
            if self.waited[eng][x] < i:
                self.waited[eng][x] = i
                waits.append(("eng", x, i))
                self.eng_ops[x][i].sig = True
        op.waits = waits
        if not op.dma:
            op.idx = len(self.eng_ops[eng])
            self.eng_ops[eng].append(op)
        else:
            op.idx = -1
        self.ops.append(op)
        for r in wrects:
            lst = self.acc.setdefault(r[0], [])
            lst[:] = [rec for rec in lst if not (r[1] <= rec[0][1] and rec[0][2] <= r[2]
                                                  and r[3] <= rec[0][3] and rec[0][4] <= r[4])]
            lst.append((r, op.gid, True, eng if not op.dma else None))
        for r in rrects:
            lst = self.acc.setdefault(r[0], [])
            if not op.dma:
                lst[:] = [rec for rec in lst if not (not rec[2] and rec[3] == eng and rec[0] == r)]
            lst.append((r, op.gid, False, eng if not op.dma else None))
        return op

    def fence(self, eng, reads=(), writes=()):
        return self.add(eng, None, reads, writes)

    def barrier(self):
        pend = sorted(self.unwaited)
        self.unwaited = set()
        last = {e: (self.eng_ops[e][-1].gid if self.eng_ops[e] else None) for e in self.ENG}
        for e in self.ENG:
            o = _Op()
            o.eng = e
            o.fn = None
            o.dma = False
            o.semkey = None
            o.sig = False
            o.gid = len(self.ops)
            waits = []
            for x in self.ENG:
                if x == e or last[x] is None:
                    continue
                i = self.ops[last[x]].idx
                if self.waited[e][x] < i:
                    self.waited[e][x] = i
                    waits.append(("eng", x, i))
                    self.eng_ops[x][i].sig = True
            for d in pend:
                if d not in self.dma_waited[e]:
                    self.dma_waited[e].add(d)
                    waits.append(("dma", d))
            o.waits = waits
            o.idx = len(self.eng_ops[e])
            self.eng_ops[e].append(o)
            self.ops.append(o)
        self.acc = {}

    def emit(self, sems_ctx):
        nc = self.nc
        engobj = {"pe": nc.tensor, "act": nc.scalar, "dve": nc.vector, "pool": nc.gpsimd, "sp": nc.sync}
        esem = {e: sems_ctx.enter_context(nc.semaphore("s_" + e)) for e in self.ENG}
        dsem = {k: sems_ctx.enter_context(nc.semaphore("d_%d" % i)) for i, k in enumerate(self.dma_keys)}
        for e in self.ENG:
            c = 0
            for o in self.eng_ops[e]:
                if o.sig:
                    c += 1
                    o.cnt = c
        n_inst = 0
        for o in self.ops:
            eo = engobj[o.eng]
            for w in o.waits:
                if w[0] == "dma":
                    d = self.ops[w[1]]
                    eo.wait_ge(dsem[d.semkey], d.dmaval)
                else:
                    eo.wait_ge(esem[w[1]], self.eng_ops[w[1]][w[2]].cnt)
            if o.fn is None:
                if o.sig:
                    eo.nop().then_inc(esem[o.eng], 1)
                continue
            inst = o.fn(eo)
            n_inst += 1
            if o.dma:
                inst.then_inc(dsem[o.semkey], 16)
            elif o.sig:
                inst.then_inc(esem[o.eng], 1)
        return n_inst


class KB:
    def __init__(self, nc):
        self.nc = nc
        self.S = Sched(nc)
        self.uid = 0

    def name(self, n):
        self.uid += 1
        return "%s_%d" % (n, self.uid)

    def dma(self, out, in_, key, eng="sp", slow=False):
        if slow:
            self.S.add(eng, lambda e: e.dma_start(out=out, in_=in_, allow_slow_non_contiguous=True), reads=[in_], writes=[out], dma=key)
        else:
            self.S.add(eng, lambda e: e.dma_start(out=out, in_=in_), reads=[in_], writes=[out], dma=key)

    def mm(self, out, lhsT, rhs, start=True, stop=True):
        self.S.add("pe", lambda e: e.matmul(out, lhsT=lhsT, rhs=rhs, start=start, stop=stop),
                   reads=[lhsT, rhs], writes=[out])

    def tr(self, out, in_, ident):
        self.S.add("pe", lambda e: e.transpose(out=out, in_=in_, identity=ident), reads=[in_, ident], writes=[out])

    def act(self, out, in_, func, bias=None, scale=None, accum=None):
        kw = {}
        rd = [in_]
        wr = [out]
        if bias is not None:
            kw["bias"] = bias
            if not isinstance(bias, float):
                rd.append(bias)
        if scale is not None:
            kw["scale"] = scale
            if not isinstance(scale, float):
                rd.append(scale)
        if accum is not None:
            kw["accum_out"] = accum
            wr.append(accum)
        self.S.add("act", lambda e: e.activation(out=out, in_=in_, func=func, **kw), reads=rd, writes=wr)

    def copy(self, eng, out, in_):
        if eng == "act":
            self.S.add("act", lambda e: e.copy(out=out, in_=in_), reads=[in_], writes=[out])
        else:
            self.S.add(eng, lambda e: e.tensor_copy(out=out, in_=in_), reads=[in_], writes=[out])

    def tt(self, eng, out, in0, in1, op):
        self.S.add(eng, lambda e: e.tensor_tensor(out=out, in0=in0, in1=in1, op=op), reads=[in0, in1], writes=[out])

    def ts(self, eng, out, in0, s1, s2, op0, op1=None):
        rd = [in0]
        if not isinstance(s1, float):
            rd.append(s1)
        if s2 is not None and not isinstance(s2, float):
            rd.append(s2)
        if op1 is None:
            self.S.add(eng, lambda e: e.tensor_scalar(out=out, in0=in0, scalar1=s1, scalar2=None, op0=op0), reads=rd, writes=[out])
        else:
            self.S.add(eng, lambda e: e.tensor_scalar(out=out, in0=in0, scalar1=s1, scalar2=s2, op0=op0, op1=op1), reads=rd, writes=[out])

    def stt(self, out, in0, scalar, in1, op0, op1):
        rd = [in0, in1]
        if not isinstance(scalar, float):
            rd.append(scalar)
        self.S.add("dve", lambda e: e.scalar_tensor_tensor(out=out, in0=in0, scalar=scalar, in1=in1, op0=op0, op1=op1),
                   reads=rd, writes=[out])

    def memset(self, eng, ap, val):
        self.S.add(eng, lambda e: e.memset(ap, val), writes=[ap])

    def recip(self, out, in_):
        self.S.add("dve", lambda e: e.reciprocal(out=out, in_=in_), reads=[in_], writes=[out])

    def bn_stats(self, out, in_):
        self.S.add("dve", lambda e: e.bn_stats(out=out, in_=in_), reads=[in_], writes=[out])

    def bn_aggr(self, out, in_):
        self.S.add("dve", lambda e: e.bn_aggr(out=out, in_=in_), reads=[in_], writes=[out])

    def rstd(self, out, in_, n, tmp):
        self.act(tmp, in_, AF.Ln, bias=EPS, scale=1.0 / n)
        self.act(out, tmp, AF.Exp, scale=-0.5)


def bc3(ap2, n):
    return ap2.unsqueeze(2).to_broadcast([ap2.shape[0], ap2.shape[1], n])


def bcmid(ap2, n):
    return ap2.unsqueeze(1).to_broadcast([ap2.shape[0], n, ap2.shape[1]])


def build(n_layers=NL, dbg=False):
    nc = bass.Bass("TRN2", target_bir_lowering=False)
    K = KB(nc)
    S = K.S

    def din(name, shape, dt=F32):
        return nc.dram_tensor(name, shape, dt, kind="ExternalInput").ap()

    def dscr(name, shape, dt=F32):
        return nc.dram_tensor(name, shape, dt, kind=("ExternalOutput" if dbg else "Internal")).ap()

    x_in = din("x", [SEQ, D]); ctx_in = din("ctx", [NCTX, D])
    c_pk = din("c_pk", [128, 8]); cc_pk = din("cc_pk", [128, 8])
    w_ada = din("w_ada", [NL, D, 6 * D]); b_ada = din("b_ada", [NL, 6 * D])
    g4 = din("g4", [NL, 4 * D])
    w_in = din("w_in", [NL, D, 1992]); w_kr2 = din("w_kr2", [NL, D, 256])
    gq_pc = din("gq_pc", [NL, 128, 2]); gkv_p = din("gkv_p", [NL, 128, 1])
    w_uqx = din("w_uqx", [NL, 256, 1024]); w_ukvr = din("w_ukvr", [NL, 128, 1024])
    cm_g = din("cm_g", [NL, 256]); cm_ws = din("cm_ws", [NL, 4, 128, 128]); cm_bt = din("cm_bt", [NL, 128, 4])
    convw = din("convw", [NL, 128, 18]); convb = din("convb", [NL, 128, 6])
    ssd_sm = din("ssd_sm", [NL, 24]); ssd_g = din("ssd_g", [NL, 256])
    w_out = din("w_out", [NL, D, D]); w_ff1 = din("w_ff1", [NL, D, 4 * D]); w_ff2 = din("w_ff2", [NL, 4 * D, D])
    cosd = din("cosd", [128, SEQ]); sind = din("sind", [128, SEQ]); csq = din("csq", [128, SEQ]); csc = din("csc", [128, 512])
    tri = din("tri", [128, 4 * 128])
    out = nc.dram_tensor("out", [SEQ, D], F32, kind="ExternalOutput").ap()

    xscr = dscr("xscr", [NS, D])
    mods = dscr("mods", [NL, 2, 6, D])
    xbcs = dscr("xbcs", [768, XB_COLS])
    zdts = dscr("zdts", [NS, 264])
    yTs = dscr("yTs", [D, NS], BF16)
    qnTs = dscr("qnTs", [256, NS], BF16)
    wf1s = dscr("wf1s", [NL, 32, 128, 1024], BF16)

    GROUPS = [(0, 0, 256)] + [(g, 256 + (g - 1) * 512, 512) for g in range(1, 9)]

    def xsrc(l, s, n):
        if l == 0:
            return ctx_in[s:s + n, :] if s < NCTX else x_in[s - NCTX:s - NCTX + n, :]
        return xscr[s:s + n, :]

    def xdst(l, s, n):
        if l == n_layers - 1 and s >= NCTX:
            return out[s - NCTX:s - NCTX + n, :]
        return xscr[s:s + n, :]

    def xbcol(s):
        return 1 + s if s < NCTX else 259 + (s - NCTX)

    def brow(ap_row, n):
        return ap_row.broadcast_to([128, n])

    es0 = contextlib.ExitStack()
    sb0 = lambda n, sh, dt: es0.enter_context(nc.sbuf_tensor(n, sh, dt))
    identf = sb0("identf", [128, 128], F32)
    identb = sb0("identb", [128, 128], BF16)
    onesb = sb0("onesb", [128, 128], BF16)
    onesf = sb0("onesf", [128, 128], F32)
    trit = sb0("trit", [128, 4, 128], F32)
    zrow = sb0("zrow", [128, 8], F32)
    K.dma(trit[:].rearrange("p a b -> p (a b)"), tri[:, :], "trit")
    K.memset("pool", onesf[:], 1.0)
    K.memset("pool", onesb[:], 1.0)
    K.memset("pool", zrow[:], 0.0)
    K.tt("dve", identf[:], trit[:, 0, :], trit[:, 2, :], ALU.mult)
    K.copy("dve", identb[:], identf[:])
    for c in range(6):
        for col in (0, 257, 258, 4355):
            K.dma(xbcs[c * 128:(c + 1) * 128, col:col + 1], zrow[:, 0:1], "zpad", slow=True)
    def m_steps(l, sb, pbank):
        sca = sb("sca", [128, 2, 8], F32)
        sc2 = sb("sc2", [128, 8, 2], F32)
        wa = [sb("wa%d" % i, [128, 8, 256], F32) for i in range(2)]
        bd = [sb("bd%d" % i, [2, 256], F32) for i in range(3)]
        gg = [sb("gg%d" % i, [2, 256], F32) for i in range(3)]
        mt = [sb("mt%d" % i, [2, 256], F32) for i in range(2)]
        res = [sb("res%d" % i, [2, 256], F32) for i in range(3)]
        wv = w_ada[l].rearrange("(k p) n -> p k n", p=128)

        def init():
            K.dma(sca[:, 0, :], c_pk[:, :], "sc0")
            K.dma(sca[:, 1, :], cc_pk[:, :], "sc1")
            K.act(sca[:], sca[:], AF.Silu)
            for v in range(2):
                K.copy("dve", sc2[:, :, v], sca[:, v, :])

        def gcol(j):
            m, q = j // 4, j % 4
            if m in (1, 4):
                return (0 if m == 1 else 2) * D + q * 256
            if m in (2, 5):
                return (1 if m == 2 else 3) * D + q * 256
            return None

        def load(j):
            K.dma(wa[j % 2][:], wv[:, :, j * 256:(j + 1) * 256], "wa%d" % (j % 2))
            K.dma(bd[j % 3][:], b_ada[l:l + 1, j * 256:(j + 1) * 256].broadcast_to([2, 256]), "bd%d" % (j % 3))
            if gcol(j) is not None:
                K.dma(gg[j % 3][:], g4[l:l + 1, gcol(j):gcol(j) + 256].broadcast_to([2, 256]), "gg%d" % (j % 3))

        def compute(j):
            m, q = j // 4, j % 4
            w_ = wa[j % 2]
            for k in range(8):
                K.mm(pbank[0:2, 0:256], sc2[:, k, :], w_[:, k, :], start=(k == 0), stop=(k == 7))
            t_ = mt[j % 2]
            r_ = res[j % 3]
            K.tt("dve", t_[:], pbank[0:2, 0:256], bd[j % 3][:], ALU.add)
            if m in (0, 3):
                K.copy("dve", r_[:], t_[:])
            elif m in (1, 4):
                K.stt(r_[:], t_[:], 1.0, gg[j % 3][:], ALU.add, ALU.mult)
            else:
                K.tt("dve", r_[:], t_[:], gg[j % 3][:], ALU.mult)
            K.dma(mods[l, :, m, q * 256:(q + 1) * 256], r_[:], "res%d" % (j % 3))

        return init, load, compute

    def phase_m(l):
        with contextlib.ExitStack() as es:
            sb = lambda n, sh, dt: es.enter_context(nc.sbuf_tensor(K.name(n), sh, dt))
            pm = es.enter_context(nc.psum_tensor(K.name("pm"), [128, 512], F32))
            init, load, compute = m_steps(l, sb, pm)
            init()
            load(0)
            for j in range(24):
                if j + 1 < 24:
                    load(j + 1)
                compute(j)
            S.barrier()

    def phase_1(l, kvnT, krT2):
        with contextlib.ExitStack() as es:
            sb = lambda n, sh, dt: es.enter_context(nc.sbuf_tensor(K.name(n), sh, dt))
            ps = lambda n, sh, dt: es.enter_context(nc.psum_tensor(K.name(n), sh, dt))
            win = sb("win", [128, 8, 1992], BF16)
            wkr2 = sb("wkr2", [128, 8, 256], BF16)
            gq = sb("gq", [128, 2], F32); gkv = sb("gkv", [128, 1], F32)
            cmg = sb("cmg", [128, 256], F32); cmb = sb("cmb", [128, 4], F32)
            wsf = sb("wsf", [128, 4, 128], F32); wsT = sb("wsT", [128, 4, 128], BF16)
            ab = sb("ab", [128, 2, D], F32)
            xg = [sb("xg%d" % i, [128, D], F32) for i in range(3)]
            hT = [sb("hT%d" % i, [128, 8, 512], BF16) for i in range(2)]
            hb = [sb("hb%d" % i, [128, D], BF16) for i in range(4)]
            tmpf = sb("tmpf", [128, D], F32)
            junk = sb("junk", [128, D], BF16)
            st4 = sb("st4", [128, 3, 4], F32)
            sq = sb("sq", [128, 3, 512], BF16)
            lnq = sb("lnq", [128, 512], F32)
            rst = [sb("rst%d" % i, [128, 512], F32) for i in range(2)]
            qn = [sb("qn%d" % i, [128, 2, 512], BF16) for i in range(2)]
            cosk = sb("cosk", [128, 512], F32); sink = sb("sink", [128, 512], F32)
            t1 = sb("t1", [128, 512], F32); t2 = sb("t2", [128, 512], F32)
            xbst = [sb("xbst%d" % i, [128, 6, 512], F32) for i in range(2)]
            xcms = [sb("xcm%d" % i, [128, 4, 512], F32) for i in range(2)]
            zdt = [sb("zdt%d" % i, [128, 4, 264], F32) for i in range(2)]
            st6 = sb("st6", [128, 4, 6], F32); mv = sb("mv", [128, 4, 2], F32)
            vpe = sb("vpe", [128, 4], F32); rscm = sb("rscm", [128, 4], F32); cneg = sb("cneg", [128, 4], F32)
            vnf = [sb("vnf%d" % i, [128, 256], F32) for i in range(2)]
            vnb = [sb("vnb%d" % i, [128, 256], BF16) for i in range(2)]
            ycm = [sb("ycm%d" % i, [128, 256], BF16) for i in range(2)]
            ycmT = [sb("ycmT%d" % i, [128, 2, 512], BF16) for i in range(2)]
            pT = ps("pT", [128, D], BF16)
            pf = [ps("pf%d" % i, [128, 512], F32) for i in range(7)]
            bank = [0]

            def nb():
                bank[0] += 1
                return pf[3 + bank[0] % 4]

            K.dma(win[:], w_in[l].rearrange("(k p) n -> p k n", p=128), "win", eng="pool")
            K.dma(wkr2[:], w_kr2[l].rearrange("(k p) n -> p k n", p=128), "wkr2", eng="pool")
            K.dma(gq[:], gq_pc[l], "gq"); K.dma(gkv[:], gkv_p[l], "gkv")
            K.dma(cmg[:], brow(cm_g[l:l + 1, :], 256), "cmg"); K.dma(cmb[:], cm_bt[l], "cmb")
            K.dma(wsf[:], cm_ws[l].rearrange("g t s -> t g s"), "wsf")
            K.memset("pool", cneg[:], -0.5)
            for gi in range(4):
                p_ = nb()
                K.tr(p_[:, 0:128], wsf[:, gi, :], identf[:])
                K.copy("dve", wsT[:, gi, :], p_[:, 0:128])

            xi = [0]
            deferred = []

            def dstore(out_, in__, key):
                deferred.append((out_, in__, key))

            def flush():
                for (o_, i_, k_) in deferred:
                    K.dma(o_, i_, k_)
                del deferred[:]

            def prep(g, s0, G):
                nt = G // 128
                v = 1 if g == 0 else 0
                if g in (0, 1):
                    K.dma(ab[:, 0, :], brow(mods[l, v, 1:2, :], D), "ab0")
                    K.dma(ab[:, 1, :], brow(mods[l, v, 0:1, :], D), "ab1")
                hT_ = hT[g % 2]
                for t in range(nt):
                    x_ = xg[xi[0] % 3]; xi[0] += 1
                    K.dma(x_[:], xsrc(l, s0 + t * 128, 128), "xg%d" % ((xi[0] - 1) % 3))
                    K.act(junk[:], x_[:], AF.Square, accum=st4[:, 0, t:t + 1])
                    K.rstd(st4[:, 2, t:t + 1], st4[:, 0, t:t + 1], D, st4[:, 1, t:t + 1])
                    K.stt(tmpf[:], x_[:], st4[:, 2, t:t + 1], ab[:, 0, :], ALU.mult, ALU.mult)
                    K.tt("dve", hb[t][:], tmpf[:], ab[:, 1, :], ALU.add)

            def prep_b(g, s0, G):
                nt = G // 128
                hT_ = hT[g % 2]
                for t in range(nt):
                    for k in range(8):
                        K.tr(pT[:, k * 128:(k + 1) * 128], hb[t][:, k * 128:(k + 1) * 128], identb[:])
                    K.copy("act", hT_[:, :, t * 128:(t + 1) * 128], pT[:].rearrange("p (k t) -> p k t", k=8))

            def body(g, s0, G):
                nt = G // 128
                hT_ = hT[g % 2]

                def fm(col0, m, wt=win, p_=None):
                    if p_ is None:
                        p_ = nb()
                    for k in range(8):
                        K.mm(p_[0:m, 0:G], wt[:, k, col0:col0 + m], hT_[:, k, 0:G], start=(k == 0), stop=(k == 7))
                    return p_

                pq = [fm(0, 128, p_=pf[0]), fm(128, 128, p_=pf[1])]
                pkv = fm(256, 128, p_=pf[2])
                for c in range(2):
                    K.act(sq[:, c, 0:G], pq[c][:, 0:G], AF.Square)
                K.act(sq[:, 2, 0:G], pkv[:, 0:G], AF.Square)
                pkr = fm(0, 128, wkr2)
                if g == 0:
                    K.copy("dve", krT2[:, s0:s0 + G], pkr[:, 0:G])
                else:
                    pkrr = fm(128, 128, wkr2)
                    K.dma(cosk[:], cosd[:, s0 - NCTX:s0 - NCTX + G], "cosk")
                    K.dma(sink[:], sind[:, s0 - NCTX:s0 - NCTX + G], "sink")
                    K.tt("dve", t1[:, 0:G], pkr[:, 0:G], cosk[:, 0:G], ALU.mult)
                    K.tt("dve", t2[:, 0:G], pkrr[:, 0:G], sink[:, 0:G], ALU.mult)
                    K.tt("pool", krT2[:, s0:s0 + G], t1[:, 0:G], t2[:, 0:G], ALU.add)
                for c in range(6):
                    p_ = fm(1216 + c * 128, 128)
                    K.copy("act", xbst[g % 2][:, c, 0:G], p_[:, 0:G])
                dstore(xbcs.rearrange("(c p) n -> p c n", p=128)[:, :, xbcol(s0):xbcol(s0) + G], xbst[g % 2][:, :, 0:G], "xbst%d" % (g % 2))
                psq = nb()
                for c in range(2):
                    K.mm(psq[:, 0:G], onesb[:], sq[:, c, 0:G], start=(c == 0), stop=(c == 1))
                pskv = nb()
                K.mm(pskv[:, 0:G], onesb[:], sq[:, 2, 0:G])
                K.rstd(rst[0][:, 0:G], psq[:, 0:G], 256, lnq[:, 0:G])
                K.rstd(rst[1][:, 0:G], pskv[:, 0:G], 128, lnq[:, 0:G])
                qn_ = qn[g % 2]
                for c in range(2):
                    K.stt(qn_[:, c, 0:G], pq[c][:, 0:G], gq[:, c:c + 1], rst[0][:, 0:G], ALU.mult, ALU.mult)
                    dstore(qnTs[c * 128:(c + 1) * 128, s0:s0 + G], qn_[:, c, 0:G], "qn%d_%d" % (g % 2, c))
                K.stt(kvnT[:, s0:s0 + G], pkv[:, 0:G], gkv[:, 0:1], rst[1][:, 0:G], ALU.mult, ALU.mult)
                zdt_ = zdt[g % 2]
                xcm = xcms[g % 2]
                for t in range(nt):
                    tsl = slice(t * 128, (t + 1) * 128)
                    p_ = nb()
                    for k in range(8):
                        K.mm(p_[:, :], hT_[:, k, tsl], win[:, k, 448:960], start=(k == 0), stop=(k == 7))
                    K.copy("act", xcm[:, t, :], p_[:, :])
                    p2 = nb()
                    for k in range(8):
                        K.mm(p2[:, 0:256], hT_[:, k, tsl], win[:, k, 960:1216], start=(k == 0), stop=(k == 7))
                    for k in range(8):
                        K.mm(p2[:, 256:264], hT_[:, k, tsl], win[:, k, 1984:1992], start=(k == 0), stop=(k == 7))
                    K.copy("dve", zdt_[:, t, :], p2[:, 0:264])
                K.act(xcm[:, 0:nt, :], xcm[:, 0:nt, :], AF.Gelu_apprx_tanh)
                K.act(zdt_[:, 0:nt, 0:256], zdt_[:, 0:nt, 0:256], AF.Silu)
                dstore(zdts[s0:s0 + G, :].rearrange("(t p) n -> p t n", p=128), zdt_[:, 0:nt, :], "zdt%d" % (g % 2))

            def body_b(g, s0, G):
                nt = G // 128
                xcm = xcms[g % 2]
                for t in range(nt):
                    K.bn_stats(st6[:, t, :], xcm[:, t, 256:512])
                    K.bn_aggr(mv[:, t, :], st6[:, t, :])
                K.ts("dve", vpe[:, 0:nt], mv[:, 0:nt, 1], EPS, None, ALU.add)
                K.tt("pool", rscm[:, 0:nt], vpe[:, 0:nt], cneg[:, 0:nt], ALU.pow)
                ycmT_ = ycmT[g % 2]
                for t in range(nt):
                    vf = vnf[t % 2]; vb = vnb[t % 2]; yc = ycm[t % 2]
                    K.ts("dve", vf[:], xcm[:, t, 256:512], mv[:, t, 0:1], rscm[:, t:t + 1], ALU.subtract, ALU.mult)
                    K.tt("pool", vb[:], vf[:], cmg[:], ALU.mult)
                    p_ = nb()
                    for gi in range(4):
                        K.mm(p_[:, gi * 64:(gi + 1) * 64], wsT[:, gi, :], vb[:, gi * 64:(gi + 1) * 64])
                    for gi in range(4):
                        gs = slice(gi * 64, (gi + 1) * 64)
                        K.stt(yc[:, gs], p_[:, gs], cmb[:, gi:gi + 1], xcm[:, t, gs], ALU.add, ALU.mult)
                    for c in range(2):
                        K.tr(pT[:, c * 128:(c + 1) * 128], yc[:, c * 128:(c + 1) * 128], identb[:])
                    K.copy("act", ycmT_[:, :, t * 128:(t + 1) * 128], pT[:, 0:256].rearrange("p (c t) -> p c t", c=2))
                dstore(yTs[512:768, s0:s0 + G].rearrange("(c p) n -> p c n", p=128), ycmT_[:, :, 0:G], "ycmT%d" % (g % 2))

            prep(*GROUPS[0])
            prep_b(*GROUPS[0])
            for i_, grp_ in enumerate(GROUPS):
                if i_ + 1 < len(GROUPS):
                    prep(*GROUPS[i_ + 1])
                flush()
                body(*grp_)
                if i_ + 1 < len(GROUPS):
                    prep_b(*GROUPS[i_ + 1])
                if i_ >= 1:
                    body_b(*GROUPS[i_ - 1])
            body_b(*GROUPS[-1])
            flush()
            S.barrier()

    def phase_a(l, kvnT, krT2):
        with contextlib.ExitStack() as es:
            sb = lambda n, sh, dt: es.enter_context(nc.sbuf_tensor(K.name(n), sh, dt))
            ps = lambda n, sh, dt: es.enter_context(nc.psum_tensor(K.name(n), sh, dt))
            KT = sb("KT", [128, 4, NS], BF16)
            Vaug = sb("Vaug", [128, NCH, 4, 130], BF16)
            wuq = sb("wuq", [128, 2, 1024], BF16)
            wukv = sb("wukv", [128, 1024], BF16)
            qn = [sb("qna%d" % i, [128, 2, 512], BF16) for i in range(2)]
            cs = [sb("csa%d" % i, [128, 512], F32) for i in range(2)]
            qh = [sb("qh%d" % i, [128, 512], BF16) for i in range(2)]
            qr = [sb("qr%d" % i, [128, 512], BF16) for i in range(2)]
            PT = [sb("PT%d" % i, [128, 512], BF16) for i in range(4)]
            yat = [sb("yat%d" % i, [128, 512], F32) for i in range(4)]
            yaT = [sb("yaT%d" % i, [128, 4, 512], BF16) for i in range(2)]
            rden = sb("rden", [128, 8], F32)
            acc = [ps("acc%d" % i, [128, 512], F32) for i in range(4)]
            psc = [ps("psc%d" % i, [128, 512], F32) for i in range(3)]
            pqu = ps("pqu", [128, 512], F32)
            K.dma(wuq[:], w_uqx[l].rearrange("(c p) n -> p c n", p=128), "wuq", eng="pool")
            K.dma(wukv[:], w_ukvr[l], "wukv", eng="pool")
            K.memset("pool", Vaug[:], 1.0)
            for j in range(32):
                K.dma(wf1s[l, j].rearrange("p (k c) -> p k c", k=8),
                      w_ff1[l].rearrange("(k p) n -> p k n", p=128)[:, :, j * 128:(j + 1) * 128], "wf1cast%d" % (j % 4), eng="pool")
            for (g, s0, G) in GROUPS:
                for h in range(4):
                    p_ = psc[h % 2]
                    K.mm(p_[:, 0:G], wukv[:, h * 128:(h + 1) * 128], kvnT[:, s0:s0 + G])
                    K.copy("act" if h % 2 else "dve", KT[:, h, s0:s0 + G], p_[:, 0:G])
                for t in range(G // 128):
                    p_ = acc[t]
                    K.mm(p_[:, :], kvnT[:, s0 + t * 128:s0 + (t + 1) * 128], wukv[:, 512:1024])
                    K.copy("act" if t % 2 else "dve", Vaug[:, s0 // 128 + t, :, 0:128], p_[:, :].rearrange("p (h d) -> p h d", h=4))
            pti = [0]

            def aload(g, s0, G):
                K.dma(qn[g % 2][:, :, 0:G], qnTs[:, s0:s0 + G].rearrange("(c p) n -> p c n", p=128), "qna%d" % (g % 2))
                if g == 0:
                    K.dma(cs[g % 2][:, 0:G], csc[:, 0:G], "csa%d" % (g % 2))
                else:
                    K.dma(cs[g % 2][:, 0:G], csq[:, s0 - NCTX:s0 - NCTX + G], "csa%d" % (g % 2))

            for (g, s0, G) in GROUPS:
                nt = G // 128
                kts = [0, 1] if g == 0 else list(range(NCH))
                qn_ = qn[g % 2]; cs_ = cs[g % 2]; yaT_ = yaT[g % 2]
                if g == 0:
                    aload(g, s0, G)
                if g + 1 < len(GROUPS):
                    aload(*GROUPS[g + 1])
                for h in range(4):
                    qh_ = qh[h % 2]; qr_ = qr[h % 2]
                    for c in range(2):
                        K.mm(pqu[:, 0:G], wuq[:, c, h * 256:h * 256 + 128], qn_[:, c, 0:G], start=(c == 0), stop=(c == 1))
                    K.copy("dve", qh_[:, 0:G], pqu[:, 0:G])
                    for c in range(2):
                        K.mm(pqu[:, 0:G], wuq[:, c, h * 256 + 128:h * 256 + 256], qn_[:, c, 0:G], start=(c == 0), stop=(c == 1))
                    K.tt("dve", qr_[:, 0:G], pqu[:, 0:G], cs_[:, 0:G], ALU.mult)
                    stash = {}

                    def score(i):
                        kt = kts[i]
                        ksl = slice(kt * 128, (kt + 1) * 128)
                        p_ = psc[pti[0] % 3]
                        P_ = PT[pti[0] % 4]
                        pti[0] += 1
                        K.mm(p_[:, 0:G], KT[:, h, ksl], qh_[:, 0:G], start=True, stop=False)
                        K.mm(p_[:, 0:G], krT2[:, ksl], qr_[:, 0:G], start=False, stop=True)
                        K.act(P_[:, 0:G], p_[:, 0:G], AF.Exp, scale=SCALE)
                        stash[i] = P_

                    score(0)
                    if len(kts) > 1:
                        score(1)
                    for i, kt in enumerate(kts):
                        if i + 2 < len(kts):
                            score(i + 2)
                        P_ = stash.pop(i)
                        for qt in range(nt):
                            K.mm(acc[qt][:, 0:129], P_[:, qt * 128:(qt + 1) * 128], Vaug[:, kt, h, 0:129],
                                 start=(i == 0), stop=(i == len(kts) - 1))
                    for qt in range(nt):
                        K.recip(rden[:, qt:qt + 1], acc[qt][:, 128:129])
                        K.ts("dve", yat[qt][:, h * 128:(h + 1) * 128], acc[qt][:, 0:128], rden[:, qt:qt + 1], None, ALU.mult)
                for qt in range(nt):
                    for h in range(4):
                        K.tr(pqu[:, h * 128:(h + 1) * 128], yat[qt][:, h * 128:(h + 1) * 128], identf[:])
                    K.copy("act", yaT_[:, :, qt * 128:(qt + 1) * 128], pqu[:, 0:512].rearrange("p (h t) -> p h t", h=4))
                K.dma(yTs[0:512, s0:s0 + G].rearrange("(h p) n -> p h n", p=128), yaT_[:, :, 0:G], "yaT%d" % (g % 2))
            S.barrier()

    def phase_s(l, m_next=None):
        with contextlib.ExitStack() as es:
            sb = lambda n, sh, dt: es.enter_context(nc.sbuf_tensor(K.name(n), sh, dt))
            ps = lambda n, sh, dt: es.enter_context(nc.psum_tensor(K.name(n), sh, dt))
            cw = sb("cw", [128, 18], F32); cbias = sb("cbias", [128, 6], F32)
            sm = sb("sm", [128, 24], F32); Abc = sb("Abc", [128, 8], F32); dsum = sb("dsum", [128, 4], F32)
            sgb = sb("sgb", [128, 256], F32)
            Sb_all = sb("Sb_all", [128, NCH, 256], F32)
            yp_all = sb("yp_all", [128, NCH, 256], F32)
            cmT_all = sb("cmT_all", [128, 2, NS], BF16)
            dfsb_all = sb("dfsb_all", [128, NCH, 4], F32); decb_all = sb("decb_all", [128, NCH, 4], F32)
            stf = sb("stf", [128, 256], F32); stfb = sb("stfb", [128, 256], BF16)
            stb = sb("stb", [128, 256], F32); stbb = sb("stbb", [128, 256], BF16)
            xr = [sb("xr%d" % i, [128, 6, 514], F32) for i in range(2)]
            cvs = [sb("cv%d" % i, [128, 6, 512], F32) for i in range(2)]
            bcTs = [sb("bcT%d" % i, [128, 4, 512], BF16) for i in range(2)]
            dtr = [sb("dtr%d" % i, [128, 4, 8], F32) for i in range(2)]
            sps = [[sb("sp%d_%d" % (b_, i), [128, 4, 8], F32) for i in range(6)] for b_ in range(2)]
            E = sb("E", [128, 40], F32)
            xs_tok = sb("xs_tok", [128, 256], F32)
            bm_tok = sb("bm_tok", [128, 2, 128], BF16)
            wde = sb("wde", [128, 8], F32)
            xdt = sb("xdt", [128, 2, 256], BF16); xdte = sb("xdte", [128, 2, 256], BF16)
            Lm = sb("Lm", [128, 8, 128], F32); eL = sb("eL", [128, 8, 128], F32)
            GTm = sb("GTm", [128, 2, 2, 128], F32); W = sb("W", [128, 8, 128], BF16)
            t1 = sb("t1s", [128, 256], F32); t2 = sb("t2s", [128, 256], F32)
            sz = [sb("sz%d" % i, [128, 256], F32) for i in range(3)]
            yb = sb("yb", [128, 256], BF16); junk = sb("junks", [128, 256], BF16)
            st3 = sb("st3", [128, 3], F32)
            ysT = [sb("ysT%d" % i, [128, 2, 512], BF16) for i in range(2)]
            pcs = ps("pcs", [128, 512], F32)
            pseg = ps("pseg", [128, 1024], F32)
            pxs = ps("pxs", [128, 512], F32)
            pbm = ps("pbm", [128, D], BF16)
            pG = ps("pG", [128, 512], F32)
            pst = ps("pst", [128, 512], F32)
            pyo = ps("pyo", [128, 512], F32)

            K.dma(cw[:], convw[l], "cw"); K.dma(cbias[:], convb[l], "cbias")
            K.dma(sm[:], brow(ssd_sm[l:l + 1, :], 24), "sm")
            K.dma(sgb[:], brow(ssd_g[l:l + 1, :], 256), "sgb")
            K.act(Abc[:], sm[:, 8:16], AF.Exp)
            K.ts("dve", Abc[:], Abc[:], -1.0, None, ALU.mult)
            K.tt("dve", dsum[:], sm[:, 16:20], sm[:, 20:24], ALU.add)
            K.memset("pool", stf[:], 0.0); K.memset("pool", stfb[:], 0.0)
            K.memset("pool", stb[:], 0.0); K.memset("pool", stbb[:], 0.0)
            v4 = lambda ap: ap.rearrange("p (h d) -> p h d", h=4)

            def prologue_parts(g, s0, G):
                nt = G // 128
                xr_ = xr[g % 2]; dtr_ = dtr[g % 2]; cv = cvs[g % 2]; bcT = bcTs[g % 2]; sp_ = sps[g % 2]
                c0 = xbcol(s0)

                def conv(c):
                    K.ts("dve", cv[:, c, 0:G], xr_[:, c, 0:G], cw[:, c * 3:c * 3 + 1], cbias[:, c:c + 1], ALU.mult, ALU.add)
                    K.stt(cv[:, c, 0:G], xr_[:, c, 1:G + 1], cw[:, c * 3 + 1:c * 3 + 2], cv[:, c, 0:G], ALU.mult, ALU.add)
                    K.stt(cv[:, c, 0:G], xr_[:, c, 2:G + 2], cw[:, c * 3 + 2:c * 3 + 3], cv[:, c, 0:G], ALU.mult, ALU.add)

                def p0():
                    K.dma(xr_[:, :, 0:G + 2], xbcs.rearrange("(c p) n -> p c n", p=128)[:, :, c0 - 1:c0 + G + 1], "xr%d" % (g % 2))
                    K.dma(dtr_[:, 0:nt, :], zdts[s0:s0 + G, 256:264].rearrange("(t p) n -> p t n", p=128), "dtr%d" % (g % 2))
                    xsp, nx, mn, lg, dt, a = [t_[:, 0:nt, :] for t_ in sp_]
                    K.tt("dve", xsp, dtr_[:, 0:nt, :], bcmid(sm[:, 0:8], nt), ALU.add)
                    K.ts("dve", nx, xsp, -1.0, None, ALU.mult)
                    K.tt("dve", mn, xsp, nx, ALU.min)
                    K.act(nx, mn, AF.Exp)
                    K.act(lg, nx, AF.Ln, bias=1.0)
                    K.stt(dt, xsp, 0.0, lg, ALU.max, ALU.add)
                    K.tt("dve", a, dt, bcmid(Abc[:], nt), ALU.mult)
                    conv(0)

                def p1():
                    conv(1); conv(2)

                def p2():
                    conv(3); conv(4)

                def p3():
                    conv(5)
                    K.act(cv[:, :, 0:G], cv[:, :, 0:G], AF.Silu)
                    K.copy("pool", bcT[:, :, 0:G], cv[:, 2:6, 0:G])
                    K.copy("pool", cmT_all[:, :, s0:s0 + G], cv[:, 4:6, 0:G])
                return [p0, p1, p2, p3]

            def chunk(g, s0, G, t):
                cv = cvs[g % 2]; bcT = bcTs[g % 2]; sp_ = sps[g % 2]
                if True:
                    ci = s0 // 128 + t
                    tsl = slice(t * 128, (t + 1) * 128)
                    a_t = sp_[5][:, t, :]; dt_t = sp_[4][:, t, :]
                    for i, lt in enumerate([trit[:, 0, :], trit[:, 3, :], trit[:, 1, :], trit[:, 2, :], onesf[:]]):
                        K.mm(pcs[:, i * 8:(i + 1) * 8], lt, a_t)
                    K.act(E[:], pcs[:, 0:40], AF.Exp)
                    K.copy("pool", dfsb_all[:, ci, :], E[:, 28:32])
                    K.copy("pool", decb_all[:, ci, :], E[:, 36:40])
                    for c in range(2):
                        K.tr(pxs[:, c * 128:(c + 1) * 128], cv[:, c, tsl], identf[:])
                    K.copy("act", xs_tok[:], pxs[:, 0:256])
                    for grp in range(2):
                        K.tr(pbm[:, grp * 128:(grp + 1) * 128], bcT[:, grp, tsl], identb[:])
                    K.copy("act", bm_tok[:].rearrange("p a b -> p (a b)"), pbm[:, 0:256])
                    K.tt("dve", wde[:, 0:4], dt_t[:, 0:4], E[:, 8:12], ALU.mult)
                    K.tt("dve", wde[:, 4:8], dt_t[:, 4:8], E[:, 20:24], ALU.mult)
                    for d in range(2):
                        K.tt("dve", v4(xdt[:, d, :]), v4(xs_tok[:]), bc3(dt_t[:, d * 4:(d + 1) * 4], 64), ALU.mult)
                        K.tt("pool", v4(xdte[:, d, :]), v4(xs_tok[:]), bc3(wde[:, d * 4:(d + 1) * 4], 64), ALU.mult)
                    for j in range(8):
                        d = j // 4
                        K.ts("dve" if j % 2 else "pool", Lm[:, j, :], trit[:, 3 if d == 0 else 1, :], a_t[:, j:j + 1], 0.0, ALU.mult, ALU.add)
                        K.mm(pseg[:, j * 128:(j + 1) * 128], Lm[:, j, :], trit[:, 0 if d == 0 else 2, :])
                    K.act(eL[:].rearrange("p a b -> p (a b)"), pseg[:], AF.Exp)
                    for grp in range(2):
                        K.mm(pG[:, grp * 128:(grp + 1) * 128], bcT[:, grp, tsl], bcT[:, 2 + grp, tsl])
                    for d in range(2):
                        K.tt("dve", GTm[:, d, :, :], pG[:, 0:256].rearrange("p (a b) -> p a b", a=2),
                             bcmid(trit[:, 0 if d == 0 else 2, :], 2), ALU.mult)
                    for d in range(2):
                        for grp in range(2):
                            j0 = d * 4 + grp * 2
                            K.tt("dve", W[:, j0:j0 + 2, :], eL[:, j0:j0 + 2, :], bcmid(GTm[:, d, grp, :], 2), ALU.mult)
                    for h in range(4):
                        hs = slice(h * 64, (h + 1) * 64)
                        for d in range(2):
                            K.mm(pG[:, 256 + h * 64:256 + (h + 1) * 64], W[:, d * 4 + h, :], xdt[:, d, hs], start=(d == 0), stop=(d == 1))
                    for j in range(8):
                        d, h = j // 4, j % 4
                        K.mm(pst[:, j * 64:(j + 1) * 64], bm_tok[:, h // 2, :], xdte[:, d, h * 64:(h + 1) * 64])
                    for h in range(4):
                        hs = slice(h * 64, (h + 1) * 64)
                        K.mm(pyo[:, hs], bcT[:, 2 + h // 2, tsl], stfb[:, hs])
                    K.tt("dve", v4(t1[:]), v4(pyo[:, 0:256]), bc3(E[:, 0:4], 64), ALU.mult)
                    K.tt("dve", t1[:], pG[:, 256:512], t1[:], ALU.add)
                    K.tt("pool", v4(t2[:]), v4(xs_tok[:]), bc3(dsum[:], 64), ALU.mult)
                    K.tt("pool", yp_all[:, ci, :], t1[:], t2[:], ALU.add)
                    K.tt("dve", v4(stf[:]), v4(stf[:]), bc3(E[:, 32:36], 64), ALU.mult)
                    K.tt("dve", stf[:], pst[:, 0:256], stf[:], ALU.add)
                    K.copy("pool", stfb[:], stf[:])
                    K.copy("act", Sb_all[:, ci, :], pst[:, 256:512])


            for p_ in prologue_parts(*GROUPS[0]):
                p_()
            for i_, cur in enumerate(GROUPS):
                nt_c = cur[2] // 128
                parts = prologue_parts(*GROUPS[i_ + 1]) if i_ + 1 < len(GROUPS) else []
                per = (len(parts) + nt_c - 1) // nt_c if parts else 0
                for t in range(nt_c):
                    chunk(*cur, t)
                    for p_ in parts[t * per:(t + 1) * per]:
                        p_()

            order = [1, 0] + list(range(NCH - 1, 1, -1))
            if m_next is not None:
                m_init, m_load, m_compute = m_steps(m_next, sb, pst)
                m_init()
                m_load(0)
            for n_, ci in enumerate(order):
                if m_next is not None and n_ < 24:
                    if n_ + 1 < 24:
                        m_load(n_ + 1)
                    m_compute(n_)
                csl = slice(ci * 128, (ci + 1) * 128)
                sz_ = sz[n_ % 3]
                if n_ == 0:
                    for m_ in range(2):
                        K.dma(sz[m_ % 3][:], zdts[order[m_] * 128:(order[m_] + 1) * 128, 0:256], "sz%d" % (m_ % 3))
                if n_ + 2 < len(order):
                    K.dma(sz[(n_ + 2) % 3][:], zdts[order[n_ + 2] * 128:(order[n_ + 2] + 1) * 128, 0:256], "sz%d" % ((n_ + 2) % 3))
                for h in range(4):
                    hs = slice(h * 64, (h + 1) * 64)
                    K.mm(pyo[:, hs], cmT_all[:, h // 2, csl], stbb[:, hs])
                K.tt("dve", v4(t1[:]), v4(pyo[:, 0:256]), bc3(dfsb_all[:, ci, :], 64), ALU.mult)
                K.tt("dve", t1[:], t1[:], yp_all[:, ci, :], ALU.add)
                K.tt("dve", t1[:], t1[:], sz_[:], ALU.mult)
                K.act(junk[:], t1[:], AF.Square, accum=st3[:, 0:1])
                K.rstd(st3[:, 2:3], st3[:, 0:1], 256, st3[:, 1:2])
                K.stt(yb[:], t1[:], st3[:, 2:3], sgb[:], ALU.mult, ALU.mult)
                if ci < 2:
                    buf, slot, last = ysT[0], ci, (ci == 0)
                else:
                    gidx = (ci - 2) // 4
                    buf, slot, last = ysT[(gidx + 1) % 2], (ci - 2) % 4, ((ci - 2) % 4 == 0)
                for c in range(2):
                    K.tr(pbm[:, c * 128:(c + 1) * 128], yb[:, c * 128:(c + 1) * 128], identb[:])
                K.copy("act", buf[:, :, slot * 128:(slot + 1) * 128], pbm[:, 0:256].rearrange("p (c t) -> p c t", c=2))
                if last:
                    if ci < 2:
                        K.dma(yTs[768:1024, 0:256].rearrange("(c p) n -> p c n", p=128), buf[:, :, 0:256], "ysT0")
                    else:
                        K.dma(yTs[768:1024, 256 + gidx * 512:256 + (gidx + 1) * 512].rearrange("(c p) n -> p c n", p=128),
                              buf[:, :, :], "ysT%d" % ((gidx + 1) % 2))
                K.tt("dve", v4(stb[:]), v4(stb[:]), bc3(decb_all[:, ci, :], 64), ALU.mult)
                K.tt("dve", stb[:], stb[:], Sb_all[:, ci, :], ALU.add)
                K.copy("pool", stbb[:], stb[:])
            S.barrier()

    def phase_o(l):
        with contextlib.ExitStack() as es:
            sb = lambda n, sh, dt: es.enter_context(nc.sbuf_tensor(K.name(n), sh, dt))
            ps = lambda n, sh, dt: es.enter_context(nc.psum_tensor(K.name(n), sh, dt))
            wout = sb("wout", [128, 8, D], BF16)
            wff2 = sb("wff2", [128, 32, D], BF16)
            w1r = [sb("w1r%d" % i, [128, 8, 128], BF16) for i in range(6)]
            md = sb("md", [128, 4, D], F32)
            yT = sb("yTo", [128, 8, 512], BF16)
            xt = [sb("xt%d" % i, [128, D], F32) for i in range(5)]
            g2c = sb("g2c", [128, D], F32)
            tmpf = sb("tmpo", [128, D], F32)
            junk = sb("junko", [128, D], BF16)
            hb = [sb("hbo%d" % i, [128, D], BF16) for i in range(2)]
            h2T = [sb("h2T%d" % i, [128, 8, 512], BF16) for i in range(2)]
            rl = [sb("rl%d" % i, [128, 512], BF16) for i in range(2)]
            aT = sb("aT", [128, 32, 512], BF16)
            st = sb("sto", [128, 3, 16], F32)
            pT = ps("pTo", [128, D], BF16)
            pa = [ps("pa%d" % i, [128, 512], F32) for i in range(7)]
            bank = [0]

            def nb():
                bank[0] += 1
                return pa[bank[0] % 7]

            K.dma(wout[:], w_out[l].rearrange("(k p) n -> p k n", p=128), "wout", eng="pool")
            for jj in range(4):
                K.dma(wff2[:, jj * 8:(jj + 1) * 8, :], w_ff2[l, jj * 1024:(jj + 1) * 1024, :].rearrange("(k p) n -> p k n", p=128),
                      "wff2_%d" % jj, eng="pool")
            xi = [0]
            w1i = [0]
            sti = [0]

            def newx():
                x_ = xt[xi[0] % 5]; xkey = "xt%d" % (xi[0] % 5); xi[0] += 1
                return x_, xkey

            def post(pp, gcol, x_, gt_=None):
                si = sti[0] % 16; sti[0] += 1
                K.act(junk[:, 0:512], pp[0][:, :], AF.Square, accum=st[:, 0, si:si + 1])
                K.act(junk[:, 512:1024], pp[1][:, :], AF.Square, accum=st[:, 1, si:si + 1])
                K.tt("dve", st[:, 0, si:si + 1], st[:, 0, si:si + 1], st[:, 1, si:si + 1], ALU.add)
                K.rstd(st[:, 2, si:si + 1], st[:, 0, si:si + 1], D, st[:, 1, si:si + 1])
                for half in range(2):
                    hs = slice(half * 512, (half + 1) * 512)
                    K.stt(tmpf[:, hs], pp[half][:, :], st[:, 2, si:si + 1], (md[:, gcol, hs] if gt_ is None else gt_[:, hs]), ALU.mult, ALU.mult)
                K.tt("pool", x_[:], x_[:], tmpf[:], ALU.add)

            pend = {}

            def prep_load(g, s0, G):
                nt = G // 128
                v = 1 if g == 0 else 0
                if g in (0, 1):
                    for i, m in enumerate((2, 4, 3, 5)):
                        K.dma(md[:, i, :], brow(mods[l, v, m:m + 1, :], D), "md%d" % i)
                    if g == 0:
                        K.dma(g2c[:], brow(mods[l, 1, 5:6, :], D), "g2c")
                K.dma(yT[:, :, 0:G], yTs[:, s0:s0 + G].rearrange("(k p) n -> p k n", p=128), "yTo")
                xs_ = []
                for t in range(nt):
                    x_, xkey = newx()
                    xs_.append((x_, xkey))
                    K.dma(x_[:], xsrc(l, s0 + t * 128, 128), xkey)
                pend[g] = xs_

            def prep_op(g, s0, G, t):
                tsl = slice(t * 128, (t + 1) * 128)
                x_, xkey = pend[g][t]
                pp = [nb(), nb()]
                for half in range(2):
                    for k in range(8):
                        K.mm(pp[half][:, :], yT[:, k, tsl], wout[:, k, half * 512:(half + 1) * 512], start=(k == 0), stop=(k == 7))
                post(pp, 0, x_)
                K.dma(xscr[s0 + t * 128:s0 + (t + 1) * 128, :], x_[:], xkey, eng="pool")
                si2 = sti[0] % 16; sti[0] += 1
                K.act(junk[:], x_[:], AF.Square, accum=st[:, 0, si2:si2 + 1])
                K.rstd(st[:, 2, si2:si2 + 1], st[:, 0, si2:si2 + 1], D, st[:, 1, si2:si2 + 1])
                K.stt(tmpf[:], x_[:], st[:, 2, si2:si2 + 1], md[:, 1, :], ALU.mult, ALU.mult)
                K.tt("dve", hb[t % 2][:], tmpf[:], md[:, 2, :], ALU.add)

            def prep_tr(g, s0, G, t):
                tsl = slice(t * 128, (t + 1) * 128)
                hb_ = hb[t % 2]
                for k in range(8):
                    K.tr(pT[:, k * 128:(k + 1) * 128], hb_[:, k * 128:(k + 1) * 128], identb[:])
                K.copy("act", h2T[g % 2][:, :, tsl], pT[:].rearrange("p (k t) -> p k t", k=8))

            def ff1(g, s0, G, j0, j1):
                for j in range(j0, j1):
                    w_ = w1r[w1i[0] % 6]
                    K.dma(w_[:], wf1s[l, j].rearrange("p (k c) -> p k c", k=8), "w1r%d" % (w1i[0] % 6))
                    w1i[0] += 1
                    p_ = nb()
                    for k in range(8):
                        K.mm(p_[:, 0:G], w_[:, k, :], h2T[g % 2][:, k, 0:G], start=(k == 0), stop=(k == 7))
                    r_ = rl[j % 2]
                    K.act(r_[:, 0:G], p_[:, 0:G], AF.Relu)
                    K.tt("dve", aT[:, j, 0:G], r_[:, 0:G], p_[:, 0:G], ALU.mult)

            def ff2(g, s0, G, t):
                if True:
                    tsl = slice(t * 128, (t + 1) * 128)
                    x_, xkey = newx()
                    K.dma(x_[:], xscr[s0 + t * 128:s0 + (t + 1) * 128, :], xkey)
                    pp = [nb(), nb()]
                    for half in range(2):
                        for j in range(32):
                            K.mm(pp[half][:, :], aT[:, j, tsl], wff2[:, j, half * 512:(half + 1) * 512], start=(j == 0), stop=(j == 31))
                    post(pp, 3, x_, g2c if g == 0 else None)
                    K.dma(xdst(l, s0 + t * 128, 128), x_[:], xkey, eng="pool")

            g0 = GROUPS[0]
            prep_load(*g0)
            for t in range(g0[2] // 128):
                prep_op(*g0, t)
                prep_tr(*g0, t)
            for i_, cur in enumerate(GROUPS):
                nxt = GROUPS[i_ + 1] if i_ + 1 < len(GROUPS) else None
                nt_c = cur[2] // 128
                if nxt is not None:
                    prep_load(*nxt)
                for q_ in range(4):
                    ff1(*cur, q_ * 8, (q_ + 1) * 8)
                    if nxt is not None:
                        prep_op(*nxt, q_)
                        if q_ >= 1:
                            prep_tr(*nxt, q_ - 1)
                ff2(*cur, 0)
                if nxt is not None:
                    prep_tr(*nxt, 3)
                for t in range(1, nt_c):
                    ff2(*cur, t)
            S.barrier()

    for l in range(n_layers):
        if l == 0:
            phase_m(l)
        with contextlib.ExitStack() as esr:
            kvnT = esr.enter_context(nc.sbuf_tensor(K.name("kvnT"), [128, NS], BF16))
            krT2 = esr.enter_context(nc.sbuf_tensor(K.name("krT2"), [128, NS], BF16))
            phase_1(l, kvnT, krT2)
            phase_a(l, kvnT, krT2)
        phase_s(l, l + 1 if l + 1 < n_layers else None)
        phase_o(l)
    S.barrier()
    with contextlib.ExitStack() as es2:
        n = S.emit(es2)
    es0.close()
    return nc, n


def _rope_tables():
    rows_n = SEQ // 64
    row = np.repeat(np.arange(rows_n, dtype=np.float32), 64)
    col = np.tile(np.arange(64, dtype=np.float32), rows_n)
    inv = (np.float32(10000.0) ** (-np.arange(0, 32, 2, dtype=np.float32) / np.float32(32))).astype(np.float32)
    ar = (row[:, None] * inv).astype(np.float32)
    ac = (col[:, None] * inv).astype(np.float32)
    cos = np.zeros((64, SEQ), np.float32); sin = np.zeros((64, SEQ), np.float32)
    for d in range(64):
        ang = ar if d < 32 else ac
        f = d % 16
        cos[d] = np.cos(ang[:, f])
        sgn = -1.0 if (d % 32) < 16 else 1.0
        sin[d] = sgn * np.sin(ang[:, f])
    return cos, sin


_PERM = np.array([d + 16 if (d % 32) < 16 else d - 16 for d in range(64)])


def _host_layout(inp):
    f = lambda a: np.ascontiguousarray(a, dtype=np.float32)
    sh = {}
    sh["w_ada"] = f(inp["w_ada"]); sh["b_ada"] = f(inp["b_ada"])
    sh["g4"] = f(np.concatenate([inp["g_pre_mix"], inp["g_post_mix"], inp["g_pre_ff"], inp["g_post_ff"]], axis=1))
    w_in = inp["w_in"]
    sh["w_in"] = f(w_in)
    kr = w_in[:, :, 384:448]
    rot = kr[:, :, _PERM]
    sh["w_kr2"] = f(np.concatenate([kr, kr, rot, rot], axis=2))
    sh["gq_pc"] = f(inp["g_q"].reshape(NL, 2, 128).transpose(0, 2, 1))
    sh["gkv_p"] = f(inp["g_kv"].reshape(NL, 128, 1))
    wuq = inp["w_uq"].reshape(NL, 256, 4, 192)
    sh["w_uqx"] = f(np.concatenate([wuq, wuq[:, :, :, 128 + _PERM]], axis=3).reshape(NL, 256, 1024))
    wukv = inp["w_ukv"].reshape(NL, 128, 4, 256)
    sh["w_ukvr"] = f(np.concatenate([wukv[:, :, :, :128].reshape(NL, 128, 512), wukv[:, :, :, 128:].reshape(NL, 128, 512)], axis=2))
    sh["cm_g"] = f(inp["cm_norm_g"]); sh["cm_ws"] = f(inp["cm_w_s"])
    sh["cm_bt"] = f(inp["cm_b_s"].transpose(0, 2, 1))
    cw = inp["ssd_conv_w"].reshape(NL, 3, 6, 128).transpose(0, 3, 2, 1)
    sh["convw"] = f(cw.reshape(NL, 128, 18))
    sh["convb"] = f(inp["ssd_conv_b"].reshape(NL, 6, 128).transpose(0, 2, 1))
    sh["ssd_sm"] = f(np.concatenate([inp["ssd_dt_bias"].reshape(NL, 8), inp["ssd_a_log"].reshape(NL, 8), inp["ssd_d"].reshape(NL, 8)], axis=1))
    sh["ssd_g"] = f(inp["ssd_norm_g"])
    sh["w_out"] = f(inp["w_out"]); sh["w_ff1"] = f(inp["w_ff1"]); sh["w_ff2"] = f(inp["w_ff2"])
    cos, sin = _rope_tables()
    sh["cosd"] = f(np.concatenate([cos, cos], axis=0)); sh["sind"] = f(np.concatenate([sin, sin], axis=0))
    sh["csq"] = f(np.concatenate([cos, sin], axis=0))
    sh["csc"] = f(np.concatenate([np.ones((64, 512), np.float32), np.zeros((64, 512), np.float32)], axis=0))
    k = np.arange(128)[:, None]; l_ = np.arange(128)[None, :]
    tri = np.stack([(k <= l_), (k < l_), (k >= l_), (k > l_)], axis=1).astype(np.float32)
    sh["tri"] = f(tri.reshape(128, 512))
    sh["cc_pk"] = f(inp["c_ctx"].reshape(8, 128).T)
    return sh


_CACHE = {}


def kernel(**inputs):
    inp = {k: np.asarray(v) for k, v in inputs.items()}
    if "nc" not in _CACHE:
        _CACHE["nc"] = build()[0]
    nc = _CACHE["nc"]
    shared = _host_layout(inp)
    in_maps = []
    for b in range(8):
        m = dict(shared)
        m["x"] = np.ascontiguousarray(inp["x"][b], dtype=np.float32)
        m["ctx"] = np.ascontiguousarray(inp["ctx"][b], dtype=np.float32)
        m["c_pk"] = np.ascontiguousarray(inp["c"][b].reshape(8, 128).T, dtype=np.float32)
        in_maps.append(m)
    res = run_bass_kernel_spmd(nc, in_maps, core_ids=list(range(8)))
    return np.stack([np.asarray(r["out"], dtype=np.float32) for r in res.results], axis=0)
```

```python
import contextlib
import numpy as np
import concourse.bass as bass
import concourse.mybir as mybir
from concourse.bass_utils import run_bass_kernel_spmd

F32 = mybir.dt.float32
BF16 = mybir.dt.bfloat16
AF = mybir.ActivationFunctionType
ALU = mybir.AluOpType

EPS = 1e-6
NL = 4
D = 1024
SEQ = 4096
NCTX = 256
NS = SEQ + NCTX
NCH = NS // 128
SCALE = 192.0 ** -0.5
XB_COLS = 4356


def _esize(dt):
    return 2 if dt == BF16 else 4
class _Op:
    __slots__ = ("eng", "fn", "dma", "semkey", "waits", "sig", "cnt", "idx", "dmaval", "gid")


def _rect(ap):
    t = ap.ap
    es = _esize(ap.dtype)
    off = int(ap.offset)
    if str(ap.space) == "DRAM":
        ext = 0
        for s, c in t:
            ext += (c - 1) * abs(s)
        return (ap.name, 0, 1, off * es, (off + ext + 1) * es)
    pstep, pcnt = t[0]
    if pstep == 0:
        pstep = 1 << 40
    p0 = off // pstep
    f0 = off % pstep
    ext = 0
    for s, c in t[1:]:
        ext += (c - 1) * abs(s)
    if str(ap.space) == "PSUM":
        return (ap.name, 0, 128, (f0 * es) // 2048 * 2048, ((f0 + ext + 1) * es + 2047) // 2048 * 2048)
    return (ap.name, p0, p0 + pcnt, f0 * es, (f0 + ext + 1) * es)


class Sched:
    ENG = ("pe", "act", "dve", "pool", "sp")

    def __init__(self, nc):
        self.nc = nc
        self.ops = []
        self.acc = {}
        self.eng_ops = {e: [] for e in self.ENG}
        self.waited = {e: {x: -1 for x in self.ENG} for e in self.ENG}
        self.dma_last = {}
        self.dma_keys = {}
        self.dma_waited = {e: set() for e in self.ENG}
        self.unwaited = set()

    def add(self, eng, fn, reads=(), writes=(), dma=None):
        op = _Op()
        op.eng = eng
        op.fn = fn
        op.dma = dma is not None
        op.semkey = dma
        op.sig = False
        op.gid = len(self.ops)
        deps = set()
        rrects = [_rect(a) for a in reads]
        wrects = [_rect(a) for a in writes]
        for r in rrects:
            for rec in self.acc.get(r[0], ()):
                q = rec[0]
                if rec[2] and q[1] < r[2] and r[1] < q[2] and q[3] < r[4] and r[3] < q[4]:
                    deps.add(rec[1])
        for r in wrects:
            for rec in self.acc.get(r[0], ()):
                q = rec[0]
                if q[1] < r[2] and r[1] < q[2] and q[3] < r[4] and r[3] < q[4]:
                    deps.add(rec[1])
        if op.dma:
            prev = self.dma_last.get(dma)
            if prev is not None:
                deps.add(prev)
            self.dma_last[dma] = op.gid
            cnt = self.dma_keys.get(dma, 0) + 1
            self.dma_keys[dma] = cnt
            op.dmaval = 16 * cnt
            self.unwaited.add(op.gid)
        deps.discard(op.gid)
        waits = []
        best = {}
        for d in deps:
            o = self.ops[d]
            if o.dma:
                if d not in self.dma_waited[eng]:
                    self.dma_waited[eng].add(d)
                    self.unwaited.discard(d)
                    waits.append(("dma", d))
            else:
                if o.eng == "pe" and eng == "pe" and not op.dma:
                    continue
                if o.idx > best.get(o.eng, -1):
                    best[o.eng] = o.idx
        for x, i in best.items():
            if self.waited[eng][x] < i:
                self.waited[eng][x] = i
                waits.append(("eng", x, i))
                self.eng_ops[x][i].sig = True
        op.waits = waits
        if not op.dma:
            op.idx = len(self.eng_ops[eng])
            self.eng_ops[eng].append(op)
        else:
            op.idx = -1
        self.ops.append(op)
        for r in wrects:
            lst = self.acc.setdefault(r[0], [])
            lst[:] = [rec for rec in lst if not (r[1] <= rec[0][1] and rec[0][2] <= r[2]
                                                  and r[3] <= rec[0][3] and rec[0][4] <= r[4])]
            lst.append((r, op.gid, True, eng if not op.dma else None))
        for r in rrects:
            lst = self.acc.setdefault(r[0], [])
            if not op.dma:
                lst[:] = [rec for rec in lst if not (not rec[2] and rec[3] == eng and rec[0] == r)]
            lst.append((r, op.gid, False, eng if not op.dma else None))
        return op

    def fence(self, eng, reads=(), writes=()):
        return self.add(eng, None, reads, writes)

    def barrier(self):
        pend = sorted(self.unwaited)
        self.unwaited = set()
        last = {e: (self.eng_ops[e][-1].gid if self.eng_ops[e] else None) for e in self.ENG}
        for e in self.ENG:
            o = _Op()
            o.eng = e
            o.fn = None
            o.dma = False
            o.semkey = None
            o.sig = False
            o.gid = len(self.ops)
            waits = []
            for x in self.ENG:
                if x == e or last[x] is None:
                    continue
                i = self.ops[last[x]].idx
                if self.waited[e][x] < i:
                    self.waited[e][x] = i
                    waits.append(("eng", x, i))
                    self.eng_ops[x][i].sig = True
            for d in pend:
                if d not in self.dma_waited[e]:
                    self.dma_waited[e].add(d)
                    waits.append(("dma", d))
            o.waits = waits
            o.idx = len(self.eng_ops[e])
            self.eng_ops[e].append(o)
            self.ops.append(o)
        self.acc = {}

    def emit(self, sems_ctx):
        nc = self.nc
        engobj = {"pe": nc.tensor, "act": nc.scalar, "dve": nc.vector, "pool": nc.gpsimd, "sp": nc.sync}
        esem = {e: sems_ctx.enter_context(nc.semaphore("s_" + e)) for e in self.ENG}
        dsem = {k: sems_ctx.enter_context(nc.semaphore("d_%d" % i)) for i, k in enumerate(self.dma_keys)}
        for e in self.ENG:
            c = 0
            for o in self.eng_ops[e]:
                if o.sig:
                    c += 1
                    o.cnt = c
        n_inst = 0
        for o in self.ops:
            eo = engobj[o.eng]
            for w in o.waits:
                if w[0] == "dma":
                    d = self.ops[w[1]]
                    eo.wait_ge(dsem[d.semkey], d.dmaval)
                else:
                    eo.wait_ge(esem[w[1]], self.eng_ops[w[1]][w[2]].cnt)
            if o.fn is None:
                if o.sig:
                    eo.nop().then_inc(esem[o.eng], 1)
                continue
            inst = o.fn(eo)
            n_inst += 1
            if o.dma:
                inst.then_inc(dsem[o.semkey], 16)
            elif o.sig:
                inst.then_inc(esem[o.eng], 1)
        return n_inst


class KB:
    def __init__(self, nc):
        self.nc = nc
        self.S = Sched(nc)
        self.uid = 0

    def name(self, n):
        self.uid += 1
        return "%s_%d" % (n, self.uid)

    def dma(self, out, in_, key, eng="sp", slow=False):
        if slow:
            self.S.add(eng, lambda e: e.dma_start(out=out, in_=in_, allow_slow_non_contiguous=True), reads=[in_], writes=[out], dma=key)
        else:
            self.S.add(eng, lambda e: e.dma_start(out=out, in_=in_), reads=[in_], writes=[out], dma=key)

    def mm(self, out, lhsT, rhs, start=True, stop=True):
        self.S.add("pe", lambda e: e.matmul(out, lhsT=lhsT, rhs=rhs, start=start, stop=stop),
                   reads=[lhsT, rhs], writes=[out])

    def tr(self, out, in_, ident):
        self.S.add("pe", lambda e: e.transpose(out=out, in_=in_, identity=ident), reads=[in_, ident], writes=[out])

    def act(self, out, in_, func, bias=None, scale=None, accum=None):
        kw = {}
        rd = [in_]
        wr = [out]
        if bias is not None:
            kw["bias"] = bias
            if not isinstance(bias, float):
                rd.append(bias)
        if scale is not None:
            kw["scale"] = scale
            if not isinstance(scale, float):
                rd.append(scale)
        if accum is not None:
            kw["accum_out"] = accum
            wr.append(accum)
        self.S.add("act", lambda e: e.activation(out=out, in_=in_, func=func, **kw), reads=rd, writes=wr)

    def copy(self, eng, out, in_):
        if eng == "act":
            self.S.add("act", lambda e: e.copy(out=out, in_=in_), reads=[in_], writes=[out])
        else:
            self.S.add(eng, lambda e: e.tensor_copy(out=out, in_=in_), reads=[in_], writes=[out])

    def tt(self, eng, out, in0, in1, op):
        self.S.add(eng, lambda e: e.tensor_tensor(out=out, in0=in0, in1=in1, op=op), reads=[in0, in1], writes=[out])

    def ts(self, eng, out, in0, s1, s2, op0, op1=None):
        rd = [in0]
        if not isinstance(s1, float):
            rd.append(s1)
        if s2 is not None and not isinstance(s2, float):
            rd.append(s2)
        if op1 is None:
            self.S.add(eng, lambda e: e.tensor_scalar(out=out, in0=in0, scalar1=s1, scalar2=None, op0=op0), reads=rd, writes=[out])
        else:
            self.S.add(eng, lambda e: e.tensor_scalar(out=out, in0=in0, scalar1=s1, scalar2=s2, op0=op0, op1=op1), reads=rd, writes=[out])

    def stt(self, out, in0, scalar, in1, op0, op1):
        rd = [in0, in1]
        if not isinstance(scalar, float):
            rd.append(scalar)
        self.S.add("dve", lambda e: e.scalar_tensor_tensor(out=out, in0=in0, scalar=scalar, in1=in1, op0=op0, op1=op1),
                   reads=rd, writes=[out])

    def memset(self, eng, ap, val):
        self.S.add(eng, lambda e: e.memset(ap, val), writes=[ap])

    def recip(self, out, in_):
        self.S.add("dve", lambda e: e.reciprocal(out=out, in_=in_), reads=[in_], writes=[out])

    def bn_stats(self, out, in_):
        self.S.add("dve", lambda e: e.bn_stats(out=out, in_=in_), reads=[in_], writes=[out])

    def bn_aggr(self, out, in_):
        self.S.add("dve", lambda e: e.bn_aggr(out=out, in_=in_), reads=[in_], writes=[out])

    def rstd(self, out, in_, n, tmp):
        self.act(tmp, in_, AF.Ln, bias=EPS, scale=1.0 / n)
        self.act(out, tmp, AF.Exp, scale=-0.5)


def bc3(ap2, n):
    return ap2.unsqueeze(2).to_broadcast([ap2.shape[0], ap2.shape[1], n])


def bcmid(ap2, n):
    return ap2.unsqueeze(1).to_broadcast([ap2.shape[0], n, ap2.shape[1]])


def build(n_layers=NL, dbg=False):
    nc = bass.Bass("TRN2", target_bir_lowering=False)
    K = KB(nc)
    S = K.S

    def din(name, shape, dt=F32):
        return nc.dram_tensor(name, shape, dt, kind="ExternalInput").ap()

    def dscr(name, shape, dt=F32):
        return nc.dram_tensor(name, shape, dt, kind=("ExternalOutput" if dbg else "Internal")).ap()

    x_in = din("x", [SEQ, D]); ctx_in = din("ctx", [NCTX, D])
    c_pk = din("c_pk", [128, 8]); cc_pk = din("cc_pk", [128, 8])
    w_ada = din("w_ada", [NL, D, 6 * D]); b_ada = din("b_ada", [NL, 6 * D])
    g4 = din("g4", [NL, 4 * D])
    w_in = din("w_in", [NL, D, 1992]); w_kr2 = din("w_kr2", [NL, D, 256])
    gq_pc = din("gq_pc", [NL, 128, 2]); gkv_p = din("gkv_p", [NL, 128, 1])
    w_uqx = din("w_uqx", [NL, 256, 1024]); w_ukvr = din("w_ukvr", [NL, 128, 1024])
    cm_g = din("cm_g", [NL, 256]); cm_ws = din("cm_ws", [NL, 4, 128, 128]); cm_bt = din("cm_bt", [NL, 128, 4])
    convw = din("convw", [NL, 128, 18]); convb = din("convb", [NL, 128, 6])
    ssd_sm = din("ssd_sm", [NL, 24]); ssd_g = din("ssd_g", [NL, 256])
    w_out = din("w_out", [NL, D, D]); w_ff1 = din("w_ff1", [NL, D, 4 * D]); w_ff2 = din("w_ff2", [NL, 4 * D, D])
    cosd = din("cosd", [128, SEQ]); sind = din("sind", [128, SEQ]); csq = din("csq", [128, SEQ]); csc = din("csc", [128, 512])
    tri = din("tri", [128, 4 * 128])
    out = nc.dram_tensor("out", [SEQ, D], F32, kind="ExternalOutput").ap()

    xscr = dscr("xscr", [NS, D])
    mods = dscr("mods", [NL, 2, 6, D])
    xbcs = dscr("xbcs", [768, XB_COLS])
    zdts = dscr("zdts", [NS, 264])
    yTs = dscr("yTs", [D, NS], BF16)
    qnTs = dscr("qnTs", [256, NS], BF16)
    wf1s = dscr("wf1s", [NL, 32, 128, 1024], BF16)
    ypS = dscr("ypS", [NS, 256])

    GROUPS = [(0, 0, 256)] + [(g, 256 + (g - 1) * 512, 512) for g in range(1, 9)]

    def xsrc(l, s, n):
        if l == 0:
            return ctx_in[s:s + n, :] if s < NCTX else x_in[s - NCTX:s - NCTX + n, :]
        return xscr[s:s + n, :]

    def xdst(l, s, n):
        if l == n_layers - 1 and s >= NCTX:
            return out[s - NCTX:s - NCTX + n, :]
        return xscr[s:s + n, :]

    def xbcol(s):
        return 1 + s if s < NCTX else 259 + (s - NCTX)

    def brow(ap_row, n):
        return ap_row.broadcast_to([128, n])

    es0 = contextlib.ExitStack()
    sb0 = lambda n, sh, dt: es0.enter_context(nc.sbuf_tensor(n, sh, dt))
    identf = sb0("identf", [128, 128], F32)
    identb = sb0("identb", [128, 128], BF16)
    onesb = sb0("onesb", [128, 128], BF16)
    onesf = sb0("onesf", [128, 128], F32)
    trit = sb0("trit", [128, 4, 128], F32)
    zrow = sb0("zrow", [128, 8], F32)
    K.dma(trit[:].rearrange("p a b -> p (a b)"), tri[:, :], "trit")
    K.memset("pool", onesf[:], 1.0)
    K.memset("pool", onesb[:], 1.0)
    K.memset("pool", zrow[:], 0.0)
    K.tt("dve", identf[:], trit[:, 0, :], trit[:, 2, :], ALU.mult)
    K.copy("dve", identb[:], identf[:])
    for c in range(6):
        for col in (0, 257, 258, 4355):
            K.dma(xbcs[c * 128:(c + 1) * 128, col:col + 1], zrow[:, 0:1], "zpad", slow=True)
    def m_steps(l, sb, pbank):
        sca = sb("sca", [128, 2, 8], F32)
        sc2 = sb("sc2", [128, 8, 2], F32)
        wa = [sb("wa%d" % i, [128, 8, 256], F32) for i in range(2)]
        bd = [sb("bd%d" % i, [2, 256], F32) for i in range(3)]
        gg = [sb("gg%d" % i, [2, 256], F32) for i in range(3)]
        mt = [sb("mt%d" % i, [2, 256], F32) for i in range(2)]
        res = [sb("res%d" % i, [2, 256], F32) for i in range(3)]
        wv = w_ada[l].rearrange("(k p) n -> p k n", p=128)

        def init():
            K.dma(sca[:, 0, :], c_pk[:, :], "sc0")
            K.dma(sca[:, 1, :], cc_pk[:, :], "sc1")
            K.act(sca[:], sca[:], AF.Silu)
            for v in range(2):
                K.copy("dve", sc2[:, :, v], sca[:, v, :])

        def gcol(j):
            m, q = j // 4, j % 4
            if m in (1, 4):
                return (0 if m == 1 else 2) * D + q * 256
            if m in (2, 5):
                return (1 if m == 2 else 3) * D + q * 256
            return None

        def load(j):
            K.dma(wa[j % 2][:], wv[:, :, j * 256:(j + 1) * 256], "wa%d" % (j % 2))
            K.dma(bd[j % 3][:], b_ada[l:l + 1, j * 256:(j + 1) * 256].broadcast_to([2, 256]), "bd%d" % (j % 3))
            if gcol(j) is not None:
                K.dma(gg[j % 3][:], g4[l:l + 1, gcol(j):gcol(j) + 256].broadcast_to([2, 256]), "gg%d" % (j % 3))

        def compute(j):
            m, q = j // 4, j % 4
            w_ = wa[j % 2]
            for k in range(8):
                K.mm(pbank[0:2, 0:256], sc2[:, k, :], w_[:, k, :], start=(k == 0), stop=(k == 7))
            t_ = mt[j % 2]
            r_ = res[j % 3]
            K.tt("dve", t_[:], pbank[0:2, 0:256], bd[j % 3][:], ALU.add)
            if m in (0, 3):
                K.copy("dve", r_[:], t_[:])
            elif m in (1, 4):
                K.stt(r_[:], t_[:], 1.0, gg[j % 3][:], ALU.add, ALU.mult)
            else:
                K.tt("dve", r_[:], t_[:], gg[j % 3][:], ALU.mult)
            K.dma(mods[l, :, m, q * 256:(q + 1) * 256], r_[:], "res%d" % (j % 3))

        return init, load, compute

    def phase_m(l):
        with contextlib.ExitStack() as es:
            sb = lambda n, sh, dt: es.enter_context(nc.sbuf_tensor(K.name(n), sh, dt))
            pm = es.enter_context(nc.psum_tensor(K.name("pm"), [128, 512], F32))
            init, load, compute = m_steps(l, sb, pm)
            init()
            load(0)
            for j in range(24):
                if j + 1 < 24:
                    load(j + 1)
                compute(j)
            S.barrier()

    def phase_1(l, kvnT, krT2):
        with contextlib.ExitStack() as es:
            sb = lambda n, sh, dt: es.enter_context(nc.sbuf_tensor(K.name(n), sh, dt))
            ps = lambda n, sh, dt: es.enter_context(nc.psum_tensor(K.name(n), sh, dt))
            win = sb("win", [128, 8, 1992], BF16)
            wkr2 = sb("wkr2", [128, 8, 256], BF16)
            gq = sb("gq", [128, 2], F32); gkv = sb("gkv", [128, 1], F32)
            cmg = sb("cmg", [128, 256], F32); cmb = sb("cmb", [128, 4], F32)
            wsf = sb("wsf", [128, 4, 128], F32); wsT = sb("wsT", [128, 4, 128], BF16)
            ab = sb("ab", [128, 2, D], F32)
            xg = [sb("xg%d" % i, [128, D], F32) for i in range(3)]
            hT = [sb("hT%d" % i, [128, 8, 512], BF16) for i in range(2)]
            hb = [sb("hb%d" % i, [128, D], BF16) for i in range(4)]
            tmpf = sb("tmpf", [128, D], F32)
            junk = sb("junk", [128, D], BF16)
            st4 = sb("st4", [128, 3, 4], F32)
            sq = sb("sq", [128, 3, 512], BF16)
            lnq = sb("lnq", [128, 512], F32)
            rst = [sb("rst%d" % i, [128, 512], F32) for i in range(2)]
            qn = [sb("qn%d" % i, [128, 2, 512], BF16) for i in range(2)]
            cosk = sb("cosk", [128, 512], F32); sink = sb("sink", [128, 512], F32)
            t1 = sb("t1", [128, 512], F32); t2 = sb("t2", [128, 512], F32)
            xbst = [sb("xbst%d" % i, [128, 6, 512], F32) for i in range(2)]
            xcms = [sb("xcm%d" % i, [128, 4, 512], F32) for i in range(2)]
            zdt = [sb("zdt%d" % i, [128, 4, 264], F32) for i in range(2)]
            st6 = sb("st6", [128, 4, 6], F32); mv = sb("mv", [128, 4, 2], F32)
            vpe = sb("vpe", [128, 4], F32); rscm = sb("rscm", [128, 4], F32); cneg = sb("cneg", [128, 4], F32)
            vnf = [sb("vnf%d" % i, [128, 256], F32) for i in range(2)]
            vnb = [sb("vnb%d" % i, [128, 256], BF16) for i in range(2)]
            ycm = [sb("ycm%d" % i, [128, 256], BF16) for i in range(2)]
            ycmT = [sb("ycmT%d" % i, [128, 2, 512], BF16) for i in range(2)]
            pT = ps("pT", [128, D], BF16)
            pf = [ps("pf%d" % i, [128, 512], F32) for i in range(7)]
            bank = [0]

            def nb():
                bank[0] += 1
                return pf[3 + bank[0] % 4]

            K.dma(win[:], w_in[l].rearrange("(k p) n -> p k n", p=128), "win", eng="pool")
            K.dma(wkr2[:], w_kr2[l].rearrange("(k p) n -> p k n", p=128), "wkr2", eng="pool")
            K.dma(gq[:], gq_pc[l], "gq"); K.dma(gkv[:], gkv_p[l], "gkv")
            K.dma(cmg[:], brow(cm_g[l:l + 1, :], 256), "cmg"); K.dma(cmb[:], cm_bt[l], "cmb")
            K.dma(wsf[:], cm_ws[l].rearrange("g t s -> t g s"), "wsf")
            K.memset("pool", cneg[:], -0.5)
            for gi in range(4):
                p_ = nb()
                K.tr(p_[:, 0:128], wsf[:, gi, :], identf[:])
                K.copy("dve", wsT[:, gi, :], p_[:, 0:128])

            xi = [0]
            deferred = []

            def dstore(out_, in__, key):
                deferred.append((out_, in__, key))

            def flush():
                for (o_, i_, k_) in deferred:
                    K.dma(o_, i_, k_)
                del deferred[:]

            def prep(g, s0, G):
                nt = G // 128
                v = 1 if g == 0 else 0
                if g in (0, 1):
                    K.dma(ab[:, 0, :], brow(mods[l, v, 1:2, :], D), "ab0")
                    K.dma(ab[:, 1, :], brow(mods[l, v, 0:1, :], D), "ab1")
                hT_ = hT[g % 2]
                for t in range(nt):
                    x_ = xg[xi[0] % 3]; xi[0] += 1
                    K.dma(x_[:], xsrc(l, s0 + t * 128, 128), "xg%d" % ((xi[0] - 1) % 3))
                    K.act(junk[:], x_[:], AF.Square, accum=st4[:, 0, t:t + 1])
                    K.rstd(st4[:, 2, t:t + 1], st4[:, 0, t:t + 1], D, st4[:, 1, t:t + 1])
                    K.stt(tmpf[:], x_[:], st4[:, 2, t:t + 1], ab[:, 0, :], ALU.mult, ALU.mult)
                    K.tt("dve", hb[t][:], tmpf[:], ab[:, 1, :], ALU.add)

            def prep_b(g, s0, G):
                nt = G // 128
                hT_ = hT[g % 2]
                for t in range(nt):
                    for k in range(8):
                        K.tr(pT[:, k * 128:(k + 1) * 128], hb[t][:, k * 128:(k + 1) * 128], identb[:])
                    K.copy("act", hT_[:, :, t * 128:(t + 1) * 128], pT[:].rearrange("p (k t) -> p k t", k=8))

            def body(g, s0, G):
                nt = G // 128
                hT_ = hT[g % 2]

                def fm(col0, m, wt=win, p_=None):
                    if p_ is None:
                        p_ = nb()
                    for k in range(8):
                        K.mm(p_[0:m, 0:G], wt[:, k, col0:col0 + m], hT_[:, k, 0:G], start=(k == 0), stop=(k == 7))
                    return p_

                pq = [fm(0, 128, p_=pf[0]), fm(128, 128, p_=pf[1])]
                pkv = fm(256, 128, p_=pf[2])
                for c in range(2):
                    K.act(sq[:, c, 0:G], pq[c][:, 0:G], AF.Square)
                K.act(sq[:, 2, 0:G], pkv[:, 0:G], AF.Square)
                pkr = fm(0, 128, wkr2)
                if g == 0:
                    K.copy("dve", krT2[:, s0:s0 + G], pkr[:, 0:G])
                else:
                    pkrr = fm(128, 128, wkr2)
                    K.dma(cosk[:], cosd[:, s0 - NCTX:s0 - NCTX + G], "cosk")
                    K.dma(sink[:], sind[:, s0 - NCTX:s0 - NCTX + G], "sink")
                    K.tt("dve", t1[:, 0:G], pkr[:, 0:G], cosk[:, 0:G], ALU.mult)
                    K.tt("dve", t2[:, 0:G], pkrr[:, 0:G], sink[:, 0:G], ALU.mult)
                    K.tt("pool", krT2[:, s0:s0 + G], t1[:, 0:G], t2[:, 0:G], ALU.add)
                for c in range(6):
                    p_ = fm(1216 + c * 128, 128)
                    K.copy("act", xbst[g % 2][:, c, 0:G], p_[:, 0:G])
                dstore(xbcs.rearrange("(c p) n -> p c n", p=128)[:, :, xbcol(s0):xbcol(s0) + G], xbst[g % 2][:, :, 0:G], "xbst%d" % (g % 2))
                psq = nb()
                for c in range(2):
                    K.mm(psq[:, 0:G], onesb[:], sq[:, c, 0:G], start=(c == 0), stop=(c == 1))
                pskv = nb()
                K.mm(pskv[:, 0:G], onesb[:], sq[:, 2, 0:G])
                K.rstd(rst[0][:, 0:G], psq[:, 0:G], 256, lnq[:, 0:G])
                K.rstd(rst[1][:, 0:G], pskv[:, 0:G], 128, lnq[:, 0:G])
                qn_ = qn[g % 2]
                for c in range(2):
                    K.stt(qn_[:, c, 0:G], pq[c][:, 0:G], gq[:, c:c + 1], rst[0][:, 0:G], ALU.mult, ALU.mult)
                    dstore(qnTs[c * 128:(c + 1) * 128, s0:s0 + G], qn_[:, c, 0:G], "qn%d_%d" % (g % 2, c))
                K.stt(kvnT[:, s0:s0 + G], pkv[:, 0:G], gkv[:, 0:1], rst[1][:, 0:G], ALU.mult, ALU.mult)
                zdt_ = zdt[g % 2]
                xcm = xcms[g % 2]
                for t in range(nt):
                    tsl = slice(t * 128, (t + 1) * 128)
                    p_ = nb()
                    for k in range(8):
                        K.mm(p_[:, :], hT_[:, k, tsl], win[:, k, 448:960], start=(k == 0), stop=(k == 7))
                    K.copy("act", xcm[:, t, :], p_[:, :])
                    p2 = nb()
                    for k in range(8):
                        K.mm(p2[:, 0:256], hT_[:, k, tsl], win[:, k, 960:1216], start=(k == 0), stop=(k == 7))
                    for k in range(8):
                        K.mm(p2[:, 256:264], hT_[:, k, tsl], win[:, k, 1984:1992], start=(k == 0), stop=(k == 7))
                    K.copy("dve", zdt_[:, t, :], p2[:, 0:264])
                K.act(xcm[:, 0:nt, :], xcm[:, 0:nt, :], AF.Gelu_apprx_tanh)
                K.act(zdt_[:, 0:nt, 0:256], zdt_[:, 0:nt, 0:256], AF.Silu)
                dstore(zdts[s0:s0 + G, :].rearrange("(t p) n -> p t n", p=128), zdt_[:, 0:nt, :], "zdt%d" % (g % 2))

            def body_b(g, s0, G):
                nt = G // 128
                xcm = xcms[g % 2]
                for t in range(nt):
                    K.bn_stats(st6[:, t, :], xcm[:, t, 256:512])
                    K.bn_aggr(mv[:, t, :], st6[:, t, :])
                K.ts("dve", vpe[:, 0:nt], mv[:, 0:nt, 1], EPS, None, ALU.add)
                K.tt("pool", rscm[:, 0:nt], vpe[:, 0:nt], cneg[:, 0:nt], ALU.pow)
                ycmT_ = ycmT[g % 2]
                for t in range(nt):
                    vf = vnf[t % 2]; vb = vnb[t % 2]; yc = ycm[t % 2]
                    K.ts("dve", vf[:], xcm[:, t, 256:512], mv[:, t, 0:1], rscm[:, t:t + 1], ALU.subtract, ALU.mult)
                    K.tt("pool", vb[:], vf[:], cmg[:], ALU.mult)
                    p_ = nb()
                    for gi in range(4):
                        K.mm(p_[:, gi * 64:(gi + 1) * 64], wsT[:, gi, :], vb[:, gi * 64:(gi + 1) * 64])
                    for gi in range(4):
                        gs = slice(gi * 64, (gi + 1) * 64)
                        K.stt(yc[:, gs], p_[:, gs], cmb[:, gi:gi + 1], xcm[:, t, gs], ALU.add, ALU.mult)
                    for c in range(2):
                        K.tr(pT[:, c * 128:(c + 1) * 128], yc[:, c * 128:(c + 1) * 128], identb[:])
                    K.copy("act", ycmT_[:, :, t * 128:(t + 1) * 128], pT[:, 0:256].rearrange("p (c t) -> p c t", c=2))
                dstore(yTs[512:768, s0:s0 + G].rearrange("(c p) n -> p c n", p=128), ycmT_[:, :, 0:G], "ycmT%d" % (g % 2))

            prep(*GROUPS[0])
            prep_b(*GROUPS[0])
            for i_, grp_ in enumerate(GROUPS):
                if i_ + 1 < len(GROUPS):
                    prep(*GROUPS[i_ + 1])
                flush()
                body(*grp_)
                if i_ + 1 < len(GROUPS):
                    prep_b(*GROUPS[i_ + 1])
                if i_ >= 1:
                    body_b(*GROUPS[i_ - 1])
            body_b(*GROUPS[-1])
            flush()
            S.barrier()

    def phase_a(l, kvnT, krT2):
        with contextlib.ExitStack() as es:
            sb = lambda n, sh, dt: es.enter_context(nc.sbuf_tensor(K.name(n), sh, dt))
            ps = lambda n, sh, dt: es.enter_context(nc.psum_tensor(K.name(n), sh, dt))
            KT = sb("KT", [128, 4, NS], BF16)
            Vaug = sb("Vaug", [128, NCH, 4, 130], BF16)
            wuq = sb("wuq", [128, 2, 1024], BF16)
            wukv = sb("wukv", [128, 1024], BF16)
            qn = [sb("qna%d" % i, [128, 2, 512], BF16) for i in range(2)]
            cs = [sb("csa%d" % i, [128, 512], F32) for i in range(2)]
            qh = [sb("qh%d" % i, [128, 512], BF16) for i in range(2)]
            qr = [sb("qr%d" % i, [128, 512], BF16) for i in range(2)]
            PT = [sb("PT%d" % i, [128, 512], BF16) for i in range(4)]
            yat = [sb("yat%d" % i, [128, 512], F32) for i in range(4)]
            yaT = [sb("yaT%d" % i, [128, 4, 512], BF16) for i in range(2)]
            rden = sb("rden", [128, 8], F32)
            acc = [ps("acc%d" % i, [128, 512], F32) for i in range(4)]
            psc = [ps("psc%d" % i, [128, 512], F32) for i in range(3)]
            pqu = ps("pqu", [128, 512], F32)
            K.dma(wuq[:], w_uqx[l].rearrange("(c p) n -> p c n", p=128), "wuq", eng="pool")
            K.dma(wukv[:], w_ukvr[l], "wukv", eng="pool")
            K.memset("pool", Vaug[:], 1.0)
            for j in range(32):
                K.dma(wf1s[l, j].rearrange("p (k c) -> p k c", k=8),
                      w_ff1[l].rearrange("(k p) n -> p k n", p=128)[:, :, j * 128:(j + 1) * 128], "wf1cast%d" % (j % 4), eng="pool")
            for (g, s0, G) in GROUPS:
                for h in range(4):
                    p_ = psc[h % 2]
                    K.mm(p_[:, 0:G], wukv[:, h * 128:(h + 1) * 128], kvnT[:, s0:s0 + G])
                    K.copy("act" if h % 2 else "dve", KT[:, h, s0:s0 + G], p_[:, 0:G])
                for t in range(G // 128):
                    p_ = acc[t]
                    K.mm(p_[:, :], kvnT[:, s0 + t * 128:s0 + (t + 1) * 128], wukv[:, 512:1024])
                    K.copy("act" if t % 2 else "dve", Vaug[:, s0 // 128 + t, :, 0:128], p_[:, :].rearrange("p (h d) -> p h d", h=4))
            pti = [0]

            def aload(g, s0, G):
                K.dma(qn[g % 2][:, :, 0:G], qnTs[:, s0:s0 + G].rearrange("(c p) n -> p c n", p=128), "qna%d" % (g % 2))
                if g == 0:
                    K.dma(cs[g % 2][:, 0:G], csc[:, 0:G], "csa%d" % (g % 2))
                else:
                    K.dma(cs[g % 2][:, 0:G], csq[:, s0 - NCTX:s0 - NCTX + G], "csa%d" % (g % 2))

            for (g, s0, G) in GROUPS:
                nt = G // 128
                kts = [0, 1] if g == 0 else list(range(NCH))
                qn_ = qn[g % 2]; cs_ = cs[g % 2]; yaT_ = yaT[g % 2]
                if g == 0:
                    aload(g, s0, G)
                if g + 1 < len(GROUPS):
                    aload(*GROUPS[g + 1])
                for h in range(4):
                    qh_ = qh[h % 2]; qr_ = qr[h % 2]
                    for c in range(2):
                        K.mm(pqu[:, 0:G], wuq[:, c, h * 256:h * 256 + 128], qn_[:, c, 0:G], start=(c == 0), stop=(c == 1))
                    K.copy("dve", qh_[:, 0:G], pqu[:, 0:G])
                    for c in range(2):
                        K.mm(pqu[:, 0:G], wuq[:, c, h * 256 + 128:h * 256 + 256], qn_[:, c, 0:G], start=(c == 0), stop=(c == 1))
                    K.tt("dve", qr_[:, 0:G], pqu[:, 0:G], cs_[:, 0:G], ALU.mult)
                    stash = {}

                    def score(i):
                        kt = kts[i]
                        ksl = slice(kt * 128, (kt + 1) * 128)
                        p_ = psc[pti[0] % 3]
                        P_ = PT[pti[0] % 4]
                        pti[0] += 1
                        K.mm(p_[:, 0:G], KT[:, h, ksl], qh_[:, 0:G], start=True, stop=False)
                        K.mm(p_[:, 0:G], krT2[:, ksl], qr_[:, 0:G], start=False, stop=True)
                        K.act(P_[:, 0:G], p_[:, 0:G], AF.Exp, scale=SCALE)
                        stash[i] = P_

                    score(0)
                    if len(kts) > 1:
                        score(1)
                    for i, kt in enumerate(kts):
                        if i + 2 < len(kts):
                            score(i + 2)
                        P_ = stash.pop(i)
                        for qt in range(nt):
                            K.mm(acc[qt][:, 0:129], P_[:, qt * 128:(qt + 1) * 128], Vaug[:, kt, h, 0:129],
                                 start=(i == 0), stop=(i == len(kts) - 1))
                    for qt in range(nt):
                        K.recip(rden[:, qt:qt + 1], acc[qt][:, 128:129])
                        K.ts("dve", yat[qt][:, h * 128:(h + 1) * 128], acc[qt][:, 0:128], rden[:, qt:qt + 1], None, ALU.mult)
                for qt in range(nt):
                    for h in range(4):
                        K.tr(pqu[:, h * 128:(h + 1) * 128], yat[qt][:, h * 128:(h + 1) * 128], identf[:])
                    K.copy("act", yaT_[:, :, qt * 128:(qt + 1) * 128], pqu[:, 0:512].rearrange("p (h t) -> p h t", h=4))
                K.dma(yTs[0:512, s0:s0 + G].rearrange("(h p) n -> p h n", p=128), yaT_[:, :, 0:G], "yaT%d" % (g % 2))
            S.barrier()

    def phase_s(l, m_next=None):
        with contextlib.ExitStack() as es:
            sb = lambda n, sh, dt: es.enter_context(nc.sbuf_tensor(K.name(n), sh, dt))
            ps = lambda n, sh, dt: es.enter_context(nc.psum_tensor(K.name(n), sh, dt))
            cw = sb("cw", [128, 18], F32); cbias = sb("cbias", [128, 6], F32)
            sm = sb("sm", [128, 24], F32); Abc = sb("Abc", [128, 8], F32); dsum = sb("dsum", [128, 4], F32)
            sgb = sb("sgb", [128, 256], F32)
            Sb_all = sb("Sb_all", [128, NCH, 256], F32)
            ypt = [sb("ypt%d" % i, [128, 256], F32) for i in range(2)]
            ypl = [sb("ypl%d" % i, [128, 256], F32) for i in range(3)]
            cmT_all = sb("cmT_all", [128, 2, NS], BF16)
            dfsb_all = sb("dfsb_all", [128, NCH, 4], F32); decb_all = sb("decb_all", [128, NCH, 4], F32)
            stf = sb("stf", [128, 256], F32); stfb = sb("stfb", [128, 256], BF16)
            stb = sb("stb", [128, 256], F32); stbb = sb("stbb", [128, 256], BF16)
            xr = [sb("xr%d" % i, [128, 6, 514], F32) for i in range(2)]
            cvs = [sb("cv%d" % i, [128, 6, 512], F32) for i in range(2)]
            bcTs = [sb("bcT%d" % i, [128, 4, 512], BF16) for i in range(2)]
            dtr = [sb("dtr%d" % i, [128, 4, 8], F32) for i in range(2)]
            sps = [[sb("sp%d_%d" % (b_, i), [128, 4, 8], F32) for i in range(6)] for b_ in range(2)]
            E2 = [sb("E_%d" % i, [128, 40], F32) for i in range(2)]
            xs_tok2 = [sb("xs_tok_%d" % i, [128, 256], F32) for i in range(2)]
            bm_tok2 = [sb("bm_tok_%d" % i, [128, 2, 128], BF16) for i in range(2)]
            wde2 = [sb("wde_%d" % i, [128, 8], F32) for i in range(2)]
            xdt2 = [sb("xdt_%d" % i, [128, 2, 256], BF16) for i in range(2)]; xdte2 = [sb("xdte_%d" % i, [128, 2, 256], BF16) for i in range(2)]
            Lm2 = [sb("Lm_%d" % i, [128, 8, 128], F32) for i in range(2)]; eL2 = [sb("eL_%d" % i, [128, 8, 128], F32) for i in range(2)]
            GTm2 = [sb("GTm_%d" % i, [128, 2, 2, 128], F32) for i in range(2)]; W2 = [sb("W_%d" % i, [128, 8, 128], BF16) for i in range(2)]
            t1_2 = [sb("t1s_%d" % i, [128, 256], F32) for i in range(2)]; t2_2 = [sb("t2s_%d" % i, [128, 256], F32) for i in range(2)]
            sz = [sb("sz%d" % i, [128, 256], F32) for i in range(3)]
            yb2 = [sb("yb%d" % i, [128, 256], BF16) for i in range(2)]; junk = sb("junks", [128, 256], BF16)
            st3 = sb("st3", [128, 3], F32)
            ysT = [sb("ysT%d" % i, [128, 2, 512], BF16) for i in range(2)]
            pcs = ps("pcs", [128, 512], F32)
            pseg = ps("pseg", [128, 1024], F32)
            pxs = ps("pxs", [128, 512], F32)
            pbm = ps("pbm", [128, D], BF16)
            pG = ps("pG", [128, 512], F32)
            pst = ps("pst", [128, 512], F32)
            pyo = ps("pyo", [128, 512], F32)

            K.dma(cw[:], convw[l], "cw"); K.dma(cbias[:], convb[l], "cbias")
            K.dma(sm[:], brow(ssd_sm[l:l + 1, :], 24), "sm")
            K.dma(sgb[:], brow(ssd_g[l:l + 1, :], 256), "sgb")
            K.act(Abc[:], sm[:, 8:16], AF.Exp)
            K.ts("dve", Abc[:], Abc[:], -1.0, None, ALU.mult)
            K.tt("dve", dsum[:], sm[:, 16:20], sm[:, 20:24], ALU.add)
            K.memset("pool", stf[:], 0.0); K.memset("pool", stfb[:], 0.0)
            K.memset("pool", stb[:], 0.0); K.memset("pool", stbb[:], 0.0)
            v4 = lambda ap: ap.rearrange("p (h d) -> p h d", h=4)

            def prologue_parts(g, s0, G):
                nt = G // 128
                xr_ = xr[g % 2]; dtr_ = dtr[g % 2]; cv = cvs[g % 2]; bcT = bcTs[g % 2]; sp_ = sps[g % 2]
                c0 = xbcol(s0)

                def conv(c):
                    K.ts("dve", cv[:, c, 0:G], xr_[:, c, 0:G], cw[:, c * 3:c * 3 + 1], cbias[:, c:c + 1], ALU.mult, ALU.add)
                    K.stt(cv[:, c, 0:G], xr_[:, c, 1:G + 1], cw[:, c * 3 + 1:c * 3 + 2], cv[:, c, 0:G], ALU.mult, ALU.add)
                    K.stt(cv[:, c, 0:G], xr_[:, c, 2:G + 2], cw[:, c * 3 + 2:c * 3 + 3], cv[:, c, 0:G], ALU.mult, ALU.add)

                def p0():
                    K.dma(xr_[:, :, 0:G + 2], xbcs.rearrange("(c p) n -> p c n", p=128)[:, :, c0 - 1:c0 + G + 1], "xr%d" % (g % 2))
                    K.dma(dtr_[:, 0:nt, :], zdts[s0:s0 + G, 256:264].rearrange("(t p) n -> p t n", p=128), "dtr%d" % (g % 2))
                    xsp, nx, mn, lg, dt, a = [t_[:, 0:nt, :] for t_ in sp_]
                    K.tt("dve", xsp, dtr_[:, 0:nt, :], bcmid(sm[:, 0:8], nt), ALU.add)
                    K.ts("dve", nx, xsp, -1.0, None, ALU.mult)
                    K.tt("dve", mn, xsp, nx, ALU.min)
                    K.act(nx, mn, AF.Exp)
                    K.act(lg, nx, AF.Ln, bias=1.0)
                    K.stt(dt, xsp, 0.0, lg, ALU.max, ALU.add)
                    K.tt("dve", a, dt, bcmid(Abc[:], nt), ALU.mult)
                    conv(0)

                def p1():
                    conv(1); conv(2)

                def p2():
                    conv(3); conv(4)

                def p3():
                    conv(5)
                    K.act(cv[:, :, 0:G], cv[:, :, 0:G], AF.Silu)
                    K.copy("pool", bcT[:, :, 0:G], cv[:, 2:6, 0:G])
                    K.copy("pool", cmT_all[:, :, s0:s0 + G], cv[:, 4:6, 0:G])
                return [p0, p1, p2, p3]

            def chunk(g, s0, G, t):
                cv = cvs[g % 2]; bcT = bcTs[g % 2]; sp_ = sps[g % 2]
                if True:
                    ci = s0 // 128 + t
                    pb = ci % 2
                    E = E2[pb]; xs_tok = xs_tok2[pb]; bm_tok = bm_tok2[pb]; wde = wde2[pb]; xdt = xdt2[pb]; xdte = xdte2[pb]
                    Lm = Lm2[pb]; eL = eL2[pb]; GTm = GTm2[pb]; W = W2[pb]; t1 = t1_2[pb]; t2 = t2_2[pb]
                    tsl = slice(t * 128, (t + 1) * 128)
                    a_t = sp_[5][:, t, :]; dt_t = sp_[4][:, t, :]
                    for i, lt in enumerate([trit[:, 0, :], trit[:, 3, :], trit[:, 1, :], trit[:, 2, :], onesf[:]]):
                        K.mm(pcs[:, i * 8:(i + 1) * 8], lt, a_t)
                    K.act(E[:], pcs[:, 0:40], AF.Exp)
                    K.copy("pool", dfsb_all[:, ci, :], E[:, 28:32])
                    K.copy("pool", decb_all[:, ci, :], E[:, 36:40])
                    for c in range(2):
                        K.tr(pxs[:, c * 128:(c + 1) * 128], cv[:, c, tsl], identf[:])
                    K.copy("act", xs_tok[:], pxs[:, 0:256])
                    for grp in range(2):
                        K.tr(pbm[:, grp * 128:(grp + 1) * 128], bcT[:, grp, tsl], identb[:])
                    K.copy("act", bm_tok[:].rearrange("p a b -> p (a b)"), pbm[:, 0:256])
                    K.tt("dve", wde[:, 0:4], dt_t[:, 0:4], E[:, 8:12], ALU.mult)
                    K.tt("dve", wde[:, 4:8], dt_t[:, 4:8], E[:, 20:24], ALU.mult)
                    for d in range(2):
                        K.tt("dve", v4(xdt[:, d, :]), v4(xs_tok[:]), bc3(dt_t[:, d * 4:(d + 1) * 4], 64), ALU.mult)
                        K.tt("pool", v4(xdte[:, d, :]), v4(xs_tok[:]), bc3(wde[:, d * 4:(d + 1) * 4], 64), ALU.mult)
                    for j in range(8):
                        d = j // 4
                        K.ts("dve" if j % 2 else "pool", Lm[:, j, :], trit[:, 3 if d == 0 else 1, :], a_t[:, j:j + 1], 0.0, ALU.mult, ALU.add)
                        K.mm(pseg[:, j * 128:(j + 1) * 128], Lm[:, j, :], trit[:, 0 if d == 0 else 2, :])
                    K.act(eL[:].rearrange("p a b -> p (a b)"), pseg[:], AF.Exp)
                    for grp in range(2):
                        K.mm(pG[:, grp * 128:(grp + 1) * 128], bcT[:, grp, tsl], bcT[:, 2 + grp, tsl])
                    for d in range(2):
                        K.tt("dve", GTm[:, d, :, :], pG[:, 0:256].rearrange("p (a b) -> p a b", a=2),
                             bcmid(trit[:, 0 if d == 0 else 2, :], 2), ALU.mult)
                    for d in range(2):
                        for grp in range(2):
                            j0 = d * 4 + grp * 2
                            K.tt("dve", W[:, j0:j0 + 2, :], eL[:, j0:j0 + 2, :], bcmid(GTm[:, d, grp, :], 2), ALU.mult)
                    for h in range(4):
                        hs = slice(h * 64, (h + 1) * 64)
                        for d in range(2):
                            K.mm(pyo[:, 256 + h * 64:256 + (h + 1) * 64], W[:, d * 4 + h, :], xdt[:, d, hs], start=(d == 0), stop=(d == 1))
                    for j in range(8):
                        d, h = j // 4, j % 4
                        K.mm(pst[:, j * 64:(j + 1) * 64], bm_tok[:, h // 2, :], xdte[:, d, h * 64:(h + 1) * 64])
                    for h in range(4):
                        hs = slice(h * 64, (h + 1) * 64)
                        K.mm(pyo[:, hs], bcT[:, 2 + h // 2, tsl], stfb[:, hs])
                    K.tt("dve", v4(t1[:]), v4(pyo[:, 0:256]), bc3(E[:, 0:4], 64), ALU.mult)
                    K.tt("dve", t1[:], pyo[:, 256:512], t1[:], ALU.add)
                    K.tt("pool", v4(t2[:]), v4(xs_tok[:]), bc3(dsum[:], 64), ALU.mult)
                    K.tt("pool", ypt[pb][:], t1[:], t2[:], ALU.add)
                    K.dma(ypS[ci * 128:(ci + 1) * 128, :], ypt[pb][:], "ypt%d" % pb, eng="pool")
                    K.tt("dve", v4(stf[:]), v4(stf[:]), bc3(E[:, 32:36], 64), ALU.mult)
                    K.tt("dve", stf[:], pst[:, 0:256], stf[:], ALU.add)
                    K.copy("pool", stfb[:], stf[:])
                    K.copy("act", Sb_all[:, ci, :], pst[:, 256:512])


            for p_ in prologue_parts(*GROUPS[0]):
                p_()
            for i_, cur in enumerate(GROUPS):
                nt_c = cur[2] // 128
                parts = prologue_parts(*GROUPS[i_ + 1]) if i_ + 1 < len(GROUPS) else []
                per = (len(parts) + nt_c - 1) // nt_c if parts else 0
                for t in range(nt_c):
                    chunk(*cur, t)
                    for p_ in parts[t * per:(t + 1) * per]:
                        p_()

            order = [1, 0] + list(range(NCH - 1, 1, -1))
            if m_next is not None:
                m_init, m_load, m_compute = m_steps(m_next, sb, pst)
                m_init()
                m_load(0)
            for n_, ci in enumerate(order):
                csl = slice(ci * 128, (ci + 1) * 128)
                sz_ = sz[n_ % 3]
                yp_ = ypl[n_ % 3]
                if n_ == 0:
                    for m_ in range(2):
                        K.dma(sz[m_ % 3][:], zdts[order[m_] * 128:(order[m_] + 1) * 128, 0:256], "sz%d" % (m_ % 3))
                        K.dma(ypl[m_ % 3][:], ypS[order[m_] * 128:(order[m_] + 1) * 128, :], "ypl%d" % (m_ % 3))
                if n_ + 2 < len(order):
                    K.dma(sz[(n_ + 2) % 3][:], zdts[order[n_ + 2] * 128:(order[n_ + 2] + 1) * 128, 0:256], "sz%d" % ((n_ + 2) % 3))
                    K.dma(ypl[(n_ + 2) % 3][:], ypS[order[n_ + 2] * 128:(order[n_ + 2] + 1) * 128, :], "ypl%d" % ((n_ + 2) % 3))
                for h in range(4):
                    hs = slice(h * 64, (h + 1) * 64)
                    K.mm(pyo[:, hs], cmT_all[:, h // 2, csl], stbb[:, hs])
                if m_next is not None and n_ < 24:
                    if n_ + 1 < 24:
                        m_load(n_ + 1)
                    m_compute(n_)
                K.tt("dve", v4(stb[:]), v4(stb[:]), bc3(decb_all[:, ci, :], 64), ALU.mult)
                K.tt("dve", stb[:], stb[:], Sb_all[:, ci, :], ALU.add)
                K.copy("pool", stbb[:], stb[:])
                t1 = t1_2[n_ % 2]; yb = yb2[n_ % 2]
                K.tt("dve", v4(t1[:]), v4(pyo[:, 0:256]), bc3(dfsb_all[:, ci, :], 64), ALU.mult)
                K.tt("dve", t1[:], t1[:], yp_[:], ALU.add)
                K.tt("dve", t1[:], t1[:], sz_[:], ALU.mult)
                K.act(junk[:], t1[:], AF.Square, accum=st3[:, 0:1])
                K.rstd(st3[:, 2:3], st3[:, 0:1], 256, st3[:, 1:2])
                K.stt(yb[:], t1[:], st3[:, 2:3], sgb[:], ALU.mult, ALU.mult)
                if ci < 2:
                    buf, slot, last = ysT[0], ci, (ci == 0)
                else:
                    gidx = (ci - 2) // 4
                    buf, slot, last = ysT[(gidx + 1) % 2], (ci - 2) % 4, ((ci - 2) % 4 == 0)
                for c in range(2):
                    K.tr(pbm[:, c * 128:(c + 1) * 128], yb[:, c * 128:(c + 1) * 128], identb[:])
                K.copy("act", buf[:, :, slot * 128:(slot + 1) * 128], pbm[:, 0:256].rearrange("p (c t) -> p c t", c=2))
                if last:
                    if ci < 2:
                        K.dma(yTs[768:1024, 0:256].rearrange("(c p) n -> p c n", p=128), buf[:, :, 0:256], "ysT0")
                    else:
                        K.dma(yTs[768:1024, 256 + gidx * 512:256 + (gidx + 1) * 512].rearrange("(c p) n -> p c n", p=128),
                              buf[:, :, :], "ysT%d" % ((gidx + 1) % 2))
            S.barrier()

    def phase_o(l):
        with contextlib.ExitStack() as es:
            sb = lambda n, sh, dt: es.enter_context(nc.sbuf_tensor(K.name(n), sh, dt))
            ps = lambda n, sh, dt: es.enter_context(nc.psum_tensor(K.name(n), sh, dt))
            wout = sb("wout", [128, 8, D], BF16)
            wff2 = sb("wff2", [128, 32, D], BF16)
            w1r = [sb("w1r%d" % i, [128, 8, 128], BF16) for i in range(6)]
            md = sb("md", [128, 4, D], F32)
            yT = sb("yTo", [128, 8, 512], BF16)
            xt = [sb("xt%d" % i, [128, D], F32) for i in range(5)]
            g2c = sb("g2c", [128, D], F32)
            tmpf = sb("tmpo", [128, D], F32)
            junk = sb("junko", [128, D], BF16)
            hb = [sb("hbo%d" % i, [128, D], BF16) for i in range(2)]
            h2T = [sb("h2T%d" % i, [128, 8, 512], BF16) for i in range(2)]
            rl = [sb("rl%d" % i, [128, 512], BF16) for i in range(2)]
            aT = sb("aT", [128, 32, 512], BF16)
            st = sb("sto", [128, 3, 16], F32)
            pT = ps("pTo", [128, D], BF16)
            pa = [ps("pa%d" % i, [128, 512], F32) for i in range(7)]
            bank = [0]

            def nb():
                bank[0] += 1
                return pa[bank[0] % 7]

            K.dma(wout[:], w_out[l].rearrange("(k p) n -> p k n", p=128), "wout", eng="pool")
            for jj in range(4):
                K.dma(wff2[:, jj * 8:(jj + 1) * 8, :], w_ff2[l, jj * 1024:(jj + 1) * 1024, :].rearrange("(k p) n -> p k n", p=128),
                      "wff2_%d" % jj, eng="pool")
            xi = [0]
            w1i = [0]
            sti = [0]

            def newx():
                x_ = xt[xi[0] % 5]; xkey = "xt%d" % (xi[0] % 5); xi[0] += 1
                return x_, xkey

            def post(pp, gcol, x_, gt_=None):
                si = sti[0] % 16; sti[0] += 1
                K.act(junk[:, 0:512], pp[0][:, :], AF.Square, accum=st[:, 0, si:si + 1])
                K.act(junk[:, 512:1024], pp[1][:, :], AF.Square, accum=st[:, 1, si:si + 1])
                K.tt("dve", st[:, 0, si:si + 1], st[:, 0, si:si + 1], st[:, 1, si:si + 1], ALU.add)
                K.rstd(st[:, 2, si:si + 1], st[:, 0, si:si + 1], D, st[:, 1, si:si + 1])
                for half in range(2):
                    hs = slice(half * 512, (half + 1) * 512)
                    K.stt(tmpf[:, hs], pp[half][:, :], st[:, 2, si:si + 1], (md[:, gcol, hs] if gt_ is None else gt_[:, hs]), ALU.mult, ALU.mult)
                K.tt("pool", x_[:], x_[:], tmpf[:], ALU.add)

            pend = {}

            def prep_load(g, s0, G):
                nt = G // 128
                v = 1 if g == 0 else 0
                if g in (0, 1):
                    for i, m in enumerate((2, 4, 3, 5)):
                        K.dma(md[:, i, :], brow(mods[l, v, m:m + 1, :], D), "md%d" % i)
                    if g == 0:
                        K.dma(g2c[:], brow(mods[l, 1, 5:6, :], D), "g2c")
                K.dma(yT[:, :, 0:G], yTs[:, s0:s0 + G].rearrange("(k p) n -> p k n", p=128), "yTo")
                xs_ = []
                for t in range(nt):
                    x_, xkey = newx()
                    xs_.append((x_, xkey))
                    K.dma(x_[:], xsrc(l, s0 + t * 128, 128), xkey)
                pend[g] = xs_

            def prep_op(g, s0, G, t):
                tsl = slice(t * 128, (t + 1) * 128)
                x_, xkey = pend[g][t]
                pp = [nb(), nb()]
                for half in range(2):
                    for k in range(8):
                        K.mm(pp[half][:, :], yT[:, k, tsl], wout[:, k, half * 512:(half + 1) * 512], start=(k == 0), stop=(k == 7))
                post(pp, 0, x_)
                K.dma(xscr[s0 + t * 128:s0 + (t + 1) * 128, :], x_[:], xkey, eng="pool")
                si2 = sti[0] % 16; sti[0] += 1
                K.act(junk[:], x_[:], AF.Square, accum=st[:, 0, si2:si2 + 1])
                K.rstd(st[:, 2, si2:si2 + 1], st[:, 0, si2:si2 + 1], D, st[:, 1, si2:si2 + 1])
                K.stt(tmpf[:], x_[:], st[:, 2, si2:si2 + 1], md[:, 1, :], ALU.mult, ALU.mult)
                K.tt("dve", hb[t % 2][:], tmpf[:], md[:, 2, :], ALU.add)

            def prep_tr(g, s0, G, t):
                tsl = slice(t * 128, (t + 1) * 128)
                hb_ = hb[t % 2]
                for k in range(8):
                    K.tr(pT[:, k * 128:(k + 1) * 128], hb_[:, k * 128:(k + 1) * 128], identb[:])
                K.copy("act", h2T[g % 2][:, :, tsl], pT[:].rearrange("p (k t) -> p k t", k=8))

            def ff1(g, s0, G, j0, j1):
                for j in range(j0, j1):
                    w_ = w1r[w1i[0] % 6]
                    K.dma(w_[:], wf1s[l, j].rearrange("p (k c) -> p k c", k=8), "w1r%d" % (w1i[0] % 6))
                    w1i[0] += 1
                    p_ = nb()
                    for k in range(8):
                        K.mm(p_[:, 0:G], w_[:, k, :], h2T[g % 2][:, k, 0:G], start=(k == 0), stop=(k == 7))
                    r_ = rl[j % 2]
                    K.act(r_[:, 0:G], p_[:, 0:G], AF.Relu)
                    K.tt("dve", aT[:, j, 0:G], r_[:, 0:G], p_[:, 0:G], ALU.mult)

            def ff2(g, s0, G, t):
                if True:
                    tsl = slice(t * 128, (t + 1) * 128)
                    x_, xkey = newx()
                    K.dma(x_[:], xscr[s0 + t * 128:s0 + (t + 1) * 128, :], xkey)
                    pp = [nb(), nb()]
                    for half in range(2):
                        for j in range(32):
                            K.mm(pp[half][:, :], aT[:, j, tsl], wff2[:, j, half * 512:(half + 1) * 512], start=(j == 0), stop=(j == 31))
                    post(pp, 3, x_, g2c if g == 0 else None)
                    K.dma(xdst(l, s0 + t * 128, 128), x_[:], xkey, eng="pool")

            g0 = GROUPS[0]
            prep_load(*g0)
            for t in range(g0[2] // 128):
                prep_op(*g0, t)
                prep_tr(*g0, t)
            for i_, cur in enumerate(GROUPS):
                nxt = GROUPS[i_ + 1] if i_ + 1 < len(GROUPS) else None
                nt_c = cur[2] // 128
                if nxt is not None:
                    prep_load(*nxt)
                for q_ in range(4):
                    ff1(*cur, q_ * 8, (q_ + 1) * 8)
                    if nxt is not None:
                        prep_op(*nxt, q_)
                        if q_ >= 1:
                            prep_tr(*nxt, q_ - 1)
                ff2(*cur, 0)
                if nxt is not None:
                    prep_tr(*nxt, 3)
                for t in range(1, nt_c):
                    ff2(*cur, t)
            S.barrier()

    for l in range(n_layers):
        if l == 0:
            phase_m(l)
        with contextlib.ExitStack() as esr:
            kvnT = esr.enter_context(nc.sbuf_tensor(K.name("kvnT"), [128, NS], BF16))
            krT2 = esr.enter_context(nc.sbuf_tensor(K.name("krT2"), [128, NS], BF16))
            phase_1(l, kvnT, krT2)
            phase_a(l, kvnT, krT2)
        phase_s(l, l + 1 if l + 1 < n_layers else None)
        phase_o(l)
    S.barrier()
    with contextlib.ExitStack() as es2:
        n = S.emit(es2)
    es0.close()
    return nc, n


def _rope_tables():
    rows_n = SEQ // 64
    row = np.repeat(np.arange(rows_n, dtype=np.float32), 64)
    col = np.tile(np.arange(64, dtype=np.float32), rows_n)
    inv = (np.float32(10000.0) ** (-np.arange(0, 32, 2, dtype=np.float32) / np.float32(32))).astype(np.float32)
    ar = (row[:, None] * inv).astype(np.float32)
    ac = (col[:, None] * inv).astype(np.float32)
    cos = np.zeros((64, SEQ), np.float32); sin = np.zeros((64, SEQ), np.float32)
    for d in range(64):
        ang = ar if d < 32 else ac
        f = d % 16
        cos[d] = np.cos(ang[:, f])
        sgn = -1.0 if (d % 32) < 16 else 1.0
        sin[d] = sgn * np.sin(ang[:, f])
    return cos, sin


_PERM = np.array([d + 16 if (d % 32) < 16 else d - 16 for d in range(64)])


def _host_layout(inp):
    f = lambda a: np.ascontiguousarray(a, dtype=np.float32)
    sh = {}
    sh["w_ada"] = f(inp["w_ada"]); sh["b_ada"] = f(inp["b_ada"])
    sh["g4"] = f(np.concatenate([inp["g_pre_mix"], inp["g_post_mix"], inp["g_pre_ff"], inp["g_post_ff"]], axis=1))
    w_in = inp["w_in"]
    sh["w_in"] = f(w_in)
    kr = w_in[:, :, 384:448]
    rot = kr[:, :, _PERM]
    sh["w_kr2"] = f(np.concatenate([kr, kr, rot, rot], axis=2))
    sh["gq_pc"] = f(inp["g_q"].reshape(NL, 2, 128).transpose(0, 2, 1))
    sh["gkv_p"] = f(inp["g_kv"].reshape(NL, 128, 1))
    wuq = inp["w_uq"].reshape(NL, 256, 4, 192)
    sh["w_uqx"] = f(np.concatenate([wuq, wuq[:, :, :, 128 + _PERM]], axis=3).reshape(NL, 256, 1024))
    wukv = inp["w_ukv"].reshape(NL, 128, 4, 256)
    sh["w_ukvr"] = f(np.concatenate([wukv[:, :, :, :128].reshape(NL, 128, 512), wukv[:, :, :, 128:].reshape(NL, 128, 512)], axis=2))
    sh["cm_g"] = f(inp["cm_norm_g"]); sh["cm_ws"] = f(inp["cm_w_s"])
    sh["cm_bt"] = f(inp["cm_b_s"].transpose(0, 2, 1))
    cw = inp["ssd_conv_w"].reshape(NL, 3, 6, 128).transpose(0, 3, 2, 1)
    sh["convw"] = f(cw.reshape(NL, 128, 18))
    sh["convb"] = f(inp["ssd_conv_b"].reshape(NL, 6, 128).transpose(0, 2, 1))
    sh["ssd_sm"] = f(np.concatenate([inp["ssd_dt_bias"].reshape(NL, 8), inp["ssd_a_log"].reshape(NL, 8), inp["ssd_d"].reshape(NL, 8)], axis=1))
    sh["ssd_g"] = f(inp["ssd_norm_g"])
    sh["w_out"] = f(inp["w_out"]); sh["w_ff1"] = f(inp["w_ff1"]); sh["w_ff2"] = f(inp["w_ff2"])
    cos, sin = _rope_tables()
    sh["cosd"] = f(np.concatenate([cos, cos], axis=0)); sh["sind"] = f(np.concatenate([sin, sin], axis=0))
    sh["csq"] = f(np.concatenate([cos, sin], axis=0))
    sh["csc"] = f(np.concatenate([np.ones((64, 512), np.float32), np.zeros((64, 512), np.float32)], axis=0))
    k = np.arange(128)[:, None]; l_ = np.arange(128)[None, :]
    tri = np.stack([(k <= l_), (k < l_), (k >= l_), (k > l_)], axis=1).astype(np.float32)
    sh["tri"] = f(tri.reshape(128, 512))
    sh["cc_pk"] = f(inp["c_ctx"].reshape(8, 128).T)
    return sh


_CACHE = {}


def kernel(**inputs):
    inp = {k: np.asarray(v) for k, v in inputs.items()}
    if "nc" not in _CACHE:
        _CACHE["nc"] = build()[0]
    nc = _CACHE["nc"]
    shared = _host_layout(inp)
    in_maps = []
    for b in range(8):
        m = dict(shared)
        m["x"] = np.ascontiguousarray(inp["x"][b], dtype=np.float32)
        m["ctx"] = np.ascontiguousarray(inp["ctx"][b], dtype=np.float32)
        m["c_pk"] = np.ascontiguousarray(inp["c"][b].reshape(8, 128).T, dtype=np.float32)
        in_maps.append(m)
    res = run_bass_kernel_spmd(nc, in_maps, core_ids=list(range(8)))
    return np.stack([np.asarray(r["out"], dtype=np.float32) for r in res.results], axis=0)
```

```python
import contextlib
import numpy as np
import concourse.bass as bass
import concourse.mybir as mybir
from concourse.bass_utils import run_bass_kernel_spmd

F32 = mybir.dt.float32
BF16 = mybir.dt.bfloat16
AF = mybir.ActivationFunctionType
ALU = mybir.AluOpType

EPS = 1e-6
NL = 4
D = 1024
SEQ = 4096
NCTX = 256
NS = SEQ + NCTX
NCH = NS // 128
SCALE = 192.0 ** -0.5
XB_COLS = 4356


def _esize(dt):
    return 2 if dt == BF16 else 4
class _Op:
    __slots__ = ("eng", "fn", "dma", "semkey", "waits", "sig", "cnt", "idx", "dmaval", "gid")


def _rect(ap):
    t = ap.ap
    es = _esize(ap.dtype)
    off = int(ap.offset)
    if str(ap.space) == "DRAM":
        ext = 0
        for s, c in t:
            ext += (c - 1) * abs(s)
        return (ap.name, 0, 1, off * es, (off + ext + 1) * es)
    pstep, pcnt = t[0]
    if pstep == 0:
        pstep = 1 << 40
    p0 = off // pstep
    f0 = off % pstep
    ext = 0
    for s, c in t[1:]:
        ext += (c - 1) * abs(s)
    if str(ap.space) == "PSUM":
        return (ap.name, 0, 128, (f0 * es) // 2048 * 2048, ((f0 + ext + 1) * es + 2047) // 2048 * 2048)
    return (ap.name, p0, p0 + pcnt, f0 * es, (f0 + ext + 1) * es)


class Sched:
    ENG = ("pe", "act", "dve", "pool", "sp")

    def __init__(self, nc):
        self.nc = nc
        self.ops = []
        self.acc = {}
        self.eng_ops = {e: [] for e in self.ENG}
        self.waited = {e: {x: -1 for x in self.ENG} for e in self.ENG}
        self.dma_last = {}
        self.dma_keys = {}
        self.dma_waited = {e: set() for e in self.ENG}
        self.unwaited = set()

    def add(self, eng, fn, reads=(), writes=(), dma=None):
        op = _Op()
        op.eng = eng
        op.fn = fn
        op.dma = dma is not None
        op.semkey = dma
        op.sig = False
        op.gid = len(self.ops)
        deps = set()
        rrects = [_rect(a) for a in reads]
        wrects = [_rect(a) for a in writes]
        for r in rrects:
            for rec in self.acc.get(r[0], ()):
                q = rec[0]
                if rec[2] and q[1] < r[2] and r[1] < q[2] and q[3] < r[4] and r[3] < q[4]:
                    deps.add(rec[1])
        for r in wrects:
            for rec in self.acc.get(r[0], ()):
                q = rec[0]
                if q[1] < r[2] and r[1] < q[2] and q[3] < r[4] and r[3] < q[4]:
                    deps.add(rec[1])
        if op.dma:
            prev = self.dma_last.get(dma)
            if prev is not None:
                deps.add(prev)
            self.dma_last[dma] = op.gid
            cnt = self.dma_keys.get(dma, 0) + 1
            self.dma_keys[dma] = cnt
            op.dmaval = 16 * cnt
            self.unwaited.add(op.gid)
        deps.discard(op.gid)
        waits = []
        best = {}
        for d in deps:
            o = self.ops[d]
            if o.dma:
                if d not in self.dma_waited[eng]:
                    self.dma_waited[eng].add(d)
                    self.unwaited.discard(d)
                    waits.append(("dma", d))
            else:
                if o.eng == "pe" and eng == "pe" and not op.dma:
                    continue
                if o.idx > best.get(o.eng, -1):
                    best[o.eng] = o.idx
        for x, i in best.items():
            if self.waited[eng][x] < i:
                self.waited[eng][x] = i
                waits.append(("eng", x, i))
                self.eng_ops[x][i].sig = True
        op.waits = waits
        if not op.dma:
            op.idx = len(self.eng_ops[eng])
            self.eng_ops[eng].append(op)
        else:
            op.idx = -1
        self.ops.append(op)
        for r in wrects:
            lst = self.acc.setdefault(r[0], [])
            lst[:] = [rec for rec in lst if not (r[1] <= rec[0][1] and rec[0][2] <= r[2]
                                                  and r[3] <= rec[0][3] and rec[0][4] <= r[4])]
            lst.append((r, op.gid, True, eng if not op.dma else None))
        for r in rrects:
            lst = self.acc.setdefault(r[0], [])
            if not op.dma:
                lst[:] = [rec for rec in lst if not (not rec[2] and rec[3] == eng and rec[0] == r)]
            lst.append((r, op.gid, False, eng if not op.dma else None))
        return op

    def fence(self, eng, reads=(), writes=()):
        return self.add(eng, None, reads, writes)

    def barrier(self):
        pend = sorted(self.unwaited)
        self.unwaited = set()
        last = {e: (self.eng_ops[e][-1].gid if self.eng_ops[e] else None) for e in self.ENG}
        for e in self.ENG:
            o = _Op()
            o.eng = e
            o.fn = None
            o.dma = False
            o.semkey = None
            o.sig = False
            o.gid = len(self.ops)
            waits = []
            for x in self.ENG:
                if x == e or last[x] is None:
                    continue
                i = self.ops[last[x]].idx
                if self.waited[e][x] < i:
                    self.waited[e][x] = i
                    waits.append(("eng", x, i))
                    self.eng_ops[x][i].sig = True
            for d in pend:
                if d not in self.dma_waited[e]:
                    self.dma_waited[e].add(d)
                    waits.append(("dma", d))
            o.waits = waits
            o.idx = len(self.eng_ops[e])
            self.eng_ops[e].append(o)
            self.ops.append(o)
        self.acc = {}

    def emit(self, sems_ctx):
        nc = self.nc
        engobj = {"pe": nc.tensor, "act": nc.scalar, "dve": nc.vector, "pool": nc.gpsimd, "sp": nc.sync}
        esem = {e: sems_ctx.enter_context(nc.semaphore("s_" + e)) for e in self.ENG}
        dsem = {k: sems_ctx.enter_context(nc.semaphore("d_%d" % i)) for i, k in enumerate(self.dma_keys)}
        for e in self.ENG:
            c = 0
            for o in self.eng_ops[e]:
                if o.sig:
                    c += 1
                    o.cnt = c
        n_inst = 0
        for o in self.ops:
            eo = engobj[o.eng]
            for w in o.waits:
                if w[0] == "dma":
                    d = self.ops[w[1]]
                    eo.wait_ge(dsem[d.semkey], d.dmaval)
                else:
                    eo.wait_ge(esem[w[1]], self.eng_ops[w[1]][w[2]].cnt)
            if o.fn is None:
                if o.sig:
                    eo.nop().then_inc(esem[o.eng], 1)
                continue
            inst = o.fn(eo)
            n_inst += 1
            if o.dma:
                inst.then_inc(dsem[o.semkey], 16)
            elif o.sig:
                inst.then_inc(esem[o.eng], 1)
        return n_inst


class KB:
    def __init__(self, nc):
        self.nc = nc
        self.S = Sched(nc)
        self.uid = 0

    def name(self, n):
        self.uid += 1
        return "%s_%d" % (n, self.uid)

    def dma(self, out, in_, key, eng="sp", slow=False):
        if slow:
            self.S.add(eng, lambda e: e.dma_start(out=out, in_=in_, allow_slow_non_contiguous=True), reads=[in_], writes=[out], dma=key)
        else:
            self.S.add(eng, lambda e: e.dma_start(out=out, in_=in_), reads=[in_], writes=[out], dma=key)

    def mm(self, out, lhsT, rhs, start=True, stop=True):
        self.S.add("pe", lambda e: e.matmul(out, lhsT=lhsT, rhs=rhs, start=start, stop=stop),
                   reads=[lhsT, rhs], writes=[out])

    def tr(self, out, in_, ident):
        self.S.add("pe", lambda e: e.transpose(out=out, in_=in_, identity=ident), reads=[in_, ident], writes=[out])

    def act(self, out, in_, func, bias=None, scale=None, accum=None):
        kw = {}
        rd = [in_]
        wr = [out]
        if bias is not None:
            kw["bias"] = bias
            if not isinstance(bias, float):
                rd.append(bias)
        if scale is not None:
            kw["scale"] = scale
            if not isinstance(scale, float):
                rd.append(scale)
        if accum is not None:
            kw["accum_out"] = accum
            wr.append(accum)
        self.S.add("act", lambda e: e.activation(out=out, in_=in_, func=func, **kw), reads=rd, writes=wr)

    def copy(self, eng, out, in_):
        if eng == "act":
            self.S.add("act", lambda e: e.copy(out=out, in_=in_), reads=[in_], writes=[out])
        else:
            self.S.add(eng, lambda e: e.tensor_copy(out=out, in_=in_), reads=[in_], writes=[out])

    def tt(self, eng, out, in0, in1, op):
        self.S.add(eng, lambda e: e.tensor_tensor(out=out, in0=in0, in1=in1, op=op), reads=[in0, in1], writes=[out])

    def ts(self, eng, out, in0, s1, s2, op0, op1=None):
        rd = [in0]
        if not isinstance(s1, float):
            rd.append(s1)
        if s2 is not None and not isinstance(s2, float):
            rd.append(s2)
        if op1 is None:
            self.S.add(eng, lambda e: e.tensor_scalar(out=out, in0=in0, scalar1=s1, scalar2=None, op0=op0), reads=rd, writes=[out])
        else:
            self.S.add(eng, lambda e: e.tensor_scalar(out=out, in0=in0, scalar1=s1, scalar2=s2, op0=op0, op1=op1), reads=rd, writes=[out])

    def stt(self, out, in0, scalar, in1, op0, op1):
        rd = [in0, in1]
        if not isinstance(scalar, float):
            rd.append(scalar)
        self.S.add("dve", lambda e: e.scalar_tensor_tensor(out=out, in0=in0, scalar=scalar, in1=in1, op0=op0, op1=op1),
                   reads=rd, writes=[out])

    def memset(self, eng, ap, val):
        self.S.add(eng, lambda e: e.memset(ap, val), writes=[ap])

    def recip(self, out, in_):
        self.S.add("dve", lambda e: e.reciprocal(out=out, in_=in_), reads=[in_], writes=[out])

    def bn_stats(self, out, in_):
        self.S.add("dve", lambda e: e.bn_stats(out=out, in_=in_), reads=[in_], writes=[out])

    def bn_aggr(self, out, in_):
        self.S.add("dve", lambda e: e.bn_aggr(out=out, in_=in_), reads=[in_], writes=[out])

    def rstd(self, out, in_, n, tmp):
        self.act(tmp, in_, AF.Ln, bias=EPS, scale=1.0 / n)
        self.act(out, tmp, AF.Exp, scale=-0.5)


def bc3(ap2, n):
    return ap2.unsqueeze(2).to_broadcast([ap2.shape[0], ap2.shape[1], n])


def bcmid(ap2, n):
    return ap2.unsqueeze(1).to_broadcast([ap2.shape[0], n, ap2.shape[1]])


def build(n_layers=NL, dbg=False):
    nc = bass.Bass("TRN2", target_bir_lowering=False)
    K = KB(nc)
    S = K.S

    def din(name, shape, dt=F32):
        return nc.dram_tensor(name, shape, dt, kind="ExternalInput").ap()

    def dscr(name, shape, dt=F32):
        return nc.dram_tensor(name, shape, dt, kind=("ExternalOutput" if dbg else "Internal")).ap()

    x_in = din("x", [SEQ, D]); ctx_in = din("ctx", [NCTX, D])
    c_pk = din("c_pk", [128, 8]); cc_pk = din("cc_pk", [128, 8])
    w_ada = din("w_ada", [NL, D, 6 * D]); b_ada = din("b_ada", [NL, 6 * D])
    g4 = din("g4", [NL, 4 * D])
    w_in = din("w_in", [NL, D, 1992]); w_kr2 = din("w_kr2", [NL, D, 256])
    gq_pc = din("gq_pc", [NL, 128, 2]); gkv_p = din("gkv_p", [NL, 128, 1])
    w_uqx = din("w_uqx", [NL, 256, 1024]); w_ukvr = din("w_ukvr", [NL, 128, 1024])
    cm_g = din("cm_g", [NL, 256]); cm_ws = din("cm_ws", [NL, 4, 128, 128]); cm_bt = din("cm_bt", [NL, 128, 4])
    convw = din("convw", [NL, 128, 18]); convb = din("convb", [NL, 128, 6])
    ssd_sm = din("ssd_sm", [NL, 24]); ssd_g = din("ssd_g", [NL, 256])
    w_out = din("w_out", [NL, D, D]); w_ff1 = din("w_ff1", [NL, D, 4 * D]); w_ff2 = din("w_ff2", [NL, 4 * D, D])
    cosd = din("cosd", [128, SEQ]); sind = din("sind", [128, SEQ]); csq = din("csq", [128, SEQ]); csc = din("csc", [128, 512])
    tri = din("tri", [128, 4 * 128])
    out = nc.dram_tensor("out", [SEQ, D], F32, kind="ExternalOutput").ap()

    xscr = dscr("xscr", [NS, D])
    mods = dscr("mods", [NL, 2, 6, D])
    xbcs = dscr("xbcs", [768, XB_COLS], BF16)
    zdts = dscr("zdts", [NS, 264])
    yTs = dscr("yTs", [D, NS], BF16)
    qnTs = dscr("qnTs", [256, NS], BF16)
    wf1s = dscr("wf1s", [NL, 32, 128, 1024], BF16)
    ypS = dscr("ypS", [NS, 256])
    wf2s = dscr("wf2s", [NL, 4 * D, D], BF16)
    wos = dscr("wos", [NL, D, D], BF16)
    wins = dscr("wins", [NL, D, 1992], BF16)
    wkrs = dscr("wkrs", [NL, D, 256], BF16)

    def precast_in(l):
        for i in range(4):
            K.dma(wins[l, i * 256:(i + 1) * 256, :], w_in[l, i * 256:(i + 1) * 256, :], "cst%d" % (i % 4), eng="pool")
        K.dma(wkrs[l], w_kr2[l], "cst0", eng="pool")

    def precast_o(l):
        K.dma(wos[l], w_out[l], "cst1", eng="pool")
        for i in range(8):
            K.dma(wf2s[l, i * 512:(i + 1) * 512, :], w_ff2[l, i * 512:(i + 1) * 512, :], "cst%d" % (i % 4), eng="pool")

    GROUPS = [(0, 0, 256)] + [(g, 256 + (g - 1) * 512, 512) for g in range(1, 9)]

    def xsrc(l, s, n):
        if l == 0:
            return ctx_in[s:s + n, :] if s < NCTX else x_in[s - NCTX:s - NCTX + n, :]
        return xscr[s:s + n, :]

    def xdst(l, s, n):
        if l == n_layers - 1 and s >= NCTX:
            return out[s - NCTX:s - NCTX + n, :]
        return xscr[s:s + n, :]

    def xbcol(s):
        return 1 + s if s < NCTX else 259 + (s - NCTX)

    def brow(ap_row, n):
        return ap_row.broadcast_to([128, n])

    es0 = contextlib.ExitStack()
    sb0 = lambda n, sh, dt: es0.enter_context(nc.sbuf_tensor(n, sh, dt))
    identf = sb0("identf", [128, 128], F32)
    identb = sb0("identb", [128, 128], BF16)
    onesb = sb0("onesb", [128, 128], BF16)
    onesf = sb0("onesf", [128, 128], F32)
    trit = sb0("trit", [128, 4, 128], F32)
    zrow = sb0("zrow", [128, 8], BF16)
    K.dma(trit[:].rearrange("p a b -> p (a b)"), tri[:, :], "trit")
    K.memset("pool", onesf[:], 1.0)
    K.memset("pool", onesb[:], 1.0)
    K.memset("pool", zrow[:], 0.0)
    K.tt("dve", identf[:], trit[:, 0, :], trit[:, 2, :], ALU.mult)
    K.copy("dve", identb[:], identf[:])
    for c in range(6):
        for col in (0, 257, 258, 4355):
            K.dma(xbcs[c * 128:(c + 1) * 128, col:col + 1], zrow[:, 0:1], "zpad", slow=True)
    precast_in(0)

    def m_steps(l, sb, pbank):
        sca = sb("sca", [128, 2, 8], F32)
        sc2 = sb("sc2", [128, 8, 2], F32)
        wa = [sb("wa%d" % i, [128, 8, 256], F32) for i in range(2)]
        bd = [sb("bd%d" % i, [2, 256], F32) for i in range(3)]
        gg = [sb("gg%d" % i, [2, 256], F32) for i in range(3)]
        mt = [sb("mt%d" % i, [2, 256], F32) for i in range(2)]
        res = [sb("res%d" % i, [2, 256], F32) for i in range(3)]
        wv = w_ada[l].rearrange("(k p) n -> p k n", p=128)

        def init():
            K.dma(sca[:, 0, :], c_pk[:, :], "sc0")
            K.dma(sca[:, 1, :], cc_pk[:, :], "sc1")
            K.act(sca[:], sca[:], AF.Silu)
            for v in range(2):
                K.copy("dve", sc2[:, :, v], sca[:, v, :])

        def gcol(j):
            m, q = j // 4, j % 4
            if m in (1, 4):
                return (0 if m == 1 else 2) * D + q * 256
            if m in (2, 5):
                return (1 if m == 2 else 3) * D + q * 256
            return None

        def load(j):
            K.dma(wa[j % 2][:], wv[:, :, j * 256:(j + 1) * 256], "wa%d" % (j % 2))
            K.dma(bd[j % 3][:], b_ada[l:l + 1, j * 256:(j + 1) * 256].broadcast_to([2, 256]), "bd%d" % (j % 3))
            if gcol(j) is not None:
                K.dma(gg[j % 3][:], g4[l:l + 1, gcol(j):gcol(j) + 256].broadcast_to([2, 256]), "gg%d" % (j % 3))

        def compute(j):
            m, q = j // 4, j % 4
            w_ = wa[j % 2]
            for k in range(8):
                K.mm(pbank[0:2, 0:256], sc2[:, k, :], w_[:, k, :], start=(k == 0), stop=(k == 7))
            t_ = mt[j % 2]
            r_ = res[j % 3]
            K.tt("dve", t_[:], pbank[0:2, 0:256], bd[j % 3][:], ALU.add)
            if m in (0, 3):
                K.copy("dve", r_[:], t_[:])
            elif m in (1, 4):
                K.stt(r_[:], t_[:], 1.0, gg[j % 3][:], ALU.add, ALU.mult)
            else:
                K.tt("dve", r_[:], t_[:], gg[j % 3][:], ALU.mult)
            K.dma(mods[l, :, m, q * 256:(q + 1) * 256], r_[:], "res%d" % (j % 3))

        return init, load, compute

    def phase_m(l):
        with contextlib.ExitStack() as es:
            sb = lambda n, sh, dt: es.enter_context(nc.sbuf_tensor(K.name(n), sh, dt))
            pm = es.enter_context(nc.psum_tensor(K.name("pm"), [128, 512], F32))
            init, load, compute = m_steps(l, sb, pm)
            init()
            load(0)
            for j in range(24):
                if j + 1 < 24:
                    load(j + 1)
                compute(j)
            S.barrier()

    def phase_1(l, kvnT, krT2):
        with contextlib.ExitStack() as es:
            sb = lambda n, sh, dt: es.enter_context(nc.sbuf_tensor(K.name(n), sh, dt))
            ps = lambda n, sh, dt: es.enter_context(nc.psum_tensor(K.name(n), sh, dt))
            win = sb("win", [128, 8, 1992], BF16)
            wkr2 = sb("wkr2", [128, 8, 256], BF16)
            gq = sb("gq", [128, 2], F32); gkv = sb("gkv", [128, 1], F32)
            cmg = sb("cmg", [128, 256], F32); cmb = sb("cmb", [128, 4], F32)
            wsf = sb("wsf", [128, 4, 128], F32); wsT = sb("wsT", [128, 4, 128], BF16)
            ab = sb("ab", [128, 2, D], F32)
            xg = [sb("xg%d" % i, [128, D], F32) for i in range(3)]
            hT = [sb("hT%d" % i, [128, 8, 512], BF16) for i in range(2)]
            hb = [sb("hb%d" % i, [128, D], BF16) for i in range(4)]
            tmpf = sb("tmpf", [128, D], F32)
            junk = sb("junk", [128, D], BF16)
            st4 = sb("st4", [128, 3, 4], F32)
            sq = sb("sq", [128, 3, 512], BF16)
            lnq = sb("lnq", [128, 512], F32)
            rst = [sb("rst%d" % i, [128, 512], F32) for i in range(2)]
            qn = [sb("qn%d" % i, [128, 2, 512], BF16) for i in range(2)]
            cosk = sb("cosk", [128, 512], F32); sink = sb("sink", [128, 512], F32)
            t1 = sb("t1", [128, 512], F32); t2 = sb("t2", [128, 512], F32)
            xbst = [sb("xbst%d" % i, [128, 6, 512], BF16) for i in range(2)]
            xcms = [sb("xcm%d" % i, [128, 4, 512], F32) for i in range(2)]
            zdt = [sb("zdt%d" % i, [128, 4, 264], F32) for i in range(2)]
            st6 = sb("st6", [128, 4, 6], F32); mv = sb("mv", [128, 4, 2], F32)
            vpe = sb("vpe", [128, 4], F32); rscm = sb("rscm", [128, 4], F32); cneg = sb("cneg", [128, 4], F32)
            vnf = [sb("vnf%d" % i, [128, 256], F32) for i in range(2)]
            vnb = [sb("vnb%d" % i, [128, 256], BF16) for i in range(2)]
            ycm = [sb("ycm%d" % i, [128, 256], BF16) for i in range(2)]
            ycmT = [sb("ycmT%d" % i, [128, 2, 512], BF16) for i in range(2)]
            pT = ps("pT", [128, D], BF16)
            pf = [ps("pf%d" % i, [128, 512], F32) for i in range(7)]
            bank = [0]

            def nb():
                bank[0] += 1
                return pf[3 + bank[0] % 4]

            K.dma(win[:], wins[l].rearrange("(k p) n -> p k n", p=128), "win")
            K.dma(wkr2[:], wkrs[l].rearrange("(k p) n -> p k n", p=128), "wkr2")
            K.dma(gq[:], gq_pc[l], "gq"); K.dma(gkv[:], gkv_p[l], "gkv")
            K.dma(cmg[:], brow(cm_g[l:l + 1, :], 256), "cmg"); K.dma(cmb[:], cm_bt[l], "cmb")
            K.dma(wsf[:], cm_ws[l].rearrange("g t s -> t g s"), "wsf")
            K.memset("pool", cneg[:], -0.5)
            for gi in range(4):
                p_ = nb()
                K.tr(p_[:, 0:128], wsf[:, gi, :], identf[:])
                K.copy("dve", wsT[:, gi, :], p_[:, 0:128])

            xi = [0]
            deferred = []

            def dstore(out_, in__, key):
                deferred.append((out_, in__, key))

            def flush():
                for (o_, i_, k_) in deferred:
                    K.dma(o_, i_, k_)
                del deferred[:]

            def prep(g, s0, G):
                nt = G // 128
                v = 1 if g == 0 else 0
                if g in (0, 1):
                    K.dma(ab[:, 0, :], brow(mods[l, v, 1:2, :], D), "ab0")
                    K.dma(ab[:, 1, :], brow(mods[l, v, 0:1, :], D), "ab1")
                hT_ = hT[g % 2]
                for t in range(nt):
                    x_ = xg[xi[0] % 3]; xi[0] += 1
                    K.dma(x_[:], xsrc(l, s0 + t * 128, 128), "xg%d" % ((xi[0] - 1) % 3))
                    K.act(junk[:], x_[:], AF.Square, accum=st4[:, 0, t:t + 1])
                    K.rstd(st4[:, 2, t:t + 1], st4[:, 0, t:t + 1], D, st4[:, 1, t:t + 1])
                    K.stt(tmpf[:], x_[:], st4[:, 2, t:t + 1], ab[:, 0, :], ALU.mult, ALU.mult)
                    K.tt("dve", hb[t][:], tmpf[:], ab[:, 1, :], ALU.add)

            def prep_b(g, s0, G):
                nt = G // 128
                hT_ = hT[g % 2]
                for t in range(nt):
                    for k in range(8):
                        K.tr(pT[:, k * 128:(k + 1) * 128], hb[t][:, k * 128:(k + 1) * 128], identb[:])
                    K.copy("act", hT_[:, :, t * 128:(t + 1) * 128], pT[:].rearrange("p (k t) -> p k t", k=8))

            def body(g, s0, G):
                nt = G // 128
                hT_ = hT[g % 2]

                def fm(col0, m, wt=win, p_=None):
                    if p_ is None:
                        p_ = nb()
                    for k in range(8):
                        K.mm(p_[0:m, 0:G], wt[:, k, col0:col0 + m], hT_[:, k, 0:G], start=(k == 0), stop=(k == 7))
                    return p_

                pq = [fm(0, 128, p_=pf[0]), fm(128, 128, p_=pf[1])]
                pkv = fm(256, 128, p_=pf[2])
                for c in range(2):
                    K.act(sq[:, c, 0:G], pq[c][:, 0:G], AF.Square)
                K.act(sq[:, 2, 0:G], pkv[:, 0:G], AF.Square)
                pkr = fm(0, 128, wkr2)
                if g == 0:
                    K.copy("dve", krT2[:, s0:s0 + G], pkr[:, 0:G])
                else:
                    pkrr = fm(128, 128, wkr2)
                    K.dma(cosk[:], cosd[:, s0 - NCTX:s0 - NCTX + G], "cosk")
                    K.dma(sink[:], sind[:, s0 - NCTX:s0 - NCTX + G], "sink")
                    K.tt("dve", t1[:, 0:G], pkr[:, 0:G], cosk[:, 0:G], ALU.mult)
                    K.tt("dve", t2[:, 0:G], pkrr[:, 0:G], sink[:, 0:G], ALU.mult)
                    K.tt("pool", krT2[:, s0:s0 + G], t1[:, 0:G], t2[:, 0:G], ALU.add)
                for c in range(6):
                    p_ = fm(1216 + c * 128, 128)
                    K.copy("act", xbst[g % 2][:, c, 0:G], p_[:, 0:G])
                dstore(xbcs.rearrange("(c p) n -> p c n", p=128)[:, :, xbcol(s0):xbcol(s0) + G], xbst[g % 2][:, :, 0:G], "xbst%d" % (g % 2))
                psq = nb()
                for c in range(2):
                    K.mm(psq[:, 0:G], onesb[:], sq[:, c, 0:G], start=(c == 0), stop=(c == 1))
                pskv = nb()
                K.mm(pskv[:, 0:G], onesb[:], sq[:, 2, 0:G])
                K.rstd(rst[0][:, 0:G], psq[:, 0:G], 256, lnq[:, 0:G])
                K.rstd(rst[1][:, 0:G], pskv[:, 0:G], 128, lnq[:, 0:G])
                qn_ = qn[g % 2]
                for c in range(2):
                    K.stt(qn_[:, c, 0:G], pq[c][:, 0:G], gq[:, c:c + 1], rst[0][:, 0:G], ALU.mult, ALU.mult)
                    dstore(qnTs[c * 128:(c + 1) * 128, s0:s0 + G], qn_[:, c, 0:G], "qn%d_%d" % (g % 2, c))
                K.stt(kvnT[:, s0:s0 + G], pkv[:, 0:G], gkv[:, 0:1], rst[1][:, 0:G], ALU.mult, ALU.mult)
                zdt_ = zdt[g % 2]
                xcm = xcms[g % 2]
                for t in range(nt):
                    tsl = slice(t * 128, (t + 1) * 128)
                    p_ = nb()
                    for k in range(8):
                        K.mm(p_[:, :], hT_[:, k, tsl], win[:, k, 448:960], start=(k == 0), stop=(k == 7))
                    K.copy("act", xcm[:, t, :], p_[:, :])
                    p2 = nb()
                    for k in range(8):
                        K.mm(p2[:, 0:256], hT_[:, k, tsl], win[:, k, 960:1216], start=(k == 0), stop=(k == 7))
                    for k in range(8):
                        K.mm(p2[:, 256:264], hT_[:, k, tsl], win[:, k, 1984:1992], start=(k == 0), stop=(k == 7))
                    K.copy("dve", zdt_[:, t, :], p2[:, 0:264])
                K.act(xcm[:, 0:nt, :], xcm[:, 0:nt, :], AF.Gelu_apprx_tanh)
                K.act(zdt_[:, 0:nt, 0:256], zdt_[:, 0:nt, 0:256], AF.Silu)
                dstore(zdts[s0:s0 + G, :].rearrange("(t p) n -> p t n", p=128), zdt_[:, 0:nt, :], "zdt%d" % (g % 2))

            def body_b(g, s0, G):
                nt = G // 128
                xcm = xcms[g % 2]
                for t in range(nt):
                    K.bn_stats(st6[:, t, :], xcm[:, t, 256:512])
                    K.bn_aggr(mv[:, t, :], st6[:, t, :])
                K.ts("dve", vpe[:, 0:nt], mv[:, 0:nt, 1], EPS, None, ALU.add)
                K.tt("pool", rscm[:, 0:nt], vpe[:, 0:nt], cneg[:, 0:nt], ALU.pow)
                ycmT_ = ycmT[g % 2]
                for t in range(nt):
                    vf = vnf[t % 2]; vb = vnb[t % 2]; yc = ycm[t % 2]
                    K.ts("dve", vf[:], xcm[:, t, 256:512], mv[:, t, 0:1], rscm[:, t:t + 1], ALU.subtract, ALU.mult)
                    K.tt("pool", vb[:], vf[:], cmg[:], ALU.mult)
                    p_ = nb()
                    for gi in range(4):
                        K.mm(p_[:, gi * 64:(gi + 1) * 64], wsT[:, gi, :], vb[:, gi * 64:(gi + 1) * 64])
                    for gi in range(4):
                        gs = slice(gi * 64, (gi + 1) * 64)
                        K.stt(yc[:, gs], p_[:, gs], cmb[:, gi:gi + 1], xcm[:, t, gs], ALU.add, ALU.mult)
                    for c in range(2):
                        K.tr(pT[:, c * 128:(c + 1) * 128], yc[:, c * 128:(c + 1) * 128], identb[:])
                    K.copy("act", ycmT_[:, :, t * 128:(t + 1) * 128], pT[:, 0:256].rearrange("p (c t) -> p c t", c=2))
                dstore(yTs[512:768, s0:s0 + G].rearrange("(c p) n -> p c n", p=128), ycmT_[:, :, 0:G], "ycmT%d" % (g % 2))

            prep(*GROUPS[0])
            prep_b(*GROUPS[0])
            for i_, grp_ in enumerate(GROUPS):
                if i_ + 1 < len(GROUPS):
                    prep(*GROUPS[i_ + 1])
                flush()
                body(*grp_)
                if i_ + 1 < len(GROUPS):
                    prep_b(*GROUPS[i_ + 1])
                if i_ >= 1:
                    body_b(*GROUPS[i_ - 1])
            body_b(*GROUPS[-1])
            flush()
            S.barrier()

    def phase_a(l, kvnT, krT2):
        with contextlib.ExitStack() as es:
            sb = lambda n, sh, dt: es.enter_context(nc.sbuf_tensor(K.name(n), sh, dt))
            ps = lambda n, sh, dt: es.enter_context(nc.psum_tensor(K.name(n), sh, dt))
            KT = sb("KT", [128, 4, NS], BF16)
            Vaug = sb("Vaug", [128, NCH, 4, 130], BF16)
            wuq = sb("wuq", [128, 2, 1024], BF16)
            wukv = sb("wukv", [128, 1024], BF16)
            qn = [sb("qna%d" % i, [128, 2, 512], BF16) for i in range(2)]
            cs = [sb("csa%d" % i, [128, 512], F32) for i in range(2)]
            qh = [sb("qh%d" % i, [128, 512], BF16) for i in range(2)]
            qr = [sb("qr%d" % i, [128, 512], BF16) for i in range(2)]
            PT = [sb("PT%d" % i, [128, 512], BF16) for i in range(4)]
            yat = [sb("yat%d" % i, [128, 512], F32) for i in range(4)]
            yaT = [sb("yaT%d" % i, [128, 4, 512], BF16) for i in range(2)]
            rden = sb("rden", [128, 8], F32)
            acc = [ps("acc%d" % i, [128, 512], F32) for i in range(4)]
            psc = [ps("psc%d" % i, [128, 512], F32) for i in range(3)]
            pqu = ps("pqu", [128, 512], F32)
            K.dma(wuq[:], w_uqx[l].rearrange("(c p) n -> p c n", p=128), "wuq", eng="pool")
            K.dma(wukv[:], w_ukvr[l], "wukv", eng="pool")
            K.memset("pool", Vaug[:], 1.0)
            for j in range(32):
                K.dma(wf1s[l, j].rearrange("p (k c) -> p k c", k=8),
                      w_ff1[l].rearrange("(k p) n -> p k n", p=128)[:, :, j * 128:(j + 1) * 128], "wf1cast%d" % (j % 4), eng="pool")
            precast_o(l)
            if l + 1 < n_layers:
                precast_in(l + 1)
            for (g, s0, G) in GROUPS:
                for h in range(4):
                    p_ = psc[h % 2]
                    K.mm(p_[:, 0:G], wukv[:, h * 128:(h + 1) * 128], kvnT[:, s0:s0 + G])
                    K.copy("act" if h % 2 else "dve", KT[:, h, s0:s0 + G], p_[:, 0:G])
                for t in range(G // 128):
                    p_ = acc[t]
                    K.mm(p_[:, :], kvnT[:, s0 + t * 128:s0 + (t + 1) * 128], wukv[:, 512:1024])
                    K.copy("act" if t % 2 else "dve", Vaug[:, s0 // 128 + t, :, 0:128], p_[:, :].rearrange("p (h d) -> p h d", h=4))
            pti = [0]

            def aload(g, s0, G):
                K.dma(qn[g % 2][:, :, 0:G], qnTs[:, s0:s0 + G].rearrange("(c p) n -> p c n", p=128), "qna%d" % (g % 2))
                if g == 0:
                    K.dma(cs[g % 2][:, 0:G], csc[:, 0:G], "csa%d" % (g % 2))
                else:
                    K.dma(cs[g % 2][:, 0:G], csq[:, s0 - NCTX:s0 - NCTX + G], "csa%d" % (g % 2))

            for (g, s0, G) in GROUPS:
                nt = G // 128
                kts = [0, 1] if g == 0 else list(range(NCH))
                qn_ = qn[g % 2]; cs_ = cs[g % 2]; yaT_ = yaT[g % 2]
                if g == 0:
                    aload(g, s0, G)
                if g + 1 < len(GROUPS):
                    aload(*GROUPS[g + 1])
                for h in range(4):
                    qh_ = qh[h % 2]; qr_ = qr[h % 2]
                    for c in range(2):
                        K.mm(pqu[:, 0:G], wuq[:, c, h * 256:h * 256 + 128], qn_[:, c, 0:G], start=(c == 0), stop=(c == 1))
                    K.copy("dve", qh_[:, 0:G], pqu[:, 0:G])
                    for c in range(2):
                        K.mm(pqu[:, 0:G], wuq[:, c, h * 256 + 128:h * 256 + 256], qn_[:, c, 0:G], start=(c == 0), stop=(c == 1))
                    K.tt("dve", qr_[:, 0:G], pqu[:, 0:G], cs_[:, 0:G], ALU.mult)
                    stash = {}

                    def score(i):
                        kt = kts[i]
                        ksl = slice(kt * 128, (kt + 1) * 128)
                        p_ = psc[pti[0] % 3]
                        P_ = PT[pti[0] % 4]
                        pti[0] += 1
                        K.mm(p_[:, 0:G], KT[:, h, ksl], qh_[:, 0:G], start=True, stop=False)
                        K.mm(p_[:, 0:G], krT2[:, ksl], qr_[:, 0:G], start=False, stop=True)
                        K.act(P_[:, 0:G], p_[:, 0:G], AF.Exp, scale=SCALE)
                        stash[i] = P_

                    score(0)
                    if len(kts) > 1:
                        score(1)
                    for i, kt in enumerate(kts):
                        if i + 2 < len(kts):
                            score(i + 2)
                        P_ = stash.pop(i)
                        for qt in range(nt):
                            K.mm(acc[qt][:, 0:129], P_[:, qt * 128:(qt + 1) * 128], Vaug[:, kt, h, 0:129],
                                 start=(i == 0), stop=(i == len(kts) - 1))
                    for qt in range(nt):
                        K.recip(rden[:, qt:qt + 1], acc[qt][:, 128:129])
                        K.ts("dve", yat[qt][:, h * 128:(h + 1) * 128], acc[qt][:, 0:128], rden[:, qt:qt + 1], None, ALU.mult)
                for qt in range(nt):
                    for h in range(4):
                        K.tr(pqu[:, h * 128:(h + 1) * 128], yat[qt][:, h * 128:(h + 1) * 128], identf[:])
                    K.copy("act", yaT_[:, :, qt * 128:(qt + 1) * 128], pqu[:, 0:512].rearrange("p (h t) -> p h t", h=4))
                K.dma(yTs[0:512, s0:s0 + G].rearrange("(h p) n -> p h n", p=128), yaT_[:, :, 0:G], "yaT%d" % (g % 2))
            S.barrier()

    def phase_s(l, m_next=None):
        with contextlib.ExitStack() as es:
            sb = lambda n, sh, dt: es.enter_context(nc.sbuf_tensor(K.name(n), sh, dt))
            ps = lambda n, sh, dt: es.enter_context(nc.psum_tensor(K.name(n), sh, dt))
            cw = sb("cw", [128, 18], F32); cbias = sb("cbias", [128, 6], F32)
            sm = sb("sm", [128, 24], F32); Abc = sb("Abc", [128, 8], F32); dsum = sb("dsum", [128, 4], F32)
            sgb = sb("sgb", [128, 256], F32)
            Sb_all = sb("Sb_all", [128, NCH, 256], F32)
            ypt = [sb("ypt%d" % i, [128, 256], F32) for i in range(2)]
            ypl = [sb("ypl%d" % i, [128, 256], F32) for i in range(3)]
            cmT_all = sb("cmT_all", [128, 2, NS], BF16)
            dfsb_all = sb("dfsb_all", [128, NCH, 4], F32); decb_all = sb("decb_all", [128, NCH, 4], F32)
            stf = sb("stf", [128, 256], F32); stfb = sb("stfb", [128, 256], BF16)
            stb = sb("stb", [128, 256], F32); stbb = sb("stbb", [128, 256], BF16)
            xr = [sb("xr%d" % i, [128, 6, 514], BF16) for i in range(2)]
            dg = sb("dg", [128, 18, 128], BF16)
            cvs = [sb("cv%d" % i, [128, 6, 512], F32) for i in range(2)]
            bcTs = [sb("bcT%d" % i, [128, 4, 512], BF16) for i in range(2)]
            dtr = [sb("dtr%d" % i, [128, 4, 8], F32) for i in range(2)]
            sps = [[sb("sp%d_%d" % (b_, i), [128, 4, 8], F32) for i in range(6)] for b_ in range(2)]
            E2 = [sb("E_%d" % i, [128, 40], F32) for i in range(2)]
            xs_tok2 = [sb("xs_tok_%d" % i, [128, 256], F32) for i in range(2)]
            bm_tok2 = [sb("bm_tok_%d" % i, [128, 2, 128], BF16) for i in range(2)]
            wde2 = [sb("wde_%d" % i, [128, 8], F32) for i in range(2)]
            xdt2 = [sb("xdt_%d" % i, [128, 2, 256], BF16) for i in range(2)]; xdte2 = [sb("xdte_%d" % i, [128, 2, 256], BF16) for i in range(2)]
            Lm2 = [sb("Lm_%d" % i, [128, 8, 128], F32) for i in range(2)]; eL2 = [sb("eL_%d" % i, [128, 8, 128], F32) for i in range(2)]
            GTm2 = [sb("GTm_%d" % i, [128, 2, 2, 128], F32) for i in range(2)]; W2 = [sb("W_%d" % i, [128, 8, 128], BF16) for i in range(2)]
            t1_2 = [sb("t1s_%d" % i, [128, 256], F32) for i in range(2)]; t2_2 = [sb("t2s_%d" % i, [128, 256], F32) for i in range(2)]
            sz = [sb("sz%d" % i, [128, 256], F32) for i in range(3)]
            yb2 = [sb("yb%d" % i, [128, 256], BF16) for i in range(2)]; junk = sb("junks", [128, 256], BF16)
            st3 = sb("st3", [128, 6], F32)
            ysT = [sb("ysT%d" % i, [128, 2, 512], BF16) for i in range(2)]
            pcx = ps("pcx", [128, 512], F32)
            pconv = ps("pconv", [128, 512], F32)
            pseg = ps("pseg", [128, 1024], F32)
            pbm = ps("pbm", [128, D], BF16)
            pG = ps("pG", [128, 512], F32)
            pst = ps("pst", [128, 512], F32)
            pyo = ps("pyo", [128, 512], F32)

            K.dma(cw[:], convw[l], "cw"); K.dma(cbias[:], convb[l], "cbias")
            K.dma(sm[:], brow(ssd_sm[l:l + 1, :], 24), "sm")
            K.dma(sgb[:], brow(ssd_g[l:l + 1, :], 256), "sgb")
            K.act(Abc[:], sm[:, 8:16], AF.Exp)
            K.ts("dve", Abc[:], Abc[:], -1.0, None, ALU.mult)
            K.tt("dve", dsum[:], sm[:, 16:20], sm[:, 20:24], ALU.add)
            for i_ in range(18):
                K.ts("pool", dg[:, i_, :], identb[:], cw[:, i_:i_ + 1], 0.0, ALU.mult, ALU.add)
            K.memset("pool", stf[:], 0.0); K.memset("pool", stfb[:], 0.0)
            K.memset("pool", stb[:], 0.0); K.memset("pool", stbb[:], 0.0)
            v4 = lambda ap: ap.rearrange("p (h d) -> p h d", h=4)

            def prologue_parts(g, s0, G):
                nt = G // 128
                xr_ = xr[g % 2]; dtr_ = dtr[g % 2]; cv = cvs[g % 2]; bcT = bcTs[g % 2]; sp_ = sps[g % 2]
                c0 = xbcol(s0)

                def conv(c):
                    for tap in range(3):
                        K.mm(pconv[:, 0:G], dg[:, c * 3 + tap, :], xr_[:, c, tap:tap + G], start=(tap == 0), stop=(tap == 2))
                    K.act(cv[:, c, 0:G], pconv[:, 0:G], AF.Identity, bias=cbias[:, c:c + 1])

                def p0():
                    K.dma(xr_[:, :, 0:G + 2], xbcs.rearrange("(c p) n -> p c n", p=128)[:, :, c0 - 1:c0 + G + 1], "xr%d" % (g % 2))
                    K.dma(dtr_[:, 0:nt, :], zdts[s0:s0 + G, 256:264].rearrange("(t p) n -> p t n", p=128), "dtr%d" % (g % 2))
                    xsp, nx, mn, lg, dt, a = [t_[:, 0:nt, :] for t_ in sp_]
                    K.tt("dve", xsp, dtr_[:, 0:nt, :], bcmid(sm[:, 0:8], nt), ALU.add)
                    K.ts("dve", nx, xsp, -1.0, None, ALU.mult)
                    K.tt("dve", mn, xsp, nx, ALU.min)
                    K.act(nx, mn, AF.Exp)
                    K.act(lg, nx, AF.Ln, bias=1.0)
                    K.stt(dt, xsp, 0.0, lg, ALU.max, ALU.add)
                    K.tt("dve", a, dt, bcmid(Abc[:], nt), ALU.mult)
                    conv(0)

                def p1():
                    conv(1); conv(2)

                def p2():
                    conv(3); conv(4)

                def p3():
                    conv(5)
                    K.act(cv[:, 0:2, 0:G], cv[:, 0:2, 0:G], AF.Silu)
                    K.act(bcT[:, :, 0:G], cv[:, 2:6, 0:G], AF.Silu)
                    K.copy("pool", cmT_all[:, :, s0:s0 + G], bcT[:, 2:4, 0:G])
                return [p0, p1, p2, p3]

            def early(g, s0, G, t):
                cv = cvs[g % 2]; bcT = bcTs[g % 2]; sp_ = sps[g % 2]
                if True:
                    ci = s0 // 128 + t
                    pb = ci % 2
                    E = E2[pb]; xs_tok = xs_tok2[pb]; bm_tok = bm_tok2[pb]; wde = wde2[pb]; xdt = xdt2[pb]; xdte = xdte2[pb]
                    Lm = Lm2[pb]; eL = eL2[pb]; GTm = GTm2[pb]; W = W2[pb]; t1 = t1_2[pb]; t2 = t2_2[pb]
                    tsl = slice(t * 128, (t + 1) * 128)
                    a_t = sp_[5][:, t, :]; dt_t = sp_[4][:, t, :]
                    for i, lt in enumerate([trit[:, 0, :], trit[:, 3, :], trit[:, 1, :], trit[:, 2, :], onesf[:]]):
                        K.mm(pcx[:, i * 8:(i + 1) * 8], lt, a_t)
                    K.act(E[:], pcx[:, 0:40], AF.Exp)
                    K.copy("pool", dfsb_all[:, ci, :], E[:, 28:32])
                    K.copy("pool", decb_all[:, ci, :], E[:, 36:40])
                    for c in range(2):
                        K.tr(pcx[:, 128 + c * 128:128 + (c + 1) * 128], cv[:, c, tsl], identf[:])
                    K.copy("act", xs_tok[:], pcx[:, 128:384])
                    for grp in range(2):
                        K.tr(pbm[:, grp * 128:(grp + 1) * 128], bcT[:, grp, tsl], identb[:])
                    K.copy("act", bm_tok[:].rearrange("p a b -> p (a b)"), pbm[:, 0:256])
                    K.tt("dve", wde[:, 0:4], dt_t[:, 0:4], E[:, 8:12], ALU.mult)
                    K.tt("dve", wde[:, 4:8], dt_t[:, 4:8], E[:, 20:24], ALU.mult)
                    for d in range(2):
                        K.tt("dve", v4(xdt[:, d, :]), v4(xs_tok[:]), bc3(dt_t[:, d * 4:(d + 1) * 4], 64), ALU.mult)
                        K.tt("pool", v4(xdte[:, d, :]), v4(xs_tok[:]), bc3(wde[:, d * 4:(d + 1) * 4], 64), ALU.mult)
                    for j in range(8):
                        d = j // 4
                        K.act(Lm[:, j, :], trit[:, 3 if d == 0 else 1, :], AF.Identity, scale=a_t[:, j:j + 1])
                        K.mm(pseg[:, j * 128:(j + 1) * 128], Lm[:, j, :], trit[:, 0 if d == 0 else 2, :])
                    K.act(eL[:].rearrange("p a b -> p (a b)"), pseg[:], AF.Exp)
                    for grp in range(2):
                        K.mm(pG[:, grp * 128:(grp + 1) * 128], bcT[:, grp, tsl], bcT[:, 2 + grp, tsl])
                    for d in range(2):
                        K.tt("dve", GTm[:, d, :, :], pG[:, 0:256].rearrange("p (a b) -> p a b", a=2),
                             bcmid(trit[:, 0 if d == 0 else 2, :], 2), ALU.mult)
                    for d in range(2):
                        for grp in range(2):
                            j0 = d * 4 + grp * 2
                            K.tt("dve", W[:, j0:j0 + 2, :], eL[:, j0:j0 + 2, :], bcmid(GTm[:, d, grp, :], 2), ALU.mult)

            def late(g, s0, G, t):
                cv = cvs[g % 2]; bcT = bcTs[g % 2]; sp_ = sps[g % 2]
                if True:
                    ci = s0 // 128 + t
                    pb = ci % 2
                    E = E2[pb]; xs_tok = xs_tok2[pb]; bm_tok = bm_tok2[pb]; wde = wde2[pb]; xdt = xdt2[pb]; xdte = xdte2[pb]
                    Lm = Lm2[pb]; eL = eL2[pb]; GTm = GTm2[pb]; W = W2[pb]; t1 = t1_2[pb]; t2 = t2_2[pb]
                    tsl = slice(t * 128, (t + 1) * 128)
                    a_t = sp_[5][:, t, :]; dt_t = sp_[4][:, t, :]
                    for h in range(4):
                        hs = slice(h * 64, (h + 1) * 64)
                        for d in range(2):
                            K.mm(pyo[:, 256 + h * 64:256 + (h + 1) * 64], W[:, d * 4 + h, :], xdt[:, d, hs], start=(d == 0), stop=(d == 1))
                    for j in range(8):
                        d, h = j // 4, j % 4
                        K.mm(pst[:, j * 64:(j + 1) * 64], bm_tok[:, h // 2, :], xdte[:, d, h * 64:(h + 1) * 64])
                    for h in range(4):
                        hs = slice(h * 64, (h + 1) * 64)
                        K.mm(pyo[:, hs], bcT[:, 2 + h // 2, tsl], stfb[:, hs])
                    K.tt("dve", v4(stf[:]), v4(stf[:]), bc3(E[:, 32:36], 64), ALU.mult)
                    K.tt("dve", stf[:], pst[:, 0:256], stf[:], ALU.add)
                    K.copy("act", stfb[:], stf[:])
                    K.copy("act", Sb_all[:, ci, :], pst[:, 256:512])
                    K.tt("dve", v4(t1[:]), v4(pyo[:, 0:256]), bc3(E[:, 0:4], 64), ALU.mult)
                    K.tt("dve", t1[:], pyo[:, 256:512], t1[:], ALU.add)
                    K.tt("pool", v4(t2[:]), v4(xs_tok[:]), bc3(dsum[:], 64), ALU.mult)
                    K.tt("pool", ypt[pb][:], t1[:], t2[:], ALU.add)
                    K.dma(ypS[ci * 128:(ci + 1) * 128, :], ypt[pb][:], "ypt%d" % pb, eng="pool")


            allc = [(g, s0, G, t) for (g, s0, G) in GROUPS for t in range(G // 128)]
            for p_ in prologue_parts(*GROUPS[0]):
                p_()
            early(*allc[0])
            for k_, cur in enumerate(allc):
                g_, s0_, G_, t_ = cur
                if k_ + 1 < len(allc):
                    early(*allc[k_ + 1])
                late(*cur)
                nt_c = G_ // 128
                if g_ + 1 < len(GROUPS) and t_ < nt_c - 1:
                    parts = prologue_parts(*GROUPS[g_ + 1])
                    per = (len(parts) + nt_c - 2) // (nt_c - 1)
                    for p_ in parts[t_ * per:(t_ + 1) * per]:
                        p_()

            order = [1, 0] + list(range(NCH - 1, 1, -1))
            if m_next is not None:
                m_init, m_load, m_compute = m_steps(m_next, sb, pst)
                m_init()
                m_load(0)
            def stA(n_):
                ci = order[n_]
                csl = slice(ci * 128, (ci + 1) * 128)
                if n_ == 0:
                    for m_ in range(2):
                        K.dma(sz[m_ % 3][:], zdts[order[m_] * 128:(order[m_] + 1) * 128, 0:256], "sz%d" % (m_ % 3))
                        K.dma(ypl[m_ % 3][:], ypS[order[m_] * 128:(order[m_] + 1) * 128, :], "ypl%d" % (m_ % 3))
                if n_ + 2 < len(order):
                    K.dma(sz[(n_ + 2) % 3][:], zdts[order[n_ + 2] * 128:(order[n_ + 2] + 1) * 128, 0:256], "sz%d" % ((n_ + 2) % 3))
                    K.dma(ypl[(n_ + 2) % 3][:], ypS[order[n_ + 2] * 128:(order[n_ + 2] + 1) * 128, :], "ypl%d" % ((n_ + 2) % 3))
                for h in range(4):
                    hs = slice(h * 64, (h + 1) * 64)
                    K.mm(pyo[:, hs], cmT_all[:, h // 2, csl], stbb[:, hs])
                if m_next is not None and n_ < 24:
                    if n_ + 1 < 24:
                        m_load(n_ + 1)
                    m_compute(n_)
                K.tt("dve", v4(stb[:]), v4(stb[:]), bc3(decb_all[:, ci, :], 64), ALU.mult)
                K.tt("dve", stb[:], stb[:], Sb_all[:, ci, :], ALU.add)
                K.copy("pool", stbb[:], stb[:])

            def stB(n_):
                ci = order[n_]
                t1 = t1_2[n_ % 2]; yb = yb2[n_ % 2]
                K.tt("dve", v4(t1[:]), v4(pyo[:, 0:256]), bc3(dfsb_all[:, ci, :], 64), ALU.mult)
                K.tt("dve", t1[:], t1[:], ypl[n_ % 3][:], ALU.add)
                K.tt("dve", t1[:], t1[:], sz[n_ % 3][:], ALU.mult)
                K.act(junk[:], t1[:], AF.Square, accum=st3[:, 3 * (n_ % 2):3 * (n_ % 2) + 1])
                K.rstd(st3[:, 3 * (n_ % 2) + 2:3 * (n_ % 2) + 3], st3[:, 3 * (n_ % 2):3 * (n_ % 2) + 1], 256, st3[:, 3 * (n_ % 2) + 1:3 * (n_ % 2) + 2])
                K.stt(yb[:], t1[:], st3[:, 3 * (n_ % 2) + 2:3 * (n_ % 2) + 3], sgb[:], ALU.mult, ALU.mult)

            def stC(n_):
                ci = order[n_]
                yb = yb2[n_ % 2]
                if ci < 2:
                    buf, slot, last = ysT[0], ci, (ci == 0)
                else:
                    gidx = (ci - 2) // 4
                    buf, slot, last = ysT[(gidx + 1) % 2], (ci - 2) % 4, ((ci - 2) % 4 == 0)
                for c in range(2):
                    K.tr(pbm[:, c * 128:(c + 1) * 128], yb[:, c * 128:(c + 1) * 128], identb[:])
                K.copy("act", buf[:, :, slot * 128:(slot + 1) * 128], pbm[:, 0:256].rearrange("p (c t) -> p c t", c=2))
                if last:
                    if ci < 2:
                        K.dma(yTs[768:1024, 0:256].rearrange("(c p) n -> p c n", p=128), buf[:, :, 0:256], "ysT0")
                    else:
                        K.dma(yTs[768:1024, 256 + gidx * 512:256 + (gidx + 1) * 512].rearrange("(c p) n -> p c n", p=128),
                              buf[:, :, :], "ysT%d" % ((gidx + 1) % 2))

            NO = len(order)
            stA(0)
            for n_ in range(NO):
                stB(n_)
                if n_ + 1 < NO:
                    stA(n_ + 1)
                stC(n_)
            S.barrier()

    def phase_o(l):
        with contextlib.ExitStack() as es:
            sb = lambda n, sh, dt: es.enter_context(nc.sbuf_tensor(K.name(n), sh, dt))
            ps = lambda n, sh, dt: es.enter_context(nc.psum_tensor(K.name(n), sh, dt))
            wout = sb("wout", [128, 8, D], BF16)
            wff2 = sb("wff2", [128, 32, D], BF16)
            w1r = [sb("w1r%d" % i, [128, 8, 128], BF16) for i in range(6)]
            md = sb("md", [128, 4, D], F32)
            yT = sb("yTo", [128, 8, 512], BF16)
            xt = [sb("xt%d" % i, [128, D], F32) for i in range(5)]
            g2c = sb("g2c", [128, D], F32)
            tmpf = sb("tmpo", [128, D], F32)
            junk = sb("junko", [128, D], BF16)
            hb = [sb("hbo%d" % i, [128, D], BF16) for i in range(2)]
            h2T = [sb("h2T%d" % i, [128, 8, 512], BF16) for i in range(2)]
            rl = [sb("rl%d" % i, [128, 512], BF16) for i in range(2)]
            aT = sb("aT", [128, 32, 512], BF16)
            st = sb("sto", [128, 3, 16], F32)
            pT = ps("pTo", [128, D], BF16)
            pa = [ps("pa%d" % i, [128, 512], F32) for i in range(7)]
            bank = [0]

            def nb():
                bank[0] += 1
                return pa[bank[0] % 7]

            K.dma(wout[:], wos[l].rearrange("(k p) n -> p k n", p=128), "wout")
            xi = [0]
            w1i = [0]
            sti = [0]

            def newx():
                x_ = xt[xi[0] % 5]; xkey = "xt%d" % (xi[0] % 5); xi[0] += 1
                return x_, xkey

            def post(pp, gcol, x_, gt_=None):
                si = sti[0] % 16; sti[0] += 1
                K.act(junk[:, 0:512], pp[0][:, :], AF.Square, accum=st[:, 0, si:si + 1])
                K.act(junk[:, 512:1024], pp[1][:, :], AF.Square, accum=st[:, 1, si:si + 1])
                K.tt("dve", st[:, 0, si:si + 1], st[:, 0, si:si + 1], st[:, 1, si:si + 1], ALU.add)
                K.rstd(st[:, 2, si:si + 1], st[:, 0, si:si + 1], D, st[:, 1, si:si + 1])
                for half in range(2):
                    hs = slice(half * 512, (half + 1) * 512)
                    K.stt(tmpf[:, hs], pp[half][:, :], st[:, 2, si:si + 1], (md[:, gcol, hs] if gt_ is None else gt_[:, hs]), ALU.mult, ALU.mult)
                K.tt("pool", x_[:], x_[:], tmpf[:], ALU.add)

            pend = {}

            def prep_load(g, s0, G):
                nt = G // 128
                v = 1 if g == 0 else 0
                if g in (0, 1):
                    for i, m in enumerate((2, 4, 3, 5)):
                        K.dma(md[:, i, :], brow(mods[l, v, m:m + 1, :], D), "md%d" % i)
                    if g == 0:
                        K.dma(g2c[:], brow(mods[l, 1, 5:6, :], D), "g2c")
                K.dma(yT[:, :, 0:G], yTs[:, s0:s0 + G].rearrange("(k p) n -> p k n", p=128), "yTo")
                xs_ = []
                for t in range(nt):
                    x_, xkey = newx()
                    xs_.append((x_, xkey))
                    K.dma(x_[:], xsrc(l, s0 + t * 128, 128), xkey)
                pend[g] = xs_

            def prep_op(g, s0, G, t):
                tsl = slice(t * 128, (t + 1) * 128)
                x_, xkey = pend[g][t]
                pp = [nb(), nb()]
                for half in range(2):
                    for k in range(8):
                        K.mm(pp[half][:, :], yT[:, k, tsl], wout[:, k, half * 512:(half + 1) * 512], start=(k == 0), stop=(k == 7))
                post(pp, 0, x_)
                K.dma(xscr[s0 + t * 128:s0 + (t + 1) * 128, :], x_[:], xkey, eng="pool")
                si2 = sti[0] % 16; sti[0] += 1
                K.act(junk[:], x_[:], AF.Square, accum=st[:, 0, si2:si2 + 1])
                K.rstd(st[:, 2, si2:si2 + 1], st[:, 0, si2:si2 + 1], D, st[:, 1, si2:si2 + 1])
                K.stt(tmpf[:], x_[:], st[:, 2, si2:si2 + 1], md[:, 1, :], ALU.mult, ALU.mult)
                K.tt("dve", hb[t % 2][:], tmpf[:], md[:, 2, :], ALU.add)

            def prep_tr(g, s0, G, t):
                tsl = slice(t * 128, (t + 1) * 128)
                hb_ = hb[t % 2]
                for k in range(8):
                    K.tr(pT[:, k * 128:(k + 1) * 128], hb_[:, k * 128:(k + 1) * 128], identb[:])
                K.copy("act", h2T[g % 2][:, :, tsl], pT[:].rearrange("p (k t) -> p k t", k=8))

            def ff1(g, s0, G, j0, j1):
                for j in range(j0, j1):
                    w_ = w1r[w1i[0] % 6]
                    K.dma(w_[:], wf1s[l, j].rearrange("p (k c) -> p k c", k=8), "w1r%d" % (w1i[0] % 6))
                    w1i[0] += 1
                    p_ = nb()
                    for k in range(8):
                        K.mm(p_[:, 0:G], w_[:, k, :], h2T[g % 2][:, k, 0:G], start=(k == 0), stop=(k == 7))
                    r_ = rl[j % 2]
                    K.act(r_[:, 0:G], p_[:, 0:G], AF.Relu)
                    K.tt("dve", aT[:, j, 0:G], r_[:, 0:G], p_[:, 0:G], ALU.mult)

            def ff2(g, s0, G, t):
                if True:
                    tsl = slice(t * 128, (t + 1) * 128)
                    x_, xkey = newx()
                    K.dma(x_[:], xscr[s0 + t * 128:s0 + (t + 1) * 128, :], xkey)
                    pp = [nb(), nb()]
                    for half in range(2):
                        for j in range(32):
                            K.mm(pp[half][:, :], aT[:, j, tsl], wff2[:, j, half * 512:(half + 1) * 512], start=(j == 0), stop=(j == 31))
                    post(pp, 3, x_, g2c if g == 0 else None)
                    K.dma(xdst(l, s0 + t * 128, 128), x_[:], xkey, eng="pool")

            g0 = GROUPS[0]
            prep_load(*g0)
            for jj in range(4):
                K.dma(wff2[:, jj * 8:(jj + 1) * 8, :], wf2s[l, jj * 1024:(jj + 1) * 1024, :].rearrange("(k p) n -> p k n", p=128),
                      "wff2_%d" % jj, eng="act")
            for t in range(g0[2] // 128):
                prep_op(*g0, t)
                prep_tr(*g0, t)
            for i_, cur in enumerate(GROUPS):
                nxt = GROUPS[i_ + 1] if i_ + 1 < len(GROUPS) else None
                nt_c = cur[2] // 128
                if nxt is not None:
                    prep_load(*nxt)
                for q_ in range(4):
                    ff1(*cur, q_ * 8, (q_ + 1) * 8)
                    if nxt is not None:
                        prep_op(*nxt, q_)
                        if q_ >= 1:
                            prep_tr(*nxt, q_ - 1)
                ff2(*cur, 0)
                if nxt is not None:
                    prep_tr(*nxt, 3)
                for t in range(1, nt_c):
                    ff2(*cur, t)
            S.barrier()

    for l in range(n_layers):
        if l == 0:
            phase_m(l)
        with contextlib.ExitStack() as esr:
            kvnT = esr.enter_context(nc.sbuf_tensor(K.name("kvnT"), [128, NS], BF16))
            krT2 = esr.enter_context(nc.sbuf_tensor(K.name("krT2"), [128, NS], BF16))
            phase_1(l, kvnT, krT2)
            phase_a(l, kvnT, krT2)
        phase_s(l, l + 1 if l + 1 < n_layers else None)
        phase_o(l)
    S.barrier()
    with contextlib.ExitStack() as es2:
        n = S.emit(es2)
    es0.close()
    return nc, n


def _rope_tables():
    rows_n = SEQ // 64
    row = np.repeat(np.arange(rows_n, dtype=np.float32), 64)
    col = np.tile(np.arange(64, dtype=np.float32), rows_n)
    inv = (np.float32(10000.0) ** (-np.arange(0, 32, 2, dtype=np.float32) / np.float32(32))).astype(np.float32)
    ar = (row[:, None] * inv).astype(np.float32)
    ac = (col[:, None] * inv).astype(np.float32)
    cos = np.zeros((64, SEQ), np.float32); sin = np.zeros((64, SEQ), np.float32)
    for d in range(64):
        ang = ar if d < 32 else ac
        f = d % 16
        cos[d] = np.cos(ang[:, f])
        sgn = -1.0 if (d % 32) < 16 else 1.0
        sin[d] = sgn * np.sin(ang[:, f])
    return cos, sin


_PERM = np.array([d + 16 if (d % 32) < 16 else d - 16 for d in range(64)])


def _host_layout(inp):
    f = lambda a: np.ascontiguousarray(a, dtype=np.float32)
    sh = {}
    sh["w_ada"] = f(inp["w_ada"]); sh["b_ada"] = f(inp["b_ada"])
    sh["g4"] = f(np.concatenate([inp["g_pre_mix"], inp["g_post_mix"], inp["g_pre_ff"], inp["g_post_ff"]], axis=1))
    w_in = inp["w_in"]
    sh["w_in"] = f(w_in)
    kr = w_in[:, :, 384:448]
    rot = kr[:, :, _PERM]
    sh["w_kr2"] = f(np.concatenate([kr, kr, rot, rot], axis=2))
    sh["gq_pc"] = f(inp["g_q"].reshape(NL, 2, 128).transpose(0, 2, 1))
    sh["gkv_p"] = f(inp["g_kv"].reshape(NL, 128, 1))
    wuq = inp["w_uq"].reshape(NL, 256, 4, 192)
    sh["w_uqx"] = f(np.concatenate([wuq, wuq[:, :, :, 128 + _PERM]], axis=3).reshape(NL, 256, 1024))
    wukv = inp["w_ukv"].reshape(NL, 128, 4, 256)
    sh["w_ukvr"] = f(np.concatenate([wukv[:, :, :, :128].reshape(NL, 128, 512), wukv[:, :, :, 128:].reshape(NL, 128, 512)], axis=2))
    sh["cm_g"] = f(inp["cm_norm_g"]); sh["cm_ws"] = f(inp["cm_w_s"])
    sh["cm_bt"] = f(inp["cm_b_s"].transpose(0, 2, 1))
    cw = inp["ssd_conv_w"].reshape(NL, 3, 6, 128).transpose(0, 3, 2, 1)
    sh["convw"] = f(cw.reshape(NL, 128, 18))
    sh["convb"] = f(inp["ssd_conv_b"].reshape(NL, 6, 128).transpose(0, 2, 1))
    sh["ssd_sm"] = f(np.concatenate([inp["ssd_dt_bias"].reshape(NL, 8), inp["ssd_a_log"].reshape(NL, 8), inp["ssd_d"].reshape(NL, 8)], axis=1))
    sh["ssd_g"] = f(inp["ssd_norm_g"])
    sh["w_out"] = f(inp["w_out"]); sh["w_ff1"] = f(inp["w_ff1"]); sh["w_ff2"] = f(inp["w_ff2"])
    cos, sin = _rope_tables()
    sh["cosd"] = f(np.concatenate([cos, cos], axis=0)); sh["sind"] = f(np.concatenate([sin, sin], axis=0))
    sh["csq"] = f(np.concatenate([cos, sin], axis=0))
    sh["csc"] = f(np.concatenate([np.ones((64, 512), np.float32), np.zeros((64, 512), np.float32)], axis=0))
    k = np.arange(128)[:, None]; l_ = np.arange(128)[None, :]
    tri = np.stack([(k <= l_), (k < l_), (k >= l_), (k > l_)], axis=1).astype(np.float32)
    sh["tri"] = f(tri.reshape(128, 512))
    sh["cc_pk"] = f(inp["c_ctx"].reshape(8, 128).T)
    return sh


_CACHE = {}


def kernel(**inputs):
    inp = {k: np.asarray(v) for k, v in inputs.items()}
    if "nc" not in _CACHE:
        _CACHE["nc"] = build()[0]
    nc = _CACHE["nc"]
    shared = _host_layout(inp)
    in_maps = []
    for b in range(8):
        m = dict(shared)
        m["x"] = np.ascontiguousarray(inp["x"][b], dtype=np.float32)
        m["ctx"] = np.ascontiguousarray(inp["ctx"][b], dtype=np.float32)
        m["c_pk"] = np.ascontiguousarray(inp["c"][b].reshape(8, 128).T, dtype=np.float32)
        in_maps.append(m)
    res = run_bass_kernel_spmd(nc, in_maps, core_ids=list(range(8)))
    return np.stack([np.asarray(r["out"], dtype=np.float32) for r in res.results], axis=0)
```

```python
import contextlib
import numpy as np
import concourse.bass as bass
import concourse.mybir as mybir
from concourse.bass_utils import run_bass_kernel_spmd

F32 = mybir.dt.float32
BF16 = mybir.dt.bfloat16
AF = mybir.ActivationFunctionType
ALU = mybir.AluOpType

EPS = 1e-6
NL = 4
D = 1024
SEQ = 4096
NCTX = 256
NS = SEQ + NCTX
NCH = NS // 128
SCALE = 192.0 ** -0.5
XB_COLS = 4356


def _esize(dt):
    return 2 if dt == BF16 else 4
class _Op:
    __slots__ = ("eng", "fn", "dma", "semkey", "waits", "sig", "cnt", "idx", "dmaval", "gid")


def _rect(ap):
    t = ap.ap
    es = _esize(ap.dtype)
    off = int(ap.offset)
    if str(ap.space) == "DRAM":
        ext = 0
        for s, c in t:
            ext += (c - 1) * abs(s)
        return (ap.name, 0, 1, off * es, (off + ext + 1) * es)
    pstep, pcnt = t[0]
    if pstep == 0:
        pstep = 1 << 40
    p0 = off // pstep
    f0 = off % pstep
    ext = 0
    for s, c in t[1:]:
        ext += (c - 1) * abs(s)
    if str(ap.space) == "PSUM":
        return (ap.name, 0, 128, (f0 * es) // 2048 * 2048, ((f0 + ext + 1) * es + 2047) // 2048 * 2048)
    return (ap.name, p0, p0 + pcnt, f0 * es, (f0 + ext + 1) * es)


class Sched:
    ENG = ("pe", "act", "dve", "pool", "sp")

    def __init__(self, nc):
        self.nc = nc
        self.ops = []
        self.acc = {}
        self.eng_ops = {e: [] for e in self.ENG}
        self.waited = {e: {x: -1 for x in self.ENG} for e in self.ENG}
        self.dma_last = {}
        self.dma_keys = {}
        self.dma_waited = {e: set() for e in self.ENG}
        self.unwaited = set()

    def add(self, eng, fn, reads=(), writes=(), dma=None):
        op = _Op()
        op.eng = eng
        op.fn = fn
        op.dma = dma is not None
        op.semkey = dma
        op.sig = False
        op.gid = len(self.ops)
        deps = set()
        rrects = [_rect(a) for a in reads]
        wrects = [_rect(a) for a in writes]
        for r in rrects:
            for rec in self.acc.get(r[0], ()):
                q = rec[0]
                if rec[2] and q[1] < r[2] and r[1] < q[2] and q[3] < r[4] and r[3] < q[4]:
                    deps.add(rec[1])
        for r in wrects:
            for rec in self.acc.get(r[0], ()):
                q = rec[0]
                if q[1] < r[2] and r[1] < q[2] and q[3] < r[4] and r[3] < q[4]:
                    deps.add(rec[1])
        if op.dma:
            prev = self.dma_last.get(dma)
            if prev is not None:
                deps.add(prev)
            self.dma_last[dma] = op.gid
            cnt = self.dma_keys.get(dma, 0) + 1
            self.dma_keys[dma] = cnt
            op.dmaval = 16 * cnt
            self.unwaited.add(op.gid)
        deps.discard(op.gid)
        waits = []
        best = {}
        for d in deps:
            o = self.ops[d]
            if o.dma:
                if d not in self.dma_waited[eng]:
                    self.dma_waited[eng].add(d)
                    self.unwaited.discard(d)
                    waits.append(("dma", d))
            else:
                if o.eng == "pe" and eng == "pe" and not op.dma:
                    continue
                if o.idx > best.get(o.eng, -1):
                    best[o.eng] = o.idx
        for x, i in best.items():
            if self.waited[eng][x] < i:
                self.waited[eng][x] = i
                waits.append(("eng", x, i))
                self.eng_ops[x][i].sig = True
        op.waits = waits
        if not op.dma:
            op.idx = len(self.eng_ops[eng])
            self.eng_ops[eng].append(op)
        else:
            op.idx = -1
        self.ops.append(op)
        for r in wrects:
            lst = self.acc.setdefault(r[0], [])
            lst[:] = [rec for rec in lst if not (r[1] <= rec[0][1] and rec[0][2] <= r[2]
                                                  and r[3] <= rec[0][3] and rec[0][4] <= r[4])]
            lst.append((r, op.gid, True, eng if not op.dma else None))
        for r in rrects:
            lst = self.acc.setdefault(r[0], [])
            if not op.dma:
                lst[:] = [rec for rec in lst if not (not rec[2] and rec[3] == eng and rec[0] == r)]
            lst.append((r, op.gid, False, eng if not op.dma else None))
        return op

    def fence(self, eng, reads=(), writes=()):
        return self.add(eng, None, reads, writes)

    def barrier(self):
        pend = sorted(self.unwaited)
        self.unwaited = set()
        last = {e: (self.eng_ops[e][-1].gid if self.eng_ops[e] else None) for e in self.ENG}
        for e in self.ENG:
            o = _Op()
            o.eng = e
            o.fn = None
            o.dma = False
            o.semkey = None
            o.sig = False
            o.gid = len(self.ops)
            waits = []
            for x in self.ENG:
                if x == e or last[x] is None:
                    continue
                i = self.ops[last[x]].idx
                if self.waited[e][x] < i:
                    self.waited[e][x] = i
                    waits.append(("eng", x, i))
                    self.eng_ops[x][i].sig = True
            for d in pend:
                if d not in self.dma_waited[e]:
                    self.dma_waited[e].add(d)
                    waits.append(("dma", d))
            o.waits = waits
            o.idx = len(self.eng_ops[e])
            self.eng_ops[e].append(o)
            self.ops.append(o)
        self.acc = {}

    def emit(self, sems_ctx):
        nc = self.nc
        engobj = {"pe": nc.tensor, "act": nc.scalar, "dve": nc.vector, "pool": nc.gpsimd, "sp": nc.sync}
        esem = {e: sems_ctx.enter_context(nc.semaphore("s_" + e)) for e in self.ENG}
        dsem = {k: sems_ctx.enter_context(nc.semaphore("d_%d" % i)) for i, k in enumerate(self.dma_keys)}
        for e in self.ENG:
            c = 0
            for o in self.eng_ops[e]:
                if o.sig:
                    c += 1
                    o.cnt = c
        n_inst = 0
        for o in self.ops:
            eo = engobj[o.eng]
            for w in o.waits:
                if w[0] == "dma":
                    d = self.ops[w[1]]
                    eo.wait_ge(dsem[d.semkey], d.dmaval)
                else:
                    eo.wait_ge(esem[w[1]], self.eng_ops[w[1]][w[2]].cnt)
            if o.fn is None:
                if o.sig:
                    eo.nop().then_inc(esem[o.eng], 1)
                continue
            inst = o.fn(eo)
            n_inst += 1
            if o.dma:
                inst.then_inc(dsem[o.semkey], 16)
            elif o.sig:
                inst.then_inc(esem[o.eng], 1)
        return n_inst


class KB:
    def __init__(self, nc):
        self.nc = nc
        self.S = Sched(nc)
        self.uid = 0
        self.pkeys = {}

    def name(self, n):
        self.uid += 1
        return "%s_%d" % (n, self.uid)

    def barrier(self):
        self.S.barrier()
        self.pkeys = {}

    def dma(self, out, in_, key, eng="sp", slow=False):
        sw = (eng == "pool")
        pool_ = self.pkeys.setdefault(sw, {})
        key = pool_.setdefault(key, ("s%d" if sw else "h%d") % len(pool_))
        if slow:
            self.S.add(eng, lambda e: e.dma_start(out=out, in_=in_, allow_slow_non_contiguous=True), reads=[in_], writes=[out], dma=key)
        else:
            self.S.add(eng, lambda e: e.dma_start(out=out, in_=in_), reads=[in_], writes=[out], dma=key)

    def mm(self, out, lhsT, rhs, start=True, stop=True):
        self.S.add("pe", lambda e: e.matmul(out, lhsT=lhsT, rhs=rhs, start=start, stop=stop),
                   reads=[lhsT, rhs], writes=[out])

    def tr(self, out, in_, ident):
        self.S.add("pe", lambda e: e.transpose(out=out, in_=in_, identity=ident), reads=[in_, ident], writes=[out])

    def act(self, out, in_, func, bias=None, scale=None, accum=None):
        kw = {}
        rd = [in_]
        wr = [out]
        if bias is not None:
            kw["bias"] = bias
            if not isinstance(bias, float):
                rd.append(bias)
        if scale is not None:
            kw["scale"] = scale
            if not isinstance(scale, float):
                rd.append(scale)
        if accum is not None:
            kw["accum_out"] = accum
            wr.append(accum)
        self.S.add("act", lambda e: e.activation(out=out, in_=in_, func=func, **kw), reads=rd, writes=wr)

    def copy(self, eng, out, in_):
        if eng == "act":
            self.S.add("act", lambda e: e.copy(out=out, in_=in_), reads=[in_], writes=[out])
        else:
            self.S.add(eng, lambda e: e.tensor_copy(out=out, in_=in_), reads=[in_], writes=[out])

    def tt(self, eng, out, in0, in1, op):
        self.S.add(eng, lambda e: e.tensor_tensor(out=out, in0=in0, in1=in1, op=op), reads=[in0, in1], writes=[out])

    def ts(self, eng, out, in0, s1, s2, op0, op1=None):
        rd = [in0]
        if not isinstance(s1, float):
            rd.append(s1)
        if s2 is not None and not isinstance(s2, float):
            rd.append(s2)
        if op1 is None:
            self.S.add(eng, lambda e: e.tensor_scalar(out=out, in0=in0, scalar1=s1, scalar2=None, op0=op0), reads=rd, writes=[out])
        else:
            self.S.add(eng, lambda e: e.tensor_scalar(out=out, in0=in0, scalar1=s1, scalar2=s2, op0=op0, op1=op1), reads=rd, writes=[out])

    def stt(self, out, in0, scalar, in1, op0, op1):
        rd = [in0, in1]
        if not isinstance(scalar, float):
            rd.append(scalar)
        self.S.add("dve", lambda e: e.scalar_tensor_tensor(out=out, in0=in0, scalar=scalar, in1=in1, op0=op0, op1=op1),
                   reads=rd, writes=[out])

    def memset(self, eng, ap, val):
        self.S.add(eng, lambda e: e.memset(ap, val), writes=[ap])

    def recip(self, out, in_):
        self.S.add("dve", lambda e: e.reciprocal(out=out, in_=in_), reads=[in_], writes=[out])

    def bn_stats(self, out, in_):
        self.S.add("dve", lambda e: e.bn_stats(out=out, in_=in_), reads=[in_], writes=[out])

    def bn_aggr(self, out, in_):
        self.S.add("dve", lambda e: e.bn_aggr(out=out, in_=in_), reads=[in_], writes=[out])

    def rstd(self, out, in_, n, tmp):
        self.act(tmp, in_, AF.Ln, bias=EPS, scale=1.0 / n)
        self.act(out, tmp, AF.Exp, scale=-0.5)


def bc3(ap2, n):
    return ap2.unsqueeze(2).to_broadcast([ap2.shape[0], ap2.shape[1], n])


def bcmid(ap2, n):
    return ap2.unsqueeze(1).to_broadcast([ap2.shape[0], n, ap2.shape[1]])


def build(n_layers=NL, dbg=False):
    nc = bass.Bass("TRN2", target_bir_lowering=False)
    K = KB(nc)
    S = K.S

    def din(name, shape, dt=F32):
        return nc.dram_tensor(name, shape, dt, kind="ExternalInput").ap()

    def dscr(name, shape, dt=F32):
        return nc.dram_tensor(name, shape, dt, kind=("ExternalOutput" if dbg else "Internal")).ap()

    x_in = din("x", [SEQ, D]); ctx_in = din("ctx", [NCTX, D])
    c_pk = din("c_pk", [128, 8]); cc_pk = din("cc_pk", [128, 8])
    w_ada = din("w_ada", [NL, D, 6 * D]); b_ada = din("b_ada", [NL, 6 * D])
    g4 = din("g4", [NL, 4 * D])
    w_in = din("w_in", [NL, D, 1992]); w_kr2 = din("w_kr2", [NL, D, 256])
    gq_pc = din("gq_pc", [NL, 128, 2]); gkv_p = din("gkv_p", [NL, 128, 1])
    w_uqx = din("w_uqx", [NL, 256, 1024]); w_ukvr = din("w_ukvr", [NL, 128, 1024])
    cm_g = din("cm_g", [NL, 256]); cm_ws = din("cm_ws", [NL, 4, 128, 128]); cm_bt = din("cm_bt", [NL, 128, 4])
    convw = din("convw", [NL, 128, 18]); convb = din("convb", [NL, 128, 6])
    ssd_sm = din("ssd_sm", [NL, 24]); ssd_g = din("ssd_g", [NL, 256])
    w_out = din("w_out", [NL, D, D]); w_ff1 = din("w_ff1", [NL, D, 4 * D]); w_ff2 = din("w_ff2", [NL, 4 * D, D])
    cosd = din("cosd", [128, SEQ]); sind = din("sind", [128, SEQ]); csq = din("csq", [128, SEQ]); csc = din("csc", [128, 512])
    tri = din("tri", [128, 4 * 128])
    out = nc.dram_tensor("out", [SEQ, D], F32, kind="ExternalOutput").ap()

    xscr = dscr("xscr", [NS, D])
    mods = dscr("mods", [NL, 2, 6, D])
    xbcs = dscr("xbcs", [768, XB_COLS], BF16)
    zdts = dscr("zdts", [NS, 264])
    yTs = dscr("yTs", [D, NS], BF16)
    qnTs = dscr("qnTs", [256, NS], BF16)
    wf1s = dscr("wf1s", [NL, 32, 128, 1024], BF16)
    ypS = dscr("ypS", [NS, 256])
    wf2s = dscr("wf2s", [NL, 4 * D, D], BF16)
    wos = dscr("wos", [NL, D, D], BF16)
    wins = dscr("wins", [NL, D, 1992], BF16)
    wkrs = dscr("wkrs", [NL, D, 256], BF16)

    def precast_in(l):
        for i in range(4):
            K.dma(wins[l, i * 256:(i + 1) * 256, :], w_in[l, i * 256:(i + 1) * 256, :], "cst%d" % (i % 4), eng="pool")
        K.dma(wkrs[l], w_kr2[l], "cst0", eng="pool")

    def precast_o(l):
        K.dma(wos[l], w_out[l], "cst1", eng="pool")
        for i in range(8):
            K.dma(wf2s[l, i * 512:(i + 1) * 512, :], w_ff2[l, i * 512:(i + 1) * 512, :], "cst%d" % (i % 4), eng="pool")

    GROUPS = [(0, 0, 256)] + [(g, 256 + (g - 1) * 512, 512) for g in range(1, 9)]

    def xsrc(l, s, n):
        if l == 0:
            return ctx_in[s:s + n, :] if s < NCTX else x_in[s - NCTX:s - NCTX + n, :]
        return xscr[s:s + n, :]

    def xdst(l, s, n):
        if l == n_layers - 1 and s >= NCTX:
            return out[s - NCTX:s - NCTX + n, :]
        return xscr[s:s + n, :]

    def xbcol(s):
        return 1 + s if s < NCTX else 259 + (s - NCTX)

    def brow(ap_row, n):
        return ap_row.broadcast_to([128, n])

    es0 = contextlib.ExitStack()
    sb0 = lambda n, sh, dt: es0.enter_context(nc.sbuf_tensor(n, sh, dt))
    identf = sb0("identf", [128, 128], F32)
    identb = sb0("identb", [128, 128], BF16)
    onesb = sb0("onesb", [128, 128], BF16)
    onesf = sb0("onesf", [128, 128], F32)
    trit = sb0("trit", [128, 4, 128], F32)
    zrow = sb0("zrow", [128, 8], BF16)
    K.dma(trit[:].rearrange("p a b -> p (a b)"), tri[:, :], "trit")
    K.memset("pool", onesf[:], 1.0)
    K.memset("pool", onesb[:], 1.0)
    K.memset("pool", zrow[:], 0.0)
    K.tt("dve", identf[:], trit[:, 0, :], trit[:, 2, :], ALU.mult)
    K.copy("dve", identb[:], identf[:])
    precast_in(0)

    def zero_pads():
        for i_, col in enumerate((0, 257, 258, 4355)):
            K.dma(xbcs.rearrange("(c p) n -> p c n", p=128)[:, :, col:col + 1], zrow[:, 0:6].unsqueeze(2), "zpad%d" % i_, slow=True)

    def m_steps(l, sb, pbank, two_q=False):
        sca = sb("sca", [128, 2, 8], F32)
        sc2 = sb("sc2", [128, 8, 2], F32)
        wa = [sb("wa%d" % i, [128, 8, 256], F32) for i in range(2)]
        bd = [sb("bd%d" % i, [2, 256], F32) for i in range(3)]
        gg = [sb("gg%d" % i, [2, 256], F32) for i in range(3)]
        mt = [sb("mt%d" % i, [2, 256], F32) for i in range(2)]
        res = [sb("res%d" % i, [2, 256], F32) for i in range(3)]
        wv = w_ada[l].rearrange("(k p) n -> p k n", p=128)

        def init():
            K.dma(sca[:, 0, :], c_pk[:, :], "sc0")
            K.dma(sca[:, 1, :], cc_pk[:, :], "sc1")
            K.act(sca[:], sca[:], AF.Silu)
            for v in range(2):
                K.copy("dve", sc2[:, :, v], sca[:, v, :])

        def gcol(j):
            m, q = j // 4, j % 4
            if m in (1, 4):
                return (0 if m == 1 else 2) * D + q * 256
            if m in (2, 5):
                return (1 if m == 2 else 3) * D + q * 256
            return None

        def load(j):
            K.dma(wa[j % 2][:], wv[:, :, j * 256:(j + 1) * 256], "wa%d" % (j % 2), eng=("act" if (two_q and j % 2) else "sp"))
            K.dma(bd[j % 3][:], b_ada[l:l + 1, j * 256:(j + 1) * 256].broadcast_to([2, 256]), "bd%d" % (j % 3))
            if gcol(j) is not None:
                K.dma(gg[j % 3][:], g4[l:l + 1, gcol(j):gcol(j) + 256].broadcast_to([2, 256]), "gg%d" % (j % 3))

        def compute(j):
            m, q = j // 4, j % 4
            w_ = wa[j % 2]
            for k in range(8):
                K.mm(pbank[0:2, 0:256], sc2[:, k, :], w_[:, k, :], start=(k == 0), stop=(k == 7))
            t_ = mt[j % 2]
            r_ = res[j % 3]
            K.tt("dve", t_[:], pbank[0:2, 0:256], bd[j % 3][:], ALU.add)
            if m in (0, 3):
                K.copy("dve", r_[:], t_[:])
            elif m in (1, 4):
                K.stt(r_[:], t_[:], 1.0, gg[j % 3][:], ALU.add, ALU.mult)
            else:
                K.tt("dve", r_[:], t_[:], gg[j % 3][:], ALU.mult)
            K.dma(mods[l, :, m, q * 256:(q + 1) * 256], r_[:], "res%d" % (j % 3))

        return init, load, compute

    def phase_m(l):
        with contextlib.ExitStack() as es:
            sb = lambda n, sh, dt: es.enter_context(nc.sbuf_tensor(K.name(n), sh, dt))
            pm = es.enter_context(nc.psum_tensor(K.name("pm"), [128, 512], F32))
            init, load, compute = m_steps(l, sb, pm, two_q=True)
            init()
            load(0)
            for j in range(24):
                if j + 1 < 24:
                    load(j + 1)
                compute(j)
            K.barrier()

    def phase_1(l, kvnT, krT2):
        with contextlib.ExitStack() as es:
            sb = lambda n, sh, dt: es.enter_context(nc.sbuf_tensor(K.name(n), sh, dt))
            ps = lambda n, sh, dt: es.enter_context(nc.psum_tensor(K.name(n), sh, dt))
            win = sb("win", [128, 8, 1992], BF16)
            wkr2 = sb("wkr2", [128, 8, 256], BF16)
            gq = sb("gq", [128, 2], F32); gkv = sb("gkv", [128, 1], F32)
            cmg = sb("cmg", [128, 256], F32); cmb = sb("cmb", [128, 4], F32)
            wsf = sb("wsf", [128, 4, 128], F32); wsT = sb("wsT", [128, 4, 128], BF16)
            ab = sb("ab", [128, 2, D], F32)
            xg = [sb("xg%d" % i, [128, D], F32) for i in range(3)]
            hT = [sb("hT%d" % i, [128, 8, 512], BF16) for i in range(2)]
            hb = [sb("hb%d" % i, [128, D], BF16) for i in range(4)]
            tmpf = sb("tmpf", [128, D], F32)
            junk = sb("junk", [128, D], BF16)
            st4 = sb("st4", [128, 3, 4], F32)
            sq = sb("sq", [128, 3, 512], BF16)
            lnq = sb("lnq", [128, 512], F32)
            rst = [sb("rst%d" % i, [128, 512], F32) for i in range(2)]
            qn = [sb("qn%d" % i, [128, 2, 512], BF16) for i in range(2)]
            cosk = sb("cosk", [128, 512], F32); sink = sb("sink", [128, 512], F32)
            t1 = sb("t1", [128, 512], F32); t2 = sb("t2", [128, 512], F32)
            xbst = [sb("xbst%d" % i, [128, 6, 512], BF16) for i in range(2)]
            xcms = [sb("xcm%d" % i, [128, 4, 512], F32) for i in range(2)]
            zdt = [sb("zdt%d" % i, [128, 4, 264], F32) for i in range(2)]
            st6 = sb("st6", [128, 4, 6], F32); mv = sb("mv", [128, 4, 2], F32)
            vpe = sb("vpe", [128, 4], F32); rscm = sb("rscm", [128, 4], F32); cneg = sb("cneg", [128, 4], F32)
            vnf = [sb("vnf%d" % i, [128, 256], F32) for i in range(4)]
            vnb = [sb("vnb%d" % i, [128, 256], BF16) for i in range(4)]
            ycm = [sb("ycm%d" % i, [128, 256], BF16) for i in range(4)]
            ycmT = [sb("ycmT%d" % i, [128, 2, 512], BF16) for i in range(2)]
            pT = ps("pT", [128, D], BF16)
            pf = [ps("pf%d" % i, [128, 512], F32) for i in range(7)]
            bank = [0]

            def nb():
                bank[0] += 1
                return pf[3 + bank[0] % 4]

            K.dma(win[:], wins[l].rearrange("(k p) n -> p k n", p=128), "win")
            K.dma(wkr2[:], wkrs[l].rearrange("(k p) n -> p k n", p=128), "wkr2")
            K.dma(gq[:], gq_pc[l], "gq"); K.dma(gkv[:], gkv_p[l], "gkv")
            K.dma(cmg[:], brow(cm_g[l:l + 1, :], 256), "cmg"); K.dma(cmb[:], cm_bt[l], "cmb")
            K.dma(wsf[:], cm_ws[l].rearrange("g t s -> t g s"), "wsf")
            K.memset("pool", cneg[:], -0.5)
            for gi in range(4):
                p_ = nb()
                K.tr(p_[:, 0:128], wsf[:, gi, :], identf[:])
                K.copy("dve", wsT[:, gi, :], p_[:, 0:128])

            xi = [0]
            deferred = []

            def dstore(out_, in__, key):
                deferred.append((out_, in__, key))

            def flush():
                for (o_, i_, k_) in deferred:
                    K.dma(o_, i_, k_)
                del deferred[:]

            def prep(g, s0, G):
                nt = G // 128
                v = 1 if g == 0 else 0
                if g in (0, 1):
                    K.dma(ab[:, 0, :], brow(mods[l, v, 1:2, :], D), "ab0")
                    K.dma(ab[:, 1, :], brow(mods[l, v, 0:1, :], D), "ab1")
                hT_ = hT[g % 2]
                for t in range(nt):
                    x_ = xg[xi[0] % 3]; xi[0] += 1
                    K.dma(x_[:], xsrc(l, s0 + t * 128, 128), "xg%d" % ((xi[0] - 1) % 3))
                    K.act(junk[:], x_[:], AF.Square, accum=st4[:, 0, t:t + 1])
                    K.rstd(st4[:, 2, t:t + 1], st4[:, 0, t:t + 1], D, st4[:, 1, t:t + 1])
                    K.stt(tmpf[:], x_[:], st4[:, 2, t:t + 1], ab[:, 0, :], ALU.mult, ALU.mult)
                    K.tt("dve", hb[t][:], tmpf[:], ab[:, 1, :], ALU.add)

            def prep_b(g, s0, G):
                nt = G // 128
                hT_ = hT[g % 2]
                for t in range(nt):
                    for k in range(8):
                        K.tr(pT[:, k * 128:(k + 1) * 128], hb[t][:, k * 128:(k + 1) * 128], identb[:])
                    K.copy("act", hT_[:, :, t * 128:(t + 1) * 128], pT[:].rearrange("p (k t) -> p k t", k=8))

            def body(g, s0, G, B=None, P=None):
                nt = G // 128
                hT_ = hT[g % 2]
                if B is not None:
                    gm_b1(*B)

                def fm(col0, m, wt=win, p_=None):
                    if p_ is None:
                        p_ = nb()
                    for k in range(8):
                        K.mm(p_[0:m, 0:G], wt[:, k, col0:col0 + m], hT_[:, k, 0:G], start=(k == 0), stop=(k == 7))
                    return p_

                pq = [fm(0, 128, p_=pf[0]), fm(128, 128, p_=pf[1])]
                pkv = fm(256, 128, p_=pf[2])
                for c in range(2):
                    K.act(sq[:, c, 0:G], pq[c][:, 0:G], AF.Square)
                K.act(sq[:, 2, 0:G], pkv[:, 0:G], AF.Square)
                if B is not None:
                    gm_b2(*B)
                pkr = fm(0, 128, wkr2)
                if g == 0:
                    K.copy("dve", krT2[:, s0:s0 + G], pkr[:, 0:G])
                else:
                    pkrr = fm(128, 128, wkr2)
                    K.dma(cosk[:], cosd[:, s0 - NCTX:s0 - NCTX + G], "cosk")
                    K.dma(sink[:], sind[:, s0 - NCTX:s0 - NCTX + G], "sink")
                    K.tt("dve", t1[:, 0:G], pkr[:, 0:G], cosk[:, 0:G], ALU.mult)
                    K.tt("dve", t2[:, 0:G], pkrr[:, 0:G], sink[:, 0:G], ALU.mult)
                    K.tt("pool", krT2[:, s0:s0 + G], t1[:, 0:G], t2[:, 0:G], ALU.add)
                for c in range(6):
                    p_ = fm(1216 + c * 128, 128)
                    K.copy("act", xbst[g % 2][:, c, 0:G], p_[:, 0:G])
                    if B is not None:
                        if c < 4:
                            gm_b3(*B, c)
                        if c >= 2:
                            gm_b4(*B, c - 2)
                dstore(xbcs.rearrange("(c p) n -> p c n", p=128)[:, :, xbcol(s0):xbcol(s0) + G], xbst[g % 2][:, :, 0:G], "xbst%d" % (g % 2))
                psq = nb()
                for c in range(2):
                    K.mm(psq[:, 0:G], onesb[:], sq[:, c, 0:G], start=(c == 0), stop=(c == 1))
                pskv = nb()
                K.mm(pskv[:, 0:G], onesb[:], sq[:, 2, 0:G])
                K.rstd(rst[0][:, 0:G], psq[:, 0:G], 256, lnq[:, 0:G])
                K.rstd(rst[1][:, 0:G], pskv[:, 0:G], 128, lnq[:, 0:G])
                qn_ = qn[g % 2]
                for c in range(2):
                    K.stt(qn_[:, c, 0:G], pq[c][:, 0:G], gq[:, c:c + 1], rst[0][:, 0:G], ALU.mult, ALU.mult)
                    dstore(qnTs[c * 128:(c + 1) * 128, s0:s0 + G], qn_[:, c, 0:G], "qn%d_%d" % (g % 2, c))
                K.stt(kvnT[:, s0:s0 + G], pkv[:, 0:G], gkv[:, 0:1], rst[1][:, 0:G], ALU.mult, ALU.mult)
                zdt_ = zdt[g % 2]
                xcm = xcms[g % 2]
                for t in range(nt):
                    tsl = slice(t * 128, (t + 1) * 128)
                    p_ = nb()
                    for k in range(8):
                        K.mm(p_[:, :], hT_[:, k, tsl], win[:, k, 448:960], start=(k == 0), stop=(k == 7))
                    K.copy("act", xcm[:, t, :], p_[:, :])
                    p2 = nb()
                    for k in range(8):
                        K.mm(p2[:, 0:256], hT_[:, k, tsl], win[:, k, 960:1216], start=(k == 0), stop=(k == 7))
                    for k in range(8):
                        K.mm(p2[:, 256:264], hT_[:, k, tsl], win[:, k, 1984:1992], start=(k == 0), stop=(k == 7))
                    K.copy("dve", zdt_[:, t, :], p2[:, 0:264])
                    if P is not None:
                        prep_tr(*P, t)
                K.act(xcm[:, 0:nt, :], xcm[:, 0:nt, :], AF.Gelu_apprx_tanh)
                K.act(zdt_[:, 0:nt, 0:256], zdt_[:, 0:nt, 0:256], AF.Silu)
                dstore(zdts[s0:s0 + G, :].rearrange("(t p) n -> p t n", p=128), zdt_[:, 0:nt, :], "zdt%d" % (g % 2))
                if B is not None:
                    gm_store(*B)

            def gm_b1(g, s0, G):
                nt = G // 128
                xcm = xcms[g % 2]
                for t in range(nt):
                    K.bn_stats(st6[:, t, :], xcm[:, t, 256:512])
                    K.bn_aggr(mv[:, t, :], st6[:, t, :])
                K.ts("dve", vpe[:, 0:nt], mv[:, 0:nt, 1], EPS, None, ALU.add)
                K.tt("pool", rscm[:, 0:nt], vpe[:, 0:nt], cneg[:, 0:nt], ALU.pow)

            def gm_b2(g, s0, G):
                xcm = xcms[g % 2]
                for t in range(G // 128):
                    K.ts("dve", vnf[t][:], xcm[:, t, 256:512], mv[:, t, 0:1], rscm[:, t:t + 1], ALU.subtract, ALU.mult)
                    K.tt("pool", vnb[t][:], vnf[t][:], cmg[:], ALU.mult)

            def gm_b3(g, s0, G, t):
                if t >= G // 128:
                    return
                xcm = xcms[g % 2]
                p_ = nb()
                for gi in range(4):
                    K.mm(p_[:, gi * 64:(gi + 1) * 64], wsT[:, gi, :], vnb[t][:, gi * 64:(gi + 1) * 64])
                for gi in range(4):
                    gs = slice(gi * 64, (gi + 1) * 64)
                    K.stt(ycm[t][:, gs], p_[:, gs], cmb[:, gi:gi + 1], xcm[:, t, gs], ALU.add, ALU.mult)

            def gm_b4(g, s0, G, t):
                if t >= G // 128:
                    return
                for c in range(2):
                    K.tr(pT[:, c * 128:(c + 1) * 128], ycm[t][:, c * 128:(c + 1) * 128], identb[:])
                K.copy("act", ycmT[g % 2][:, :, t * 128:(t + 1) * 128], pT[:, 0:256].rearrange("p (c t) -> p c t", c=2))

            def gm_store(g, s0, G):
                dstore(yTs[512:768, s0:s0 + G].rearrange("(c p) n -> p c n", p=128), ycmT[g % 2][:, :, 0:G], "ycmT%d" % (g % 2))

            def prep_tr(g, s0, G, t):
                if t >= G // 128:
                    return
                for k in range(8):
                    K.tr(pT[:, k * 128:(k + 1) * 128], hb[t][:, k * 128:(k + 1) * 128], identb[:])
                K.copy("act", hT[g % 2][:, :, t * 128:(t + 1) * 128], pT[:].rearrange("p (k t) -> p k t", k=8))

            prep(*GROUPS[0])
            prep_b(*GROUPS[0])
            NG = len(GROUPS)
            for i_, grp_ in enumerate(GROUPS):
                nxt = GROUPS[i_ + 1] if i_ + 1 < NG else None
                prv = GROUPS[i_ - 1] if i_ >= 1 else None
                if nxt is not None:
                    prep(*nxt)
                flush()
                body(*grp_, B=prv, P=nxt)
                if nxt is not None and grp_[2] // 128 < nxt[2] // 128:
                    for t in range(grp_[2] // 128, nxt[2] // 128):
                        prep_tr(*nxt, t)
            last = GROUPS[-1]
            gm_b1(*last); gm_b2(*last)
            for t in range(last[2] // 128):
                gm_b3(*last, t); gm_b4(*last, t)
            gm_store(*last)
            flush()
            K.barrier()

    def phase_a(l, kvnT, krT2):
        with contextlib.ExitStack() as es:
            sb = lambda n, sh, dt: es.enter_context(nc.sbuf_tensor(K.name(n), sh, dt))
            ps = lambda n, sh, dt: es.enter_context(nc.psum_tensor(K.name(n), sh, dt))
            KT = sb("KT", [128, 4, NS], BF16)
            Vaug = sb("Vaug", [128, NCH, 4, 130], BF16)
            wuq = sb("wuq", [128, 2, 1024], BF16)
            wukv = sb("wukv", [128, 1024], BF16)
            qn = [sb("qna%d" % i, [128, 2, 512], BF16) for i in range(2)]
            cs = [sb("csa%d" % i, [128, 512], F32) for i in range(2)]
            qh = [sb("qh%d" % i, [128, 512], BF16) for i in range(2)]
            qr = [sb("qr%d" % i, [128, 512], BF16) for i in range(2)]
            PT = [sb("PT%d" % i, [128, 512], BF16) for i in range(4)]
            yat = [sb("yat%d" % i, [128, 512], F32) for i in range(4)]
            yaT = [sb("yaT%d" % i, [128, 4, 512], BF16) for i in range(2)]
            rden = sb("rden", [128, 8], F32)
            acc = [ps("acc%d" % i, [128, 512], F32) for i in range(4)]
            psc = [ps("psc%d" % i, [128, 512], F32) for i in range(3)]
            pqu = ps("pqu", [128, 512], F32)
            K.dma(wuq[:], w_uqx[l].rearrange("(c p) n -> p c n", p=128), "wuq", eng="pool")
            K.dma(wukv[:], w_ukvr[l], "wukv", eng="pool")
            K.memset("pool", Vaug[:], 1.0)
            for j in range(32):
                K.dma(wf1s[l, j].rearrange("p (k c) -> p k c", k=8),
                      w_ff1[l].rearrange("(k p) n -> p k n", p=128)[:, :, j * 128:(j + 1) * 128], "wf1cast%d" % (j % 4), eng="pool")
            precast_o(l)
            if l + 1 < n_layers:
                precast_in(l + 1)
            for (g, s0, G) in GROUPS:
                for h in range(4):
                    p_ = psc[h % 2]
                    K.mm(p_[:, 0:G], wukv[:, h * 128:(h + 1) * 128], kvnT[:, s0:s0 + G])
                    K.copy("act" if h % 2 else "dve", KT[:, h, s0:s0 + G], p_[:, 0:G])
                for t in range(G // 128):
                    p_ = acc[t]
                    K.mm(p_[:, :], kvnT[:, s0 + t * 128:s0 + (t + 1) * 128], wukv[:, 512:1024])
                    K.copy("act" if t % 2 else "dve", Vaug[:, s0 // 128 + t, :, 0:128], p_[:, :].rearrange("p (h d) -> p h d", h=4))
            pti = [0]

            def aload(g, s0, G):
                K.dma(qn[g % 2][:, :, 0:G], qnTs[:, s0:s0 + G].rearrange("(c p) n -> p c n", p=128), "qna%d" % (g % 2))
                if g == 0:
                    K.dma(cs[g % 2][:, 0:G], csc[:, 0:G], "csa%d" % (g % 2))
                else:
                    K.dma(cs[g % 2][:, 0:G], csq[:, s0 - NCTX:s0 - NCTX + G], "csa%d" % (g % 2))

            def qup(g, G, h):
                qn_ = qn[g % 2]; cs_ = cs[g % 2]
                for c in range(2):
                    K.mm(pqu[:, 0:G], wuq[:, c, h * 256:h * 256 + 128], qn_[:, c, 0:G], start=(c == 0), stop=(c == 1))
                K.copy("dve", qh[h % 2][:, 0:G], pqu[:, 0:G])
                for c in range(2):
                    K.mm(pqu[:, 0:G], wuq[:, c, h * 256 + 128:h * 256 + 256], qn_[:, c, 0:G], start=(c == 0), stop=(c == 1))
                K.tt("dve", qr[h % 2][:, 0:G], pqu[:, 0:G], cs_[:, 0:G], ALU.mult)

            for (g, s0, G) in GROUPS:
                nt = G // 128
                kts = [0, 1] if g == 0 else list(range(NCH))
                qn_ = qn[g % 2]; cs_ = cs[g % 2]; yaT_ = yaT[g % 2]
                if g == 0:
                    aload(g, s0, G)
                if g + 1 < len(GROUPS):
                    aload(*GROUPS[g + 1])
                for h in range(4):
                    qh_ = qh[h % 2]; qr_ = qr[h % 2]
                    if h == 0:
                        qup(g, G, 0)
                    stash = {}

                    def score(i):
                        kt = kts[i]
                        ksl = slice(kt * 128, (kt + 1) * 128)
                        p_ = psc[pti[0] % 3]
                        P_ = PT[pti[0] % 4]
                        pti[0] += 1
                        K.mm(p_[:, 0:G], KT[:, h, ksl], qh_[:, 0:G], start=True, stop=False)
                        K.mm(p_[:, 0:G], krT2[:, ksl], qr_[:, 0:G], start=False, stop=True)
                        K.act(P_[:, 0:G], p_[:, 0:G], AF.Exp, scale=SCALE)
                        stash[i] = P_

                    score(0)
                    if len(kts) > 1:
                        score(1)
                    for i, kt in enumerate(kts):
                        if i + 2 < len(kts):
                            score(i + 2)
                        if h < 3 and i == max(0, len(kts) - 4):
                            qup(g, G, h + 1)
                        P_ = stash.pop(i)
                        for qt in range(nt):
                            K.mm(acc[qt][:, 0:129], P_[:, qt * 128:(qt + 1) * 128], Vaug[:, kt, h, 0:129],
                                 start=(i == 0), stop=(i == len(kts) - 1))
                    for qt in range(nt):
                        K.recip(rden[:, qt:qt + 1], acc[qt][:, 128:129])
                        K.ts("dve", yat[qt][:, h * 128:(h + 1) * 128], acc[qt][:, 0:128], rden[:, qt:qt + 1], None, ALU.mult)
                for qt in range(nt):
                    for h in range(4):
                        K.tr(pqu[:, h * 128:(h + 1) * 128], yat[qt][:, h * 128:(h + 1) * 128], identf[:])
                    K.copy("act", yaT_[:, :, qt * 128:(qt + 1) * 128], pqu[:, 0:512].rearrange("p (h t) -> p h t", h=4))
                K.dma(yTs[0:512, s0:s0 + G].rearrange("(h p) n -> p h n", p=128), yaT_[:, :, 0:G], "yaT%d" % (g % 2))
            K.barrier()

    def phase_s(l, m_next=None):
        with contextlib.ExitStack() as es:
            sb = lambda n, sh, dt: es.enter_context(nc.sbuf_tensor(K.name(n), sh, dt))
            ps = lambda n, sh, dt: es.enter_context(nc.psum_tensor(K.name(n), sh, dt))
            cw = sb("cw", [128, 18], F32); cbias = sb("cbias", [128, 6], F32)
            sm = sb("sm", [128, 24], F32); Abc = sb("Abc", [128, 8], F32); dsum = sb("dsum", [128, 4], F32)
            sgb = sb("sgb", [128, 256], F32)
            Sb_all = sb("Sb_all", [128, NCH, 256], F32)
            ypt = [sb("ypt%d" % i, [128, 256], F32) for i in range(2)]
            ypl = [sb("ypl%d" % i, [128, 256], F32) for i in range(3)]
            cmT_all = sb("cmT_all", [128, 2, NS], BF16)
            dfsb_all = sb("dfsb_all", [128, NCH, 4], F32); decb_all = sb("decb_all", [128, NCH, 4], F32)
            stf = sb("stf", [128, 256], F32); stfb = sb("stfb", [128, 256], BF16)
            stb = sb("stb", [128, 256], F32); stbb = sb("stbb", [128, 256], BF16)
            xr = [sb("xr%d" % i, [128, 6, 514], BF16) for i in range(2)]
            dg = sb("dg", [128, 18, 128], BF16)
            cvs = [sb("cv%d" % i, [128, 6, 512], F32) for i in range(2)]
            bcTs = [sb("bcT%d" % i, [128, 4, 512], BF16) for i in range(2)]
            dtr = [sb("dtr%d" % i, [128, 4, 8], F32) for i in range(2)]
            sps = [[sb("sp%d_%d" % (b_, i), [128, 4, 8], F32) for i in range(6)] for b_ in range(2)]
            E2 = [sb("E_%d" % i, [128, 40], F32) for i in range(2)]
            xs_tok2 = [sb("xs_tok_%d" % i, [128, 256], F32) for i in range(2)]
            bm_tok2 = [sb("bm_tok_%d" % i, [128, 2, 128], BF16) for i in range(2)]
            wde2 = [sb("wde_%d" % i, [128, 8], F32) for i in range(2)]
            xdt2 = [sb("xdt_%d" % i, [128, 2, 256], BF16) for i in range(2)]; xdte2 = [sb("xdte_%d" % i, [128, 2, 256], BF16) for i in range(2)]
            Lm2 = [sb("Lm_%d" % i, [128, 8, 128], F32) for i in range(2)]; eL2 = [sb("eL_%d" % i, [128, 8, 128], F32) for i in range(2)]
            GTm2 = [sb("GTm_%d" % i, [128, 2, 2, 128], F32) for i in range(2)]; W2 = [sb("W_%d" % i, [128, 8, 128], BF16) for i in range(2)]
            t1_2 = [sb("t1s_%d" % i, [128, 256], F32) for i in range(2)]; t2_2 = [sb("t2s_%d" % i, [128, 256], F32) for i in range(2)]
            sz = [sb("sz%d" % i, [128, 256], F32) for i in range(3)]
            yb2 = [sb("yb%d" % i, [128, 256], BF16) for i in range(2)]; junk = sb("junks", [128, 256], BF16)
            st3 = sb("st3", [128, 6], F32)
            ysT = [sb("ysT%d" % i, [128, 2, 512], BF16) for i in range(2)]
            pcx = ps("pcx", [128, 512], F32)
            pconv = ps("pconv", [128, 512], F32)
            pseg = ps("pseg", [128, 1024], F32)
            pbm = ps("pbm", [128, D], BF16)
            pG = ps("pG", [128, 512], F32)
            pst = ps("pst", [128, 512], F32)
            pyo = ps("pyo", [128, 512], F32)

            K.dma(cw[:], convw[l], "cw"); K.dma(cbias[:], convb[l], "cbias")
            K.dma(sm[:], brow(ssd_sm[l:l + 1, :], 24), "sm")
            K.dma(sgb[:], brow(ssd_g[l:l + 1, :], 256), "sgb")
            K.act(Abc[:], sm[:, 8:16], AF.Exp)
            K.ts("dve", Abc[:], Abc[:], -1.0, None, ALU.mult)
            K.tt("dve", dsum[:], sm[:, 16:20], sm[:, 20:24], ALU.add)
            for i_ in range(18):
                K.ts("pool", dg[:, i_, :], identb[:], cw[:, i_:i_ + 1], 0.0, ALU.mult, ALU.add)
            K.memset("pool", stf[:], 0.0); K.memset("pool", stfb[:], 0.0)
            K.memset("pool", stb[:], 0.0); K.memset("pool", stbb[:], 0.0)
            v4 = lambda ap: ap.rearrange("p (h d) -> p h d", h=4)

            def prologue_parts(g, s0, G):
                nt = G // 128
                xr_ = xr[g % 2]; dtr_ = dtr[g % 2]; cv = cvs[g % 2]; bcT = bcTs[g % 2]; sp_ = sps[g % 2]
                c0 = xbcol(s0)

                def conv(c):
                    for tap in range(3):
                        K.mm(pconv[:, 0:G], dg[:, c * 3 + tap, :], xr_[:, c, tap:tap + G], start=(tap == 0), stop=(tap == 2))
                    K.act(cv[:, c, 0:G], pconv[:, 0:G], AF.Identity, bias=cbias[:, c:c + 1])

                def p0():
                    K.dma(xr_[:, :, 0:G + 2], xbcs.rearrange("(c p) n -> p c n", p=128)[:, :, c0 - 1:c0 + G + 1], "xr%d" % (g % 2))
                    K.dma(dtr_[:, 0:nt, :], zdts[s0:s0 + G, 256:264].rearrange("(t p) n -> p t n", p=128), "dtr%d" % (g % 2))
                    xsp, nx, mn, lg, dt, a = [t_[:, 0:nt, :] for t_ in sp_]
                    K.tt("dve", xsp, dtr_[:, 0:nt, :], bcmid(sm[:, 0:8], nt), ALU.add)
                    K.ts("dve", nx, xsp, -1.0, None, ALU.mult)
                    K.tt("dve", mn, xsp, nx, ALU.min)
                    K.act(nx, mn, AF.Exp)
                    K.act(lg, nx, AF.Ln, bias=1.0)
                    K.stt(dt, xsp, 0.0, lg, ALU.max, ALU.add)
                    K.tt("dve", a, dt, bcmid(Abc[:], nt), ALU.mult)
                    conv(0)

                def p1():
                    conv(1); conv(2)

                def p2():
                    conv(3); conv(4)

                def p3():
                    conv(5)
                    K.act(cv[:, 0:2, 0:G], cv[:, 0:2, 0:G], AF.Silu)
                    K.act(bcT[:, :, 0:G], cv[:, 2:6, 0:G], AF.Silu)
                    K.copy("pool", cmT_all[:, :, s0:s0 + G], bcT[:, 2:4, 0:G])
                return [p0, p1, p2, p3]

            def early(g, s0, G, t):
                cv = cvs[g % 2]; bcT = bcTs[g % 2]; sp_ = sps[g % 2]
                if True:
                    ci = s0 // 128 + t
                    pb = ci % 2
                    E = E2[pb]; xs_tok = xs_tok2[pb]; bm_tok = bm_tok2[pb]; wde = wde2[pb]; xdt = xdt2[pb]; xdte = xdte2[pb]
                    Lm = Lm2[pb]; eL = eL2[pb]; GTm = GTm2[pb]; W = W2[pb]; t1 = t1_2[pb]; t2 = t2_2[pb]
                    tsl = slice(t * 128, (t + 1) * 128)
                    a_t = sp_[5][:, t, :]; dt_t = sp_[4][:, t, :]
                    for i, lt in enumerate([trit[:, 0, :], trit[:, 3, :], trit[:, 1, :], trit[:, 2, :], onesf[:]]):
                        K.mm(pcx[:, i * 8:(i + 1) * 8], lt, a_t)
                    K.act(E[:], pcx[:, 0:40], AF.Exp)
                    K.copy("pool", dfsb_all[:, ci, :], E[:, 28:32])
                    K.copy("pool", decb_all[:, ci, :], E[:, 36:40])
                    for c in range(2):
                        K.tr(pcx[:, 128 + c * 128:128 + (c + 1) * 128], cv[:, c, tsl], identf[:])
                    K.copy("act", xs_tok[:], pcx[:, 128:384])
                    for grp in range(2):
                        K.tr(pbm[:, grp * 128:(grp + 1) * 128], bcT[:, grp, tsl], identb[:])
                    K.copy("act", bm_tok[:].rearrange("p a b -> p (a b)"), pbm[:, 0:256])
                    K.tt("dve", wde[:, 0:4], dt_t[:, 0:4], E[:, 8:12], ALU.mult)
                    K.tt("dve", wde[:, 4:8], dt_t[:, 4:8], E[:, 20:24], ALU.mult)
                    for d in range(2):
                        K.tt("dve", v4(xdt[:, d, :]), v4(xs_tok[:]), bc3(dt_t[:, d * 4:(d + 1) * 4], 64), ALU.mult)
                        K.tt("pool", v4(xdte[:, d, :]), v4(xs_tok[:]), bc3(wde[:, d * 4:(d + 1) * 4], 64), ALU.mult)
                    for j in range(8):
                        d = j // 4
                        K.act(Lm[:, j, :], trit[:, 3 if d == 0 else 1, :], AF.Identity, scale=a_t[:, j:j + 1])
                        K.mm(pseg[:, j * 128:(j + 1) * 128], Lm[:, j, :], trit[:, 0 if d == 0 else 2, :])
                    K.act(eL[:].rearrange("p a b -> p (a b)"), pseg[:], AF.Exp)
                    for grp in range(2):
                        K.mm(pG[:, grp * 128:(grp + 1) * 128], bcT[:, grp, tsl], bcT[:, 2 + grp, tsl])
                    for d in range(2):
                        K.tt("dve", GTm[:, d, :, :], pG[:, 0:256].rearrange("p (a b) -> p a b", a=2),
                             bcmid(trit[:, 0 if d == 0 else 2, :], 2), ALU.mult)
                    for d in range(2):
                        for grp in range(2):
                            j0 = d * 4 + grp * 2
                            K.tt("dve", W[:, j0:j0 + 2, :], eL[:, j0:j0 + 2, :], bcmid(GTm[:, d, grp, :], 2), ALU.mult)

            def late(g, s0, G, t):
                cv = cvs[g % 2]; bcT = bcTs[g % 2]; sp_ = sps[g % 2]
                if True:
                    ci = s0 // 128 + t
                    pb = ci % 2
                    E = E2[pb]; xs_tok = xs_tok2[pb]; bm_tok = bm_tok2[pb]; wde = wde2[pb]; xdt = xdt2[pb]; xdte = xdte2[pb]
                    Lm = Lm2[pb]; eL = eL2[pb]; GTm = GTm2[pb]; W = W2[pb]; t1 = t1_2[pb]; t2 = t2_2[pb]
                    tsl = slice(t * 128, (t + 1) * 128)
                    a_t = sp_[5][:, t, :]; dt_t = sp_[4][:, t, :]
                    for h in range(4):
                        hs = slice(h * 64, (h + 1) * 64)
                        for d in range(2):
                            K.mm(pyo[:, 256 + h * 64:256 + (h + 1) * 64], W[:, d * 4 + h, :], xdt[:, d, hs], start=(d == 0), stop=(d == 1))
                    for j in range(8):
                        d, h = j // 4, j % 4
                        K.mm(pst[:, j * 64:(j + 1) * 64], bm_tok[:, h // 2, :], xdte[:, d, h * 64:(h + 1) * 64])
                    for h in range(4):
                        hs = slice(h * 64, (h + 1) * 64)
                        K.mm(pyo[:, hs], bcT[:, 2 + h // 2, tsl], stfb[:, hs])
                    K.tt("dve", v4(stf[:]), v4(stf[:]), bc3(E[:, 32:36], 64), ALU.mult)
                    K.tt("dve", stf[:], pst[:, 0:256], stf[:], ALU.add)
                    K.copy("act", stfb[:], stf[:])
                    K.copy("act", Sb_all[:, ci, :], pst[:, 256:512])
                    K.tt("dve", v4(t1[:]), v4(pyo[:, 0:256]), bc3(E[:, 0:4], 64), ALU.mult)
                    K.tt("dve", t1[:], pyo[:, 256:512], t1[:], ALU.add)
                    K.tt("pool", v4(t2[:]), v4(xs_tok[:]), bc3(dsum[:], 64), ALU.mult)
                    K.tt("pool", ypt[pb][:], t1[:], t2[:], ALU.add)
                    K.dma(ypS[ci * 128:(ci + 1) * 128, :], ypt[pb][:], "ypt%d" % pb, eng="pool")


            allc = [(g, s0, G, t) for (g, s0, G) in GROUPS for t in range(G // 128)]
            for p_ in prologue_parts(*GROUPS[0]):
                p_()
            early(*allc[0])
            for k_, cur in enumerate(allc):
                g_, s0_, G_, t_ = cur
                if k_ + 1 < len(allc):
                    early(*allc[k_ + 1])
                late(*cur)
                nt_c = G_ // 128
                if g_ + 1 < len(GROUPS) and t_ < nt_c - 1:
                    parts = prologue_parts(*GROUPS[g_ + 1])
                    per = (len(parts) + nt_c - 2) // (nt_c - 1)
                    for p_ in parts[t_ * per:(t_ + 1) * per]:
                        p_()

            order = [1, 0] + list(range(NCH - 1, 1, -1))
            if m_next is not None:
                m_init, m_load, m_compute = m_steps(m_next, sb, pst)
                m_init()
                m_load(0)
            def stA(n_):
                ci = order[n_]
                csl = slice(ci * 128, (ci + 1) * 128)
                if n_ == 0:
                    for m_ in range(2):
                        K.dma(sz[m_ % 3][:], zdts[order[m_] * 128:(order[m_] + 1) * 128, 0:256], "sz%d" % (m_ % 3))
                        K.dma(ypl[m_ % 3][:], ypS[order[m_] * 128:(order[m_] + 1) * 128, :], "ypl%d" % (m_ % 3))
                if n_ + 2 < len(order):
                    K.dma(sz[(n_ + 2) % 3][:], zdts[order[n_ + 2] * 128:(order[n_ + 2] + 1) * 128, 0:256], "sz%d" % ((n_ + 2) % 3))
                    K.dma(ypl[(n_ + 2) % 3][:], ypS[order[n_ + 2] * 128:(order[n_ + 2] + 1) * 128, :], "ypl%d" % ((n_ + 2) % 3))
                for h in range(4):
                    hs = slice(h * 64, (h + 1) * 64)
                    K.mm(pyo[:, hs], cmT_all[:, h // 2, csl], stbb[:, hs])
                if m_next is not None and n_ < 24:
                    if n_ + 1 < 24:
                        m_load(n_ + 1)
                    m_compute(n_)
                K.tt("dve", v4(stb[:]), v4(stb[:]), bc3(decb_all[:, ci, :], 64), ALU.mult)
                K.tt("dve", stb[:], stb[:], Sb_all[:, ci, :], ALU.add)
                K.copy("pool", stbb[:], stb[:])

            def stB(n_):
                ci = order[n_]
                t1 = t1_2[n_ % 2]; yb = yb2[n_ % 2]
                K.tt("dve", v4(t1[:]), v4(pyo[:, 0:256]), bc3(dfsb_all[:, ci, :], 64), ALU.mult)
                K.tt("dve", t1[:], t1[:], ypl[n_ % 3][:], ALU.add)
                K.tt("dve", t1[:], t1[:], sz[n_ % 3][:], ALU.mult)
                K.act(junk[:], t1[:], AF.Square, accum=st3[:, 3 * (n_ % 2):3 * (n_ % 2) + 1])
                K.rstd(st3[:, 3 * (n_ % 2) + 2:3 * (n_ % 2) + 3], st3[:, 3 * (n_ % 2):3 * (n_ % 2) + 1], 256, st3[:, 3 * (n_ % 2) + 1:3 * (n_ % 2) + 2])
                K.stt(yb[:], t1[:], st3[:, 3 * (n_ % 2) + 2:3 * (n_ % 2) + 3], sgb[:], ALU.mult, ALU.mult)

            def stC(n_):
                ci = order[n_]
                yb = yb2[n_ % 2]
                if ci < 2:
                    buf, slot, last = ysT[0], ci, (ci == 0)
                else:
                    gidx = (ci - 2) // 4
                    buf, slot, last = ysT[(gidx + 1) % 2], (ci - 2) % 4, ((ci - 2) % 4 == 0)
                for c in range(2):
                    K.tr(pbm[:, c * 128:(c + 1) * 128], yb[:, c * 128:(c + 1) * 128], identb[:])
                K.copy("act", buf[:, :, slot * 128:(slot + 1) * 128], pbm[:, 0:256].rearrange("p (c t) -> p c t", c=2))
                if last:
                    if ci < 2:
                        K.dma(yTs[768:1024, 0:256].rearrange("(c p) n -> p c n", p=128), buf[:, :, 0:256], "ysT0")
                    else:
                        K.dma(yTs[768:1024, 256 + gidx * 512:256 + (gidx + 1) * 512].rearrange("(c p) n -> p c n", p=128),
                              buf[:, :, :], "ysT%d" % ((gidx + 1) % 2))

            NO = len(order)
            stA(0)
            for n_ in range(NO):
                stB(n_)
                if n_ + 1 < NO:
                    stA(n_ + 1)
                stC(n_)
            K.barrier()

    def phase_o(l):
        with contextlib.ExitStack() as es:
            sb = lambda n, sh, dt: es.enter_context(nc.sbuf_tensor(K.name(n), sh, dt))
            ps = lambda n, sh, dt: es.enter_context(nc.psum_tensor(K.name(n), sh, dt))
            wout = sb("wout", [128, 8, D], BF16)
            wff2 = sb("wff2", [128, 32, D], BF16)
            w1r = [sb("w1r%d" % i, [128, 8, 128], BF16) for i in range(6)]
            md = sb("md", [128, 4, D], F32)
            yT = sb("yTo", [128, 8, 512], BF16)
            xt = [sb("xt%d" % i, [128, D], F32) for i in range(5)]
            g2c = sb("g2c", [128, D], F32)
            tmpf = sb("tmpo", [128, D], F32)
            junk = sb("junko", [128, D], BF16)
            hb = [sb("hbo%d" % i, [128, D], BF16) for i in range(2)]
            h2T = [sb("h2T%d" % i, [128, 8, 512], BF16) for i in range(2)]
            rl = [sb("rl%d" % i, [128, 512], BF16) for i in range(2)]
            aT = sb("aT", [128, 32, 512], BF16)
            st = sb("sto", [128, 3, 16], F32)
            pT = ps("pTo", [128, D], BF16)
            pa = [ps("pa%d" % i, [128, 512], F32) for i in range(7)]
            bank = [0]

            def nb():
                bank[0] += 1
                return pa[bank[0] % 7]

            K.dma(wout[:], wos[l].rearrange("(k p) n -> p k n", p=128), "wout")
            xi = [0]
            w1i = [0]
            sti = [0]

            def newx():
                x_ = xt[xi[0] % 5]; xkey = "xt%d" % (xi[0] % 5); xi[0] += 1
                return x_, xkey

            def post(pp, gcol, x_, gt_=None):
                si = sti[0] % 16; sti[0] += 1
                K.act(junk[:, 0:512], pp[0][:, :], AF.Square, accum=st[:, 0, si:si + 1])
                K.act(junk[:, 512:1024], pp[1][:, :], AF.Square, accum=st[:, 1, si:si + 1])
                K.tt("dve", st[:, 0, si:si + 1], st[:, 0, si:si + 1], st[:, 1, si:si + 1], ALU.add)
                K.rstd(st[:, 2, si:si + 1], st[:, 0, si:si + 1], D, st[:, 1, si:si + 1])
                for half in range(2):
                    hs = slice(half * 512, (half + 1) * 512)
                    K.stt(tmpf[:, hs], pp[half][:, :], st[:, 2, si:si + 1], (md[:, gcol, hs] if gt_ is None else gt_[:, hs]), ALU.mult, ALU.mult)
                K.tt("pool", x_[:], x_[:], tmpf[:], ALU.add)

            pend = {}

            def prep_load(g, s0, G):
                nt = G // 128
                v = 1 if g == 0 else 0
                if g in (0, 1):
                    for i, m in enumerate((2, 4, 3, 5)):
                        K.dma(md[:, i, :], brow(mods[l, v, m:m + 1, :], D), "md%d" % i)
                    if g == 0:
                        K.dma(g2c[:], brow(mods[l, 1, 5:6, :], D), "g2c")
                K.dma(yT[:, :, 0:G], yTs[:, s0:s0 + G].rearrange("(k p) n -> p k n", p=128), "yTo")
                xs_ = []
                for t in range(nt):
                    x_, xkey = newx()
                    xs_.append((x_, xkey))
                    K.dma(x_[:], xsrc(l, s0 + t * 128, 128), xkey)
                pend[g] = xs_

            def prep_op(g, s0, G, t):
                tsl = slice(t * 128, (t + 1) * 128)
                x_, xkey = pend[g][t]
                pp = [nb(), nb()]
                for half in range(2):
                    for k in range(8):
                        K.mm(pp[half][:, :], yT[:, k, tsl], wout[:, k, half * 512:(half + 1) * 512], start=(k == 0), stop=(k == 7))
                post(pp, 0, x_)
                K.dma(xscr[s0 + t * 128:s0 + (t + 1) * 128, :], x_[:], xkey, eng="pool")
                si2 = sti[0] % 16; sti[0] += 1
                K.act(junk[:], x_[:], AF.Square, accum=st[:, 0, si2:si2 + 1])
                K.rstd(st[:, 2, si2:si2 + 1], st[:, 0, si2:si2 + 1], D, st[:, 1, si2:si2 + 1])
                K.stt(tmpf[:], x_[:], st[:, 2, si2:si2 + 1], md[:, 1, :], ALU.mult, ALU.mult)
                K.tt("dve", hb[t % 2][:], tmpf[:], md[:, 2, :], ALU.add)

            def prep_tr(g, s0, G, t):
                tsl = slice(t * 128, (t + 1) * 128)
                hb_ = hb[t % 2]
                for k in range(8):
                    K.tr(pT[:, k * 128:(k + 1) * 128], hb_[:, k * 128:(k + 1) * 128], identb[:])
                K.copy("act", h2T[g % 2][:, :, tsl], pT[:].rearrange("p (k t) -> p k t", k=8))

            def ff1(g, s0, G, j0, j1):
                for j in range(j0, j1):
                    w_ = w1r[w1i[0] % 6]
                    K.dma(w_[:], wf1s[l, j].rearrange("p (k c) -> p k c", k=8), "w1r%d" % (w1i[0] % 6))
                    w1i[0] += 1
                    p_ = nb()
                    for k in range(8):
                        K.mm(p_[:, 0:G], w_[:, k, :], h2T[g % 2][:, k, 0:G], start=(k == 0), stop=(k == 7))
                    r_ = rl[j % 2]
                    K.act(r_[:, 0:G], p_[:, 0:G], AF.Relu)
                    K.tt("dve", aT[:, j, 0:G], r_[:, 0:G], p_[:, 0:G], ALU.mult)

            def ff2(g, s0, G, t):
                if True:
                    tsl = slice(t * 128, (t + 1) * 128)
                    x_, xkey = newx()
                    K.dma(x_[:], xscr[s0 + t * 128:s0 + (t + 1) * 128, :], xkey)
                    pp = [nb(), nb()]
                    for half in range(2):
                        for j in range(32):
                            K.mm(pp[half][:, :], aT[:, j, tsl], wff2[:, j, half * 512:(half + 1) * 512], start=(j == 0), stop=(j == 31))
                    post(pp, 3, x_, g2c if g == 0 else None)
                    K.dma(xdst(l, s0 + t * 128, 128), x_[:], xkey, eng="pool")

            g0 = GROUPS[0]
            prep_load(*g0)
            for jj in range(4):
                K.dma(wff2[:, jj * 8:(jj + 1) * 8, :], wf2s[l, jj * 1024:(jj + 1) * 1024, :].rearrange("(k p) n -> p k n", p=128),
                      "wff2_%d" % jj, eng="act")
            for t in range(g0[2] // 128):
                prep_op(*g0, t)
                prep_tr(*g0, t)
            for i_, cur in enumerate(GROUPS):
                nxt = GROUPS[i_ + 1] if i_ + 1 < len(GROUPS) else None
                nt_c = cur[2] // 128
                if nxt is not None:
                    prep_load(*nxt)
                for q_ in range(4):
                    ff1(*cur, q_ * 8, (q_ + 1) * 8)
                    if nxt is not None:
                        prep_op(*nxt, q_)
                        if q_ >= 1:
                            prep_tr(*nxt, q_ - 1)
                ff2(*cur, 0)
                if nxt is not None:
                    prep_tr(*nxt, 3)
                for t in range(1, nt_c):
                    ff2(*cur, t)
            K.barrier()

    for l in range(n_layers):
        if l == 0:
            phase_m(l)
            zero_pads()
        with contextlib.ExitStack() as esr:
            kvnT = esr.enter_context(nc.sbuf_tensor(K.name("kvnT"), [128, NS], BF16))
            krT2 = esr.enter_context(nc.sbuf_tensor(K.name("krT2"), [128, NS], BF16))
            phase_1(l, kvnT, krT2)
            phase_a(l, kvnT, krT2)
        phase_s(l, l + 1 if l + 1 < n_layers else None)
        phase_o(l)
    K.barrier()
    with contextlib.ExitStack() as es2:
        n = S.emit(es2)
    es0.close()
    return nc, n


def _rope_tables():
    rows_n = SEQ // 64
    row = np.repeat(np.arange(rows_n, dtype=np.float32), 64)
    col = np.tile(np.arange(64, dtype=np.float32), rows_n)
    inv = (np.float32(10000.0) ** (-np.arange(0, 32, 2, dtype=np.float32) / np.float32(32))).astype(np.float32)
    ar = (row[:, None] * inv).astype(np.float32)
    ac = (col[:, None] * inv).astype(np.float32)
    cos = np.zeros((64, SEQ), np.float32); sin = np.zeros((64, SEQ), np.float32)
    for d in range(64):
        ang = ar if d < 32 else ac
        f = d % 16
        cos[d] = np.cos(ang[:, f])
        sgn = -1.0 if (d % 32) < 16 else 1.0
        sin[d] = sgn * np.sin(ang[:, f])
    return cos, sin


_PERM = np.array([d + 16 if (d % 32) < 16 else d - 16 for d in range(64)])


def _host_layout(inp):
    f = lambda a: np.ascontiguousarray(a, dtype=np.float32)
    sh = {}
    sh["w_ada"] = f(inp["w_ada"]); sh["b_ada"] = f(inp["b_ada"])
    sh["g4"] = f(np.concatenate([inp["g_pre_mix"], inp["g_post_mix"], inp["g_pre_ff"], inp["g_post_ff"]], axis=1))
    w_in = inp["w_in"]
    sh["w_in"] = f(w_in)
    kr = w_in[:, :, 384:448]
    rot = kr[:, :, _PERM]
    sh["w_kr2"] = f(np.concatenate([kr, kr, rot, rot], axis=2))
    sh["gq_pc"] = f(inp["g_q"].reshape(NL, 2, 128).transpose(0, 2, 1))
    sh["gkv_p"] = f(inp["g_kv"].reshape(NL, 128, 1))
    wuq = inp["w_uq"].reshape(NL, 256, 4, 192)
    sh["w_uqx"] = f(np.concatenate([wuq, wuq[:, :, :, 128 + _PERM]], axis=3).reshape(NL, 256, 1024))
    wukv = inp["w_ukv"].reshape(NL, 128, 4, 256)
    sh["w_ukvr"] = f(np.concatenate([wukv[:, :, :, :128].reshape(NL, 128, 512), wukv[:, :, :, 128:].reshape(NL, 128, 512)], axis=2))
    sh["cm_g"] = f(inp["cm_norm_g"]); sh["cm_ws"] = f(inp["cm_w_s"])
    sh["cm_bt"] = f(inp["cm_b_s"].transpose(0, 2, 1))
    cw = inp["ssd_conv_w"].reshape(NL, 3, 6, 128).transpose(0, 3, 2, 1)
    sh["convw"] = f(cw.reshape(NL, 128, 18))
    sh["convb"] = f(inp["ssd_conv_b"].reshape(NL, 6, 128).transpose(0, 2, 1))
    sh["ssd_sm"] = f(np.concatenate([inp["ssd_dt_bias"].reshape(NL, 8), inp["ssd_a_log"].reshape(NL, 8), inp["ssd_d"].reshape(NL, 8)], axis=1))
    sh["ssd_g"] = f(inp["ssd_norm_g"])
    sh["w_out"] = f(inp["w_out"]); sh["w_ff1"] = f(inp["w_ff1"]); sh["w_ff2"] = f(inp["w_ff2"])
    cos, sin = _rope_tables()
    sh["cosd"] = f(np.concatenate([cos, cos], axis=0)); sh["sind"] = f(np.concatenate([sin, sin], axis=0))
    sh["csq"] = f(np.concatenate([cos, sin], axis=0))
    sh["csc"] = f(np.concatenate([np.ones((64, 512), np.float32), np.zeros((64, 512), np.float32)], axis=0))
    k = np.arange(128)[:, None]; l_ = np.arange(128)[None, :]
    tri = np.stack([(k <= l_), (k < l_), (k >= l_), (k > l_)], axis=1).astype(np.float32)
    sh["tri"] = f(tri.reshape(128, 512))
    sh["cc_pk"] = f(inp["c_ctx"].reshape(8, 128).T)
    return sh


_CACHE = {}


def kernel(**inputs):
    inp = {k: np.asarray(v) for k, v in inputs.items()}
    if "nc" not in _CACHE:
        _CACHE["nc"] = build()[0]
    nc = _CACHE["nc"]
    shared = _host_layout(inp)
    in_maps = []
    for b in range(8):
        m = dict(shared)
        m["x"] = np.ascontiguousarray(inp["x"][b], dtype=np.float32)
        m["ctx"] = np.ascontiguousarray(inp["ctx"][b], dtype=np.float32)
        m["c_pk"] = np.ascontiguousarray(inp["c"][b].reshape(8, 128).T, dtype=np.float32)
        in_maps.append(m)
    res = run_bass_kernel_spmd(nc, in_maps, core_ids=list(range(8)))
    return np.stack([np.asarray(r["out"], dtype=np.float32) for r in res.results], axis=0)
```
